# Optimizing a Trainium2 kernel written in Bass

```python
import jax
import jax.numpy as jnp
from jax import lax
import numpy as np

D_MODEL = 1024
BATCH = 8
SEQ = 4096
DEPTH = 1

HEAD_DIM = 64
RWKV_HEADS = 8
RWKV_WIDTH = RWKV_HEADS * HEAD_DIM
MOBA_HEADS = 8
MOBA_WIDTH = MOBA_HEADS * HEAD_DIM
DECAY_LORA = 64
AAA_LORA = 64
GATE_LORA = 160
MOBA_BLOCK = 256
MOBA_TOPK = 3
Q_CHUNK = 64
ROT_DIM = HEAD_DIM // 4
ROPE_THETA = 500000.0
D_FF = 2816
CONV_WIDTH = 3
NORM_EPS = 1e-6
GN_EPS = 64e-5
N_BRANCH = 2
RWKV_IN = 3 * RWKV_WIDTH + DECAY_LORA + AAA_LORA + GATE_LORA
MOBA_IN = 3 * MOBA_WIDTH
GATE_IN = N_BRANCH * D_MODEL
IN_COLS = RWKV_IN + MOBA_IN + GATE_IN

kernel_name = 'hybrid_rwkv7_moba_convffn_block'


def rms_norm(x, g):
    xf = x.astype(jnp.float32)
    y = xf * lax.rsqrt(jnp.mean(xf * xf, axis=-1, keepdims=True) + NORM_EPS)
    return (y * g.astype(jnp.float32)).astype(x.dtype)


def shift_right(z):
    return jnp.pad(z, ((0, 0), (1, 0), (0, 0)))[:, :-1]


def partial_rope(x, positions):
    half = ROT_DIM // 2
    inv_freq = ROPE_THETA ** (-jnp.arange(half, dtype=jnp.float32) / half)
    ang = positions.astype(jnp.float32)[..., None] * inv_freq
    cos = jnp.cos(ang)[:, :, None, :]
    sin = jnp.sin(ang)[:, :, None, :]
    xr = x[..., :ROT_DIM].astype(jnp.float32)
    x1, x2 = xr[..., :half], xr[..., half:]
    rot = jnp.concatenate([x1 * cos - x2 * sin, x2 * cos + x1 * sin], axis=-1)
    return jnp.concatenate([rot.astype(x.dtype), x[..., ROT_DIM:]], axis=-1)


def rwkv7_time_mix(z, w_decay_up, decay_bias, w_aaa_up, aaa_bias, w_gate_up,
                   k_k, k_a, r_k, ln_g, ln_b):
    B, S, _ = z.shape
    H, N, C = RWKV_HEADS, HEAD_DIM, RWKV_WIDTH
    f32 = jnp.float32
    o3 = 3 * C
    o4 = o3 + DECAY_LORA
    o5 = o4 + AAA_LORA
    r, k, v = z[..., :C], z[..., C:2 * C], z[..., 2 * C:o3]
    xw, xa, xg = z[..., o3:o4], z[..., o4:o5], z[..., o5:]
    w_log = -jax.nn.softplus(-(decay_bias + jnp.tanh(xw) @ w_decay_up)) - 0.5
    decay = jnp.exp(-jnp.exp(w_log.astype(f32)))
    a = jax.nn.sigmoid(aaa_bias + xa @ w_aaa_up)
    g = jax.nn.sigmoid(xg) @ w_gate_up
    kk = (k * k_k).astype(f32).reshape(B, S, H, N)
    kk = kk / jnp.maximum(jnp.sqrt(jnp.sum(kk * kk, axis=-1, keepdims=True)), 1e-12)
    k = k * (1.0 + (a - 1.0) * k_a)

    def time_major(t):
        return t.astype(f32).reshape(B, S, H, N).transpose(1, 0, 2, 3)

    kk_t = kk.transpose(1, 0, 2, 3)
    inputs = (time_major(r), time_major(decay), time_major(k), time_major(v),
              -kk_t, kk_t * time_major(a))

    def step(state, inp):
        r_t, w_t, k_t, v_t, a_t, b_t = inp
        sa = jnp.einsum('bhvk,bhk->bhv', state, a_t)
        state = (state * w_t[:, :, None, :] + sa[..., None] * b_t[:, :, None, :]
                 + v_t[..., None] * k_t[:, :, None, :])
        return state, jnp.einsum('bhvk,bhk->bhv', state, r_t)

    state0 = jnp.zeros((B, H, N, N), f32)
    _, y = lax.scan(step, state0, inputs)
    y = y.transpose(1, 0, 2, 3)
    mu = jnp.mean(y, axis=-1, keepdims=True)
    var = jnp.mean(jnp.square(y - mu), axis=-1, keepdims=True)
    y = ((y - mu) * lax.rsqrt(var + GN_EPS)).reshape(B, S, C)
    y = y * ln_g.astype(f32) + ln_b.astype(f32)
    rh = r.astype(f32).reshape(B, S, H, N)
    kh = k.astype(f32).reshape(B, S, H, N)
    vh = v.astype(f32).reshape(B, S, H, N)
    bonus = (jnp.sum(rh * kh * r_k.astype(f32), axis=-1, keepdims=True) * vh).reshape(B, S, C)
    return ((y + bonus) * g.astype(f32)).astype(z.dtype)


def moba_attention(q, k, v):
    B, S, H, Dh = q.shape
    f32 = jnp.float32
    n_blk = -(-S // MOBA_BLOCK)
    S_pad = n_blk * MOBA_BLOCK
    pad = ((0, 0), (0, S_pad - S), (0, 0), (0, 0))
    qt = jnp.pad(q, pad).transpose(0, 2, 1, 3)
    kt = jnp.pad(k, pad).transpose(0, 2, 1, 3)
    vt = jnp.pad(v, pad).transpose(0, 2, 1, 3)
    kb = kt.reshape(B, H, n_blk, MOBA_BLOCK, Dh)
    vb = vt.reshape(B, H, n_blk, MOBA_BLOCK, Dh)
    k_mean = jnp.mean(kb.astype(f32), axis=3)
    gate = jnp.einsum('bhsd,bhnd->bhsn', qt.astype(f32), k_mean)
    q_blk = jnp.arange(S_pad) // MOBA_BLOCK
    past = jnp.arange(n_blk)[None, :] < q_blk[:, None]
    gate = jnp.where(past, gate, -jnp.inf)
    n_sel = min(MOBA_TOPK, n_blk)
    _, sel = lax.top_k(gate, n_sel)
    slot_ok = jnp.arange(n_sel)[None, :] < q_blk[:, None]
    b_idx = jnp.arange(B)[:, None, None, None]
    h_idx = jnp.arange(H)[None, :, None, None]
    scale = Dh ** -0.5
    q_off = jnp.arange(Q_CHUNK)
    k_off = jnp.arange(MOBA_BLOCK)
    n_keys_sel = n_sel * MOBA_BLOCK

    def attend_chunk(c):
        start = c * Q_CHUNK
        blk = start // MOBA_BLOCK
        q_c = lax.dynamic_slice_in_dim(qt, start, Q_CHUNK, axis=2)
        sel_c = lax.dynamic_slice_in_dim(sel, start, Q_CHUNK, axis=2)
        ok_c = lax.dynamic_slice_in_dim(slot_ok, start, Q_CHUNK, axis=0)
        k_sel = kb[b_idx, h_idx, sel_c]
        v_sel = vb[b_idx, h_idx, sel_c]
        k_own = lax.dynamic_index_in_dim(kb, blk, axis=2, keepdims=False)
        v_own = lax.dynamic_index_in_dim(vb, blk, axis=2, keepdims=False)
        s_sel = jnp.einsum('bhqd,bhqnkd->bhqnk', q_c, k_sel, preferred_element_type=f32) * scale
        s_sel = jnp.where(ok_c[None, None, :, :, None], s_sel, -jnp.inf)
        s_own = jnp.einsum('bhqd,bhkd->bhqk', q_c, k_own, preferred_element_type=f32) * scale
        causal = (blk * MOBA_BLOCK + k_off)[None, :] <= (start + q_off)[:, None]
        s_own = jnp.where(causal, s_own, -jnp.inf)
        s_all = jnp.concatenate([s_sel.reshape(B, H, Q_CHUNK, n_keys_sel), s_own], axis=-1)
        p = jax.nn.softmax(s_all, axis=-1).astype(v.dtype)
        p_sel = p[..., :n_keys_sel].reshape(B, H, Q_CHUNK, n_sel, MOBA_BLOCK)
        p_own = p[..., n_keys_sel:]
        o = (jnp.einsum('bhqnk,bhqnkd->bhqd', p_sel, v_sel, preferred_element_type=f32)
             + jnp.einsum('bhqk,bhkd->bhqd', p_own, v_own, preferred_element_type=f32))
        return o.astype(v.dtype)

    out = lax.map(attend_chunk, jnp.arange(S_pad // Q_CHUNK))
    out = out.transpose(1, 0, 3, 2, 4).reshape(B, S_pad, H * Dh)
    return out[:, :S]


def conv_ffn(h, w_up, conv_w, conv_b, w_down):
    u = h @ w_up
    a, b = u[..., :D_FF], u[..., D_FF:]
    a = lax.conv_general_dilated(
        a, conv_w.reshape(CONV_WIDTH, 1, D_FF), window_strides=(1,),
        padding=[(CONV_WIDTH - 1, 0)], dimension_numbers=('NWC', 'WIO', 'NWC'),
        feature_group_count=D_FF) + conv_b
    return (jax.nn.gelu(a, approximate=False) * b) @ w_down


def setup_inputs(seed: int = 0) -> dict:
    key = jax.random.key(seed)
    ks = iter(jax.random.split(key, 32))
    L = DEPTH
    f32 = jnp.float32

    def nrm(shape, scale):
        return jax.random.normal(next(ks), shape, f32) * scale

    def unif(shape, lo, hi):
        return jax.random.uniform(next(ks), shape, f32, lo, hi)

    x = nrm((BATCH, SEQ, D_MODEL), 1.0)
    positions = jnp.broadcast_to(jnp.arange(SEQ, dtype=jnp.int32), (BATCH, SEQ))
    return {
        'x': x,
        'positions': positions,
        'norm1_g': 1.0 + nrm((L, D_MODEL), 0.02),
        'w_in': nrm((L, D_MODEL, IN_COLS), D_MODEL ** -0.5),
        'rwkv_mu': unif((L, RWKV_IN), 0.0, 1.0),
        'w_decay_up': nrm((L, DECAY_LORA, RWKV_WIDTH), 0.5 * DECAY_LORA ** -0.5),
        'decay_bias': unif((L, RWKV_WIDTH), -6.5, -1.5),
        'w_aaa_up': nrm((L, AAA_LORA, RWKV_WIDTH), 0.5 * AAA_LORA ** -0.5),
        'aaa_bias': nrm((L, RWKV_WIDTH), 0.1),
        'w_gate_up': nrm((L, GATE_LORA, RWKV_WIDTH), GATE_LORA ** -0.5),
        'rwkv_k_k': 0.85 + nrm((L, RWKV_WIDTH), 0.05),
        'rwkv_k_a': 1.0 + nrm((L, RWKV_WIDTH), 0.05),
        'rwkv_r_k': nrm((L, RWKV_HEADS, HEAD_DIM), 0.1),
        'rwkv_ln_g': 1.0 + nrm((L, RWKV_WIDTH), 0.02),
        'rwkv_ln_b': nrm((L, RWKV_WIDTH), 0.02),
        'q_norm_g': 1.0 + nrm((L, HEAD_DIM), 0.02),
        'k_norm_g': 1.0 + nrm((L, HEAD_DIM), 0.02),
        'w_branch_a': nrm((L, RWKV_WIDTH, D_MODEL), RWKV_WIDTH ** -0.5),
        'w_branch_b': nrm((L, MOBA_WIDTH, D_MODEL), MOBA_WIDTH ** -0.5),
        'w_out': nrm((L, D_MODEL, D_MODEL), D_MODEL ** -0.5),
        'norm2_g': 1.0 + nrm((L, D_MODEL), 0.02),
        'w_ffn_up': nrm((L, D_MODEL, 2 * D_FF), D_MODEL ** -0.5),
        'ffn_conv_w': nrm((L, CONV_WIDTH, D_FF), CONV_WIDTH ** -0.5),
        'ffn_conv_b': nrm((L, D_FF), 0.02),
        'w_ffn_down': nrm((L, D_FF, D_MODEL), D_FF ** -0.5),
    }


def reference(x, positions, norm1_g, w_in, rwkv_mu, w_decay_up, decay_bias, w_aaa_up,
              aaa_bias, w_gate_up, rwkv_k_k, rwkv_k_a, rwkv_r_k, rwkv_ln_g, rwkv_ln_b,
              q_norm_g, k_norm_g, w_branch_a, w_branch_b, w_out, norm2_g, w_ffn_up,
              ffn_conv_w, ffn_conv_b, w_ffn_down):
    B, S, _ = x.shape
    for l in range(DEPTH):
        h = rms_norm(x, norm1_g[l])
        proj = h @ w_in[l]
        z_rwkv = proj[..., :RWKV_IN]
        z_moba = proj[..., RWKV_IN:RWKV_IN + MOBA_IN]
        gate_pre = proj[..., RWKV_IN + MOBA_IN:]
        z_rwkv = z_rwkv + rwkv_mu[l] * (shift_right(z_rwkv) - z_rwkv)
        y_a = rwkv7_time_mix(z_rwkv, w_decay_up[l], decay_bias[l], w_aaa_up[l], aaa_bias[l],
                             w_gate_up[l], rwkv_k_k[l], rwkv_k_a[l], rwkv_r_k[l],
                             rwkv_ln_g[l], rwkv_ln_b[l])
        q = z_moba[..., :MOBA_WIDTH].reshape(B, S, MOBA_HEADS, HEAD_DIM)
        k = z_moba[..., MOBA_WIDTH:2 * MOBA_WIDTH].reshape(B, S, MOBA_HEADS, HEAD_DIM)
        v = z_moba[..., 2 * MOBA_WIDTH:].reshape(B, S, MOBA_HEADS, HEAD_DIM)
        q = partial_rope(rms_norm(q, q_norm_g[l]), positions)
        k = partial_rope(rms_norm(k, k_norm_g[l]), positions)
        y_b = moba_attention(q, k, v)
        u_a = y_a @ w_branch_a[l]
        u_b = y_b @ w_branch_b[l]
        g_a = jax.nn.sigmoid(gate_pre[..., :D_MODEL])
        g_b = jax.nn.sigmoid(gate_pre[..., D_MODEL:])
        x = x + (g_a * u_a + g_b * u_b) @ w_out[l]
        h2 = rms_norm(x, norm2_g[l])
        x = x + conv_ffn(h2, w_ffn_up[l], ffn_conv_w[l], ffn_conv_b[l], w_ffn_down[l])
    return x
```

```python
import numpy as np
import concourse.bass as bass
import concourse.mybir as mybir
from concourse.bass_utils import run_bass_kernel_spmd

F32 = mybir.dt.float32
BF16 = mybir.dt.bfloat16
I32 = mybir.dt.int32
ALU = mybir.AluOpType
AF = mybir.ActivationFunctionType
AX = mybir.AxisListType


class Buf:
    __slots__ = ("name", "lastw", "readers")

    def __init__(self, name):
        self.name = name
        self.lastw = None
        self.readers = {}


class Ctx:
    def __init__(self, nc, n_dma_sems=24):
        self.nc = nc
        self.eng = {"pe": nc.tensor, "act": nc.scalar, "dve": nc.vector,
                    "pool": nc.gpsimd, "sp": nc.sync}
        self.sem = {}
        self.cnt = {}
        self.waited = {e: {} for e in self.eng}
        self._stack = []
        for e in ("pe", "act", "dve", "pool"):
            cm = nc.semaphore("s_" + e)
            self.sem[e] = cm.__enter__()
            self._stack.append(cm)
            self.cnt[e] = 0
        self.dsem = []
        self.dpool = {"hw": [], "sw": []}
        for i in range(n_dma_sems):
            cm = nc.semaphore("d%d" % i)
            self.dsem.append([cm.__enter__(), 0])
            self._stack.append(cm)
            self.dpool["sw" if i < 8 else "hw"].append(i)
        self.dnext = {"hw": 0, "sw": 0}
        self.semh = {}
        for e in self.sem:
            self.semh[("e", e)] = self.sem[e]
        for i, (h, _) in enumerate(self.dsem):
            self.semh[("d", i)] = h
        self.nwait = 0
        self.ninst = 0

    def _wait(self, e, toks):
        w = self.waited[e]
        best = {}
        for t in toks:
            if t is None:
                continue
            k, v = t[0], t[1]
            if w.get(k, 0) >= v:
                continue
            if best.get(k, 0) < v:
                best[k] = v
        for k, v in best.items():
            self.eng[e].wait_ge(self.semh[k], v)
            w[k] = v
            self.nwait += 1

    def _deps(self, e, rd, wr):
        toks = []
        me = ("e", e)
        for b in rd:
            if b.lastw is not None:
                if not (e == "pe" and b.lastw[0] == me):
                    toks.append(b.lastw)
        for b in wr:
            if b.lastw is not None and not (e == "pe" and b.lastw[0] == me):
                toks.append(b.lastw)
            for k, t in b.readers.items():
                if not (e == "pe" and k == me):
                    toks.append(t)
        return toks

    def _mark(self, tok, rd, wr):
        for b in rd:
            b.readers[tok[0]] = tok
        for b in wr:
            b.lastw = tok
            b.readers = {}

    def op(self, e, fn, rd=(), wr=()):
        self._wait(e, self._deps(e, rd, wr))
        ins = fn(self.eng[e])
        self.cnt[e] += 1
        ins.then_inc(self.sem[e], 1)
        tok = (("e", e), self.cnt[e])
        self._mark(tok, rd, wr)
        self.ninst += 1
        return tok

    def group(self, e, fns, rd=(), wr=()):
        self._wait(e, self._deps(e, rd, wr))
        ins = None
        for fn in fns:
            ins = fn(self.eng[e])
            self.ninst += 1
        self.cnt[e] += 1
        ins.then_inc(self.sem[e], 1)
        tok = (("e", e), self.cnt[e])
        self._mark(tok, rd, wr)
        return tok

    def dma(self, q, out, in_, rd=(), wr=(), **kw):
        kind = "sw" if q == "pool" else "hw"
        pool = self.dpool[kind]
        i = pool[self.dnext[kind] % len(pool)]
        self.dnext[kind] += 1
        h, c = self.dsem[i]
        k = ("d", i)
        toks = self._deps(q, rd, wr)
        if c > 0:
            toks.append((k, 16 * c))
        self._wait(q, toks)
        self.eng[q].dma_start(out=out, in_=in_, **kw).then_inc(h, 16)
        self.dsem[i][1] = c + 1
        tok = (k, 16 * (c + 1))
        self._mark(tok, rd, wr)
        self.ninst += 1
        return tok

    def wait_all(self, e, bufs):
        toks = []
        for b in bufs:
            toks.append(b.lastw)
            toks.extend(b.readers.values())
        self._wait(e, toks)

    def barrier(self, bufs=()):
        toks = []
        for e in self.sem:
            if self.cnt[e] > 0:
                toks.append((("e", e), self.cnt[e]))
        for i, (h, c) in enumerate(self.dsem):
            if c > 0:
                toks.append((("d", i), 16 * c))
        for e in self.eng:
            self._wait(e, toks)

from contextlib import ExitStack
import math
import os

S = 4096
DM = 1024
NT = 32
FMC = 1312
TMC = 4096
DFF = 2816
NFF = 22
BIG = 30000.0


def build(debug=False, upto=99):
    nc = bass.Bass("TRN2", target_bir_lowering=False)
    okind = "ExternalOutput" if debug else "Internal"

    def DIN(name, shape, dt=F32):
        return nc.dram_tensor(name, list(shape), dt, kind="ExternalInput").ap()

    x = DIN("x", [S, DM]); pos = DIN("pos", [128, NT], I32)
    w_in = DIN("w_in", [DM, 5408]); n1g = DIN("n1g", [128, 8]); mu_fm = DIN("mu_fm", [128, 11]); mu_v = DIN("mu_v", [1, 512])
    wdec_d = DIN("wdec", [65, 512]); waaa_d = DIN("waaa", [65, 512]); wgate_d = DIN("wgate", [160, 512])
    kk_d = DIN("kk_fm", [128, 4]); ka_d = DIN("ka_fm", [128, 4]); rkb_d = DIN("rkb", [128, 8])
    lng_d = DIN("lng", [1, 512]); lnb_d = DIN("lnb", [1, 512]); qkg_d = DIN("qkg", [1, 1024])
    wba_d = DIN("wba", [512, DM]); wbb_d = DIN("wbb", [512, DM]); wout_d = DIN("wout", [DM, DM]); n2g = DIN("n2g", [128, 8])
    wup_d = DIN("wup", [DM, 2 * DFF]); cw_d = DIN("cw", [128, NFF * 3]); cb_d = DIN("cb", [128, NFF]); wdn_d = DIN("wdn", [DFF, DM])
    ident_d = DIN("ident", [128, 128]); mu2_d = DIN("mu2", [128, 512]); sl2_d = DIN("sl2", [128, 256]); bones_d = DIN("bones", [128, 128])
    cbias2_d = DIN("cbias2", [128, 512]); sel65_d = DIN("sel65", [65, 64]); pok_d = DIN("pok", [1, 256]); pbias_d = DIN("pbias", [1, 256]); invf_d = DIN("invf", [1, 8])
    out = nc.dram_tensor("out", [S, DM], F32, kind="ExternalOutput").ap()
    zF = nc.dram_tensor("zF", [1408, S], F32, kind=okind).ap()
    zT = nc.dram_tensor("zT", [S, TMC], F32, kind=okind).ap()
    yaF = nc.dram_tensor("yaF", [512, S], BF16, kind=okind).ap()
    ybF = nc.dram_tensor("ybF", [512, S], BF16, kind=okind).ap()
    x1s = nc.dram_tensor("x1s", [S, DM], F32, kind=okind).ap()
    h2F = nc.dram_tensor("h2F", [DM, S], BF16, kind=okind).ap()

    c = Ctx(nc)
    PSB = [(nc.alloc_psum_tensor("psb%d" % i, [128, 512], F32), Buf("psb%d" % i)) for i in range(8)]
    pst = {"i": 0, "n": 8}

    def ps():
        i = pst["i"] % pst["n"]
        pst["i"] += 1
        return PSB[i]

    rr = {"i": 0}

    def ev():
        rr["i"] += 1
        return "act" if rr["i"] % 2 else "dve"

    def bcast(ap, shape, axis):
        return ap.unsqueeze(axis).to_broadcast(list(shape))

    with ExitStack() as es:
        def T(name, shape, dt=F32):
            return es.enter_context(nc.sbuf_tensor("s_" + name, list(shape), dt)), Buf(name)
        w, bw = T("w1", [128, 8, 5408], BF16)
        g1, bg1 = T("g1", [128, 8]); muf, bmuf = T("muf", [128, 11])
        idb, bidb = T("idb1", [128, 128], BF16)
        c.dma("sp", g1[:], n1g, wr=[bg1]); c.dma("sp", muf[:], mu_fm, wr=[bmuf])
        c.dma("pool", idb[:], ident_d, wr=[bidb])
        wb = [Buf("w1_%d" % k) for k in range(8)]
        for kc in range(8):
            c.dma("pool", w[:, kc, :], w_in[kc * 128:(kc + 1) * 128, :], wr=[wb[kc]])
            c.op("dve", lambda e, kc=kc: e.tensor_scalar(w[:, kc, :], w[:, kc, :], g1[:, kc:kc + 1], None, ALU.mult),
                 rd=[wb[kc], bg1], wr=[wb[kc]])
        ss, bss = T("ss1", [128, NT]); c.op("dve", lambda e: e.memset(ss[:], 0.0), wr=[bss])
        rs, brs = T("rs1", [128, NT])
        junk, bjunk = T("junk1", [128, DM])
        xts = [T("xt%d" % i, [128, DM]) for i in range(2)]
        hb, bhb = T("hb1", [128, DM], BF16)
        hTs = [es.enter_context(nc.sbuf_tensor("hT%d" % i, [128, 8, 512], BF16)) for i in range(2)]
        hTb = [[Buf("hT%d_%d" % (i, s)) for s in range(4)] for i in range(2)]
        stgs = [T("stg%d" % i, [128, TMC]) for i in range(2)]
        zsb = [T("zsb%d" % j, [128, 513]) for j in range(11)]
        for j in range(11):
            c.op("pool", lambda e, j=j: e.memset(zsb[j][0][:, 0:1], 0.0), wr=[zsb[j][1]])
        tds = [T("td%d" % i, [128, 512]) for i in range(2)]
        ostg = [T("ostg%d" % i, [128, 512]) for i in range(3)]
        no = 0
        import os
        for st in range(int(os.environ.get('NST', '8'))):
            hT = hTs[st % 2]
            for sub in range(4):
                tt = st * 4 + sub
                xt, bxt = xts[tt % 2]
                c.dma("sp", xt[:], x[tt * 128:(tt + 1) * 128, :], wr=[bxt])
                c.op("act", lambda e: e.activation(junk[:], xt[:], AF.Square, accum_out=ss[:, tt:tt + 1]), rd=[bxt], wr=[bjunk, bss])
                c.op("dve", lambda e: e.tensor_scalar(rs[:, tt:tt + 1], ss[:, tt:tt + 1], 1.0 / DM, 1e-6, ALU.mult, ALU.add), rd=[bss], wr=[brs])
                c.op("act", lambda e: e.activation(rs[:, tt:tt + 1], rs[:, tt:tt + 1], AF.Ln), rd=[brs], wr=[brs])
                c.op("act", lambda e: e.activation(rs[:, tt:tt + 1], rs[:, tt:tt + 1], AF.Exp, scale=-0.5), rd=[brs], wr=[brs])
                c.op("dve", lambda e: e.tensor_scalar(hb[:], xt[:], rs[:, tt:tt + 1], None, ALU.mult), rd=[bxt, brs], wr=[bhb])
                p, bp = ps(); pb = p[:].bitcast(BF16)
                c.group("pe", [lambda e, k=k: e.transpose(pb[:, k * 128:(k + 1) * 128], hb[:, k * 128:(k + 1) * 128], idb[:]) for k in range(8)],
                        rd=[bhb, bidb], wr=[bp])
                c.op("act", lambda e: e.copy(hT[:, :, sub * 128:(sub + 1) * 128], pb.rearrange("p (k t) -> p k t", k=8)), rd=[bp], wr=[hTb[st % 2][sub]])
                stg, bstg = stgs[tt % 2]
                for gi in range(8):
                    p, bp = ps()
                    c.group("pe", [lambda e, k=k, p=p: e.matmul(p[:], hT[:, k, sub * 128:(sub + 1) * 128], w[:, k, FMC + gi * 512:FMC + (gi + 1) * 512],
                                                               start=(k == 0), stop=(k == 7)) for k in range(8)],
                            rd=[hTb[st % 2][sub]] + wb, wr=[bp])
                    en = ev()
                    if en == "act":
                        c.op("act", lambda e, p=p: e.copy(stg[:, gi * 512:(gi + 1) * 512], p[:]), rd=[bp], wr=[bstg])
                    else:
                        c.op("dve", lambda e, p=p: e.tensor_copy(stg[:, gi * 512:(gi + 1) * 512], p[:]), rd=[bp], wr=[bstg])
                c.dma("sp", zT[tt * 128:(tt + 1) * 128, :], stg[:], rd=[bstg])
            for j in range(11):
                ncol = 32 if j == 10 else 128
                z, bz = zsb[j]
                p, bp = ps()
                c.group("pe", [lambda e, k=k, p=p: e.matmul(p[0:ncol, :], w[:, k, j * 128:j * 128 + ncol], hT[:, k, :], start=(k == 0), stop=(k == 7)) for k in range(8)],
                        rd=hTb[st % 2] + wb, wr=[bp])
                c.op("act", lambda e, p=p: e.copy(z[0:ncol, 1:513], p[0:ncol, :]), rd=[bp], wr=[bz])
                td, btd = tds[j % 2]
                c.op("dve", lambda e: e.tensor_tensor(td[0:ncol, :], z[0:ncol, 0:512], z[0:ncol, 1:513], ALU.subtract), rd=[bz], wr=[btd])
                o, bo = ostg[no % 3]; no += 1
                c.op("dve", lambda e: e.scalar_tensor_tensor(o[0:ncol, :], td[0:ncol, :], muf[0:ncol, j:j + 1], z[0:ncol, 1:513], ALU.mult, ALU.add),
                     rd=[btd, bz, bmuf], wr=[bo])
                c.op("act", lambda e: e.copy(z[0:ncol, 0:1], z[0:ncol, 512:513]), rd=[bz], wr=[bz])
                c.dma("sp", zF[j * 128:j * 128 + ncol, st * 512:(st + 1) * 512], o[0:ncol, :], rd=[bo])
        c.barrier()

    if upto <= 1:
        print("ninst", c.ninst, "nwait", c.nwait); return nc
    with ExitStack() as es:
        def T(name, shape, dt=F32):
            return es.enter_context(nc.sbuf_tensor("s_" + name, list(shape), dt)), Buf(name)
        idb, bidb = T("idb2", [128, 128], BF16); c.dma("pool", idb[:], ident_d, wr=[bidb])
        wdec, bwdec = T("wdec", [65, 512], BF16); c.dma("pool", wdec[:], wdec_d, wr=[bwdec])
        waaa, bwaaa = T("waaa", [65, 512], BF16); c.dma("pool", waaa[:], waaa_d, wr=[bwaaa])
        wgate, bwgate = T("wgate", [128, 2, 512], BF16)
        c.dma("pool", wgate[:, 0, :], wgate_d[0:128, :], wr=[bwgate]); c.dma("pool", wgate[0:32, 1, :], wgate_d[128:160, :], wr=[bwgate])
        kkf, bkkf = T("kkf", [128, 4]); c.dma("sp", kkf[:], kk_d, wr=[bkkf])
        kaf, bkaf = T("kaf", [128, 4]); c.dma("sp", kaf[:], ka_d, wr=[bkaf])
        c0f, bc0f = T("c0f", [128, 4])
        c.op("dve", lambda e: e.tensor_scalar(c0f[:], kaf[:], -1.0, 1.0, ALU.mult, ALU.add), rd=[bkaf], wr=[bc0f])
        rkb, brkb = T("rkb", [128, 8]); c.dma("sp", rkb[:], rkb_d, wr=[brkb])
        lng, blng = T("lng", [128, 512]); c.dma("sp", lng[:], lng_d.partition_broadcast(128), wr=[blng])
        lnb, blnb = T("lnb", [128, 512]); c.dma("sp", lnb[:], lnb_d.partition_broadcast(128), wr=[blnb])
        muv, bmuv = T("muv", [128, 512]); c.dma("sp", muv[:], mu_v.partition_broadcast(128), wr=[bmuv])
        MU2, bMU2 = T("MU2", [128, 512]); c.dma("sp", MU2[:], mu2_d, wr=[bMU2])
        SL2, bSL2 = T("SL2", [128, 256]); c.dma("sp", SL2[:], sl2_d, wr=[bSL2])
        bones, bbones = T("bones", [128, 128], BF16); c.dma("pool", bones[:], bones_d, wr=[bbones])
        S32, bS32 = T("S32", [128, 4, 64]); c.op("dve", lambda e: e.memset(S32[:], 0.0), wr=[bS32])
        Sb, bSb = T("Sb", [128, 4, 64], BF16); c.op("dve", lambda e: e.memset(Sb[:], 0.0), wr=[bSb])
        tha = [T("tha%d" % i, [65, 128], BF16) for i in range(2)]
        xaa = [T("xaa%d" % i, [65, 128], BF16) for i in range(2)]
        for i in range(2):
            c.op("dve", lambda e, i=i: e.memset(tha[i][0][:], 1.0), wr=[tha[i][1]])
            c.op("dve", lambda e, i=i: e.memset(xaa[i][0][:], 1.0), wr=[xaa[i][1]])
        rFs = [T("rF%d" % i, [128, 4, 128]) for i in range(2)]
        kFs = [T("kF%d" % i, [128, 4, 128]) for i in range(2)]
        xws = [T("xw%d" % i, [64, 128]) for i in range(2)]
        xas = [T("xa%d" % i, [64, 128]) for i in range(2)]
        xg0s = [T("xg0%d" % i, [128, 128]) for i in range(2)]
        xg1s = [T("xg1%d" % i, [32, 128]) for i in range(2)]
        vTs = [T("vT%d" % i, [128, 512]) for i in range(2)]
        vPs = [T("vP%d" % i, [128, 512]) for i in range(2)]
        for i in range(2):
            c.op("dve", lambda e, i=i: e.memset(vPs[i][0][:], 0.0), wr=[vPs[i][1]])
        v32, bv32 = T("v32", [128, 512]); vb, bvb = T("vb", [128, 512], BF16)
        sg0, bsg0 = T("sg0", [128, 128], BF16); sg1, bsg1 = T("sg1", [32, 128], BF16)
        tg, btg = T("tg", [128, 512]); sgt, bsgt = T("sgt", [128, 128])
        logw, blogw = T("logw", [128, 4, 128]); lgi, blgi = T("lgi", [128, 4, 128]); lge, blge = T("lge", [128, 4, 128])
        alr, balr = T("alr", [128, 4, 128]); gT, bgT = T("gT", [128, 512])
        kkr, bkkr = T("kkr", [128, 4, 128]); sqb, bsqb = T("sqb", [128, 512], BF16); rn, brn = T("rn", [128, 512])
        kkn, bkkn = T("kkn", [128, 4, 128]); fF, bfF = T("fF", [128, 4, 128]); kM, bkM = T("kM", [128, 4, 128])
        gin, bgin = T("gin", [128, 4, 128]); ginv, bginv = T("ginv", [128, 4, 128]); gex, bgex = T("gex", [128, 4, 128])
        AR, bAR = T("ARZ", [128, 8, 2, 128], BF16); bF, bbF = T("bF", [128, 4, 128])
        Bt, bBt = T("BtZ", [128, 8, 128], BF16); Kt, bKt = T("KtZ", [128, 8, 128], BF16)
        for (t_, b_) in ((AR, bAR), (Bt, bBt), (Kt, bKt)):
            c.op("pool", lambda e, t_=t_: e.memset(t_[:], 0.0), wr=[b_])
        Dd, bDd = T("Dd", [128, 4, 128]); Bh, bBh = T("Bh", [128, 4, 128], BF16); Kh, bKh = T("Kh", [128, 4, 128], BF16)
        BKhT, bBKhT = T("BKhT", [128, 1024], BF16)
        rk, brk = T("rk", [128, 4, 128]); coef, bcoef = T("coef", [128, 8])
        MAB, bMAB = T("MAB", [128, 8, 2, 128], BF16); MAK, bMAK = T("MAK", [128, 8, 2, 128], BF16); MABT, bMABT = T("MABT", [128, 8, 128], BF16)
        Pk = [T("Pk%d" % i, [128, 8, 128], BF16) for i in range(2)]
        PTk = [T("PTk%d" % i, [128, 8, 128], BF16) for i in range(2)]
        ACk = [T("ACk%d" % i, [128, 8, 128], BF16) for i in range(2)]
        XT, bXT = T("XT", [128, 512], BF16); UT, bUT = T("UT", [128, 512], BF16)
        tmpS, btmpS = T("tmpS", [128, 4, 64])
        s1, bs1 = T("s1", [128, 8]); s2, bs2 = T("s2", [128, 8]); mean, bmean = T("mean", [128, 8]); var, bvar = T("var", [128, 8])
        sqt, bsqt = T("sqt", [128, 512]); yn, byn = T("yn", [128, 512]); bon, bbon = T("bon", [128, 512])
        yab, byab = T("yab", [128, 512], BF16); yaT, byaT = T("yaT", [128, 4, 128], BF16)

        zFr = zF[0:512, :].rearrange("(c p) t -> p c t", p=128)
        zFk = zF[512:1024, :].rearrange("(c p) t -> p c t", p=128)
        yaFv = yaF.rearrange("(c p) t -> p c t", p=128)

        def loads(ch):
            i = ch % 2; t0 = ch * 128
            c.dma("sp", rFs[i][0][:], zFr[:, :, t0:t0 + 128], wr=[rFs[i][1]])
            c.dma("sp", kFs[i][0][:], zFk[:, :, t0:t0 + 128], wr=[kFs[i][1]])
            c.dma("sp", xws[i][0][:], zF[1024:1088, t0:t0 + 128], wr=[xws[i][1]])
            c.dma("sp", xas[i][0][:], zF[1088:1152, t0:t0 + 128], wr=[xas[i][1]])
            c.dma("sp", xg0s[i][0][:], zF[1152:1280, t0:t0 + 128], wr=[xg0s[i][1]])
            c.dma("sp", xg1s[i][0][:], zF[1280:1312, t0:t0 + 128], wr=[xg1s[i][1]])
            c.dma("sp", vTs[i][0][:], zT[t0:t0 + 128, 0:512], wr=[vTs[i][1]])
            if ch == 0:
                c.dma("sp", vPs[i][0][1:128, :], zT[0:127, 0:512], wr=[vPs[i][1]])
            else:
                c.dma("sp", vPs[i][0][:], zT[t0 - 1:t0 + 127, 0:512], wr=[vPs[i][1]])

        loads(0)
        NCH = int(os.environ.get('NCH', str(NT)))
        STG = int(os.environ.get('STG', '99'))
        def chunk(ch):
            if ch + 1 < NCH:
                loads(ch + 1)
            i = ch % 2; t0 = ch * 128
            rF, brF = rFs[i]; kF, bkF = kFs[i]; xw, bxw = xws[i]; xa, bxa = xas[i]
            xg0, bxg0 = xg0s[i]; xg1, bxg1 = xg1s[i]; vT, bvT = vTs[i]; vP, bvP = vPs[i]
            th, bth = tha[i]; xab, bxab = xaa[i]
            c.op("pool", lambda e: e.tensor_tensor(bon[:], vP[:], vT[:], ALU.subtract), rd=[bvP, bvT], wr=[bbon])
            c.op("pool", lambda e: e.tensor_tensor(bon[:], bon[:], muv[:], ALU.mult), rd=[bbon, bmuv], wr=[bbon])
            c.op("pool", lambda e: e.tensor_tensor(v32[:], bon[:], vT[:], ALU.add), rd=[bbon, bvT], wr=[bv32])
            c.op("pool", lambda e: e.tensor_copy(vb[:], v32[:]), rd=[bv32], wr=[bvb])
            if STG <= 1:
                return
            c.op("act", lambda e: e.activation(th[0:64, :], xw[:], AF.Tanh), rd=[bxw], wr=[bth])
            c.op("dve", lambda e: e.tensor_copy(xab[0:64, :], xa[:]), rd=[bxa], wr=[bxab])
            c.op("act", lambda e: e.activation(sgt[:], xg0[:], AF.Tanh, scale=0.5), rd=[bxg0], wr=[bsgt])
            c.op("dve", lambda e: e.tensor_scalar(sg0[:], sgt[:], 0.5, 0.5, ALU.mult, ALU.add), rd=[bsgt], wr=[bsg0])
            c.op("act", lambda e: e.activation(sgt[0:32, :], xg1[:], AF.Tanh, scale=0.5), rd=[bxg1], wr=[bsgt])
            c.op("dve", lambda e: e.tensor_scalar(sg1[:], sgt[0:32, :], 0.5, 0.5, ALU.mult, ALU.add), rd=[bsgt], wr=[bsg1])
            if STG <= 2:
                return
            p, bp = ps()
            c.group("pe", [lambda e, q=q, p=p: e.matmul(p[:, q * 128:(q + 1) * 128], wdec[:, q * 128:(q + 1) * 128], th[:], start=True, stop=True) for q in range(4)],
                    rd=[bwdec, bth], wr=[bp])
            c.op("act", lambda e, p=p: e.activation(tg[:], p[:], AF.Tanh, scale=0.5), rd=[bp], wr=[btg])
            c.op("dve", lambda e: e.tensor_scalar(logw[:].rearrange("p a b -> p (a b)"), tg[:], -0.5 * math.exp(-0.5), -0.5 * math.exp(-0.5), ALU.mult, ALU.add),
                 rd=[btg], wr=[blogw])
            for q in range(4):
                c.op("dve", lambda e, q=q: e.tensor_tensor_scan(lgi[:, q, :], logw[:, q, :], logw[:, q, :], 0.0, ALU.add, ALU.bypass), rd=[blogw], wr=[blgi])
            c.op("pool", lambda e: e.tensor_tensor(lge[:], lgi[:], logw[:], ALU.subtract), rd=[blgi, blogw], wr=[blge])
            if STG <= 3:
                return
            p, bp = ps()
            c.group("pe", [lambda e, q=q, p=p: e.matmul(p[:, q * 128:(q + 1) * 128], waaa[:, q * 128:(q + 1) * 128], xab[:], start=True, stop=True) for q in range(4)],
                    rd=[bwaaa, bxab], wr=[bp])
            c.op("act", lambda e, p=p: e.activation(tg[:], p[:], AF.Tanh, scale=0.5), rd=[bp], wr=[btg])
            c.op("dve", lambda e: e.tensor_scalar(alr[:].rearrange("p a b -> p (a b)"), tg[:], 0.5, 0.5, ALU.mult, ALU.add), rd=[btg], wr=[balr])
            p, bp = ps()
            c.group("pe", [lambda e, p=p: e.matmul(p[:], sg0[:], wgate[:, 0, :], start=True, stop=False),
                           lambda e, p=p: e.matmul(p[:], sg1[:], wgate[0:32, 1, :], start=False, stop=True)], rd=[bsg0, bsg1, bwgate], wr=[bp])
            c.op("act", lambda e, p=p: e.copy(gT[:], p[:]), rd=[bp], wr=[bgT])
            if STG <= 4:
                return
            for q in range(4):
                c.op("dve", lambda e, q=q: e.tensor_scalar(kkr[:, q, :], kF[:, q, :], kkf[:, q:q + 1], None, ALU.mult), rd=[bkF, bkkf], wr=[bkkr])
            c.op("pool", lambda e: e.tensor_tensor(sqb[:], kkr[:].rearrange("p a b -> p (a b)"), kkr[:].rearrange("p a b -> p (a b)"), ALU.mult), rd=[bkkr], wr=[bsqb])
            p, bp = ps()
            c.op("pe", lambda e, p=p: e.matmul(p[:], bones[:], sqb[:], start=True, stop=True), rd=[bbones, bsqb], wr=[bp])
            c.op("act", lambda e, p=p: e.activation(rn[:], p[:], AF.Ln), rd=[bp], wr=[brn])
            c.op("act", lambda e: e.activation(rn[:], rn[:], AF.Exp, scale=-0.5), rd=[brn], wr=[brn])
            c.op("dve", lambda e: e.tensor_tensor(kkn[:].rearrange("p a b -> p (a b)"), kkr[:].rearrange("p a b -> p (a b)"), rn[:], ALU.mult), rd=[bkkr, brn], wr=[bkkn])
            if STG <= 5:
                return
            for q in range(4):
                c.op("dve", lambda e, q=q: e.tensor_scalar(fF[:, q, :], alr[:, q, :], kaf[:, q:q + 1], c0f[:, q:q + 1], ALU.mult, ALU.add), rd=[balr, bkaf, bc0f], wr=[bfF])
            c.op("pool", lambda e: e.tensor_tensor(kM[:], kF[:], fF[:], ALU.mult), rd=[bkF, bfF], wr=[bkM])
            c.op("act", lambda e: e.activation(gin[:], lgi[:], AF.Exp), rd=[blgi], wr=[bgin])
            c.op("act", lambda e: e.activation(ginv[:], lgi[:], AF.Exp, scale=-1.0), rd=[blgi], wr=[bginv])
            c.op("act", lambda e: e.activation(gex[:], lge[:], AF.Exp), rd=[blge], wr=[bgex])
            for q in range(4):
                c.op("act", lambda e, q=q: e.activation(Dd[:, q, :], lgi[:, q, :], AF.Exp, bias=lgi[:, q, 127:128], scale=-1.0), rd=[blgi], wr=[bDd])
            c.op("pool", lambda e: e.tensor_tensor(bF[:], kkn[:], alr[:], ALU.mult), rd=[bkkn, balr], wr=[bbF])
            for hh in range(2):
                r0, r1 = hh * 64, (hh + 1) * 64
                ARv = AR[r0:r1, :, :, :].rearrange("p (q two) a t -> p q two a t", two=2)[:, :, hh, :, :]
                Btv = Bt[r0:r1, :, :].rearrange("p (q two) t -> p q two t", two=2)[:, :, hh, :]
                Ktv = Kt[r0:r1, :, :].rearrange("p (q two) t -> p q two t", two=2)[:, :, hh, :]
                c.op("dve", lambda e: e.tensor_tensor(ARv[:, :, 1, :], rF[r0:r1, :, :], gin[r0:r1, :, :], ALU.mult), rd=[brF, bgin], wr=[bAR])
                c.op("dve", lambda e: e.scalar_tensor_tensor(ARv[:, :, 0, :], kkn[r0:r1, :, :], -1.0, gex[r0:r1, :, :], ALU.mult, ALU.mult), rd=[bkkn, bgex], wr=[bAR])
                c.op("dve", lambda e: e.tensor_tensor(Btv, bF[r0:r1, :, :], ginv[r0:r1, :, :], ALU.mult), rd=[bbF, bginv], wr=[bBt])
                c.op("pool", lambda e: e.tensor_tensor(Ktv, kM[r0:r1, :, :], ginv[r0:r1, :, :], ALU.mult), rd=[bkM, bginv], wr=[bKt])
            c.op("pool", lambda e: e.tensor_tensor(Bh[:], bF[:], Dd[:], ALU.mult), rd=[bbF, bDd], wr=[bBh])
            c.op("pool", lambda e: e.tensor_tensor(Kh[:], kM[:], Dd[:], ALU.mult), rd=[bkM, bDd], wr=[bKh])
            if STG <= 6:
                return
            p, bp = ps(); pb = p[:].bitcast(BF16)
            c.group("pe", [lambda e, q=q: e.transpose(pb[:, q * 128:(q + 1) * 128], Bh[:, q, :], idb[:]) for q in range(4)] +
                          [lambda e, q=q: e.transpose(pb[:, 512 + q * 128:512 + (q + 1) * 128], Kh[:, q, :], idb[:]) for q in range(4)],
                    rd=[bBh, bKh, bidb], wr=[bp])
            c.op("act", lambda e: e.copy(BKhT[:], pb), rd=[bp], wr=[bBKhT])
            if STG <= 7:
                return
            c.op("pool", lambda e: e.tensor_tensor(rk[:], rF[:], kM[:], ALU.mult), rd=[brF, bkM], wr=[brk])
            p, bp = ps()
            c.group("pe", [lambda e, q=q, p=p: e.matmul(p[:, 2 * q:2 * q + 2], rk[:, q, :], rkb[:, 2 * q:2 * q + 2], start=True, stop=True) for q in range(4)],
                    rd=[brk, brkb], wr=[bp])
            c.op("dve", lambda e, p=p: e.tensor_copy(coef[:], p[:, 0:8]), rd=[bp], wr=[bcoef])
            if STG <= 8:
                return
            for q in range(4):
                pA, bpA = ps(); pB, bpB = ps(); pC, bpC = ps()
                fa, fb, fc = [], [], []
                for hh in range(2):
                    h = 2 * q + hh
                    arr = AR[:, h, :, :].rearrange("p a b -> p (a b)")
                    fa.append(lambda e, hh=hh, h=h, arr=arr: e.matmul(pA[:, hh * 256:(hh + 1) * 256], Bt[:, h, :], arr, start=True, stop=True))
                    fb.append(lambda e, hh=hh, h=h, arr=arr: e.matmul(pB[:, hh * 256:(hh + 1) * 256], Kt[:, h, :], arr, start=True, stop=True))
                    fc.append(lambda e, hh=hh, h=h: e.matmul(pC[:, hh * 128:(hh + 1) * 128], AR[:, h, 0, :], Bt[:, h, :], start=True, stop=True))
                c.group("pe", fa, rd=[bBt, bAR], wr=[bpA])
                c.group("pe", fb, rd=[bKt, bAR], wr=[bpB])
                c.group("pe", fc, rd=[bBt, bAR], wr=[bpC])
                c.op("dve", lambda e: e.tensor_tensor(MAB[:, 2 * q:2 * q + 2, :, :].rearrange("p a b c -> p (a b c)"), pA[:], MU2[:], ALU.mult), rd=[bpA, bMU2], wr=[bMAB])
                c.op("dve", lambda e: e.tensor_tensor(MAK[:, 2 * q:2 * q + 2, :, :].rearrange("p a b c -> p (a b c)"), pB[:], MU2[:], ALU.mult), rd=[bpB, bMU2], wr=[bMAK])
                c.op("dve", lambda e: e.tensor_tensor(MABT[:, 2 * q:2 * q + 2, :].rearrange("p a b -> p (a b)"), pC[:, 0:256], SL2[:], ALU.mult), rd=[bpC, bSL2], wr=[bMABT])
            if STG <= 9:
                return
            AC0, bAC0 = ACk[0]
            c.op("pool", lambda e: e.tensor_tensor(AC0[:], MAB[:, :, 0, :], bcast(idb[:], [128, 8, 128], 1), ALU.add), rd=[bMAB, bidb], wr=[bAC0])
            Pp = lambda h: MAB[:, h, 0, :]
            PTp = lambda h: MABT[:, h, :]
            bPp, bPTp = bMAB, bMABT
            ACp, bACp = AC0, bAC0
            for lv in range(1, 7):
                Pn, bPn = Pk[lv % 2]; PTn, bPTn = PTk[lv % 2]; ACn, bACn = ACk[lv % 2]
                for grp in range(2):
                    hs = range(4 * grp, 4 * grp + 4)
                    if lv < 6:
                        p, bp = ps()
                        c.group("pe", [lambda e, h=h, p=p, Pp=Pp, PTp=PTp: e.matmul(p[:, (h % 4) * 128:(h % 4 + 1) * 128], PTp(h), Pp(h), start=True, stop=True) for h in hs],
                                rd=[bPp, bPTp], wr=[bp])
                        en = ev()
                        if en == "act":
                            c.op("act", lambda e, p=p: e.copy(Pn[:, 4 * grp:4 * grp + 4, :].rearrange("p a b -> p (a b)"), p[:]), rd=[bp], wr=[bPn])
                        else:
                            c.op("dve", lambda e, p=p: e.tensor_copy(Pn[:, 4 * grp:4 * grp + 4, :].rearrange("p a b -> p (a b)"), p[:]), rd=[bp], wr=[bPn])
                    p, bp = ps()
                    c.group("pe", [lambda e, h=h, p=p, Pp=Pp, PTp=PTp: e.matmul(p[:, (h % 4) * 128:(h % 4 + 1) * 128], Pp(h), PTp(h), start=True, stop=True) for h in hs],
                            rd=[bPp, bPTp], wr=[bp])
                    en = ev()
                    if en == "act":
                        c.op("act", lambda e, p=p: e.copy(PTn[:, 4 * grp:4 * grp + 4, :].rearrange("p a b -> p (a b)"), p[:]), rd=[bp], wr=[bPTn])
                    else:
                        c.op("dve", lambda e, p=p: e.tensor_copy(PTn[:, 4 * grp:4 * grp + 4, :].rearrange("p a b -> p (a b)"), p[:]), rd=[bp], wr=[bPTn])
                    p, bp = ps()
                    fs = []
                    for h in hs:
                        fs.append(lambda e, h=h, p=p, ACp=ACp: e.matmul(p[:, (h % 4) * 128:(h % 4 + 1) * 128], idb[:], ACp[:, h, :], start=True, stop=False))
                        fs.append(lambda e, h=h, p=p, ACp=ACp: e.matmul(p[:, (h % 4) * 128:(h % 4 + 1) * 128], PTn[:, h, :], ACp[:, h, :], start=False, stop=True))
                    c.group("pe", fs, rd=[bidb, bACp, bPTn], wr=[bp])
                    en = ev()
                    if en == "act":
                        c.op("act", lambda e, p=p: e.copy(ACn[:, 4 * grp:4 * grp + 4, :].rearrange("p a b -> p (a b)"), p[:]), rd=[bp], wr=[bACn])
                    else:
                        c.op("dve", lambda e, p=p: e.tensor_copy(ACn[:, 4 * grp:4 * grp + 4, :].rearrange("p a b -> p (a b)"), p[:]), rd=[bp], wr=[bACn])
                Pp = (lambda Pn: (lambda h: Pn[:, h, :]))(Pn)
                PTp = (lambda PTn: (lambda h: PTn[:, h, :]))(PTn)
                bPp, bPTp = bPn, bPTn
                ACp, bACp = ACn, bACn
            Minv, bMinv = ACp, bACp
            if STG <= 10:
                return
            pX, bpX = ps()
            fs = []
            for h in range(8):
                q, r0 = h // 2, (h % 2) * 64
                fs.append(lambda e, h=h: e.matmul(pX[:, h * 64:(h + 1) * 64], MAK[:, h, 0, :], vb[:, h * 64:(h + 1) * 64], start=True, stop=False))
                fs.append(lambda e, h=h, q=q: e.matmul(pX[:, h * 64:(h + 1) * 64], AR[:, h, 0, :], Sb[:, q, :], start=False, stop=True))
            c.group("pe", fs, rd=[bMAK, bvb, bAR, bSb], wr=[bpX])
            c.op("act", lambda e: e.copy(XT[:], pX[:]), rd=[bpX], wr=[bXT])
            pU, bpU = ps()
            c.group("pe", [lambda e, h=h: e.matmul(pU[:, h * 64:(h + 1) * 64], Minv[:, h, :], XT[:, h * 64:(h + 1) * 64], start=True, stop=True) for h in range(8)],
                    rd=[bMinv, bXT], wr=[bpU])
            c.op("dve", lambda e: e.tensor_copy(UT[:], pU[:]), rd=[bpU], wr=[bUT])
            pY, bpY = ps()
            fs = []
            for h in range(8):
                q, r0 = h // 2, (h % 2) * 64
                fs.append(lambda e, h=h: e.matmul(pY[:, h * 64:(h + 1) * 64], MAK[:, h, 1, :], vb[:, h * 64:(h + 1) * 64], start=True, stop=False))
                fs.append(lambda e, h=h: e.matmul(pY[:, h * 64:(h + 1) * 64], MAB[:, h, 1, :], UT[:, h * 64:(h + 1) * 64], start=False, stop=False))
                fs.append(lambda e, h=h, q=q: e.matmul(pY[:, h * 64:(h + 1) * 64], AR[:, h, 1, :], Sb[:, q, :], start=False, stop=True))
            c.group("pe", fs, rd=[bMAK, bMAB, bvb, bUT, bAR, bSb], wr=[bpY])
            pS, bpS = ps()
            fs = []
            for q in range(4):
                fs.append(lambda e, q=q: e.matmul(pS[:, q * 128:(q + 1) * 128], BKhT[:, q * 128:(q + 1) * 128], UT[:, q * 128:(q + 1) * 128], start=True, stop=False))
                fs.append(lambda e, q=q: e.matmul(pS[:, q * 128:(q + 1) * 128], BKhT[:, 512 + q * 128:512 + (q + 1) * 128], vb[:, q * 128:(q + 1) * 128], start=False, stop=True))
            c.group("pe", fs, rd=[bBKhT, bUT, bvb], wr=[bpS])
            pSv = pS[:].rearrange("p (q c) -> p q c", q=4)
            c.op("dve", lambda e: e.tensor_tensor(tmpS[:], S32[:], gin[:, :, 127:128].to_broadcast([128, 4, 64]), ALU.mult), rd=[bS32, bgin], wr=[btmpS])
            c.op("dve", lambda e: e.tensor_tensor(S32[0:64, :, :], tmpS[0:64, :, :], pSv[0:64, :, 0:64], ALU.add), rd=[btmpS, bpS], wr=[bS32])
            c.op("dve", lambda e: e.tensor_tensor(S32[64:128, :, :], tmpS[64:128, :, :], pSv[64:128, :, 64:128], ALU.add), rd=[btmpS, bpS], wr=[bS32])
            c.op("dve", lambda e: e.tensor_copy(Sb[:], S32[:]), rd=[bS32], wr=[bSb])
            if STG <= 11:
                return
            pYv = pY[:].rearrange("p (h d) -> p h d", h=8)
            c.op("dve", lambda e: e.tensor_reduce(s1[:], pYv, AX.X, ALU.add), rd=[bpY], wr=[bs1])
            c.op("act", lambda e: e.activation(sqt[:], pY[:], AF.Square), rd=[bpY], wr=[bsqt])
            c.op("dve", lambda e: e.tensor_reduce(s2[:], sqt[:].rearrange("p (h d) -> p h d", h=8), AX.X, ALU.add), rd=[bsqt], wr=[bs2])
            c.op("dve", lambda e: e.tensor_scalar(mean[:], s1[:], 1.0 / 64, None, ALU.mult), rd=[bs1], wr=[bmean])
            c.op("dve", lambda e: e.tensor_tensor(var[:], mean[:], mean[:], ALU.mult), rd=[bmean], wr=[bvar])
            c.op("dve", lambda e: e.scalar_tensor_tensor(var[:], s2[:], 1.0 / 64, var[:], ALU.mult, ALU.subtract), rd=[bs2, bvar], wr=[bvar])
            c.op("dve", lambda e: e.tensor_scalar(var[:], var[:], 64e-5, None, ALU.add), rd=[bvar], wr=[bvar])
            c.op("act", lambda e: e.activation(var[:], var[:], AF.Ln), rd=[bvar], wr=[bvar])
            c.op("act", lambda e: e.activation(var[:], var[:], AF.Exp, scale=-0.5), rd=[bvar], wr=[bvar])
            ynv = yn[:].rearrange("p (h d) -> p h d", h=8)
            c.op("dve", lambda e: e.tensor_tensor(ynv, pYv, bcast(mean[:], [128, 8, 64], 2), ALU.subtract), rd=[bpY, bmean], wr=[byn])
            c.op("pool", lambda e: e.tensor_tensor(ynv, ynv, bcast(var[:], [128, 8, 64], 2), ALU.mult), rd=[byn, bvar], wr=[byn])
            c.op("pool", lambda e: e.tensor_tensor(yn[:], yn[:], lng[:], ALU.mult), rd=[byn, blng], wr=[byn])
            c.op("dve", lambda e: e.tensor_tensor(yn[:], yn[:], lnb[:], ALU.add), rd=[byn, blnb], wr=[byn])
            c.op("pool", lambda e: e.tensor_tensor(bon[:].rearrange("p (h d) -> p h d", h=8), v32[:].rearrange("p (h d) -> p h d", h=8), bcast(coef[:], [128, 8, 64], 2), ALU.mult),
                 rd=[bv32, bcoef], wr=[bbon])
            c.op("dve", lambda e: e.tensor_tensor(yn[:], yn[:], bon[:], ALU.add), rd=[byn, bbon], wr=[byn])
            c.op("dve", lambda e: e.tensor_tensor(yab[:], yn[:], gT[:], ALU.mult), rd=[byn, bgT], wr=[byab])
            p, bp = ps(); pb = p[:].bitcast(BF16)
            c.group("pe", [lambda e, q=q: e.transpose(pb[:, q * 128:(q + 1) * 128], yab[:, q * 128:(q + 1) * 128], idb[:]) for q in range(4)], rd=[byab, bidb], wr=[bp])
            c.op("act", lambda e: e.copy(yaT[:].rearrange("p a b -> p (a b)"), pb[:, 0:512]), rd=[bp], wr=[byaT])
            c.dma("sp", yaFv[:, :, t0:t0 + 128], yaT[:], rd=[byaT])
        for ch in range(NCH):
            chunk(ch)
        c.barrier()

    if upto <= 2:
        print("ninst", c.ninst, "nwait", c.nwait); return nc
    with ExitStack() as es:
        def T(name, shape, dt=F32):
            return es.enter_context(nc.sbuf_tensor("s_" + name, list(shape), dt)), Buf(name)
        pst["n"] = 6; pst["i"] = 0
        pO = [PSB[6], PSB[7]]
        idb, bidb = T("idb3", [128, 128], BF16); c.dma("pool", idb[:], ident_d, wr=[bidb])
        CB, bCB = T("CB", [128, 2, 256], BF16); c.dma("pool", CB[:].rearrange("p a b -> p (a b)"), cbias2_d, wr=[bCB])
        s65, bs65 = T("s65", [65, 64], BF16); c.dma("pool", s65[:], sel65_d, wr=[bs65])
        pok, bpok = T("pok", [128, 16, 16]); c.dma("sp", pok[:].rearrange("p a b -> p (a b)"), pok_d.partition_broadcast(128), wr=[bpok])
        pbi, bpbi = T("pbi", [128, 16, 16]); c.dma("sp", pbi[:].rearrange("p a b -> p (a b)"), pbias_d.partition_broadcast(128), wr=[bpbi])
        invf, binvf = T("invf", [128, 8]); c.dma("sp", invf[:], invf_d.partition_broadcast(128), wr=[binvf])
        qkg, bqkg = T("qkg", [128, 1024]); c.dma("sp", qkg[:], qkg_d.partition_broadcast(128), wr=[bqkg])
        posi, bposi = T("posi", [128, NT], I32); c.dma("sp", posi[:], pos, wr=[bposi])
        posf, bposf = T("posf", [128, NT])
        c.op("dve", lambda e: e.tensor_copy(posf[:], posi[:]), rd=[bposi], wr=[bposf])
        yy, byy = T("yy", [128, NT, 8]); yi, byi = T("yi", [128, NT, 8], I32); yf, byf = T("yf", [128, NT, 8])
        sinT, bsinT = T("sinT", [128, NT, 8]); cosT, bcosT = T("cosT", [128, NT, 8])
        c.op("dve", lambda e: e.tensor_copy(yy[:], bcast(invf[:], [128, NT, 8], 1)), rd=[binvf], wr=[byy])
        c.op("dve", lambda e: e.tensor_tensor(yy[:], yy[:], bcast(posf[:], [128, NT, 8], 2), ALU.mult), rd=[byy, bposf], wr=[byy])
        for (dst, bdst, off) in ((sinT, bsinT, 0.0), (cosT, bcosT, 0.25)):
            if off != 0.0:
                c.op("dve", lambda e: e.tensor_scalar(yy[:], yy[:], off, None, ALU.add), rd=[byy], wr=[byy])
            c.op("dve", lambda e: e.tensor_copy(yi[:], yy[:]), rd=[byy], wr=[byi])
            c.op("dve", lambda e: e.tensor_copy(yf[:], yi[:]), rd=[byi], wr=[byf])
            c.op("dve", lambda e: e.tensor_tensor(yf[:], yy[:], yf[:], ALU.subtract), rd=[byy, byf], wr=[byf])
            c.op("act", lambda e, dst=dst: e.activation(dst[:], yf[:], AF.Sin, scale=2.0 * math.pi), rd=[byf], wr=[bdst])
        KT = es.enter_context(nc.sbuf_tensor("s_KT", [80, 8, S], BF16)); bKT = [Buf("KT%d" % g) for g in range(8)]
        VA = es.enter_context(nc.sbuf_tensor("s_VA", [128, NT, 8, 65], BF16)); bVA = [Buf("VA%d" % g) for g in range(8)]
        c.op("pool", lambda e: e.memset(VA[:], 1.0), wr=bVA)
        kmT, bkmT = T("kmT", [64, 8, 16], BF16); c.op("dve", lambda e: e.memset(kmT[:], 0.0), wr=[bkmT])
        km32, bkm32 = T("km32", [64, 8])
        qkvs = [T("qkv%d" % i, [128, 1536]) for i in range(2)]
        sq3, bsq3 = T("sq3", [128, 1024]); ss3, bss3 = T("ss3", [128, 16]); qn, bqn = T("qn", [128, 16, 64])
        rt, brt = T("rt", [128, 4, 16, 8])
        QA, bQA = T("QA", [128, 8, 80], BF16); KA, bKA = T("KA", [128, 8, 80], BF16)
        QT, bQT = T("QT", [64, 8, 128], BF16)
        QTAs = [T("QTA%d" % i, [80, 8, 512], BF16) for i in range(2)]
        gm, bgm = T("gm", [128, 8, 16]); mx, bmx = T("mx", [128, 8, 8]); sel, bsel = T("sel", [128, 8, 16])
        PTs = [T("PT%d" % i, [128, 512], BF16) for i in range(4)]
        OT, bOT = T("OT", [65, 512]); rr, brr = T("rr", [65, 512], BF16); r32, br32 = T("r32", [65, 512])
        c.op("dve", lambda e: e.memset(rr[:], 0.0), wr=[brr])
        yTs = [T("yT%d" % i, [64, 512], BF16) for i in range(2)]
        cnt3 = {"pt": 0, "ld": 0}

        def ld3(tt):
            c.dma("sp", qkvs[tt % 2][0][:], zT[tt * 128:(tt + 1) * 128, 512:2048], wr=[qkvs[tt % 2][1]])

        def pre3(tt):
            if tt + 1 < NT:
                ld3(tt + 1)
            qkv, bqkv = qkvs[tt % 2]
            t0 = tt * 128; qb = tt // 2; g = tt // 4; ti = tt % 4
            QTA, bQTA = QTAs[g % 2]
            c.op("act", lambda e: e.activation(sq3[:], qkv[:, 0:1024], AF.Square), rd=[bqkv], wr=[bsq3])
            c.op("dve", lambda e: e.tensor_reduce(ss3[:], sq3[:].rearrange("p (h d) -> p h d", h=16), AX.X, ALU.add), rd=[bsq3], wr=[bss3])
            c.op("dve", lambda e: e.tensor_scalar(ss3[:], ss3[:], 1.0 / 64, 1e-6, ALU.mult, ALU.add), rd=[bss3], wr=[bss3])
            c.op("act", lambda e: e.activation(ss3[:], ss3[:], AF.Ln), rd=[bss3], wr=[bss3])
            c.op("act", lambda e: e.activation(ss3[:], ss3[:], AF.Exp, scale=-0.5), rd=[bss3], wr=[bss3])
            c.op("dve", lambda e: e.tensor_tensor(qn[:], qkv[:, 0:1024].rearrange("p (h d) -> p h d", h=16), bcast(ss3[:], [128, 16, 64], 2), ALU.mult), rd=[bqkv, bss3], wr=[bqn])
            c.op("pool", lambda e: e.tensor_tensor(qn[:].rearrange("p h d -> p (h d)"), qn[:].rearrange("p h d -> p (h d)"), qkg[:], ALU.mult), rd=[bqn, bqkg], wr=[bqn])
            cs = cosT[:, tt:tt + 1, :].to_broadcast([128, 16, 8]); sn = sinT[:, tt:tt + 1, :].to_broadcast([128, 16, 8])
            c.op("dve", lambda e: e.tensor_tensor(rt[:, 0, :, :], qn[:, :, 0:8], cs, ALU.mult), rd=[bqn, bcosT], wr=[brt])
            c.op("dve", lambda e: e.tensor_tensor(rt[:, 1, :, :], qn[:, :, 8:16], sn, ALU.mult), rd=[bqn, bsinT], wr=[brt])
            c.op("pool", lambda e: e.tensor_tensor(rt[:, 2, :, :], qn[:, :, 8:16], cs, ALU.mult), rd=[bqn, bcosT], wr=[brt])
            c.op("pool", lambda e: e.tensor_tensor(rt[:, 3, :, :], qn[:, :, 0:8], sn, ALU.mult), rd=[bqn, bsinT], wr=[brt])
            c.op("dve", lambda e: e.tensor_tensor(qn[:, :, 0:8], rt[:, 0, :, :], rt[:, 1, :, :], ALU.subtract), rd=[brt], wr=[bqn])
            c.op("dve", lambda e: e.tensor_tensor(qn[:, :, 8:16], rt[:, 2, :, :], rt[:, 3, :, :], ALU.add), rd=[brt], wr=[bqn])
            c.op("dve", lambda e: e.tensor_copy(QA[:, :, 0:64], qn[:, 0:8, :]), rd=[bqn], wr=[bQA])
            c.op("pool", lambda e: e.tensor_copy(KA[:, :, 0:64], qn[:, 8:16, :]), rd=[bqn], wr=[bKA])
            c.op("pool", lambda e: e.memset(KA[:, :, 64:80], 0.0), wr=[bKA])
            c.op("pool", lambda e: e.memset(KA[:, :, 64 + qb:65 + qb], 1.0), wr=[bKA])
            p, bp = ps(); pb = p[:].bitcast(BF16)
            c.group("pe", [lambda e, h=h: e.transpose(pb[0:80, h * 128:(h + 1) * 128], KA[:, h, :], idb[:]) for h in range(8)], rd=[bKA, bidb], wr=[bp])
            c.op("dve", lambda e: e.tensor_copy(KT[:, :, t0:t0 + 128], pb[0:80, :].rearrange("p (h t) -> p h t", h=8)), rd=[bp], wr=[bKT[g]])
            c.op("pool", lambda e: e.tensor_copy(VA[:, tt, :, 0:64], qkv[:, 1024:1536].rearrange("p (h d) -> p h d", h=8)), rd=[bqkv], wr=[bVA[g]])
            if qb > 0:
                p, bp = ps(); pb = p[:].bitcast(BF16)
                c.group("pe", [lambda e, h=h: e.transpose(pb[0:64, h * 128:(h + 1) * 128], QA[:, h, 0:64], idb[:]) for h in range(8)], rd=[bQA, bidb], wr=[bp])
                c.op("dve", lambda e: e.tensor_copy(QT[:], pb[0:64, :].rearrange("p (h t) -> p h t", h=8)), rd=[bp], wr=[bQT])
                p, bp = ps()
                c.group("pe", [lambda e, h=h, p=p: e.matmul(p[:, h * 16:(h + 1) * 16], QT[:, h, :], kmT[:, h, :], start=True, stop=True) for h in range(8)],
                        rd=[bQT, bkmT], wr=[bp])
                c.op("dve", lambda e, p=p: e.tensor_tensor(gm[:], p[:, 0:128].rearrange("p (h j) -> p h j", h=8), pbi[:, qb:qb + 1, :].to_broadcast([128, 8, 16]), ALU.add),
                     rd=[bp, bpbi], wr=[bgm])
                for h in range(8):
                    c.op("dve", lambda e, h=h: e.max(mx[:, h, :], gm[:, h, :]), rd=[bgm], wr=[bmx])
                c.op("dve", lambda e: e.tensor_tensor(sel[:], gm[:], mx[:, :, 2:3].to_broadcast([128, 8, 16]), ALU.is_ge), rd=[bgm, bmx], wr=[bsel])
                c.op("dve", lambda e: e.tensor_tensor(sel[:], sel[:], pok[:, qb:qb + 1, :].to_broadcast([128, 8, 16]), ALU.mult), rd=[bsel, bpok], wr=[bsel])
                c.op("dve", lambda e: e.tensor_scalar(QA[:, :, 64:80], sel[:], BIG, -BIG, ALU.mult, ALU.add), rd=[bsel], wr=[bQA])
            else:
                c.op("dve", lambda e: e.memset(QA[:, :, 64:80], -BIG), wr=[bQA])
            p, bp = ps(); pb = p[:].bitcast(BF16)
            c.group("pe", [lambda e, h=h: e.transpose(pb[0:80, h * 128:(h + 1) * 128], QA[:, h, :], idb[:]) for h in range(8)], rd=[bQA, bidb], wr=[bp])
            c.op("dve", lambda e: e.tensor_copy(QTA[:, :, ti * 128:(ti + 1) * 128], pb[0:80, :].rearrange("p (h t) -> p h t", h=8)), rd=[bp], wr=[bQTA])
            if tt % 2 == 1:
                c.op("dve", lambda e: e.tensor_reduce(km32[:], KT[0:64, :, qb * 256:(qb + 1) * 256], AX.X, ALU.add), rd=[bKT[g]], wr=[bkm32])
                c.op("dve", lambda e: e.tensor_scalar(kmT[:, :, qb], km32[:], 1.0 / 256, None, ALU.mult), rd=[bkm32], wr=[bkmT])

        def att3(g, h):
            QTA, bQTA = QTAs[g % 2]
            pOh, bOh = pO[h % 2]
            Oh = pOh[0:65, :]
            kbufs = bKT[0:g + 1]; vbufs = bVA[0:g + 1]
            first = True
            for kt in range(0, 4 * g + 2):
                p, bp = ps()
                c.op("pe", lambda e, p=p: e.matmul(p[:], KT[:, h, kt * 128:(kt + 1) * 128], QTA[:, h, :], start=True, stop=True), rd=kbufs + [bQTA], wr=[bp])
                PT, bPT = PTs[cnt3["pt"] % 4]; cnt3["pt"] += 1
                c.op("act", lambda e, p=p: e.activation(PT[:], p[:], AF.Exp, scale=0.125), rd=[bp], wr=[bPT])
                c.op("pe", lambda e, st_=first: e.matmul(Oh, VA[:, kt, h, :], PT[:], start=st_, stop=False), rd=vbufs + [bPT], wr=[bOh])
                first = False
            for half in range(2):
                for kti in range(2):
                    kt = 4 * g + 2 * half + kti
                    qs = slice(half * 256, (half + 1) * 256)
                    p, bp = ps()
                    c.group("pe", [lambda e, p=p: e.matmul(p[:, 0:256], KT[0:64, h, kt * 128:(kt + 1) * 128], QTA[0:64, h, qs], start=True, stop=False),
                                   lambda e, p=p: e.matmul(p[:, 0:256], idb[:], CB[:, kti, :], start=False, stop=True)], rd=kbufs + [bQTA, bidb, bCB], wr=[bp])
                    PT, bPT = PTs[cnt3["pt"] % 4]; cnt3["pt"] += 1
                    c.op("act", lambda e, p=p: e.activation(PT[:, 0:256], p[:, 0:256], AF.Exp, scale=0.125), rd=[bp], wr=[bPT])
                    last = (half == 1 and kti == 1)
                    c.op("pe", lambda e: e.matmul(pOh[0:65, qs], VA[:, kt, h, :], PT[:, 0:256], start=False, stop=last), rd=vbufs + [bPT], wr=[bOh])
            c.op("dve", lambda e: e.tensor_copy(OT[:], Oh), rd=[bOh], wr=[bOT])
            c.op("dve", lambda e: e.reciprocal(r32[64:65, :], OT[64:65, :]), rd=[bOT], wr=[br32])
            c.op("dve", lambda e: e.tensor_copy(rr[64:65, :], r32[64:65, :]), rd=[br32], wr=[brr])
            p, bp = ps()
            c.op("pe", lambda e, p=p: e.matmul(p[0:64, :], s65[:], rr[:], start=True, stop=True), rd=[bs65, brr], wr=[bp])
            yT, byT = yTs[h % 2]
            c.op("dve", lambda e, p=p: e.tensor_tensor(yT[:], OT[0:64, :], p[0:64, :], ALU.mult), rd=[bOT, bp], wr=[byT])
            c.dma("sp", ybF[h * 64:(h + 1) * 64, g * 512:(g + 1) * 512], yT[:], rd=[byT])

        ld3(0)
        for tt in range(4):
            pre3(tt)
        for g in range(8):
            for h in range(8):
                att3(g, h)
                if g + 1 < 8 and h % 2 == 1:
                    pre3(4 * (g + 1) + h // 2)
        pst["n"] = 8
        c.barrier()

    if upto <= 3:
        print("ninst", c.ninst, "nwait", c.nwait); return nc
    with ExitStack() as es:
        def T(name, shape, dt=F32):
            return es.enter_context(nc.sbuf_tensor("s_" + name, list(shape), dt)), Buf(name)
        idb, bidb = T("idb4", [128, 128], BF16); c.dma("pool", idb[:], ident_d, wr=[bidb])
        wba, bwba = T("wba", [128, 4, DM], BF16); wbb, bwbb = T("wbb", [128, 4, DM], BF16); wo, bwo = T("wo", [128, 8, DM], BF16)
        for k in range(4):
            c.dma("pool", wba[:, k, :], wba_d[k * 128:(k + 1) * 128, :], wr=[bwba])
            c.dma("pool", wbb[:, k, :], wbb_d[k * 128:(k + 1) * 128, :], wr=[bwbb])
        for k in range(8):
            c.dma("pool", wo[:, k, :], wout_d[k * 128:(k + 1) * 128, :], wr=[bwo])
        xts = [T("x4%d" % i, [128, DM]) for i in range(2)]
        gps = [T("gp%d" % i, [128, 2048]) for i in range(2)]
        yas = [T("ya4%d" % i, [128, 4, 128], BF16) for i in range(2)]
        ybs = [T("yb4%d" % i, [128, 4, 128], BF16) for i in range(2)]
        m1, bm1 = T("m1", [128, DM]); m2, bm2 = T("m2", [128, DM]); mb, bmb = T("mb", [128, DM], BF16)
        mT, bmT = T("mT", [128, 8, 128], BF16)
        x1, bx1 = T("x1", [128, DM]); junk, bjunk = T("junk4", [128, DM]); h2, bh2 = T("h2", [128, DM], BF16)
        h2T, bh2T = T("h2T", [128, 8, 128], BF16)
        ss, bss = T("ss4", [128, NT]); c.op("dve", lambda e: e.memset(ss[:], 0.0), wr=[bss])
        rs, brs = T("rs4", [128, NT])
        yaFv = yaF.rearrange("(c p) t -> p c t", p=128); ybFv = ybF.rearrange("(c p) t -> p c t", p=128)
        h2Fv = h2F.rearrange("(c p) t -> p c t", p=128)

        def loads4(tt):
            i = tt % 2; t0 = tt * 128
            c.dma("sp", xts[i][0][:], x[t0:t0 + 128, :], wr=[xts[i][1]])
            c.dma("sp", gps[i][0][:], zT[t0:t0 + 128, 2048:4096], wr=[gps[i][1]])
            c.dma("sp", yas[i][0][:], yaFv[:, :, t0:t0 + 128], wr=[yas[i][1]])
            c.dma("sp", ybs[i][0][:], ybFv[:, :, t0:t0 + 128], wr=[ybs[i][1]])
        loads4(0)
        for tt in range(NT):
            if tt + 1 < NT:
                loads4(tt + 1)
            i = tt % 2; t0 = tt * 128
            xt, bxt = xts[i]; gp, bgp = gps[i]; ya, bya = yas[i]; yb, byb = ybs[i]
            c.op("act", lambda e: e.activation(gp[:], gp[:], AF.Sigmoid), rd=[bgp], wr=[bgp])
            for half in range(2):
                hs = slice(half * 512, (half + 1) * 512)
                pa, bpa = ps()
                c.group("pe", [lambda e, k=k: e.matmul(pa[:], ya[:, k, :], wba[:, k, hs], start=(k == 0), stop=(k == 3)) for k in range(4)], rd=[bya, bwba], wr=[bpa])
                c.op("dve", lambda e: e.tensor_tensor(m1[:, hs], pa[:], gp[:, half * 512:(half + 1) * 512], ALU.mult), rd=[bpa, bgp], wr=[bm1])
                pb_, bpb_ = ps()
                c.group("pe", [lambda e, k=k: e.matmul(pb_[:], yb[:, k, :], wbb[:, k, hs], start=(k == 0), stop=(k == 3)) for k in range(4)], rd=[byb, bwbb], wr=[bpb_])
                c.op("dve", lambda e: e.tensor_tensor(m2[:, hs], pb_[:], gp[:, 1024 + half * 512:1024 + (half + 1) * 512], ALU.mult), rd=[bpb_, bgp], wr=[bm2])
            c.op("pool", lambda e: e.tensor_tensor(mb[:], m1[:], m2[:], ALU.add), rd=[bm1, bm2], wr=[bmb])
            p, bp = ps(); pb = p[:].bitcast(BF16)
            c.group("pe", [lambda e, k=k: e.transpose(pb[:, k * 128:(k + 1) * 128], mb[:, k * 128:(k + 1) * 128], idb[:]) for k in range(8)], rd=[bmb, bidb], wr=[bp])
            c.op("act", lambda e: e.copy(mT[:].rearrange("p a b -> p (a b)"), pb), rd=[bp], wr=[bmT])
            for half in range(2):
                hs = slice(half * 512, (half + 1) * 512)
                po, bpo = ps()
                c.group("pe", [lambda e, k=k: e.matmul(po[:], mT[:, k, :], wo[:, k, hs], start=(k == 0), stop=(k == 7)) for k in range(8)], rd=[bmT, bwo], wr=[bpo])
                c.op("dve", lambda e: e.tensor_tensor(x1[:, hs], po[:], xt[:, hs], ALU.add), rd=[bpo, bxt], wr=[bx1])
            c.dma("sp", x1s[t0:t0 + 128, :], x1[:], rd=[bx1])
            c.op("act", lambda e: e.activation(junk[:], x1[:], AF.Square, accum_out=ss[:, tt:tt + 1]), rd=[bx1], wr=[bjunk, bss])
            c.op("dve", lambda e: e.tensor_scalar(rs[:, tt:tt + 1], ss[:, tt:tt + 1], 1.0 / DM, 1e-6, ALU.mult, ALU.add), rd=[bss], wr=[brs])
            c.op("act", lambda e: e.activation(rs[:, tt:tt + 1], rs[:, tt:tt + 1], AF.Ln), rd=[brs], wr=[brs])
            c.op("act", lambda e: e.activation(rs[:, tt:tt + 1], rs[:, tt:tt + 1], AF.Exp, scale=-0.5), rd=[brs], wr=[brs])
            c.op("dve", lambda e: e.tensor_scalar(h2[:], x1[:], rs[:, tt:tt + 1], None, ALU.mult), rd=[bx1, brs], wr=[bh2])
            p, bp = ps(); pb = p[:].bitcast(BF16)
            c.group("pe", [lambda e, k=k: e.transpose(pb[:, k * 128:(k + 1) * 128], h2[:, k * 128:(k + 1) * 128], idb[:]) for k in range(8)], rd=[bh2, bidb], wr=[bp])
            c.op("act", lambda e: e.copy(h2T[:].rearrange("p a b -> p (a b)"), pb), rd=[bp], wr=[bh2T])
            c.dma("sp", h2Fv[:, :, t0:t0 + 128], h2T[:], rd=[bh2T])
        c.barrier()

    if upto <= 4:
        print("ninst", c.ninst, "nwait", c.nwait); return nc
    with ExitStack() as es:
        def T(name, shape, dt=F32):
            return es.enter_context(nc.sbuf_tensor("s_" + name, list(shape), dt)), Buf(name)
        wup, bwup = T("wup", [128, 8, 2 * DFF], BF16); wdn, bwdn = T("wdn", [128, NFF, DM], BF16)
        g2, bg2 = T("g2", [128, 8]); c.dma("sp", g2[:], n2g, wr=[bg2])
        wub = [Buf("wup_%d" % k) for k in range(8)]
        for k in range(8):
            c.dma("pool", wup[:, k, :], wup_d[k * 128:(k + 1) * 128, :], wr=[wub[k]])
            c.op("dve", lambda e, k=k: e.tensor_scalar(wup[:, k, :], wup[:, k, :], g2[:, k:k + 1], None, ALU.mult), rd=[wub[k], bg2], wr=[wub[k]])
        for f in range(NFF):
            c.dma("pool", wdn[:, f, :], wdn_d[f * 128:(f + 1) * 128, :], wr=[bwdn])
        cw, bcw = T("cw", [128, NFF, 3]); c.dma("sp", cw[:].rearrange("p a b -> p (a b)"), cw_d, wr=[bcw])
        cbt, bcbt = T("cbt", [128, NFF]); c.dma("sp", cbt[:], cb_d, wr=[bcbt])
        cr, bcr = T("cr", [128, NFF, 2]); c.op("dve", lambda e: e.memset(cr[:], 0.0), wr=[bcr])
        h2s = [T("h2s%d" % i, [128, 8, 512], BF16) for i in range(2)]
        asb = [T("asb%d" % i, [128, 514]) for i in range(2)]
        acc = [T("acc%d" % i, [128, 512]) for i in range(2)]
        hg = es.enter_context(nc.sbuf_tensor("hg", [128, NFF, 512], BF16)); bhg = [Buf("hg%d" % f) for f in range(NFF)]
        x1t = [T("x1t%d" % i, [128, DM]) for i in range(2)]
        ost = [T("ost%d" % i, [128, DM]) for i in range(2)]
        h2Fv = h2F.rearrange("(c p) t -> p c t", p=128)
        c.dma("sp", h2s[0][0][:], h2Fv[:, :, 0:512], wr=[h2s[0][1]])
        nx = 0
        for st in range(8):
            if st + 1 < 8:
                c.dma("sp", h2s[(st + 1) % 2][0][:], h2Fv[:, :, (st + 1) * 512:(st + 2) * 512], wr=[h2s[(st + 1) % 2][1]])
            hT, bhT = h2s[st % 2]
            for f in range(NFF):
                pa, bpa = ps()
                c.group("pe", [lambda e, k=k: e.matmul(pa[:], wup[:, k, f * 128:(f + 1) * 128], hT[:, k, :], start=(k == 0), stop=(k == 7)) for k in range(8)], rd=[bhT] + wub, wr=[bpa])
                pg, bpg = ps()
                c.group("pe", [lambda e, k=k: e.matmul(pg[:], wup[:, k, DFF + f * 128:DFF + (f + 1) * 128], hT[:, k, :], start=(k == 0), stop=(k == 7)) for k in range(8)], rd=[bhT] + wub, wr=[bpg])
                a, ba = asb[f % 2]; ac, bac = acc[f % 2]
                c.op("act", lambda e: e.copy(a[:, 0:2], cr[:, f, :]), rd=[bcr], wr=[ba])
                c.op("act", lambda e: e.copy(a[:, 2:514], pa[:]), rd=[bpa], wr=[ba])
                c.op("act", lambda e: e.copy(cr[:, f, :], a[:, 512:514]), rd=[ba], wr=[bcr])
                c.op("dve", lambda e: e.tensor_scalar(ac[:], a[:, 0:512], cw[:, f, 0:1], None, ALU.mult), rd=[ba, bcw], wr=[bac])
                c.op("dve", lambda e: e.scalar_tensor_tensor(ac[:], a[:, 1:513], cw[:, f, 1:2], ac[:], ALU.mult, ALU.add), rd=[ba, bcw, bac], wr=[bac])
                c.op("dve", lambda e: e.scalar_tensor_tensor(ac[:], a[:, 2:514], cw[:, f, 2:3], ac[:], ALU.mult, ALU.add), rd=[ba, bcw, bac], wr=[bac])
                c.op("act", lambda e: e.activation(ac[:], ac[:], AF.Gelu, bias=cbt[:, f:f + 1]), rd=[bac, bcbt], wr=[bac])
                c.op("dve", lambda e: e.tensor_tensor(hg[:, f, :], ac[:], pg[:], ALU.mult), rd=[bac, bpg], wr=[bhg[f]])
            for sub in range(4):
                tt = st * 4 + sub; t0 = tt * 128
                xx, bxx = x1t[nx % 2]; oo, boo = ost[nx % 2]; nx += 1
                c.dma("sp", xx[:], x1s[t0:t0 + 128, :], wr=[bxx])
                for half in range(2):
                    hs = slice(half * 512, (half + 1) * 512)
                    po, bpo = ps()
                    c.group("pe", [lambda e, f=f: e.matmul(po[:], hg[:, f, sub * 128:(sub + 1) * 128], wdn[:, f, hs], start=(f == 0), stop=(f == NFF - 1)) for f in range(NFF)],
                            rd=bhg + [bwdn], wr=[bpo])
                    c.op("dve", lambda e: e.tensor_tensor(oo[:, hs], po[:], xx[:, hs], ALU.add), rd=[bpo, bxx], wr=[boo])
                c.dma("sp", out[t0:t0 + 128, :], oo[:], rd=[boo])
        c.barrier()
    print("ninst", c.ninst, "nwait", c.nwait, {e: c.cnt[e] for e in c.cnt})
    return nc


def _consts():
    i = np.arange(128)
    su = (i[:, None] < i[None, :]).astype(np.float32)
    ui = (i[:, None] <= i[None, :]).astype(np.float32)
    mu2 = np.concatenate([su, ui, su, ui], axis=1)
    sl = (i[None, :] < i[:, None]).astype(np.float32)
    sl2 = np.concatenate([sl, sl], axis=1)
    bones = ((i[:, None] // 64) == (i[None, :] // 64)).astype(np.float32)
    q2 = np.arange(256)
    cb0 = np.where(i[:, None] > q2[None, :], -BIG, 0.0).astype(np.float32)
    cb1 = np.where(i[:, None] + 128 > q2[None, :], -BIG, 0.0).astype(np.float32)
    cbias2 = np.concatenate([cb0, cb1], axis=1)
    sel65 = np.zeros((65, 64), np.float32); sel65[64, :] = 1.0
    j = np.arange(16)
    pok = (j[None, :] < j[:, None]).astype(np.float32)
    pbias = ((pok - 1.0) * 1e30).astype(np.float32)
    half = 8
    invf = (500000.0 ** (-np.arange(half, dtype=np.float32) / half)).astype(np.float32) / np.float32(2.0 * math.pi)
    return dict(ident=np.eye(128, dtype=np.float32), mu2=mu2, sl2=sl2, bones=bones, cbias2=cbias2, sel65=sel65,
                pok=pok.reshape(1, 256), pbias=pbias.reshape(1, 256), invf=invf.reshape(1, 8).astype(np.float32))


def _fm(v, n):
    return np.ascontiguousarray(np.asarray(v, np.float32).reshape(n, 128).T)


def _prep(inp):
    f = lambda a: np.ascontiguousarray(np.asarray(a, dtype=np.float32))
    w_in = f(inp["w_in"][0])
    fm_cols = np.r_[0:512, 512:1024, 1536:1600, 1600:1664, 1664:1824]
    tm_cols = np.r_[1024:1536, 1824:3360, 3360:5408]
    mu = f(inp["rwkv_mu"][0])
    mu_fm = np.zeros(1408, np.float32); mu_fm[:FMC] = mu[fm_cols]
    rk = f(inp["rwkv_r_k"][0])
    rkb = np.zeros((128, 8), np.float32)
    for h in range(8):
        rkb[(h % 2) * 64:(h % 2 + 1) * 64, h] = rk[h]
    cw = f(inp["ffn_conv_w"][0])
    cwl = np.ascontiguousarray(cw.reshape(3, NFF, 128).transpose(2, 1, 0)).reshape(128, NFF * 3)
    shared = dict(
        w_in=np.ascontiguousarray(w_in[:, np.r_[fm_cols, tm_cols]]),
        n1g=_fm(inp["norm1_g"][0], 8), mu_fm=_fm(mu_fm, 11), mu_v=f(mu[1024:1536]).reshape(1, 512),
        wdec=np.concatenate([f(inp["w_decay_up"][0]), f(inp["decay_bias"][0]).reshape(1, 512)], 0),
        waaa=np.concatenate([f(inp["w_aaa_up"][0]), f(inp["aaa_bias"][0]).reshape(1, 512)], 0),
        wgate=f(inp["w_gate_up"][0]), kk_fm=_fm(inp["rwkv_k_k"][0], 4), ka_fm=_fm(inp["rwkv_k_a"][0], 4), rkb=rkb,
        lng=f(inp["rwkv_ln_g"][0]).reshape(1, 512), lnb=f(inp["rwkv_ln_b"][0]).reshape(1, 512),
        qkg=np.concatenate([np.tile(f(inp["q_norm_g"][0]), 8), np.tile(f(inp["k_norm_g"][0]), 8)]).reshape(1, 1024),
        wba=f(inp["w_branch_a"][0]), wbb=f(inp["w_branch_b"][0]), wout=f(inp["w_out"][0]), n2g=_fm(inp["norm2_g"][0], 8),
        wup=f(inp["w_ffn_up"][0]), cw=cwl, cb=_fm(inp["ffn_conv_b"][0], NFF), wdn=f(inp["w_ffn_down"][0]),
    )
    shared.update(_consts())
    xs = np.asarray(inp["x"], np.float32); ps_ = np.asarray(inp["positions"], np.int32)
    maps = []
    for b in range(8):
        m = dict(shared)
        m["x"] = np.ascontiguousarray(xs[b])
        m["pos"] = np.ascontiguousarray(ps_[b].reshape(NT, 128).T)
        maps.append(m)
    return maps


def kernel(**inputs):
    maps = _prep(inputs)
    nc = build()
    res = run_bass_kernel_spmd(nc, maps, core_ids=list(range(8)))
    return np.stack([np.asarray(r["out"], np.float32) for r in res.results], axis=0)
```

```python
import numpy as np
import concourse.bass as bass
import concourse.mybir as mybir
from concourse.bass_utils import run_bass_kernel_spmd

F32 = mybir.dt.float32
BF16 = mybir.dt.bfloat16
I32 = mybir.dt.int32
ALU = mybir.AluOpType
AF = mybir.ActivationFunctionType
AX = mybir.AxisListType


class Buf:
    __slots__ = ("name", "lastw", "readers")

    def __init__(self, name):
        self.name = name
        self.lastw = None
        self.readers = {}


class Ctx:
    def __init__(self, nc, n_dma_sems=24):
        self.nc = nc
        self.eng = {"pe": nc.tensor, "act": nc.scalar, "dve": nc.vector,
                    "pool": nc.gpsimd, "sp": nc.sync}
        self.sem = {}
        self.cnt = {}
        self.waited = {e: {} for e in self.eng}
        self._stack = []
        for e in ("pe", "act", "dve", "pool"):
            cm = nc.semaphore("s_" + e)
            self.sem[e] = cm.__enter__()
            self._stack.append(cm)
            self.cnt[e] = 0
        self.dsem = []
        self.dpool = {"hw": [], "sw": []}
        for i in range(n_dma_sems):
            cm = nc.semaphore("d%d" % i)
            self.dsem.append([cm.__enter__(), 0])
            self._stack.append(cm)
            self.dpool["sw" if i < 8 else "hw"].append(i)
        self.dnext = {"hw": 0, "sw": 0}
        self.semh = {}
        for e in self.sem:
            self.semh[("e", e)] = self.sem[e]
        for i, (h, _) in enumerate(self.dsem):
            self.semh[("d", i)] = h
        self.nwait = 0
        self.ninst = 0

    def _wait(self, e, toks):
        w = self.waited[e]
        best = {}
        for t in toks:
            if t is None:
                continue
            k, v = t[0], t[1]
            if w.get(k, 0) >= v:
                continue
            if best.get(k, 0) < v:
                best[k] = v
        for k, v in best.items():
            self.eng[e].wait_ge(self.semh[k], v)
            w[k] = v
            self.nwait += 1

    def _deps(self, e, rd, wr):
        toks = []
        me = ("e", e)
        for b in rd:
            if b.lastw is not None:
                if not (e == "pe" and b.lastw[0] == me):
                    toks.append(b.lastw)
        for b in wr:
            if b.lastw is not None and not (e == "pe" and b.lastw[0] == me):
                toks.append(b.lastw)
            for k, t in b.readers.items():
                if not (e == "pe" and k == me):
                    toks.append(t)
        return toks

    def _mark(self, tok, rd, wr):
        for b in rd:
            b.readers[tok[0]] = tok
        for b in wr:
            b.lastw = tok
            b.readers = {}

    def op(self, e, fn, rd=(), wr=()):
        self._wait(e, self._deps(e, rd, wr))
        ins = fn(self.eng[e])
        self.cnt[e] += 1
        ins.then_inc(self.sem[e], 1)
        tok = (("e", e), self.cnt[e])
        self._mark(tok, rd, wr)
        self.ninst += 1
        return tok

    def group(self, e, fns, rd=(), wr=()):
        self._wait(e, self._deps(e, rd, wr))
        ins = None
        for fn in fns:
            ins = fn(self.eng[e])
            self.ninst += 1
        self.cnt[e] += 1
        ins.then_inc(self.sem[e], 1)
        tok = (("e", e), self.cnt[e])
        self._mark(tok, rd, wr)
        return tok

    def dma(self, q, out, in_, rd=(), wr=(), **kw):
        kind = "sw" if q == "pool" else "hw"
        pool = self.dpool[kind]
        i = pool[self.dnext[kind] % len(pool)]
        self.dnext[kind] += 1
        h, c = self.dsem[i]
        k = ("d", i)
        toks = self._deps(q, rd, wr)
        if c > 0:
            toks.append((k, 16 * c))
        self._wait(q, toks)
        self.eng[q].dma_start(out=out, in_=in_, **kw).then_inc(h, 16)
        self.dsem[i][1] = c + 1
        tok = (k, 16 * (c + 1))
        self._mark(tok, rd, wr)
        self.ninst += 1
        return tok

    def wait_all(self, e, bufs):
        toks = []
        for b in bufs:
            toks.append(b.lastw)
            toks.extend(b.readers.values())
        self._wait(e, toks)

    def barrier(self, bufs=()):
        toks = []
        for e in self.sem:
            if self.cnt[e] > 0:
                toks.append((("e", e), self.cnt[e]))
        for i, (h, c) in enumerate(self.dsem):
            if c > 0:
                toks.append((("d", i), 16 * c))
        for e in self.eng:
            self._wait(e, toks)

from contextlib import ExitStack
import math
import os

S = 4096
DM = 1024
NT = 32
FMC = 1312
TMC = 4096
DFF = 2816
NFF = 22
BIG = 30000.0


def build(debug=False, upto=99):
    nc = bass.Bass("TRN2", target_bir_lowering=False)
    okind = "ExternalOutput" if debug else "Internal"

    def DIN(name, shape, dt=F32):
        return nc.dram_tensor(name, list(shape), dt, kind="ExternalInput").ap()

    x = DIN("x", [S, DM]); pos = DIN("pos", [128, NT], I32)
    w_in = DIN("w_in", [DM, 5408]); n1g = DIN("n1g", [128, 8]); mu_fm = DIN("mu_fm", [128, 11]); mu_v = DIN("mu_v", [1, 512])
    wdec_d = DIN("wdec", [65, 512]); waaa_d = DIN("waaa", [65, 512]); wgate_d = DIN("wgate", [160, 512])
    kk_d = DIN("kk_fm", [128, 4]); ka_d = DIN("ka_fm", [128, 4]); rkb_d = DIN("rkb", [128, 8])
    lng_d = DIN("lng", [1, 512]); lnb_d = DIN("lnb", [1, 512]); qkg_d = DIN("qkg", [1, 1024])
    wba_d = DIN("wba", [512, DM]); wbb_d = DIN("wbb", [512, DM]); wout_d = DIN("wout", [DM, DM]); n2g = DIN("n2g", [128, 8])
    wup_d = DIN("wup", [DM, 2 * DFF]); cw_d = DIN("cw", [128, NFF * 3]); cb_d = DIN("cb", [128, NFF]); wdn_d = DIN("wdn", [DFF, DM])
    ident_d = DIN("ident", [128, 128]); mu2_d = DIN("mu2", [128, 512]); sl2_d = DIN("sl2", [128, 256]); bones_d = DIN("bones", [128, 128])
    cbias2_d = DIN("cbias2", [128, 512]); sel65_d = DIN("sel65", [65, 64]); pok_d = DIN("pok", [1, 256]); pbias_d = DIN("pbias", [1, 256]); invf_d = DIN("invf", [1, 8])
    out = nc.dram_tensor("out", [S, DM], F32, kind="ExternalOutput").ap()
    zF = nc.dram_tensor("zF", [1408, S], F32, kind=okind).ap()
    zT = nc.dram_tensor("zT", [S, TMC], F32, kind=okind).ap()
    yaF = nc.dram_tensor("yaF", [512, S], BF16, kind=okind).ap()
    ybF = nc.dram_tensor("ybF", [512, S], BF16, kind=okind).ap()
    x1s = nc.dram_tensor("x1s", [S, DM], F32, kind=okind).ap()
    h2F = nc.dram_tensor("h2F", [DM, S], BF16, kind=okind).ap()

    c = Ctx(nc)
    PSB = [(nc.alloc_psum_tensor("psb%d" % i, [128, 512], F32), Buf("psb%d" % i)) for i in range(8)]
    pst = {"i": 0, "n": 8, "off": 0}

    def ps():
        i = pst["off"] + pst["i"] % pst["n"]
        pst["i"] += 1
        return PSB[i]

    rr = {"i": 0}

    def ev():
        rr["i"] += 1
        return "act" if rr["i"] % 2 else "dve"

    def bcast(ap, shape, axis):
        return ap.unsqueeze(axis).to_broadcast(list(shape))

    with ExitStack() as es:
        def T(name, shape, dt=F32):
            return es.enter_context(nc.sbuf_tensor("s_" + name, list(shape), dt)), Buf(name)
        w, bw = T("w1", [128, 8, 5408], BF16)
        g1, bg1 = T("g1", [128, 8]); muf, bmuf = T("muf", [128, 11])
        idb, bidb = T("idb1", [128, 128], BF16)
        c.dma("sp", g1[:], n1g, wr=[bg1]); c.dma("sp", muf[:], mu_fm, wr=[bmuf])
        c.dma("pool", idb[:], ident_d, wr=[bidb])
        wb = [Buf("w1_%d" % k) for k in range(8)]
        for kc in range(8):
            c.dma("pool", w[:, kc, :], w_in[kc * 128:(kc + 1) * 128, :], wr=[wb[kc]])
            c.op("dve", lambda e, kc=kc: e.tensor_scalar(w[:, kc, :], w[:, kc, :], g1[:, kc:kc + 1], None, ALU.mult),
                 rd=[wb[kc], bg1], wr=[wb[kc]])
        ss, bss = T("ss1", [128, NT]); c.op("dve", lambda e: e.memset(ss[:], 0.0), wr=[bss])
        rs, brs = T("rs1", [128, NT])
        junk, bjunk = T("junk1", [128, DM])
        xts = [T("xt%d" % i, [128, DM]) for i in range(2)]
        hb, bhb = T("hb1", [128, DM], BF16)
        hTs = [es.enter_context(nc.sbuf_tensor("hT%d" % i, [128, 8, 512], BF16)) for i in range(2)]
        hTb = [[Buf("hT%d_%d" % (i, s)) for s in range(4)] for i in range(2)]
        stgs = [T("stg%d" % i, [128, TMC]) for i in range(2)]
        zsb = [T("zsb%d" % j, [128, 513]) for j in range(11)]
        for j in range(11):
            c.op("pool", lambda e, j=j: e.memset(zsb[j][0][:, 0:1], 0.0), wr=[zsb[j][1]])
        tds = [T("td%d" % i, [128, 512]) for i in range(2)]
        ostg = [T("ostg%d" % i, [128, 512]) for i in range(3)]
        no = 0
        import os
        for st in range(int(os.environ.get('NST', '8'))):
            hT = hTs[st % 2]
            for sub in range(4):
                tt = st * 4 + sub
                xt, bxt = xts[tt % 2]
                c.dma("sp", xt[:], x[tt * 128:(tt + 1) * 128, :], wr=[bxt])
                c.op("act", lambda e: e.activation(junk[:], xt[:], AF.Square, accum_out=ss[:, tt:tt + 1]), rd=[bxt], wr=[bjunk, bss])
                c.op("dve", lambda e: e.tensor_scalar(rs[:, tt:tt + 1], ss[:, tt:tt + 1], 1.0 / DM, 1e-6, ALU.mult, ALU.add), rd=[bss], wr=[brs])
                c.op("act", lambda e: e.activation(rs[:, tt:tt + 1], rs[:, tt:tt + 1], AF.Ln), rd=[brs], wr=[brs])
                c.op("act", lambda e: e.activation(rs[:, tt:tt + 1], rs[:, tt:tt + 1], AF.Exp, scale=-0.5), rd=[brs], wr=[brs])
                c.op("dve", lambda e: e.tensor_scalar(hb[:], xt[:], rs[:, tt:tt + 1], None, ALU.mult), rd=[bxt, brs], wr=[bhb])
                p, bp = ps(); pb = p[:].bitcast(BF16)
                c.group("pe", [lambda e, k=k: e.transpose(pb[:, k * 128:(k + 1) * 128], hb[:, k * 128:(k + 1) * 128], idb[:]) for k in range(8)],
                        rd=[bhb, bidb], wr=[bp])
                c.op("act", lambda e: e.copy(hT[:, :, sub * 128:(sub + 1) * 128], pb.rearrange("p (k t) -> p k t", k=8)), rd=[bp], wr=[hTb[st % 2][sub]])
                stg, bstg = stgs[tt % 2]
                for gi in range(8):
                    p, bp = ps()
                    c.group("pe", [lambda e, k=k, p=p: e.matmul(p[:], hT[:, k, sub * 128:(sub + 1) * 128], w[:, k, FMC + gi * 512:FMC + (gi + 1) * 512],
                                                               start=(k == 0), stop=(k == 7)) for k in range(8)],
                            rd=[hTb[st % 2][sub]] + wb, wr=[bp])
                    en = ev()
                    if en == "act":
                        c.op("act", lambda e, p=p: e.copy(stg[:, gi * 512:(gi + 1) * 512], p[:]), rd=[bp], wr=[bstg])
                    else:
                        c.op("dve", lambda e, p=p: e.tensor_copy(stg[:, gi * 512:(gi + 1) * 512], p[:]), rd=[bp], wr=[bstg])
                c.dma("sp", zT[tt * 128:(tt + 1) * 128, :], stg[:], rd=[bstg])
            for j in range(11):
                ncol = 32 if j == 10 else 128
                z, bz = zsb[j]
                p, bp = ps()
                c.group("pe", [lambda e, k=k, p=p: e.matmul(p[0:ncol, :], w[:, k, j * 128:j * 128 + ncol], hT[:, k, :], start=(k == 0), stop=(k == 7)) for k in range(8)],
                        rd=hTb[st % 2] + wb, wr=[bp])
                c.op("act", lambda e, p=p: e.copy(z[0:ncol, 1:513], p[0:ncol, :]), rd=[bp], wr=[bz])
                td, btd = tds[j % 2]
                c.op("dve", lambda e: e.tensor_tensor(td[0:ncol, :], z[0:ncol, 0:512], z[0:ncol, 1:513], ALU.subtract), rd=[bz], wr=[btd])
                o, bo = ostg[no % 3]; no += 1
                c.op("dve", lambda e: e.scalar_tensor_tensor(o[0:ncol, :], td[0:ncol, :], muf[0:ncol, j:j + 1], z[0:ncol, 1:513], ALU.mult, ALU.add),
                     rd=[btd, bz, bmuf], wr=[bo])
                c.op("act", lambda e: e.copy(z[0:ncol, 0:1], z[0:ncol, 512:513]), rd=[bz], wr=[bz])
                c.dma("sp", zF[j * 128:j * 128 + ncol, st * 512:(st + 1) * 512], o[0:ncol, :], rd=[bo])
        c.barrier()

    if upto <= 1:
        print("ninst", c.ninst, "nwait", c.nwait); return nc
    with ExitStack() as es:
        def T(name, shape, dt=F32):
            return es.enter_context(nc.sbuf_tensor("s_" + name, list(shape), dt)), Buf(name)
        idb, bidb = T("idb2", [128, 128], BF16); c.dma("pool", idb[:], ident_d, wr=[bidb])
        wdec, bwdec = T("wdec", [65, 512], BF16); c.dma("pool", wdec[:], wdec_d, wr=[bwdec])
        waaa, bwaaa = T("waaa", [65, 512], BF16); c.dma("pool", waaa[:], waaa_d, wr=[bwaaa])
        wgate, bwgate = T("wgate", [128, 2, 512], BF16)
        c.dma("pool", wgate[:, 0, :], wgate_d[0:128, :], wr=[bwgate]); c.dma("pool", wgate[0:32, 1, :], wgate_d[128:160, :], wr=[bwgate])
        kkf, bkkf = T("kkf", [128, 4]); c.dma("sp", kkf[:], kk_d, wr=[bkkf])
        kaf, bkaf = T("kaf", [128, 4]); c.dma("sp", kaf[:], ka_d, wr=[bkaf])
        c0f, bc0f = T("c0f", [128, 4])
        c.op("dve", lambda e: e.tensor_scalar(c0f[:], kaf[:], -1.0, 1.0, ALU.mult, ALU.add), rd=[bkaf], wr=[bc0f])
        rkb, brkb = T("rkb", [128, 8]); c.dma("sp", rkb[:], rkb_d, wr=[brkb])
        lng, blng = T("lng", [128, 512]); c.dma("sp", lng[:], lng_d.partition_broadcast(128), wr=[blng])
        lnb, blnb = T("lnb", [128, 512]); c.dma("sp", lnb[:], lnb_d.partition_broadcast(128), wr=[blnb])
        muv, bmuv = T("muv", [128, 512]); c.dma("sp", muv[:], mu_v.partition_broadcast(128), wr=[bmuv])
        MU2, bMU2 = T("MU2", [128, 512]); c.dma("sp", MU2[:], mu2_d, wr=[bMU2])
        SL2, bSL2 = T("SL2", [128, 256]); c.dma("sp", SL2[:], sl2_d, wr=[bSL2])
        bones, bbones = T("bones", [128, 128], BF16); c.dma("pool", bones[:], bones_d, wr=[bbones])
        S32, bS32 = T("S32", [128, 4, 64]); c.op("dve", lambda e: e.memset(S32[:], 0.0), wr=[bS32])
        Sb, bSb = T("Sb", [128, 4, 64], BF16); c.op("dve", lambda e: e.memset(Sb[:], 0.0), wr=[bSb])
        tha = [T("tha%d" % i, [65, 128], BF16) for i in range(2)]
        xaa = [T("xaa%d" % i, [65, 128], BF16) for i in range(2)]
        for i in range(2):
            c.op("dve", lambda e, i=i: e.memset(tha[i][0][:], 1.0), wr=[tha[i][1]])
            c.op("dve", lambda e, i=i: e.memset(xaa[i][0][:], 1.0), wr=[xaa[i][1]])
        rFs = [T("rF%d" % i, [128, 4, 128]) for i in range(2)]
        kFs = [T("kF%d" % i, [128, 4, 128]) for i in range(2)]
        xws = [T("xw%d" % i, [64, 128]) for i in range(2)]
        xas = [T("xa%d" % i, [64, 128]) for i in range(2)]
        xg0s = [T("xg0%d" % i, [128, 128]) for i in range(2)]
        xg1s = [T("xg1%d" % i, [32, 128]) for i in range(2)]
        vTs = [T("vT%d" % i, [128, 512]) for i in range(2)]
        vPs = [T("vP%d" % i, [128, 512]) for i in range(2)]
        for i in range(2):
            c.op("dve", lambda e, i=i: e.memset(vPs[i][0][:], 0.0), wr=[vPs[i][1]])
        v32, bv32 = T("v32", [128, 512]); vb, bvb = T("vb", [128, 512], BF16)
        sg0, bsg0 = T("sg0", [128, 128], BF16); sg1, bsg1 = T("sg1", [32, 128], BF16)
        tg, btg = T("tg", [128, 512]); sgt, bsgt = T("sgt", [128, 128])
        logw, blogw = T("logw", [128, 4, 128]); lgi, blgi = T("lgi", [128, 4, 128]); lge, blge = T("lge", [128, 4, 128])
        alr, balr = T("alr", [128, 4, 128]); gT, bgT = T("gT", [128, 512])
        kkr, bkkr = T("kkr", [128, 4, 128]); sqb, bsqb = T("sqb", [128, 512], BF16); rn, brn = T("rn", [128, 512])
        kkn, bkkn = T("kkn", [128, 4, 128]); fF, bfF = T("fF", [128, 4, 128]); kM, bkM = T("kM", [128, 4, 128])
        gin, bgin = T("gin", [128, 4, 128]); ginv, bginv = T("ginv", [128, 4, 128]); gex, bgex = T("gex", [128, 4, 128])
        AR, bAR = T("ARZ", [128, 8, 2, 128], BF16); bF, bbF = T("bF", [128, 4, 128])
        Bt, bBt = T("BtZ", [128, 8, 128], BF16); Kt, bKt = T("KtZ", [128, 8, 128], BF16)
        for (t_, b_) in ((AR, bAR), (Bt, bBt), (Kt, bKt)):
            c.op("pool", lambda e, t_=t_: e.memset(t_[:], 0.0), wr=[b_])
        Dd, bDd = T("Dd", [128, 4, 128]); Bh, bBh = T("Bh", [128, 4, 128], BF16); Kh, bKh = T("Kh", [128, 4, 128], BF16)
        BKhT, bBKhT = T("BKhT", [128, 1024], BF16)
        rk, brk = T("rk", [128, 4, 128]); coef, bcoef = T("coef", [128, 8])
        MAB, bMAB = T("MAB", [128, 8, 2, 128], BF16); MAK, bMAK = T("MAK", [128, 8, 2, 128], BF16); MABT, bMABT = T("MABT", [128, 8, 128], BF16)
        Pk = [T("Pk%d" % i, [128, 8, 128], BF16) for i in range(2)]
        PTk = [T("PTk%d" % i, [128, 8, 128], BF16) for i in range(2)]
        ACk = [T("ACk%d" % i, [128, 8, 128], BF16) for i in range(2)]
        XT, bXT = T("XT", [128, 512], BF16); UT, bUT = T("UT", [128, 512], BF16)
        tmpS, btmpS = T("tmpS", [128, 4, 64])
        s1, bs1 = T("s1", [128, 8]); s2, bs2 = T("s2", [128, 8]); mean, bmean = T("mean", [128, 8]); var, bvar = T("var", [128, 8])
        sqt, bsqt = T("sqt", [128, 512]); yn, byn = T("yn", [128, 512]); bon, bbon = T("bon", [128, 512])
        yab, byab = T("yab", [128, 512], BF16); yaT, byaT = T("yaT", [128, 4, 128], BF16)

        zFr = zF[0:512, :].rearrange("(c p) t -> p c t", p=128)
        zFk = zF[512:1024, :].rearrange("(c p) t -> p c t", p=128)
        yaFv = yaF.rearrange("(c p) t -> p c t", p=128)

        def loads(ch):
            i = ch % 2; t0 = ch * 128
            c.dma("sp", rFs[i][0][:], zFr[:, :, t0:t0 + 128], wr=[rFs[i][1]])
            c.dma("sp", kFs[i][0][:], zFk[:, :, t0:t0 + 128], wr=[kFs[i][1]])
            c.dma("sp", xws[i][0][:], zF[1024:1088, t0:t0 + 128], wr=[xws[i][1]])
            c.dma("sp", xas[i][0][:], zF[1088:1152, t0:t0 + 128], wr=[xas[i][1]])
            c.dma("sp", xg0s[i][0][:], zF[1152:1280, t0:t0 + 128], wr=[xg0s[i][1]])
            c.dma("sp", xg1s[i][0][:], zF[1280:1312, t0:t0 + 128], wr=[xg1s[i][1]])
            c.dma("sp", vTs[i][0][:], zT[t0:t0 + 128, 0:512], wr=[vTs[i][1]])
            if ch == 0:
                c.dma("sp", vPs[i][0][1:128, :], zT[0:127, 0:512], wr=[vPs[i][1]])
            else:
                c.dma("sp", vPs[i][0][:], zT[t0 - 1:t0 + 127, 0:512], wr=[vPs[i][1]])

        loads(0)
        NCH = int(os.environ.get('NCH', str(NT)))
        STG = int(os.environ.get('STG', '99'))
        def chunk(ch):
            if ch + 1 < NCH:
                loads(ch + 1)
            i = ch % 2; t0 = ch * 128
            rF, brF = rFs[i]; kF, bkF = kFs[i]; xw, bxw = xws[i]; xa, bxa = xas[i]
            xg0, bxg0 = xg0s[i]; xg1, bxg1 = xg1s[i]; vT, bvT = vTs[i]; vP, bvP = vPs[i]
            th, bth = tha[i]; xab, bxab = xaa[i]
            c.op("pool", lambda e: e.tensor_tensor(bon[:], vP[:], vT[:], ALU.subtract), rd=[bvP, bvT], wr=[bbon])
            c.op("pool", lambda e: e.tensor_tensor(bon[:], bon[:], muv[:], ALU.mult), rd=[bbon, bmuv], wr=[bbon])
            c.op("pool", lambda e: e.tensor_tensor(v32[:], bon[:], vT[:], ALU.add), rd=[bbon, bvT], wr=[bv32])
            c.op("pool", lambda e: e.tensor_copy(vb[:], v32[:]), rd=[bv32], wr=[bvb])
            if STG <= 1:
                return
            c.op("act", lambda e: e.activation(th[0:64, :], xw[:], AF.Tanh), rd=[bxw], wr=[bth])
            c.op("dve", lambda e: e.tensor_copy(xab[0:64, :], xa[:]), rd=[bxa], wr=[bxab])
            c.op("act", lambda e: e.activation(sgt[:], xg0[:], AF.Tanh, scale=0.5), rd=[bxg0], wr=[bsgt])
            c.op("dve", lambda e: e.tensor_scalar(sg0[:], sgt[:], 0.5, 0.5, ALU.mult, ALU.add), rd=[bsgt], wr=[bsg0])
            c.op("act", lambda e: e.activation(sgt[0:32, :], xg1[:], AF.Tanh, scale=0.5), rd=[bxg1], wr=[bsgt])
            c.op("dve", lambda e: e.tensor_scalar(sg1[:], sgt[0:32, :], 0.5, 0.5, ALU.mult, ALU.add), rd=[bsgt], wr=[bsg1])
            if STG <= 2:
                return
            p, bp = ps()
            c.group("pe", [lambda e, q=q, p=p: e.matmul(p[:, q * 128:(q + 1) * 128], wdec[:, q * 128:(q + 1) * 128], th[:], start=True, stop=True) for q in range(4)],
                    rd=[bwdec, bth], wr=[bp])
            c.op("act", lambda e, p=p: e.activation(tg[:], p[:], AF.Tanh, scale=0.5), rd=[bp], wr=[btg])
            c.op("dve", lambda e: e.tensor_scalar(logw[:].rearrange("p a b -> p (a b)"), tg[:], -0.5 * math.exp(-0.5), -0.5 * math.exp(-0.5), ALU.mult, ALU.add),
                 rd=[btg], wr=[blogw])
            for q in range(4):
                c.op("dve", lambda e, q=q: e.tensor_tensor_scan(lgi[:, q, :], logw[:, q, :], logw[:, q, :], 0.0, ALU.add, ALU.bypass), rd=[blogw], wr=[blgi])
            c.op("pool", lambda e: e.tensor_tensor(lge[:], lgi[:], logw[:], ALU.subtract), rd=[blgi, blogw], wr=[blge])
            if STG <= 3:
                return
            p, bp = ps()
            c.group("pe", [lambda e, q=q, p=p: e.matmul(p[:, q * 128:(q + 1) * 128], waaa[:, q * 128:(q + 1) * 128], xab[:], start=True, stop=True) for q in range(4)],
                    rd=[bwaaa, bxab], wr=[bp])
            c.op("act", lambda e, p=p: e.activation(tg[:], p[:], AF.Tanh, scale=0.5), rd=[bp], wr=[btg])
            c.op("dve", lambda e: e.tensor_scalar(alr[:].rearrange("p a b -> p (a b)"), tg[:], 0.5, 0.5, ALU.mult, ALU.add), rd=[btg], wr=[balr])
            p, bp = ps()
            c.group("pe", [lambda e, p=p: e.matmul(p[:], sg0[:], wgate[:, 0, :], start=True, stop=False),
                           lambda e, p=p: e.matmul(p[:], sg1[:], wgate[0:32, 1, :], start=False, stop=True)], rd=[bsg0, bsg1, bwgate], wr=[bp])
            c.op("act", lambda e, p=p: e.copy(gT[:], p[:]), rd=[bp], wr=[bgT])
            if STG <= 4:
                return
            for q in range(4):
                c.op("dve", lambda e, q=q: e.tensor_scalar(kkr[:, q, :], kF[:, q, :], kkf[:, q:q + 1], None, ALU.mult), rd=[bkF, bkkf], wr=[bkkr])
            c.op("pool", lambda e: e.tensor_tensor(sqb[:], kkr[:].rearrange("p a b -> p (a b)"), kkr[:].rearrange("p a b -> p (a b)"), ALU.mult), rd=[bkkr], wr=[bsqb])
            p, bp = ps()
            c.op("pe", lambda e, p=p: e.matmul(p[:], bones[:], sqb[:], start=True, stop=True), rd=[bbones, bsqb], wr=[bp])
            c.op("act", lambda e, p=p: e.activation(rn[:], p[:], AF.Ln), rd=[bp], wr=[brn])
            c.op("act", lambda e: e.activation(rn[:], rn[:], AF.Exp, scale=-0.5), rd=[brn], wr=[brn])
            c.op("dve", lambda e: e.tensor_tensor(kkn[:].rearrange("p a b -> p (a b)"), kkr[:].rearrange("p a b -> p (a b)"), rn[:], ALU.mult), rd=[bkkr, brn], wr=[bkkn])
            if STG <= 5:
                return
            for q in range(4):
                c.op("dve", lambda e, q=q: e.tensor_scalar(fF[:, q, :], alr[:, q, :], kaf[:, q:q + 1], c0f[:, q:q + 1], ALU.mult, ALU.add), rd=[balr, bkaf, bc0f], wr=[bfF])
            c.op("pool", lambda e: e.tensor_tensor(kM[:], kF[:], fF[:], ALU.mult), rd=[bkF, bfF], wr=[bkM])
            c.op("act", lambda e: e.activation(gin[:], lgi[:], AF.Exp), rd=[blgi], wr=[bgin])
            c.op("act", lambda e: e.activation(ginv[:], lgi[:], AF.Exp, scale=-1.0), rd=[blgi], wr=[bginv])
            c.op("act", lambda e: e.activation(gex[:], lge[:], AF.Exp), rd=[blge], wr=[bgex])
            for q in range(4):
                c.op("act", lambda e, q=q: e.activation(Dd[:, q, :], lgi[:, q, :], AF.Exp, bias=lgi[:, q, 127:128], scale=-1.0), rd=[blgi], wr=[bDd])
            c.op("pool", lambda e: e.tensor_tensor(bF[:], kkn[:], alr[:], ALU.mult), rd=[bkkn, balr], wr=[bbF])
            for hh in range(2):
                r0, r1 = hh * 64, (hh + 1) * 64
                ARv = AR[r0:r1, :, :, :].rearrange("p (q two) a t -> p q two a t", two=2)[:, :, hh, :, :]
                Btv = Bt[r0:r1, :, :].rearrange("p (q two) t -> p q two t", two=2)[:, :, hh, :]
                Ktv = Kt[r0:r1, :, :].rearrange("p (q two) t -> p q two t", two=2)[:, :, hh, :]
                c.op("dve", lambda e: e.tensor_tensor(ARv[:, :, 1, :], rF[r0:r1, :, :], gin[r0:r1, :, :], ALU.mult), rd=[brF, bgin], wr=[bAR])
                c.op("dve", lambda e: e.scalar_tensor_tensor(ARv[:, :, 0, :], kkn[r0:r1, :, :], -1.0, gex[r0:r1, :, :], ALU.mult, ALU.mult), rd=[bkkn, bgex], wr=[bAR])
                c.op("dve", lambda e: e.tensor_tensor(Btv, bF[r0:r1, :, :], ginv[r0:r1, :, :], ALU.mult), rd=[bbF, bginv], wr=[bBt])
                c.op("pool", lambda e: e.tensor_tensor(Ktv, kM[r0:r1, :, :], ginv[r0:r1, :, :], ALU.mult), rd=[bkM, bginv], wr=[bKt])
            c.op("pool", lambda e: e.tensor_tensor(Bh[:], bF[:], Dd[:], ALU.mult), rd=[bbF, bDd], wr=[bBh])
            c.op("pool", lambda e: e.tensor_tensor(Kh[:], kM[:], Dd[:], ALU.mult), rd=[bkM, bDd], wr=[bKh])
            if STG <= 6:
                return
            p, bp = ps(); pb = p[:].bitcast(BF16)
            c.group("pe", [lambda e, q=q: e.transpose(pb[:, q * 128:(q + 1) * 128], Bh[:, q, :], idb[:]) for q in range(4)] +
                          [lambda e, q=q: e.transpose(pb[:, 512 + q * 128:512 + (q + 1) * 128], Kh[:, q, :], idb[:]) for q in range(4)],
                    rd=[bBh, bKh, bidb], wr=[bp])
            c.op("act", lambda e: e.copy(BKhT[:], pb), rd=[bp], wr=[bBKhT])
            if STG <= 7:
                return
            c.op("pool", lambda e: e.tensor_tensor(rk[:], rF[:], kM[:], ALU.mult), rd=[brF, bkM], wr=[brk])
            p, bp = ps()
            c.group("pe", [lambda e, q=q, p=p: e.matmul(p[:, 2 * q:2 * q + 2], rk[:, q, :], rkb[:, 2 * q:2 * q + 2], start=True, stop=True) for q in range(4)],
                    rd=[brk, brkb], wr=[bp])
            c.op("dve", lambda e, p=p: e.tensor_copy(coef[:], p[:, 0:8]), rd=[bp], wr=[bcoef])
            if STG <= 8:
                return
            for q in range(4):
                pA, bpA = ps(); pB, bpB = ps(); pC, bpC = ps()
                fa, fb, fc = [], [], []
                for hh in range(2):
                    h = 2 * q + hh
                    arr = AR[:, h, :, :].rearrange("p a b -> p (a b)")
                    fa.append(lambda e, hh=hh, h=h, arr=arr: e.matmul(pA[:, hh * 256:(hh + 1) * 256], Bt[:, h, :], arr, start=True, stop=True))
                    fb.append(lambda e, hh=hh, h=h, arr=arr: e.matmul(pB[:, hh * 256:(hh + 1) * 256], Kt[:, h, :], arr, start=True, stop=True))
                    fc.append(lambda e, hh=hh, h=h: e.matmul(pC[:, hh * 128:(hh + 1) * 128], AR[:, h, 0, :], Bt[:, h, :], start=True, stop=True))
                c.group("pe", fa, rd=[bBt, bAR], wr=[bpA])
                c.group("pe", fb, rd=[bKt, bAR], wr=[bpB])
                c.group("pe", fc, rd=[bBt, bAR], wr=[bpC])
                c.op("dve", lambda e: e.tensor_tensor(MAB[:, 2 * q:2 * q + 2, :, :].rearrange("p a b c -> p (a b c)"), pA[:], MU2[:], ALU.mult), rd=[bpA, bMU2], wr=[bMAB])
                c.op("dve", lambda e: e.tensor_tensor(MAK[:, 2 * q:2 * q + 2, :, :].rearrange("p a b c -> p (a b c)"), pB[:], MU2[:], ALU.mult), rd=[bpB, bMU2], wr=[bMAK])
                c.op("dve", lambda e: e.tensor_tensor(MABT[:, 2 * q:2 * q + 2, :].rearrange("p a b -> p (a b)"), pC[:, 0:256], SL2[:], ALU.mult), rd=[bpC, bSL2], wr=[bMABT])
            if STG <= 9:
                return
            AC0, bAC0 = ACk[0]
            c.op("pool", lambda e: e.tensor_tensor(AC0[:], MAB[:, :, 0, :], bcast(idb[:], [128, 8, 128], 1), ALU.add), rd=[bMAB, bidb], wr=[bAC0])
            Pp = lambda h: MAB[:, h, 0, :]
            PTp = lambda h: MABT[:, h, :]
            bPp, bPTp = bMAB, bMABT
            ACp, bACp = AC0, bAC0
            for lv in range(1, 7):
                Pn, bPn = Pk[lv % 2]; PTn, bPTn = PTk[lv % 2]; ACn, bACn = ACk[lv % 2]
                for grp in range(2):
                    hs = range(4 * grp, 4 * grp + 4)
                    if lv < 6:
                        p, bp = ps()
                        c.group("pe", [lambda e, h=h, p=p, Pp=Pp, PTp=PTp: e.matmul(p[:, (h % 4) * 128:(h % 4 + 1) * 128], PTp(h), Pp(h), start=True, stop=True) for h in hs],
                                rd=[bPp, bPTp], wr=[bp])
                        en = ev()
                        if en == "act":
                            c.op("act", lambda e, p=p: e.copy(Pn[:, 4 * grp:4 * grp + 4, :].rearrange("p a b -> p (a b)"), p[:]), rd=[bp], wr=[bPn])
                        else:
                            c.op("dve", lambda e, p=p: e.tensor_copy(Pn[:, 4 * grp:4 * grp + 4, :].rearrange("p a b -> p (a b)"), p[:]), rd=[bp], wr=[bPn])
                    p, bp = ps()
                    c.group("pe", [lambda e, h=h, p=p, Pp=Pp, PTp=PTp: e.matmul(p[:, (h % 4) * 128:(h % 4 + 1) * 128], Pp(h), PTp(h), start=True, stop=True) for h in hs],
                            rd=[bPp, bPTp], wr=[bp])
                    en = ev()
                    if en == "act":
                        c.op("act", lambda e, p=p: e.copy(PTn[:, 4 * grp:4 * grp + 4, :].rearrange("p a b -> p (a b)"), p[:]), rd=[bp], wr=[bPTn])
                    else:
                        c.op("dve", lambda e, p=p: e.tensor_copy(PTn[:, 4 * grp:4 * grp + 4, :].rearrange("p a b -> p (a b)"), p[:]), rd=[bp], wr=[bPTn])
                    p, bp = ps()
                    fs = []
                    for h in hs:
                        fs.append(lambda e, h=h, p=p, ACp=ACp: e.matmul(p[:, (h % 4) * 128:(h % 4 + 1) * 128], idb[:], ACp[:, h, :], start=True, stop=False))
                        fs.append(lambda e, h=h, p=p, ACp=ACp: e.matmul(p[:, (h % 4) * 128:(h % 4 + 1) * 128], PTn[:, h, :], ACp[:, h, :], start=False, stop=True))
                    c.group("pe", fs, rd=[bidb, bACp, bPTn], wr=[bp])
                    en = ev()
                    if en == "act":
                        c.op("act", lambda e, p=p: e.copy(ACn[:, 4 * grp:4 * grp + 4, :].rearrange("p a b -> p (a b)"), p[:]), rd=[bp], wr=[bACn])
                    else:
                        c.op("dve", lambda e, p=p: e.tensor_copy(ACn[:, 4 * grp:4 * grp + 4, :].rearrange("p a b -> p (a b)"), p[:]), rd=[bp], wr=[bACn])
                Pp = (lambda Pn: (lambda h: Pn[:, h, :]))(Pn)
                PTp = (lambda PTn: (lambda h: PTn[:, h, :]))(PTn)
                bPp, bPTp = bPn, bPTn
                ACp, bACp = ACn, bACn
            Minv, bMinv = ACp, bACp
            if STG <= 10:
                return
            pX, bpX = ps()
            fs = []
            for h in range(8):
                q, r0 = h // 2, (h % 2) * 64
                fs.append(lambda e, h=h: e.matmul(pX[:, h * 64:(h + 1) * 64], MAK[:, h, 0, :], vb[:, h * 64:(h + 1) * 64], start=True, stop=False))
                fs.append(lambda e, h=h, q=q: e.matmul(pX[:, h * 64:(h + 1) * 64], AR[:, h, 0, :], Sb[:, q, :], start=False, stop=True))
            c.group("pe", fs, rd=[bMAK, bvb, bAR, bSb], wr=[bpX])
            c.op("act", lambda e: e.copy(XT[:], pX[:]), rd=[bpX], wr=[bXT])
            pU, bpU = ps()
            c.group("pe", [lambda e, h=h: e.matmul(pU[:, h * 64:(h + 1) * 64], Minv[:, h, :], XT[:, h * 64:(h + 1) * 64], start=True, stop=True) for h in range(8)],
                    rd=[bMinv, bXT], wr=[bpU])
            c.op("dve", lambda e: e.tensor_copy(UT[:], pU[:]), rd=[bpU], wr=[bUT])
            pY, bpY = ps()
            fs = []
            for h in range(8):
                q, r0 = h // 2, (h % 2) * 64
                fs.append(lambda e, h=h: e.matmul(pY[:, h * 64:(h + 1) * 64], MAK[:, h, 1, :], vb[:, h * 64:(h + 1) * 64], start=True, stop=False))
                fs.append(lambda e, h=h: e.matmul(pY[:, h * 64:(h + 1) * 64], MAB[:, h, 1, :], UT[:, h * 64:(h + 1) * 64], start=False, stop=False))
                fs.append(lambda e, h=h, q=q: e.matmul(pY[:, h * 64:(h + 1) * 64], AR[:, h, 1, :], Sb[:, q, :], start=False, stop=True))
            c.group("pe", fs, rd=[bMAK, bMAB, bvb, bUT, bAR, bSb], wr=[bpY])
            pS, bpS = ps()
            fs = []
            for q in range(4):
                fs.append(lambda e, q=q: e.matmul(pS[:, q * 128:(q + 1) * 128], BKhT[:, q * 128:(q + 1) * 128], UT[:, q * 128:(q + 1) * 128], start=True, stop=False))
                fs.append(lambda e, q=q: e.matmul(pS[:, q * 128:(q + 1) * 128], BKhT[:, 512 + q * 128:512 + (q + 1) * 128], vb[:, q * 128:(q + 1) * 128], start=False, stop=True))
            c.group("pe", fs, rd=[bBKhT, bUT, bvb], wr=[bpS])
            pSv = pS[:].rearrange("p (q c) -> p q c", q=4)
            c.op("dve", lambda e: e.tensor_tensor(tmpS[:], S32[:], gin[:, :, 127:128].to_broadcast([128, 4, 64]), ALU.mult), rd=[bS32, bgin], wr=[btmpS])
            c.op("dve", lambda e: e.tensor_tensor(S32[0:64, :, :], tmpS[0:64, :, :], pSv[0:64, :, 0:64], ALU.add), rd=[btmpS, bpS], wr=[bS32])
            c.op("dve", lambda e: e.tensor_tensor(S32[64:128, :, :], tmpS[64:128, :, :], pSv[64:128, :, 64:128], ALU.add), rd=[btmpS, bpS], wr=[bS32])
            c.op("dve", lambda e: e.tensor_copy(Sb[:], S32[:]), rd=[bS32], wr=[bSb])
            if STG <= 11:
                return
            pYv = pY[:].rearrange("p (h d) -> p h d", h=8)
            c.op("dve", lambda e: e.tensor_reduce(s1[:], pYv, AX.X, ALU.add), rd=[bpY], wr=[bs1])
            c.op("act", lambda e: e.activation(sqt[:], pY[:], AF.Square), rd=[bpY], wr=[bsqt])
            c.op("dve", lambda e: e.tensor_reduce(s2[:], sqt[:].rearrange("p (h d) -> p h d", h=8), AX.X, ALU.add), rd=[bsqt], wr=[bs2])
            c.op("dve", lambda e: e.tensor_scalar(mean[:], s1[:], 1.0 / 64, None, ALU.mult), rd=[bs1], wr=[bmean])
            c.op("dve", lambda e: e.tensor_tensor(var[:], mean[:], mean[:], ALU.mult), rd=[bmean], wr=[bvar])
            c.op("dve", lambda e: e.scalar_tensor_tensor(var[:], s2[:], 1.0 / 64, var[:], ALU.mult, ALU.subtract), rd=[bs2, bvar], wr=[bvar])
            c.op("dve", lambda e: e.tensor_scalar(var[:], var[:], 64e-5, None, ALU.add), rd=[bvar], wr=[bvar])
            c.op("act", lambda e: e.activation(var[:], var[:], AF.Ln), rd=[bvar], wr=[bvar])
            c.op("act", lambda e: e.activation(var[:], var[:], AF.Exp, scale=-0.5), rd=[bvar], wr=[bvar])
            ynv = yn[:].rearrange("p (h d) -> p h d", h=8)
            c.op("dve", lambda e: e.tensor_tensor(ynv, pYv, bcast(mean[:], [128, 8, 64], 2), ALU.subtract), rd=[bpY, bmean], wr=[byn])
            c.op("pool", lambda e: e.tensor_tensor(ynv, ynv, bcast(var[:], [128, 8, 64], 2), ALU.mult), rd=[byn, bvar], wr=[byn])
            c.op("pool", lambda e: e.tensor_tensor(yn[:], yn[:], lng[:], ALU.mult), rd=[byn, blng], wr=[byn])
            c.op("dve", lambda e: e.tensor_tensor(yn[:], yn[:], lnb[:], ALU.add), rd=[byn, blnb], wr=[byn])
            c.op("pool", lambda e: e.tensor_tensor(bon[:].rearrange("p (h d) -> p h d", h=8), v32[:].rearrange("p (h d) -> p h d", h=8), bcast(coef[:], [128, 8, 64], 2), ALU.mult),
                 rd=[bv32, bcoef], wr=[bbon])
            c.op("dve", lambda e: e.tensor_tensor(yn[:], yn[:], bon[:], ALU.add), rd=[byn, bbon], wr=[byn])
            c.op("dve", lambda e: e.tensor_tensor(yab[:], yn[:], gT[:], ALU.mult), rd=[byn, bgT], wr=[byab])
            p, bp = ps(); pb = p[:].bitcast(BF16)
            c.group("pe", [lambda e, q=q: e.transpose(pb[:, q * 128:(q + 1) * 128], yab[:, q * 128:(q + 1) * 128], idb[:]) for q in range(4)], rd=[byab, bidb], wr=[bp])
            c.op("act", lambda e: e.copy(yaT[:].rearrange("p a b -> p (a b)"), pb[:, 0:512]), rd=[bp], wr=[byaT])
            c.dma("sp", yaFv[:, :, t0:t0 + 128], yaT[:], rd=[byaT])
        for ch in range(NCH):
            chunk(ch)
        c.barrier()

    if upto <= 2:
        print("ninst", c.ninst, "nwait", c.nwait); return nc
    with ExitStack() as es:
        def T(name, shape, dt=F32):
            return es.enter_context(nc.sbuf_tensor("s_" + name, list(shape), dt)), Buf(name)
        pst["n"] = 2; pst["i"] = 0; pst["off"] = 4
        pO = [PSB[6], PSB[7]]
        qst = {"i": 0}
        def psq():
            qst["i"] += 1
            return PSB[qst["i"] % 4]
        idb, bidb = T("idb3", [128, 128], BF16); c.dma("pool", idb[:], ident_d, wr=[bidb])
        CB, bCB = T("CB", [128, 2, 256], BF16); c.dma("pool", CB[:].rearrange("p a b -> p (a b)"), cbias2_d, wr=[bCB])
        s65, bs65 = T("s65", [65, 64], BF16); c.dma("pool", s65[:], sel65_d, wr=[bs65])
        pok, bpok = T("pok", [128, 16, 16]); c.dma("sp", pok[:].rearrange("p a b -> p (a b)"), pok_d.partition_broadcast(128), wr=[bpok])
        pbi, bpbi = T("pbi", [128, 16, 16]); c.dma("sp", pbi[:].rearrange("p a b -> p (a b)"), pbias_d.partition_broadcast(128), wr=[bpbi])
        invf, binvf = T("invf", [128, 8]); c.dma("sp", invf[:], invf_d.partition_broadcast(128), wr=[binvf])
        qkg, bqkg = T("qkg", [128, 1024]); c.dma("sp", qkg[:], qkg_d.partition_broadcast(128), wr=[bqkg])
        posi, bposi = T("posi", [128, NT], I32); c.dma("sp", posi[:], pos, wr=[bposi])
        posf, bposf = T("posf", [128, NT])
        c.op("dve", lambda e: e.tensor_copy(posf[:], posi[:]), rd=[bposi], wr=[bposf])
        yy, byy = T("yy", [128, NT, 8]); yi, byi = T("yi", [128, NT, 8], I32); yf, byf = T("yf", [128, NT, 8])
        sinT, bsinT = T("sinT", [128, NT, 8]); cosT, bcosT = T("cosT", [128, NT, 8])
        c.op("dve", lambda e: e.tensor_copy(yy[:], bcast(invf[:], [128, NT, 8], 1)), rd=[binvf], wr=[byy])
        c.op("dve", lambda e: e.tensor_tensor(yy[:], yy[:], bcast(posf[:], [128, NT, 8], 2), ALU.mult), rd=[byy, bposf], wr=[byy])
        for (dst, bdst, off) in ((sinT, bsinT, 0.0), (cosT, bcosT, 0.25)):
            if off != 0.0:
                c.op("dve", lambda e: e.tensor_scalar(yy[:], yy[:], off, None, ALU.add), rd=[byy], wr=[byy])
            c.op("dve", lambda e: e.tensor_copy(yi[:], yy[:]), rd=[byy], wr=[byi])
            c.op("dve", lambda e: e.tensor_copy(yf[:], yi[:]), rd=[byi], wr=[byf])
            c.op("dve", lambda e: e.tensor_tensor(yf[:], yy[:], yf[:], ALU.subtract), rd=[byy, byf], wr=[byf])
            c.op("act", lambda e, dst=dst: e.activation(dst[:], yf[:], AF.Sin, scale=2.0 * math.pi), rd=[byf], wr=[bdst])
        KT = es.enter_context(nc.sbuf_tensor("s_KT", [80, 8, S], BF16)); bKT = [Buf("KT%d" % g) for g in range(8)]
        VA = es.enter_context(nc.sbuf_tensor("s_VA", [128, NT, 8, 65], BF16)); bVA = [Buf("VA%d" % g) for g in range(8)]
        c.op("pool", lambda e: e.memset(VA[:], 1.0), wr=bVA)
        kmT, bkmT = T("kmT", [64, 8, 16], BF16); c.op("dve", lambda e: e.memset(kmT[:], 0.0), wr=[bkmT])
        km32, bkm32 = T("km32", [64, 8])
        qkvs = [T("qkv%d" % i, [128, 1536]) for i in range(2)]
        sq3, bsq3 = T("sq3", [128, 1024]); ss3, bss3 = T("ss3", [128, 16]); qn, bqn = T("qn", [128, 16, 64])
        rt, brt = T("rt", [128, 4, 16, 8])
        QA, bQA = T("QA", [128, 8, 80], BF16); KA, bKA = T("KA", [128, 8, 80], BF16)
        QT, bQT = T("QT", [64, 8, 128], BF16)
        QTAs = [T("QTA%d" % i, [80, 8, 512], BF16) for i in range(2)]
        gm, bgm = T("gm", [128, 8, 16]); mx, bmx = T("mx", [128, 8, 8]); sel, bsel = T("sel", [128, 8, 16])
        PTs = [T("PT%d" % i, [128, 512], BF16) for i in range(4)]
        OT, bOT = T("OT", [65, 512]); rr, brr = T("rr", [65, 512], BF16); r32, br32 = T("r32", [65, 512])
        c.op("dve", lambda e: e.memset(rr[:], 0.0), wr=[brr])
        yTs = [T("yT%d" % i, [64, 512], BF16) for i in range(2)]
        cnt3 = {"pt": 0, "ld": 0}

        def ld3(tt):
            c.dma("sp", qkvs[tt % 2][0][:], zT[tt * 128:(tt + 1) * 128, 512:2048], wr=[qkvs[tt % 2][1]])

        def pre3(tt):
            if tt + 1 < NT:
                ld3(tt + 1)
            qkv, bqkv = qkvs[tt % 2]
            t0 = tt * 128; qb = tt // 2; g = tt // 4; ti = tt % 4
            QTA, bQTA = QTAs[g % 2]
            c.op("act", lambda e: e.activation(sq3[:], qkv[:, 0:1024], AF.Square), rd=[bqkv], wr=[bsq3])
            c.op("dve", lambda e: e.tensor_reduce(ss3[:], sq3[:].rearrange("p (h d) -> p h d", h=16), AX.X, ALU.add), rd=[bsq3], wr=[bss3])
            c.op("dve", lambda e: e.tensor_scalar(ss3[:], ss3[:], 1.0 / 64, 1e-6, ALU.mult, ALU.add), rd=[bss3], wr=[bss3])
            c.op("act", lambda e: e.activation(ss3[:], ss3[:], AF.Ln), rd=[bss3], wr=[bss3])
            c.op("act", lambda e: e.activation(ss3[:], ss3[:], AF.Exp, scale=-0.5), rd=[bss3], wr=[bss3])
            c.op("dve", lambda e: e.tensor_tensor(qn[:], qkv[:, 0:1024].rearrange("p (h d) -> p h d", h=16), bcast(ss3[:], [128, 16, 64], 2), ALU.mult), rd=[bqkv, bss3], wr=[bqn])
            c.op("pool", lambda e: e.tensor_tensor(qn[:].rearrange("p h d -> p (h d)"), qn[:].rearrange("p h d -> p (h d)"), qkg[:], ALU.mult), rd=[bqn, bqkg], wr=[bqn])
            cs = cosT[:, tt:tt + 1, :].to_broadcast([128, 16, 8]); sn = sinT[:, tt:tt + 1, :].to_broadcast([128, 16, 8])
            c.op("dve", lambda e: e.tensor_tensor(rt[:, 0, :, :], qn[:, :, 0:8], cs, ALU.mult), rd=[bqn, bcosT], wr=[brt])
            c.op("dve", lambda e: e.tensor_tensor(rt[:, 1, :, :], qn[:, :, 8:16], sn, ALU.mult), rd=[bqn, bsinT], wr=[brt])
            c.op("pool", lambda e: e.tensor_tensor(rt[:, 2, :, :], qn[:, :, 8:16], cs, ALU.mult), rd=[bqn, bcosT], wr=[brt])
            c.op("pool", lambda e: e.tensor_tensor(rt[:, 3, :, :], qn[:, :, 0:8], sn, ALU.mult), rd=[bqn, bsinT], wr=[brt])
            c.op("dve", lambda e: e.tensor_tensor(qn[:, :, 0:8], rt[:, 0, :, :], rt[:, 1, :, :], ALU.subtract), rd=[brt], wr=[bqn])
            c.op("dve", lambda e: e.tensor_tensor(qn[:, :, 8:16], rt[:, 2, :, :], rt[:, 3, :, :], ALU.add), rd=[brt], wr=[bqn])
            c.op("dve", lambda e: e.tensor_copy(QA[:, :, 0:64], qn[:, 0:8, :]), rd=[bqn], wr=[bQA])
            c.op("pool", lambda e: e.tensor_copy(KA[:, :, 0:64], qn[:, 8:16, :]), rd=[bqn], wr=[bKA])
            c.op("pool", lambda e: e.memset(KA[:, :, 64:80], 0.0), wr=[bKA])
            c.op("pool", lambda e: e.memset(KA[:, :, 64 + qb:65 + qb], 1.0), wr=[bKA])
            p, bp = ps(); pb = p[:].bitcast(BF16)
            c.group("pe", [lambda e, h=h: e.transpose(pb[0:80, h * 128:(h + 1) * 128], KA[:, h, :], idb[:]) for h in range(8)], rd=[bKA, bidb], wr=[bp])
            c.op("dve", lambda e: e.tensor_copy(KT[:, :, t0:t0 + 128], pb[0:80, :].rearrange("p (h t) -> p h t", h=8)), rd=[bp], wr=[bKT[g]])
            c.op("pool", lambda e: e.tensor_copy(VA[:, tt, :, 0:64], qkv[:, 1024:1536].rearrange("p (h d) -> p h d", h=8)), rd=[bqkv], wr=[bVA[g]])
            if qb > 0:
                p, bp = ps(); pb = p[:].bitcast(BF16)
                c.group("pe", [lambda e, h=h: e.transpose(pb[0:64, h * 128:(h + 1) * 128], QA[:, h, 0:64], idb[:]) for h in range(8)], rd=[bQA, bidb], wr=[bp])
                c.op("dve", lambda e: e.tensor_copy(QT[:], pb[0:64, :].rearrange("p (h t) -> p h t", h=8)), rd=[bp], wr=[bQT])
                p, bp = ps()
                c.group("pe", [lambda e, h=h, p=p: e.matmul(p[:, h * 16:(h + 1) * 16], QT[:, h, :], kmT[:, h, :], start=True, stop=True) for h in range(8)],
                        rd=[bQT, bkmT], wr=[bp])
                c.op("dve", lambda e, p=p: e.tensor_tensor(gm[:], p[:, 0:128].rearrange("p (h j) -> p h j", h=8), pbi[:, qb:qb + 1, :].to_broadcast([128, 8, 16]), ALU.add),
                     rd=[bp, bpbi], wr=[bgm])
                for h in range(8):
                    c.op("dve", lambda e, h=h: e.max(mx[:, h, :], gm[:, h, :]), rd=[bgm], wr=[bmx])
                c.op("dve", lambda e: e.tensor_tensor(sel[:], gm[:], mx[:, :, 2:3].to_broadcast([128, 8, 16]), ALU.is_ge), rd=[bgm, bmx], wr=[bsel])
                c.op("dve", lambda e: e.tensor_tensor(sel[:], sel[:], pok[:, qb:qb + 1, :].to_broadcast([128, 8, 16]), ALU.mult), rd=[bsel, bpok], wr=[bsel])
                c.op("dve", lambda e: e.tensor_scalar(QA[:, :, 64:80], sel[:], BIG, -BIG, ALU.mult, ALU.add), rd=[bsel], wr=[bQA])
            else:
                c.op("dve", lambda e: e.memset(QA[:, :, 64:80], -BIG), wr=[bQA])
            p, bp = ps(); pb = p[:].bitcast(BF16)
            c.group("pe", [lambda e, h=h: e.transpose(pb[0:80, h * 128:(h + 1) * 128], QA[:, h, :], idb[:]) for h in range(8)], rd=[bQA, bidb], wr=[bp])
            c.op("dve", lambda e: e.tensor_copy(QTA[:, :, ti * 128:(ti + 1) * 128], pb[0:80, :].rearrange("p (h t) -> p h t", h=8)), rd=[bp], wr=[bQTA])
            if tt % 2 == 1:
                c.op("dve", lambda e: e.tensor_reduce(km32[:], KT[0:64, :, qb * 256:(qb + 1) * 256], AX.X, ALU.add), rd=[bKT[g]], wr=[bkm32])
                c.op("dve", lambda e: e.tensor_scalar(kmT[:, :, qb], km32[:], 1.0 / 256, None, ALU.mult), rd=[bkm32], wr=[bkmT])

        s65f, bs65f = T("s65f", [65, 64]); c.dma("sp", s65f[:], sel65_d, wr=[bs65f])

        def steps3(g, h):
            QTA, bQTA = QTAs[g % 2]
            pOh, bOh = pO[h % 2]
            kbufs = bKT[0:g + 1]; vbufs = bVA[0:g + 1]
            st = []
            nk = 4 * g + 2
            for kt in range(nk):
                d = {}
                def qk(d=d, kt=kt):
                    d["p"], d["bp"] = psq()
                    c.op("pe", lambda e: e.matmul(d["p"][:], KT[:, h, kt * 128:(kt + 1) * 128], QTA[:, h, :], start=True, stop=True), rd=kbufs + [bQTA], wr=[d["bp"]])
                def ex(d=d):
                    d["PT"], d["bPT"] = PTs[cnt3["pt"] % 4]; cnt3["pt"] += 1
                    c.op("act", lambda e: e.activation(d["PT"][:], d["p"][:], AF.Exp, scale=0.125), rd=[d["bp"]], wr=[d["bPT"]])
                def pv(d=d, kt=kt):
                    c.op("pe", lambda e: e.matmul(pOh[0:65, :], VA[:, kt, h, :], d["PT"][:], start=(kt == 0), stop=False), rd=vbufs + [d["bPT"]], wr=[bOh])
                st.append((qk, ex, pv))
            for half in range(2):
                for kti in range(2):
                    kt = 4 * g + 2 * half + kti
                    qs = slice(half * 256, (half + 1) * 256)
                    last = (half == 1 and kti == 1)
                    d = {}
                    def qk(d=d, kt=kt, qs=qs, kti=kti):
                        d["p"], d["bp"] = psq()
                        c.group("pe", [lambda e: e.matmul(d["p"][:, 0:256], KT[0:64, h, kt * 128:(kt + 1) * 128], QTA[0:64, h, qs], start=True, stop=False),
                                       lambda e: e.matmul(d["p"][:, 0:256], idb[:], CB[:, kti, :], start=False, stop=True)], rd=kbufs + [bQTA, bidb, bCB], wr=[d["bp"]])
                    def ex(d=d):
                        d["PT"], d["bPT"] = PTs[cnt3["pt"] % 4]; cnt3["pt"] += 1
                        c.op("act", lambda e: e.activation(d["PT"][:, 0:256], d["p"][:, 0:256], AF.Exp, scale=0.125), rd=[d["bp"]], wr=[d["bPT"]])
                    def pv(d=d, kt=kt, qs=qs, last=last):
                        c.op("pe", lambda e: e.matmul(pOh[0:65, qs], VA[:, kt, h, :], d["PT"][:, 0:256], start=False, stop=last), rd=vbufs + [d["bPT"]], wr=[bOh])
                    st.append((qk, ex, pv))

            def fin():
                c.op("dve", lambda e: e.tensor_copy(OT[:], pOh[0:65, :]), rd=[bOh], wr=[bOT])
                p, bp = ps()
                c.op("pe", lambda e: e.matmul(p[0:64, :], s65f[:], OT[:], start=True, stop=True), rd=[bs65f, bOT], wr=[bp])
                yT, byT = yTs[h % 2]
                c.op("dve", lambda e: e.reciprocal(r32[0:64, :], p[0:64, :]), rd=[bp], wr=[br32])
                c.op("dve", lambda e: e.tensor_tensor(yT[:], OT[0:64, :], r32[0:64, :], ALU.mult), rd=[bOT, br32], wr=[byT])
                c.dma("sp", ybF[h * 64:(h + 1) * 64, g * 512:(g + 1) * 512], yT[:], rd=[byT])
            return st, fin

        ld3(0)
        for tt in range(4):
            pre3(tt)
        LOOK = 2
        for g in range(8):
            allst = []
            for h in range(8):
                st, fin = steps3(g, h)
                for i, s in enumerate(st):
                    allst.append((s, fin if i == len(st) - 1 else None, h))
            n = len(allst)
            for i in range(min(LOOK, n)):
                allst[i][0][0]()
            for i in range(n):
                (qk, ex, pv), fin, h = allst[i]
                ex()
                if i + LOOK < n:
                    allst[i + LOOK][0][0]()
                pv()
                if fin is not None:
                    fin()
                    if g + 1 < 8 and h % 2 == 1:
                        pre3(4 * (g + 1) + h // 2)
        pst["n"] = 8; pst["off"] = 0
        c.barrier()

    if upto <= 3:
        print("ninst", c.ninst, "nwait", c.nwait); return nc
    with ExitStack() as es:
        def T(name, shape, dt=F32):
            return es.enter_context(nc.sbuf_tensor("s_" + name, list(shape), dt)), Buf(name)
        idb, bidb = T("idb4", [128, 128], BF16); c.dma("pool", idb[:], ident_d, wr=[bidb])
        wba, bwba = T("wba", [128, 4, DM], BF16); wbb, bwbb = T("wbb", [128, 4, DM], BF16); wo, bwo = T("wo", [128, 8, DM], BF16)
        for k in range(4):
            c.dma("pool", wba[:, k, :], wba_d[k * 128:(k + 1) * 128, :], wr=[bwba])
            c.dma("pool", wbb[:, k, :], wbb_d[k * 128:(k + 1) * 128, :], wr=[bwbb])
        for k in range(8):
            c.dma("pool", wo[:, k, :], wout_d[k * 128:(k + 1) * 128, :], wr=[bwo])
        xts = [T("x4%d" % i, [128, DM]) for i in range(2)]
        gps = [T("gp%d" % i, [128, 2048]) for i in range(2)]
        yas = [T("ya4%d" % i, [128, 4, 128], BF16) for i in range(2)]
        ybs = [T("yb4%d" % i, [128, 4, 128], BF16) for i in range(2)]
        m1, bm1 = T("m1", [128, DM]); m2, bm2 = T("m2", [128, DM]); mb, bmb = T("mb", [128, DM], BF16)
        mT, bmT = T("mT", [128, 8, 128], BF16)
        x1, bx1 = T("x1", [128, DM]); junk, bjunk = T("junk4", [128, DM]); h2, bh2 = T("h2", [128, DM], BF16)
        h2T, bh2T = T("h2T", [128, 8, 128], BF16)
        ss, bss = T("ss4", [128, NT]); c.op("dve", lambda e: e.memset(ss[:], 0.0), wr=[bss])
        rs, brs = T("rs4", [128, NT])
        yaFv = yaF.rearrange("(c p) t -> p c t", p=128); ybFv = ybF.rearrange("(c p) t -> p c t", p=128)
        h2Fv = h2F.rearrange("(c p) t -> p c t", p=128)

        def loads4(tt):
            i = tt % 2; t0 = tt * 128
            c.dma("sp", xts[i][0][:], x[t0:t0 + 128, :], wr=[xts[i][1]])
            c.dma("sp", gps[i][0][:], zT[t0:t0 + 128, 2048:4096], wr=[gps[i][1]])
            c.dma("sp", yas[i][0][:], yaFv[:, :, t0:t0 + 128], wr=[yas[i][1]])
            c.dma("sp", ybs[i][0][:], ybFv[:, :, t0:t0 + 128], wr=[ybs[i][1]])
        loads4(0)
        for tt in range(NT):
            if tt + 1 < NT:
                loads4(tt + 1)
            i = tt % 2; t0 = tt * 128
            xt, bxt = xts[i]; gp, bgp = gps[i]; ya, bya = yas[i]; yb, byb = ybs[i]
            c.op("act", lambda e: e.activation(gp[:], gp[:], AF.Sigmoid), rd=[bgp], wr=[bgp])
            for half in range(2):
                hs = slice(half * 512, (half + 1) * 512)
                pa, bpa = ps()
                c.group("pe", [lambda e, k=k: e.matmul(pa[:], ya[:, k, :], wba[:, k, hs], start=(k == 0), stop=(k == 3)) for k in range(4)], rd=[bya, bwba], wr=[bpa])
                c.op("dve", lambda e: e.tensor_tensor(m1[:, hs], pa[:], gp[:, half * 512:(half + 1) * 512], ALU.mult), rd=[bpa, bgp], wr=[bm1])
                pb_, bpb_ = ps()
                c.group("pe", [lambda e, k=k: e.matmul(pb_[:], yb[:, k, :], wbb[:, k, hs], start=(k == 0), stop=(k == 3)) for k in range(4)], rd=[byb, bwbb], wr=[bpb_])
                c.op("dve", lambda e: e.tensor_tensor(m2[:, hs], pb_[:], gp[:, 1024 + half * 512:1024 + (half + 1) * 512], ALU.mult), rd=[bpb_, bgp], wr=[bm2])
            c.op("pool", lambda e: e.tensor_tensor(mb[:], m1[:], m2[:], ALU.add), rd=[bm1, bm2], wr=[bmb])
            p, bp = ps(); pb = p[:].bitcast(BF16)
            c.group("pe", [lambda e, k=k: e.transpose(pb[:, k * 128:(k + 1) * 128], mb[:, k * 128:(k + 1) * 128], idb[:]) for k in range(8)], rd=[bmb, bidb], wr=[bp])
            c.op("act", lambda e: e.copy(mT[:].rearrange("p a b -> p (a b)"), pb), rd=[bp], wr=[bmT])
            for half in range(2):
                hs = slice(half * 512, (half + 1) * 512)
                po, bpo = ps()
                c.group("pe", [lambda e, k=k: e.matmul(po[:], mT[:, k, :], wo[:, k, hs], start=(k == 0), stop=(k == 7)) for k in range(8)], rd=[bmT, bwo], wr=[bpo])
                c.op("dve", lambda e: e.tensor_tensor(x1[:, hs], po[:], xt[:, hs], ALU.add), rd=[bpo, bxt], wr=[bx1])
            c.dma("sp", x1s[t0:t0 + 128, :], x1[:], rd=[bx1])
            c.op("act", lambda e: e.activation(junk[:], x1[:], AF.Square, accum_out=ss[:, tt:tt + 1]), rd=[bx1], wr=[bjunk, bss])
            c.op("dve", lambda e: e.tensor_scalar(rs[:, tt:tt + 1], ss[:, tt:tt + 1], 1.0 / DM, 1e-6, ALU.mult, ALU.add), rd=[bss], wr=[brs])
            c.op("act", lambda e: e.activation(rs[:, tt:tt + 1], rs[:, tt:tt + 1], AF.Ln), rd=[brs], wr=[brs])
            c.op("act", lambda e: e.activation(rs[:, tt:tt + 1], rs[:, tt:tt + 1], AF.Exp, scale=-0.5), rd=[brs], wr=[brs])
            c.op("dve", lambda e: e.tensor_scalar(h2[:], x1[:], rs[:, tt:tt + 1], None, ALU.mult), rd=[bx1, brs], wr=[bh2])
            p, bp = ps(); pb = p[:].bitcast(BF16)
            c.group("pe", [lambda e, k=k: e.transpose(pb[:, k * 128:(k + 1) * 128], h2[:, k * 128:(k + 1) * 128], idb[:]) for k in range(8)], rd=[bh2, bidb], wr=[bp])
            c.op("act", lambda e: e.copy(h2T[:].rearrange("p a b -> p (a b)"), pb), rd=[bp], wr=[bh2T])
            c.dma("sp", h2Fv[:, :, t0:t0 + 128], h2T[:], rd=[bh2T])
        c.barrier()

    if upto <= 4:
        print("ninst", c.ninst, "nwait", c.nwait); return nc
    with ExitStack() as es:
        def T(name, shape, dt=F32):
            return es.enter_context(nc.sbuf_tensor("s_" + name, list(shape), dt)), Buf(name)
        wup, bwup = T("wup", [128, 8, 2 * DFF], BF16); wdn, bwdn = T("wdn", [128, NFF, DM], BF16)
        g2, bg2 = T("g2", [128, 8]); c.dma("sp", g2[:], n2g, wr=[bg2])
        wub = [Buf("wup_%d" % k) for k in range(8)]
        for k in range(8):
            c.dma("pool", wup[:, k, :], wup_d[k * 128:(k + 1) * 128, :], wr=[wub[k]])
            c.op("dve", lambda e, k=k: e.tensor_scalar(wup[:, k, :], wup[:, k, :], g2[:, k:k + 1], None, ALU.mult), rd=[wub[k], bg2], wr=[wub[k]])
        for f in range(NFF):
            c.dma("pool", wdn[:, f, :], wdn_d[f * 128:(f + 1) * 128, :], wr=[bwdn])
        cw, bcw = T("cw", [128, NFF, 3]); c.dma("sp", cw[:].rearrange("p a b -> p (a b)"), cw_d, wr=[bcw])
        cbt, bcbt = T("cbt", [128, NFF]); c.dma("sp", cbt[:], cb_d, wr=[bcbt])
        cr, bcr = T("cr", [128, NFF, 2]); c.op("dve", lambda e: e.memset(cr[:], 0.0), wr=[bcr])
        h2s = [T("h2s%d" % i, [128, 8, 512], BF16) for i in range(2)]
        asb = [T("asb%d" % i, [128, 514]) for i in range(2)]
        acc = [T("acc%d" % i, [128, 512]) for i in range(2)]
        hg = es.enter_context(nc.sbuf_tensor("hg", [128, NFF, 512], BF16)); bhg = [Buf("hg%d" % f) for f in range(NFF)]
        x1t = [T("x1t%d" % i, [128, DM]) for i in range(2)]
        ost = [T("ost%d" % i, [128, DM]) for i in range(2)]
        h2Fv = h2F.rearrange("(c p) t -> p c t", p=128)
        c.dma("sp", h2s[0][0][:], h2Fv[:, :, 0:512], wr=[h2s[0][1]])
        nx = 0
        for st in range(8):
            if st + 1 < 8:
                c.dma("sp", h2s[(st + 1) % 2][0][:], h2Fv[:, :, (st + 1) * 512:(st + 2) * 512], wr=[h2s[(st + 1) % 2][1]])
            hT, bhT = h2s[st % 2]
            for f in range(NFF):
                pa, bpa = ps()
                c.group("pe", [lambda e, k=k: e.matmul(pa[:], wup[:, k, f * 128:(f + 1) * 128], hT[:, k, :], start=(k == 0), stop=(k == 7)) for k in range(8)], rd=[bhT] + wub, wr=[bpa])
                pg, bpg = ps()
                c.group("pe", [lambda e, k=k: e.matmul(pg[:], wup[:, k, DFF + f * 128:DFF + (f + 1) * 128], hT[:, k, :], start=(k == 0), stop=(k == 7)) for k in range(8)], rd=[bhT] + wub, wr=[bpg])
                a, ba = asb[f % 2]; ac, bac = acc[f % 2]
                c.op("act", lambda e: e.copy(a[:, 0:2], cr[:, f, :]), rd=[bcr], wr=[ba])
                c.op("act", lambda e: e.copy(a[:, 2:514], pa[:]), rd=[bpa], wr=[ba])
                c.op("act", lambda e: e.copy(cr[:, f, :], a[:, 512:514]), rd=[ba], wr=[bcr])
                c.op("dve", lambda e: e.tensor_scalar(ac[:], a[:, 0:512], cw[:, f, 0:1], None, ALU.mult), rd=[ba, bcw], wr=[bac])
                c.op("dve", lambda e: e.scalar_tensor_tensor(ac[:], a[:, 1:513], cw[:, f, 1:2], ac[:], ALU.mult, ALU.add), rd=[ba, bcw, bac], wr=[bac])
                c.op("dve", lambda e: e.scalar_tensor_tensor(ac[:], a[:, 2:514], cw[:, f, 2:3], ac[:], ALU.mult, ALU.add), rd=[ba, bcw, bac], wr=[bac])
                c.op("act", lambda e: e.activation(ac[:], ac[:], AF.Gelu, bias=cbt[:, f:f + 1]), rd=[bac, bcbt], wr=[bac])
                c.op("dve", lambda e: e.tensor_tensor(hg[:, f, :], ac[:], pg[:], ALU.mult), rd=[bac, bpg], wr=[bhg[f]])
            for sub in range(4):
                tt = st * 4 + sub; t0 = tt * 128
                xx, bxx = x1t[nx % 2]; oo, boo = ost[nx % 2]; nx += 1
                c.dma("sp", xx[:], x1s[t0:t0 + 128, :], wr=[bxx])
                for half in range(2):
                    hs = slice(half * 512, (half + 1) * 512)
                    po, bpo = ps()
                    c.group("pe", [lambda e, f=f: e.matmul(po[:], hg[:, f, sub * 128:(sub + 1) * 128], wdn[:, f, hs], start=(f == 0), stop=(f == NFF - 1)) for f in range(NFF)],
                            rd=bhg + [bwdn], wr=[bpo])
                    c.op("dve", lambda e: e.tensor_tensor(oo[:, hs], po[:], xx[:, hs], ALU.add), rd=[bpo, bxx], wr=[boo])
                c.dma("sp", out[t0:t0 + 128, :], oo[:], rd=[boo])
        c.barrier()
    print("ninst", c.ninst, "nwait", c.nwait, {e: c.cnt[e] for e in c.cnt})
    return nc


def _consts():
    i = np.arange(128)
    su = (i[:, None] < i[None, :]).astype(np.float32)
    ui = (i[:, None] <= i[None, :]).astype(np.float32)
    mu2 = np.concatenate([su, ui, su, ui], axis=1)
    sl = (i[None, :] < i[:, None]).astype(np.float32)
    sl2 = np.concatenate([sl, sl], axis=1)
    bones = ((i[:, None] // 64) == (i[None, :] // 64)).astype(np.float32)
    q2 = np.arange(256)
    cb0 = np.where(i[:, None] > q2[None, :], -BIG, 0.0).astype(np.float32)
    cb1 = np.where(i[:, None] + 128 > q2[None, :], -BIG, 0.0).astype(np.float32)
    cbias2 = np.concatenate([cb0, cb1], axis=1)
    sel65 = np.zeros((65, 64), np.float32); sel65[64, :] = 1.0
    j = np.arange(16)
    pok = (j[None, :] < j[:, None]).astype(np.float32)
    pbias = ((pok - 1.0) * 1e30).astype(np.float32)
    half = 8
    invf = (500000.0 ** (-np.arange(half, dtype=np.float32) / half)).astype(np.float32) / np.float32(2.0 * math.pi)
    return dict(ident=np.eye(128, dtype=np.float32), mu2=mu2, sl2=sl2, bones=bones, cbias2=cbias2, sel65=sel65,
                pok=pok.reshape(1, 256), pbias=pbias.reshape(1, 256), invf=invf.reshape(1, 8).astype(np.float32))


def _fm(v, n):
    return np.ascontiguousarray(np.asarray(v, np.float32).reshape(n, 128).T)


def _prep(inp):
    f = lambda a: np.ascontiguousarray(np.asarray(a, dtype=np.float32))
    w_in = f(inp["w_in"][0])
    fm_cols = np.r_[0:512, 512:1024, 1536:1600, 1600:1664, 1664:1824]
    tm_cols = np.r_[1024:1536, 1824:3360, 3360:5408]
    mu = f(inp["rwkv_mu"][0])
    mu_fm = np.zeros(1408, np.float32); mu_fm[:FMC] = mu[fm_cols]
    rk = f(inp["rwkv_r_k"][0])
    rkb = np.zeros((128, 8), np.float32)
    for h in range(8):
        rkb[(h % 2) * 64:(h % 2 + 1) * 64, h] = rk[h]
    cw = f(inp["ffn_conv_w"][0])
    cwl = np.ascontiguousarray(cw.reshape(3, NFF, 128).transpose(2, 1, 0)).reshape(128, NFF * 3)
    shared = dict(
        w_in=np.ascontiguousarray(w_in[:, np.r_[fm_cols, tm_cols]]),
        n1g=_fm(inp["norm1_g"][0], 8), mu_fm=_fm(mu_fm, 11), mu_v=f(mu[1024:1536]).reshape(1, 512),
        wdec=np.concatenate([f(inp["w_decay_up"][0]), f(inp["decay_bias"][0]).reshape(1, 512)], 0),
        waaa=np.concatenate([f(inp["w_aaa_up"][0]), f(inp["aaa_bias"][0]).reshape(1, 512)], 0),
        wgate=f(inp["w_gate_up"][0]), kk_fm=_fm(inp["rwkv_k_k"][0], 4), ka_fm=_fm(inp["rwkv_k_a"][0], 4), rkb=rkb,
        lng=f(inp["rwkv_ln_g"][0]).reshape(1, 512), lnb=f(inp["rwkv_ln_b"][0]).reshape(1, 512),
        qkg=np.concatenate([np.tile(f(inp["q_norm_g"][0]), 8), np.tile(f(inp["k_norm_g"][0]), 8)]).reshape(1, 1024),
        wba=f(inp["w_branch_a"][0]), wbb=f(inp["w_branch_b"][0]), wout=f(inp["w_out"][0]), n2g=_fm(inp["norm2_g"][0], 8),
        wup=f(inp["w_ffn_up"][0]), cw=cwl, cb=_fm(inp["ffn_conv_b"][0], NFF), wdn=f(inp["w_ffn_down"][0]),
    )
    shared.update(_consts())
    xs = np.asarray(inp["x"], np.float32); ps_ = np.asarray(inp["positions"], np.int32)
    maps = []
    for b in range(8):
        m = dict(shared)
        m["x"] = np.ascontiguousarray(xs[b])
        m["pos"] = np.ascontiguousarray(ps_[b].reshape(NT, 128).T)
        maps.append(m)
    return maps


def kernel(**inputs):
    maps = _prep(inputs)
    nc = build()
    res = run_bass_kernel_spmd(nc, maps, core_ids=list(range(8)))
    return np.stack([np.asarray(r["out"], np.float32) for r in res.results], axis=0)
```

```python
import numpy as np
import concourse.bass as bass
import concourse.mybir as mybir
from concourse.bass_utils import run_bass_kernel_spmd

F32 = mybir.dt.float32
BF16 = mybir.dt.bfloat16
I32 = mybir.dt.int32
ALU = mybir.AluOpType
AF = mybir.ActivationFunctionType
AX = mybir.AxisListType


class Buf:
    __slots__ = ("name", "lastw", "readers")

    def __init__(self, name):
        self.name = name
        self.lastw = None
        self.readers = {}


class Ctx:
    def __init__(self, nc, n_dma_sems=24):
        self.nc = nc
        self.eng = {"pe": nc.tensor, "act": nc.scalar, "dve": nc.vector,
                    "pool": nc.gpsimd, "sp": nc.sync}
        self.sem = {}
        self.cnt = {}
        self.waited = {e: {} for e in self.eng}
        self._stack = []
        for e in ("pe", "act", "dve", "pool"):
            cm = nc.semaphore("s_" + e)
            self.sem[e] = cm.__enter__()
            self._stack.append(cm)
            self.cnt[e] = 0
        self.dsem = []
        self.dpool = {"hw": [], "sw": []}
        for i in range(n_dma_sems):
            cm = nc.semaphore("d%d" % i)
            self.dsem.append([cm.__enter__(), 0])
            self._stack.append(cm)
            self.dpool["sw" if i < 8 else "hw"].append(i)
        self.dnext = {"hw": 0, "sw": 0}
        self.semh = {}
        for e in self.sem:
            self.semh[("e", e)] = self.sem[e]
        for i, (h, _) in enumerate(self.dsem):
            self.semh[("d", i)] = h
        self.nwait = 0
        self.ninst = 0

    def _wait(self, e, toks):
        w = self.waited[e]
        best = {}
        for t in toks:
            if t is None:
                continue
            k, v = t[0], t[1]
            if w.get(k, 0) >= v:
                continue
            if best.get(k, 0) < v:
                best[k] = v
        for k, v in best.items():
            self.eng[e].wait_ge(self.semh[k], v)
            w[k] = v
            self.nwait += 1

    def _deps(self, e, rd, wr):
        toks = []
        me = ("e", e)
        for b in rd:
            if b.lastw is not None:
                if not (e == "pe" and b.lastw[0] == me):
                    toks.append(b.lastw)
        for b in wr:
            if b.lastw is not None and not (e == "pe" and b.lastw[0] == me):
                toks.append(b.lastw)
            for k, t in b.readers.items():
                if not (e == "pe" and k == me):
                    toks.append(t)
        return toks

    def _mark(self, tok, rd, wr):
        for b in rd:
            b.readers[tok[0]] = tok
        for b in wr:
            b.lastw = tok
            b.readers = {}

    def op(self, e, fn, rd=(), wr=()):
        self._wait(e, self._deps(e, rd, wr))
        ins = fn(self.eng[e])
        self.cnt[e] += 1
        ins.then_inc(self.sem[e], 1)
        tok = (("e", e), self.cnt[e])
        self._mark(tok, rd, wr)
        self.ninst += 1
        return tok

    def group(self, e, fns, rd=(), wr=()):
        self._wait(e, self._deps(e, rd, wr))
        ins = None
        for fn in fns:
            ins = fn(self.eng[e])
            self.ninst += 1
        self.cnt[e] += 1
        ins.then_inc(self.sem[e], 1)
        tok = (("e", e), self.cnt[e])
        self._mark(tok, rd, wr)
        return tok

    def dma(self, q, out, in_, rd=(), wr=(), **kw):
        kind = "sw" if q == "pool" else "hw"
        pool = self.dpool[kind]
        i = pool[self.dnext[kind] % len(pool)]
        self.dnext[kind] += 1
        h, c = self.dsem[i]
        k = ("d", i)
        toks = self._deps(q, rd, wr)
        if c > 0:
            toks.append((k, 16 * c))
        self._wait(q, toks)
        self.eng[q].dma_start(out=out, in_=in_, **kw).then_inc(h, 16)
        self.dsem[i][1] = c + 1
        tok = (k, 16 * (c + 1))
        self._mark(tok, rd, wr)
        self.ninst += 1
        return tok

    def wait_all(self, e, bufs):
        toks = []
        for b in bufs:
            toks.append(b.lastw)
            toks.extend(b.readers.values())
        self._wait(e, toks)

    def barrier(self, bufs=()):
        toks = []
        for e in self.sem:
            if self.cnt[e] > 0:
                toks.append((("e", e), self.cnt[e]))
        for i, (h, c) in enumerate(self.dsem):
            if c > 0:
                toks.append((("d", i), 16 * c))
        for e in self.eng:
            self._wait(e, toks)

from contextlib import ExitStack
import math
import os

S = 4096
DM = 1024
NT = 32
FMC = 1312
TMC = 4096
DFF = 2816
NFF = 22
BIG = 30000.0


def build(debug=False, upto=99):
    nc = bass.Bass("TRN2", target_bir_lowering=False)
    okind = "ExternalOutput" if debug else "Internal"

    def DIN(name, shape, dt=F32):
        return nc.dram_tensor(name, list(shape), dt, kind="ExternalInput").ap()

    x = DIN("x", [S, DM]); pos = DIN("pos", [128, NT], I32)
    w_in = DIN("w_in", [DM, 5408]); n1g = DIN("n1g", [128, 8]); mu_fm = DIN("mu_fm", [128, 11]); mu_v = DIN("mu_v", [1, 512])
    wdec_d = DIN("wdec", [65, 512]); waaa_d = DIN("waaa", [65, 512]); wgate_d = DIN("wgate", [160, 512])
    kk_d = DIN("kk_fm", [128, 4]); ka_d = DIN("ka_fm", [128, 4]); rkb_d = DIN("rkb", [128, 8])
    lng_d = DIN("lng", [1, 512]); lnb_d = DIN("lnb", [1, 512]); qkg_d = DIN("qkg", [1, 1024])
    wba_d = DIN("wba", [512, DM]); wbb_d = DIN("wbb", [512, DM]); wout_d = DIN("wout", [DM, DM]); n2g = DIN("n2g", [128, 8])
    wup_d = DIN("wup", [DM, 2 * DFF]); cw_d = DIN("cw", [128, NFF * 3]); cb_d = DIN("cb", [128, NFF]); wdn_d = DIN("wdn", [DFF, DM])
    ident_d = DIN("ident", [128, 128]); mu2_d = DIN("mu2", [128, 512]); sl2_d = DIN("sl2", [128, 256]); bones_d = DIN("bones", [128, 128])
    cbias2_d = DIN("cbias2", [128, 512]); sel65_d = DIN("sel65", [65, 64]); pok_d = DIN("pok", [1, 256]); pbias_d = DIN("pbias", [1, 256]); invf_d = DIN("invf", [1, 8])
    out = nc.dram_tensor("out", [S, DM], F32, kind="ExternalOutput").ap()
    zF = nc.dram_tensor("zF", [1408, S], F32, kind=okind).ap()
    zT = nc.dram_tensor("zT", [S, TMC], F32, kind=okind).ap()
    yaF = nc.dram_tensor("yaF", [512, S], BF16, kind=okind).ap()
    ybF = nc.dram_tensor("ybF", [512, S], BF16, kind=okind).ap()
    x1s = nc.dram_tensor("x1s", [S, DM], F32, kind=okind).ap()
    h2F = nc.dram_tensor("h2F", [DM, S], BF16, kind=okind).ap()

    c = Ctx(nc)
    PSB = [(nc.alloc_psum_tensor("psb%d" % i, [128, 512], F32), Buf("psb%d" % i)) for i in range(8)]
    pst = {"i": 0, "n": 8, "off": 0}

    def ps():
        i = pst["off"] + pst["i"] % pst["n"]
        pst["i"] += 1
        return PSB[i]

    rr = {"i": 0}

    def ev():
        rr["i"] += 1
        return "act" if rr["i"] % 2 else "dve"

    def bcast(ap, shape, axis):
        return ap.unsqueeze(axis).to_broadcast(list(shape))

    with ExitStack() as es:
        def T(name, shape, dt=F32):
            return es.enter_context(nc.sbuf_tensor("s_" + name, list(shape), dt)), Buf(name)
        w, bw = T("w1", [128, 8, 5408], BF16)
        g1, bg1 = T("g1", [128, 8]); muf, bmuf = T("muf", [128, 11])
        idb, bidb = T("idb1", [128, 128], BF16)
        c.dma("sp", g1[:], n1g, wr=[bg1]); c.dma("sp", muf[:], mu_fm, wr=[bmuf])
        c.dma("pool", idb[:], ident_d, wr=[bidb])
        wb = [Buf("w1_%d" % k) for k in range(8)]
        for kc in range(8):
            c.dma("pool", w[:, kc, :], w_in[kc * 128:(kc + 1) * 128, :], wr=[wb[kc]])
            c.op("dve", lambda e, kc=kc: e.tensor_scalar(w[:, kc, :], w[:, kc, :], g1[:, kc:kc + 1], None, ALU.mult),
                 rd=[wb[kc], bg1], wr=[wb[kc]])
        ss, bss = T("ss1", [128, NT]); c.op("dve", lambda e: e.memset(ss[:], 0.0), wr=[bss])
        rs, brs = T("rs1", [128, NT])
        junk, bjunk = T("junk1", [128, DM])
        xts = [T("xt%d" % i, [128, DM]) for i in range(2)]
        hb, bhb = T("hb1", [128, DM], BF16)
        hTs = [es.enter_context(nc.sbuf_tensor("hT%d" % i, [128, 8, 512], BF16)) for i in range(2)]
        hTb = [[Buf("hT%d_%d" % (i, s)) for s in range(4)] for i in range(2)]
        stgs = [T("stg%d" % i, [128, TMC]) for i in range(2)]
        zsb = [T("zsb%d" % j, [128, 513]) for j in range(11)]
        for j in range(11):
            c.op("pool", lambda e, j=j: e.memset(zsb[j][0][:, 0:1], 0.0), wr=[zsb[j][1]])
        tds = [T("td%d" % i, [128, 512]) for i in range(2)]
        ostg = [T("ostg%d" % i, [128, 512]) for i in range(3)]
        no = 0
        import os
        for st in range(int(os.environ.get('NST', '8'))):
            hT = hTs[st % 2]
            for sub in range(4):
                tt = st * 4 + sub
                xt, bxt = xts[tt % 2]
                c.dma("sp", xt[:], x[tt * 128:(tt + 1) * 128, :], wr=[bxt])
                c.op("act", lambda e: e.activation(junk[:], xt[:], AF.Square, accum_out=ss[:, tt:tt + 1]), rd=[bxt], wr=[bjunk, bss])
                c.op("dve", lambda e: e.tensor_scalar(rs[:, tt:tt + 1], ss[:, tt:tt + 1], 1.0 / DM, 1e-6, ALU.mult, ALU.add), rd=[bss], wr=[brs])
                c.op("act", lambda e: e.activation(rs[:, tt:tt + 1], rs[:, tt:tt + 1], AF.Ln), rd=[brs], wr=[brs])
                c.op("act", lambda e: e.activation(rs[:, tt:tt + 1], rs[:, tt:tt + 1], AF.Exp, scale=-0.5), rd=[brs], wr=[brs])
                c.op("dve", lambda e: e.tensor_scalar(hb[:], xt[:], rs[:, tt:tt + 1], None, ALU.mult), rd=[bxt, brs], wr=[bhb])
                p, bp = ps(); pb = p[:].bitcast(BF16)
                c.group("pe", [lambda e, k=k: e.transpose(pb[:, k * 128:(k + 1) * 128], hb[:, k * 128:(k + 1) * 128], idb[:]) for k in range(8)],
                        rd=[bhb, bidb], wr=[bp])
                c.op("act", lambda e: e.copy(hT[:, :, sub * 128:(sub + 1) * 128], pb.rearrange("p (k t) -> p k t", k=8)), rd=[bp], wr=[hTb[st % 2][sub]])
                stg, bstg = stgs[tt % 2]
                for gi in range(8):
                    p, bp = ps()
                    c.group("pe", [lambda e, k=k, p=p: e.matmul(p[:], hT[:, k, sub * 128:(sub + 1) * 128], w[:, k, FMC + gi * 512:FMC + (gi + 1) * 512],
                                                               start=(k == 0), stop=(k == 7)) for k in range(8)],
                            rd=[hTb[st % 2][sub]] + wb, wr=[bp])
                    en = ev()
                    if en == "act":
                        c.op("act", lambda e, p=p: e.copy(stg[:, gi * 512:(gi + 1) * 512], p[:]), rd=[bp], wr=[bstg])
                    else:
                        c.op("dve", lambda e, p=p: e.tensor_copy(stg[:, gi * 512:(gi + 1) * 512], p[:]), rd=[bp], wr=[bstg])
                c.dma("sp", zT[tt * 128:(tt + 1) * 128, :], stg[:], rd=[bstg])
            for j in range(11):
                ncol = 32 if j == 10 else 128
                z, bz = zsb[j]
                p, bp = ps()
                c.group("pe", [lambda e, k=k, p=p: e.matmul(p[0:ncol, :], w[:, k, j * 128:j * 128 + ncol], hT[:, k, :], start=(k == 0), stop=(k == 7)) for k in range(8)],
                        rd=hTb[st % 2] + wb, wr=[bp])
                c.op("act", lambda e, p=p: e.copy(z[0:ncol, 1:513], p[0:ncol, :]), rd=[bp], wr=[bz])
                td, btd = tds[j % 2]
                c.op("dve", lambda e: e.tensor_tensor(td[0:ncol, :], z[0:ncol, 0:512], z[0:ncol, 1:513], ALU.subtract), rd=[bz], wr=[btd])
                o, bo = ostg[no % 3]; no += 1
                c.op("dve", lambda e: e.scalar_tensor_tensor(o[0:ncol, :], td[0:ncol, :], muf[0:ncol, j:j + 1], z[0:ncol, 1:513], ALU.mult, ALU.add),
                     rd=[btd, bz, bmuf], wr=[bo])
                c.op("act", lambda e: e.copy(z[0:ncol, 0:1], z[0:ncol, 512:513]), rd=[bz], wr=[bz])
                c.dma("sp", zF[j * 128:j * 128 + ncol, st * 512:(st + 1) * 512], o[0:ncol, :], rd=[bo])
        c.barrier()

    if upto <= 1:
        print("ninst", c.ninst, "nwait", c.nwait); return nc
    with ExitStack() as es:
        def T(name, shape, dt=F32):
            return es.enter_context(nc.sbuf_tensor("s_" + name, list(shape), dt)), Buf(name)
        idb, bidb = T("idb2", [128, 128], BF16); c.dma("pool", idb[:], ident_d, wr=[bidb])
        wdec, bwdec = T("wdec", [65, 512], BF16); c.dma("pool", wdec[:], wdec_d, wr=[bwdec])
        waaa, bwaaa = T("waaa", [65, 512], BF16); c.dma("pool", waaa[:], waaa_d, wr=[bwaaa])
        wgate, bwgate = T("wgate", [128, 2, 512], BF16)
        c.dma("pool", wgate[:, 0, :], wgate_d[0:128, :], wr=[bwgate]); c.dma("pool", wgate[0:32, 1, :], wgate_d[128:160, :], wr=[bwgate])
        kkf, bkkf = T("kkf", [128, 4]); c.dma("sp", kkf[:], kk_d, wr=[bkkf])
        kaf, bkaf = T("kaf", [128, 4]); c.dma("sp", kaf[:], ka_d, wr=[bkaf])
        c0f, bc0f = T("c0f", [128, 4])
        c.op("dve", lambda e: e.tensor_scalar(c0f[:], kaf[:], -1.0, 1.0, ALU.mult, ALU.add), rd=[bkaf], wr=[bc0f])
        rkb, brkb = T("rkb", [128, 8]); c.dma("sp", rkb[:], rkb_d, wr=[brkb])
        lng, blng = T("lng", [128, 512]); c.dma("sp", lng[:], lng_d.partition_broadcast(128), wr=[blng])
        lnb, blnb = T("lnb", [128, 512]); c.dma("sp", lnb[:], lnb_d.partition_broadcast(128), wr=[blnb])
        muv, bmuv = T("muv", [128, 512]); c.dma("sp", muv[:], mu_v.partition_broadcast(128), wr=[bmuv])
        MU2, bMU2 = T("MU2", [128, 512]); c.dma("sp", MU2[:], mu2_d, wr=[bMU2])
        SL2, bSL2 = T("SL2", [128, 256]); c.dma("sp", SL2[:], sl2_d, wr=[bSL2])
        bones, bbones = T("bones", [128, 128], BF16); c.dma("pool", bones[:], bones_d, wr=[bbones])
        S32, bS32 = T("S32", [128, 4, 64]); c.op("dve", lambda e: e.memset(S32[:], 0.0), wr=[bS32])
        Sb, bSb = T("Sb", [128, 4, 64], BF16); c.op("dve", lambda e: e.memset(Sb[:], 0.0), wr=[bSb])
        tha = [T("tha%d" % i, [65, 128], BF16) for i in range(2)]
        xaa = [T("xaa%d" % i, [65, 128], BF16) for i in range(2)]
        for i in range(2):
            c.op("dve", lambda e, i=i: e.memset(tha[i][0][:], 1.0), wr=[tha[i][1]])
            c.op("dve", lambda e, i=i: e.memset(xaa[i][0][:], 1.0), wr=[xaa[i][1]])
        rFs = [T("rF%d" % i, [128, 4, 128]) for i in range(2)]
        kFs = [T("kF%d" % i, [128, 4, 128]) for i in range(2)]
        xws = [T("xw%d" % i, [64, 128]) for i in range(2)]
        xas = [T("xa%d" % i, [64, 128]) for i in range(2)]
        xg0s = [T("xg0%d" % i, [128, 128]) for i in range(2)]
        xg1s = [T("xg1%d" % i, [32, 128]) for i in range(2)]
        vTs = [T("vT%d" % i, [128, 512]) for i in range(2)]
        vPs = [T("vP%d" % i, [128, 512]) for i in range(2)]
        for i in range(2):
            c.op("dve", lambda e, i=i: e.memset(vPs[i][0][:], 0.0), wr=[vPs[i][1]])
        v32s = [T("v32%d" % i, [128, 512]) for i in range(2)]; vbs = [T("vb%d" % i, [128, 512], BF16) for i in range(2)]; vtmp, bvtmp = T("vtmp", [128, 512])
        sg0, bsg0 = T("sg0", [128, 128], BF16); sg1, bsg1 = T("sg1", [32, 128], BF16)
        tg, btg = T("tg", [128, 512]); sgt, bsgt = T("sgt", [128, 128])
        logw, blogw = T("logw", [128, 4, 128]); lgi, blgi = T("lgi", [128, 4, 128]); lge, blge = T("lge", [128, 4, 128])
        alr, balr = T("alr", [128, 4, 128]); gTs = [T("gT%d" % i, [128, 512]) for i in range(2)]
        kkr, bkkr = T("kkr", [128, 4, 128]); sqb, bsqb = T("sqb", [128, 512], BF16); rn, brn = T("rn", [128, 512])
        kkn, bkkn = T("kkn", [128, 4, 128]); fF, bfF = T("fF", [128, 4, 128]); kM, bkM = T("kM", [128, 4, 128])
        gins = [T("gin%d" % i, [128, 4, 128]) for i in range(2)]; ginv, bginv = T("ginv", [128, 4, 128]); gex, bgex = T("gex", [128, 4, 128])
        ARs = [T("ARZ%d" % i, [128, 8, 2, 128], BF16) for i in range(2)]; bF, bbF = T("bF", [128, 4, 128])
        Bt, bBt = T("BtZ", [128, 8, 128], BF16); Kt, bKt = T("KtZ", [128, 8, 128], BF16)
        for (t_, b_) in (ARs[0], ARs[1], (Bt, bBt), (Kt, bKt)):
            c.op("pool", lambda e, t_=t_: e.memset(t_[:], 0.0), wr=[b_])
        Dd, bDd = T("Dd", [128, 4, 128]); Bh, bBh = T("Bh", [128, 4, 128], BF16); Kh, bKh = T("Kh", [128, 4, 128], BF16)
        BKhTs = [T("BKhT%d" % i, [128, 1024], BF16) for i in range(2)]
        rk, brk = T("rk", [128, 4, 128]); coefs = [T("coef%d" % i, [128, 8]) for i in range(2)]
        MABs = [T("MAB%d" % i, [128, 8, 2, 128], BF16) for i in range(2)]; MAKs = [T("MAK%d" % i, [128, 8, 2, 128], BF16) for i in range(2)]; MABTs = [T("MABT%d" % i, [128, 8, 128], BF16) for i in range(2)]
        Pk = [T("Pk%d" % i, [128, 8, 128], BF16) for i in range(2)]
        PTk = [T("PTk%d" % i, [128, 8, 128], BF16) for i in range(2)]
        ACk = [T("ACk%d" % i, [128, 8, 128], BF16) for i in range(2)]
        XT, bXT = T("XT", [128, 512], BF16); UT, bUT = T("UT", [128, 512], BF16)
        tmpS, btmpS = T("tmpS", [128, 4, 64])
        s1, bs1 = T("s1", [128, 8]); s2, bs2 = T("s2", [128, 8]); mean, bmean = T("mean", [128, 8]); var, bvar = T("var", [128, 8])
        sqt, bsqt = T("sqt", [128, 512]); yn, byn = T("yn", [128, 512]); bon, bbon = T("bon", [128, 512])
        yab, byab = T("yab", [128, 512], BF16); yaT, byaT = T("yaT", [128, 4, 128], BF16)

        zFr = zF[0:512, :].rearrange("(c p) t -> p c t", p=128)
        zFk = zF[512:1024, :].rearrange("(c p) t -> p c t", p=128)
        yaFv = yaF.rearrange("(c p) t -> p c t", p=128)

        def loads(ch):
            i = ch % 2; t0 = ch * 128
            c.dma("sp", rFs[i][0][:], zFr[:, :, t0:t0 + 128], wr=[rFs[i][1]])
            c.dma("sp", kFs[i][0][:], zFk[:, :, t0:t0 + 128], wr=[kFs[i][1]])
            c.dma("sp", xws[i][0][:], zF[1024:1088, t0:t0 + 128], wr=[xws[i][1]])
            c.dma("sp", xas[i][0][:], zF[1088:1152, t0:t0 + 128], wr=[xas[i][1]])
            c.dma("sp", xg0s[i][0][:], zF[1152:1280, t0:t0 + 128], wr=[xg0s[i][1]])
            c.dma("sp", xg1s[i][0][:], zF[1280:1312, t0:t0 + 128], wr=[xg1s[i][1]])
            c.dma("sp", vTs[i][0][:], zT[t0:t0 + 128, 0:512], wr=[vTs[i][1]])
            if ch == 0:
                c.dma("sp", vPs[i][0][1:128, :], zT[0:127, 0:512], wr=[vPs[i][1]])
            else:
                c.dma("sp", vPs[i][0][:], zT[t0 - 1:t0 + 127, 0:512], wr=[vPs[i][1]])

        loads(0)
        NCH = int(os.environ.get('NCH', str(NT)))
        STG = int(os.environ.get('STG', '99'))
        pqA = {"i": 0}; pqB = {"i": 0}
        def psA():
            pqA["i"] += 1
            return PSB[pqA["i"] % 4]
        def psB():
            pqB["i"] += 1
            return PSB[4 + pqB["i"] % 4]
        def stageA(ch):
            if ch + 1 < NCH:
                loads(ch + 1)
            i = ch % 2; t0 = ch * 128
            v32, bv32 = v32s[i]; vb, bvb = vbs[i]; gT, bgT = gTs[i]; gin, bgin = gins[i]; AR, bAR = ARs[i]
            BKhT, bBKhT = BKhTs[i]; coef, bcoef = coefs[i]; MAB, bMAB = MABs[i]; MAK, bMAK = MAKs[i]; MABT, bMABT = MABTs[i]
            rF, brF = rFs[i]; kF, bkF = kFs[i]; xw, bxw = xws[i]; xa, bxa = xas[i]
            xg0, bxg0 = xg0s[i]; xg1, bxg1 = xg1s[i]; vT, bvT = vTs[i]; vP, bvP = vPs[i]
            th, bth = tha[i]; xab, bxab = xaa[i]
            c.op("pool", lambda e: e.tensor_tensor(vtmp[:], vP[:], vT[:], ALU.subtract), rd=[bvP, bvT], wr=[bvtmp])
            c.op("pool", lambda e: e.tensor_tensor(vtmp[:], vtmp[:], muv[:], ALU.mult), rd=[bvtmp, bmuv], wr=[bvtmp])
            c.op("pool", lambda e: e.tensor_tensor(v32[:], vtmp[:], vT[:], ALU.add), rd=[bvtmp, bvT], wr=[bv32])
            c.op("pool", lambda e: e.tensor_copy(vb[:], v32[:]), rd=[bv32], wr=[bvb])
            yield
            c.op("act", lambda e: e.activation(th[0:64, :], xw[:], AF.Tanh), rd=[bxw], wr=[bth])
            c.op("dve", lambda e: e.tensor_copy(xab[0:64, :], xa[:]), rd=[bxa], wr=[bxab])
            c.op("act", lambda e: e.activation(sgt[:], xg0[:], AF.Tanh, scale=0.5), rd=[bxg0], wr=[bsgt])
            c.op("dve", lambda e: e.tensor_scalar(sg0[:], sgt[:], 0.5, 0.5, ALU.mult, ALU.add), rd=[bsgt], wr=[bsg0])
            c.op("act", lambda e: e.activation(sgt[0:32, :], xg1[:], AF.Tanh, scale=0.5), rd=[bxg1], wr=[bsgt])
            c.op("dve", lambda e: e.tensor_scalar(sg1[:], sgt[0:32, :], 0.5, 0.5, ALU.mult, ALU.add), rd=[bsgt], wr=[bsg1])
            yield
            p, bp = psA()
            c.group("pe", [lambda e, q=q, p=p: e.matmul(p[:, q * 128:(q + 1) * 128], wdec[:, q * 128:(q + 1) * 128], th[:], start=True, stop=True) for q in range(4)],
                    rd=[bwdec, bth], wr=[bp])
            c.op("act", lambda e, p=p: e.activation(tg[:], p[:], AF.Tanh, scale=0.5), rd=[bp], wr=[btg])
            c.op("dve", lambda e: e.tensor_scalar(logw[:].rearrange("p a b -> p (a b)"), tg[:], -0.5 * math.exp(-0.5), -0.5 * math.exp(-0.5), ALU.mult, ALU.add),
                 rd=[btg], wr=[blogw])
            for q in range(4):
                c.op("dve", lambda e, q=q: e.tensor_tensor_scan(lgi[:, q, :], logw[:, q, :], logw[:, q, :], 0.0, ALU.add, ALU.bypass), rd=[blogw], wr=[blgi])
            c.op("pool", lambda e: e.tensor_tensor(lge[:], lgi[:], logw[:], ALU.subtract), rd=[blgi, blogw], wr=[blge])
            yield
            p, bp = psA()
            c.group("pe", [lambda e, q=q, p=p: e.matmul(p[:, q * 128:(q + 1) * 128], waaa[:, q * 128:(q + 1) * 128], xab[:], start=True, stop=True) for q in range(4)],
                    rd=[bwaaa, bxab], wr=[bp])
            c.op("act", lambda e, p=p: e.activation(tg[:], p[:], AF.Tanh, scale=0.5), rd=[bp], wr=[btg])
            c.op("dve", lambda e: e.tensor_scalar(alr[:].rearrange("p a b -> p (a b)"), tg[:], 0.5, 0.5, ALU.mult, ALU.add), rd=[btg], wr=[balr])
            p, bp = psA()
            c.group("pe", [lambda e, p=p: e.matmul(p[:], sg0[:], wgate[:, 0, :], start=True, stop=False),
                           lambda e, p=p: e.matmul(p[:], sg1[:], wgate[0:32, 1, :], start=False, stop=True)], rd=[bsg0, bsg1, bwgate], wr=[bp])
            c.op("act", lambda e, p=p: e.copy(gT[:], p[:]), rd=[bp], wr=[bgT])
            yield
            for q in range(4):
                c.op("dve", lambda e, q=q: e.tensor_scalar(kkr[:, q, :], kF[:, q, :], kkf[:, q:q + 1], None, ALU.mult), rd=[bkF, bkkf], wr=[bkkr])
            c.op("pool", lambda e: e.tensor_tensor(sqb[:], kkr[:].rearrange("p a b -> p (a b)"), kkr[:].rearrange("p a b -> p (a b)"), ALU.mult), rd=[bkkr], wr=[bsqb])
            p, bp = psA()
            c.op("pe", lambda e, p=p: e.matmul(p[:], bones[:], sqb[:], start=True, stop=True), rd=[bbones, bsqb], wr=[bp])
            c.op("act", lambda e, p=p: e.activation(rn[:], p[:], AF.Ln), rd=[bp], wr=[brn])
            c.op("act", lambda e: e.activation(rn[:], rn[:], AF.Exp, scale=-0.5), rd=[brn], wr=[brn])
            c.op("dve", lambda e: e.tensor_tensor(kkn[:].rearrange("p a b -> p (a b)"), kkr[:].rearrange("p a b -> p (a b)"), rn[:], ALU.mult), rd=[bkkr, brn], wr=[bkkn])
            yield
            for q in range(4):
                c.op("dve", lambda e, q=q: e.tensor_scalar(fF[:, q, :], alr[:, q, :], kaf[:, q:q + 1], c0f[:, q:q + 1], ALU.mult, ALU.add), rd=[balr, bkaf, bc0f], wr=[bfF])
            c.op("pool", lambda e: e.tensor_tensor(kM[:], kF[:], fF[:], ALU.mult), rd=[bkF, bfF], wr=[bkM])
            c.op("act", lambda e: e.activation(gin[:], lgi[:], AF.Exp), rd=[blgi], wr=[bgin])
            c.op("act", lambda e: e.activation(ginv[:], lgi[:], AF.Exp, scale=-1.0), rd=[blgi], wr=[bginv])
            c.op("act", lambda e: e.activation(gex[:], lge[:], AF.Exp), rd=[blge], wr=[bgex])
            for q in range(4):
                c.op("act", lambda e, q=q: e.activation(Dd[:, q, :], lgi[:, q, :], AF.Exp, bias=lgi[:, q, 127:128], scale=-1.0), rd=[blgi], wr=[bDd])
            c.op("pool", lambda e: e.tensor_tensor(bF[:], kkn[:], alr[:], ALU.mult), rd=[bkkn, balr], wr=[bbF])
            for hh in range(2):
                r0, r1 = hh * 64, (hh + 1) * 64
                ARv = AR[r0:r1, :, :, :].rearrange("p (q two) a t -> p q two a t", two=2)[:, :, hh, :, :]
                Btv = Bt[r0:r1, :, :].rearrange("p (q two) t -> p q two t", two=2)[:, :, hh, :]
                Ktv = Kt[r0:r1, :, :].rearrange("p (q two) t -> p q two t", two=2)[:, :, hh, :]
                c.op("dve", lambda e: e.tensor_tensor(ARv[:, :, 1, :], rF[r0:r1, :, :], gin[r0:r1, :, :], ALU.mult), rd=[brF, bgin], wr=[bAR])
                c.op("dve", lambda e: e.scalar_tensor_tensor(ARv[:, :, 0, :], kkn[r0:r1, :, :], -1.0, gex[r0:r1, :, :], ALU.mult, ALU.mult), rd=[bkkn, bgex], wr=[bAR])
                c.op("dve", lambda e: e.tensor_tensor(Btv, bF[r0:r1, :, :], ginv[r0:r1, :, :], ALU.mult), rd=[bbF, bginv], wr=[bBt])
                c.op("pool", lambda e: e.tensor_tensor(Ktv, kM[r0:r1, :, :], ginv[r0:r1, :, :], ALU.mult), rd=[bkM, bginv], wr=[bKt])
            c.op("pool", lambda e: e.tensor_tensor(Bh[:], bF[:], Dd[:], ALU.mult), rd=[bbF, bDd], wr=[bBh])
            c.op("pool", lambda e: e.tensor_tensor(Kh[:], kM[:], Dd[:], ALU.mult), rd=[bkM, bDd], wr=[bKh])
            yield
            p, bp = psA(); pb = p[:].bitcast(BF16)
            c.group("pe", [lambda e, q=q: e.transpose(pb[:, q * 128:(q + 1) * 128], Bh[:, q, :], idb[:]) for q in range(4)] +
                          [lambda e, q=q: e.transpose(pb[:, 512 + q * 128:512 + (q + 1) * 128], Kh[:, q, :], idb[:]) for q in range(4)],
                    rd=[bBh, bKh, bidb], wr=[bp])
            c.op("act", lambda e: e.copy(BKhT[:], pb), rd=[bp], wr=[bBKhT])
            yield
            c.op("pool", lambda e: e.tensor_tensor(rk[:], rF[:], kM[:], ALU.mult), rd=[brF, bkM], wr=[brk])
            p, bp = psA()
            c.group("pe", [lambda e, q=q, p=p: e.matmul(p[:, 2 * q:2 * q + 2], rk[:, q, :], rkb[:, 2 * q:2 * q + 2], start=True, stop=True) for q in range(4)],
                    rd=[brk, brkb], wr=[bp])
            c.op("dve", lambda e, p=p: e.tensor_copy(coef[:], p[:, 0:8]), rd=[bp], wr=[bcoef])
            yield
            for q in range(4):
                pA, bpA = psA(); pB, bpB = psA(); pC, bpC = psA()
                fa, fb, fc = [], [], []
                for hh in range(2):
                    h = 2 * q + hh
                    arr = AR[:, h, :, :].rearrange("p a b -> p (a b)")
                    fa.append(lambda e, hh=hh, h=h, arr=arr: e.matmul(pA[:, hh * 256:(hh + 1) * 256], Bt[:, h, :], arr, start=True, stop=True))
                    fb.append(lambda e, hh=hh, h=h, arr=arr: e.matmul(pB[:, hh * 256:(hh + 1) * 256], Kt[:, h, :], arr, start=True, stop=True))
                    fc.append(lambda e, hh=hh, h=h: e.matmul(pC[:, hh * 128:(hh + 1) * 128], AR[:, h, 0, :], Bt[:, h, :], start=True, stop=True))
                c.group("pe", fa, rd=[bBt, bAR], wr=[bpA])
                c.group("pe", fb, rd=[bKt, bAR], wr=[bpB])
                c.group("pe", fc, rd=[bBt, bAR], wr=[bpC])
                c.op("dve", lambda e: e.tensor_tensor(MAB[:, 2 * q:2 * q + 2, :, :].rearrange("p a b c -> p (a b c)"), pA[:], MU2[:], ALU.mult), rd=[bpA, bMU2], wr=[bMAB])
                c.op("dve", lambda e: e.tensor_tensor(MAK[:, 2 * q:2 * q + 2, :, :].rearrange("p a b c -> p (a b c)"), pB[:], MU2[:], ALU.mult), rd=[bpB, bMU2], wr=[bMAK])
                c.op("dve", lambda e: e.tensor_tensor(MABT[:, 2 * q:2 * q + 2, :].rearrange("p a b -> p (a b)"), pC[:, 0:256], SL2[:], ALU.mult), rd=[bpC, bSL2], wr=[bMABT])
            yield

        def stageB(ch):
            i = ch % 2; t0 = ch * 128
            v32, bv32 = v32s[i]; vb, bvb = vbs[i]; gT, bgT = gTs[i]; gin, bgin = gins[i]; AR, bAR = ARs[i]
            BKhT, bBKhT = BKhTs[i]; coef, bcoef = coefs[i]; MAB, bMAB = MABs[i]; MAK, bMAK = MAKs[i]; MABT, bMABT = MABTs[i]
            AC0, bAC0 = ACk[0]
            c.op("pool", lambda e: e.tensor_tensor(AC0[:], MAB[:, :, 0, :], bcast(idb[:], [128, 8, 128], 1), ALU.add), rd=[bMAB, bidb], wr=[bAC0])
            Pp = lambda h: MAB[:, h, 0, :]
            PTp = lambda h: MABT[:, h, :]
            bPp, bPTp = bMAB, bMABT
            ACp, bACp = AC0, bAC0
            for lv in range(1, 7):
                Pn, bPn = Pk[lv % 2]; PTn, bPTn = PTk[lv % 2]; ACn, bACn = ACk[lv % 2]
                for grp in range(2):
                    hs = range(4 * grp, 4 * grp + 4)
                    if lv < 6:
                        p, bp = psB()
                        c.group("pe", [lambda e, h=h, p=p, Pp=Pp, PTp=PTp: e.matmul(p[:, (h % 4) * 128:(h % 4 + 1) * 128], PTp(h), Pp(h), start=True, stop=True) for h in hs],
                                rd=[bPp, bPTp], wr=[bp])
                        en = ev()
                        if en == "act":
                            c.op("act", lambda e, p=p: e.copy(Pn[:, 4 * grp:4 * grp + 4, :].rearrange("p a b -> p (a b)"), p[:]), rd=[bp], wr=[bPn])
                        else:
                            c.op("dve", lambda e, p=p: e.tensor_copy(Pn[:, 4 * grp:4 * grp + 4, :].rearrange("p a b -> p (a b)"), p[:]), rd=[bp], wr=[bPn])
                    p, bp = psB()
                    c.group("pe", [lambda e, h=h, p=p, Pp=Pp, PTp=PTp: e.matmul(p[:, (h % 4) * 128:(h % 4 + 1) * 128], Pp(h), PTp(h), start=True, stop=True) for h in hs],
                            rd=[bPp, bPTp], wr=[bp])
                    en = ev()
                    if en == "act":
                        c.op("act", lambda e, p=p: e.copy(PTn[:, 4 * grp:4 * grp + 4, :].rearrange("p a b -> p (a b)"), p[:]), rd=[bp], wr=[bPTn])
                    else:
                        c.op("dve", lambda e, p=p: e.tensor_copy(PTn[:, 4 * grp:4 * grp + 4, :].rearrange("p a b -> p (a b)"), p[:]), rd=[bp], wr=[bPTn])
                    p, bp = psB()
                    fs = []
                    for h in hs:
                        fs.append(lambda e, h=h, p=p, ACp=ACp: e.matmul(p[:, (h % 4) * 128:(h % 4 + 1) * 128], idb[:], ACp[:, h, :], start=True, stop=False))
                        fs.append(lambda e, h=h, p=p, ACp=ACp: e.matmul(p[:, (h % 4) * 128:(h % 4 + 1) * 128], PTn[:, h, :], ACp[:, h, :], start=False, stop=True))
                    c.group("pe", fs, rd=[bidb, bACp, bPTn], wr=[bp])
                    en = ev()
                    if en == "act":
                        c.op("act", lambda e, p=p: e.copy(ACn[:, 4 * grp:4 * grp + 4, :].rearrange("p a b -> p (a b)"), p[:]), rd=[bp], wr=[bACn])
                    else:
                        c.op("dve", lambda e, p=p: e.tensor_copy(ACn[:, 4 * grp:4 * grp + 4, :].rearrange("p a b -> p (a b)"), p[:]), rd=[bp], wr=[bACn])
                yield
                Pp = (lambda Pn: (lambda h: Pn[:, h, :]))(Pn)
                PTp = (lambda PTn: (lambda h: PTn[:, h, :]))(PTn)
                bPp, bPTp = bPn, bPTn
                ACp, bACp = ACn, bACn
            Minv, bMinv = ACp, bACp
            yield
            pX, bpX = psB()
            fs = []
            for h in range(8):
                q, r0 = h // 2, (h % 2) * 64
                fs.append(lambda e, h=h: e.matmul(pX[:, h * 64:(h + 1) * 64], MAK[:, h, 0, :], vb[:, h * 64:(h + 1) * 64], start=True, stop=False))
                fs.append(lambda e, h=h, q=q: e.matmul(pX[:, h * 64:(h + 1) * 64], AR[:, h, 0, :], Sb[:, q, :], start=False, stop=True))
            c.group("pe", fs, rd=[bMAK, bvb, bAR, bSb], wr=[bpX])
            c.op("act", lambda e: e.copy(XT[:], pX[:]), rd=[bpX], wr=[bXT])
            yield
            pU, bpU = psB()
            c.group("pe", [lambda e, h=h: e.matmul(pU[:, h * 64:(h + 1) * 64], Minv[:, h, :], XT[:, h * 64:(h + 1) * 64], start=True, stop=True) for h in range(8)],
                    rd=[bMinv, bXT], wr=[bpU])
            c.op("dve", lambda e: e.tensor_copy(UT[:], pU[:]), rd=[bpU], wr=[bUT])
            yield
            pY, bpY = psB()
            fs = []
            for h in range(8):
                q, r0 = h // 2, (h % 2) * 64
                fs.append(lambda e, h=h: e.matmul(pY[:, h * 64:(h + 1) * 64], MAK[:, h, 1, :], vb[:, h * 64:(h + 1) * 64], start=True, stop=False))
                fs.append(lambda e, h=h: e.matmul(pY[:, h * 64:(h + 1) * 64], MAB[:, h, 1, :], UT[:, h * 64:(h + 1) * 64], start=False, stop=False))
                fs.append(lambda e, h=h, q=q: e.matmul(pY[:, h * 64:(h + 1) * 64], AR[:, h, 1, :], Sb[:, q, :], start=False, stop=True))
            c.group("pe", fs, rd=[bMAK, bMAB, bvb, bUT, bAR, bSb], wr=[bpY])
            yield
            pS, bpS = psB()
            fs = []
            for q in range(4):
                fs.append(lambda e, q=q: e.matmul(pS[:, q * 128:(q + 1) * 128], BKhT[:, q * 128:(q + 1) * 128], UT[:, q * 128:(q + 1) * 128], start=True, stop=False))
                fs.append(lambda e, q=q: e.matmul(pS[:, q * 128:(q + 1) * 128], BKhT[:, 512 + q * 128:512 + (q + 1) * 128], vb[:, q * 128:(q + 1) * 128], start=False, stop=True))
            c.group("pe", fs, rd=[bBKhT, bUT, bvb], wr=[bpS])
            pSv = pS[:].rearrange("p (q c) -> p q c", q=4)
            c.op("dve", lambda e: e.tensor_tensor(tmpS[:], S32[:], gin[:, :, 127:128].to_broadcast([128, 4, 64]), ALU.mult), rd=[bS32, bgin], wr=[btmpS])
            c.op("dve", lambda e: e.tensor_tensor(S32[0:64, :, :], tmpS[0:64, :, :], pSv[0:64, :, 0:64], ALU.add), rd=[btmpS, bpS], wr=[bS32])
            c.op("dve", lambda e: e.tensor_tensor(S32[64:128, :, :], tmpS[64:128, :, :], pSv[64:128, :, 64:128], ALU.add), rd=[btmpS, bpS], wr=[bS32])
            c.op("dve", lambda e: e.tensor_copy(Sb[:], S32[:]), rd=[bS32], wr=[bSb])
            yield
            pYv = pY[:].rearrange("p (h d) -> p h d", h=8)
            c.op("dve", lambda e: e.tensor_reduce(s1[:], pYv, AX.X, ALU.add), rd=[bpY], wr=[bs1])
            c.op("act", lambda e: e.activation(sqt[:], pY[:], AF.Square), rd=[bpY], wr=[bsqt])
            c.op("dve", lambda e: e.tensor_reduce(s2[:], sqt[:].rearrange("p (h d) -> p h d", h=8), AX.X, ALU.add), rd=[bsqt], wr=[bs2])
            c.op("dve", lambda e: e.tensor_scalar(mean[:], s1[:], 1.0 / 64, None, ALU.mult), rd=[bs1], wr=[bmean])
            c.op("dve", lambda e: e.tensor_tensor(var[:], mean[:], mean[:], ALU.mult), rd=[bmean], wr=[bvar])
            c.op("dve", lambda e: e.scalar_tensor_tensor(var[:], s2[:], 1.0 / 64, var[:], ALU.mult, ALU.subtract), rd=[bs2, bvar], wr=[bvar])
            c.op("dve", lambda e: e.tensor_scalar(var[:], var[:], 64e-5, None, ALU.add), rd=[bvar], wr=[bvar])
            c.op("act", lambda e: e.activation(var[:], var[:], AF.Ln), rd=[bvar], wr=[bvar])
            c.op("act", lambda e: e.activation(var[:], var[:], AF.Exp, scale=-0.5), rd=[bvar], wr=[bvar])
            ynv = yn[:].rearrange("p (h d) -> p h d", h=8)
            c.op("dve", lambda e: e.tensor_tensor(ynv, pYv, bcast(mean[:], [128, 8, 64], 2), ALU.subtract), rd=[bpY, bmean], wr=[byn])
            c.op("pool", lambda e: e.tensor_tensor(ynv, ynv, bcast(var[:], [128, 8, 64], 2), ALU.mult), rd=[byn, bvar], wr=[byn])
            c.op("pool", lambda e: e.tensor_tensor(yn[:], yn[:], lng[:], ALU.mult), rd=[byn, blng], wr=[byn])
            c.op("dve", lambda e: e.tensor_tensor(yn[:], yn[:], lnb[:], ALU.add), rd=[byn, blnb], wr=[byn])
            c.op("pool", lambda e: e.tensor_tensor(bon[:].rearrange("p (h d) -> p h d", h=8), v32[:].rearrange("p (h d) -> p h d", h=8), bcast(coef[:], [128, 8, 64], 2), ALU.mult),
                 rd=[bv32, bcoef], wr=[bbon])
            c.op("dve", lambda e: e.tensor_tensor(yn[:], yn[:], bon[:], ALU.add), rd=[byn, bbon], wr=[byn])
            c.op("dve", lambda e: e.tensor_tensor(yab[:], yn[:], gT[:], ALU.mult), rd=[byn, bgT], wr=[byab])
            p, bp = psB(); pb = p[:].bitcast(BF16)
            c.group("pe", [lambda e, q=q: e.transpose(pb[:, q * 128:(q + 1) * 128], yab[:, q * 128:(q + 1) * 128], idb[:]) for q in range(4)], rd=[byab, bidb], wr=[bp])
            c.op("act", lambda e: e.copy(yaT[:].rearrange("p a b -> p (a b)"), pb[:, 0:512]), rd=[bp], wr=[byaT])
            c.dma("sp", yaFv[:, :, t0:t0 + 128], yaT[:], rd=[byaT])
        for _ in stageA(0):
            pass
        for ch in range(NCH):
            gB = stageB(ch); gA = stageA(ch + 1) if ch + 1 < NCH else iter(())
            aliveA = aliveB = True
            while aliveA or aliveB:
                if aliveB:
                    try:
                        next(gB)
                    except StopIteration:
                        aliveB = False
                if aliveA:
                    try:
                        next(gA)
                    except StopIteration:
                        aliveA = False
        c.barrier()

    if upto <= 2:
        print("ninst", c.ninst, "nwait", c.nwait); return nc
    with ExitStack() as es:
        def T(name, shape, dt=F32):
            return es.enter_context(nc.sbuf_tensor("s_" + name, list(shape), dt)), Buf(name)
        pst["n"] = 2; pst["i"] = 0; pst["off"] = 4
        pO = [PSB[6], PSB[7]]
        qst = {"i": 0}
        def psq():
            qst["i"] += 1
            return PSB[qst["i"] % 4]
        idb, bidb = T("idb3", [128, 128], BF16); c.dma("pool", idb[:], ident_d, wr=[bidb])
        CB, bCB = T("CB", [128, 2, 256], BF16); c.dma("pool", CB[:].rearrange("p a b -> p (a b)"), cbias2_d, wr=[bCB])
        s65, bs65 = T("s65", [65, 64], BF16); c.dma("pool", s65[:], sel65_d, wr=[bs65])
        pok, bpok = T("pok", [128, 16, 16]); c.dma("sp", pok[:].rearrange("p a b -> p (a b)"), pok_d.partition_broadcast(128), wr=[bpok])
        pbi, bpbi = T("pbi", [128, 16, 16]); c.dma("sp", pbi[:].rearrange("p a b -> p (a b)"), pbias_d.partition_broadcast(128), wr=[bpbi])
        invf, binvf = T("invf", [128, 8]); c.dma("sp", invf[:], invf_d.partition_broadcast(128), wr=[binvf])
        qkg, bqkg = T("qkg", [128, 1024]); c.dma("sp", qkg[:], qkg_d.partition_broadcast(128), wr=[bqkg])
        posi, bposi = T("posi", [128, NT], I32); c.dma("sp", posi[:], pos, wr=[bposi])
        posf, bposf = T("posf", [128, NT])
        c.op("dve", lambda e: e.tensor_copy(posf[:], posi[:]), rd=[bposi], wr=[bposf])
        yy, byy = T("yy", [128, NT, 8]); yi, byi = T("yi", [128, NT, 8], I32); yf, byf = T("yf", [128, NT, 8])
        sinT, bsinT = T("sinT", [128, NT, 8]); cosT, bcosT = T("cosT", [128, NT, 8])
        c.op("dve", lambda e: e.tensor_copy(yy[:], bcast(invf[:], [128, NT, 8], 1)), rd=[binvf], wr=[byy])
        c.op("dve", lambda e: e.tensor_tensor(yy[:], yy[:], bcast(posf[:], [128, NT, 8], 2), ALU.mult), rd=[byy, bposf], wr=[byy])
        for (dst, bdst, off) in ((sinT, bsinT, 0.0), (cosT, bcosT, 0.25)):
            if off != 0.0:
                c.op("dve", lambda e: e.tensor_scalar(yy[:], yy[:], off, None, ALU.add), rd=[byy], wr=[byy])
            c.op("dve", lambda e: e.tensor_copy(yi[:], yy[:]), rd=[byy], wr=[byi])
            c.op("dve", lambda e: e.tensor_copy(yf[:], yi[:]), rd=[byi], wr=[byf])
            c.op("dve", lambda e: e.tensor_tensor(yf[:], yy[:], yf[:], ALU.subtract), rd=[byy, byf], wr=[byf])
            c.op("act", lambda e, dst=dst: e.activation(dst[:], yf[:], AF.Sin, scale=2.0 * math.pi), rd=[byf], wr=[bdst])
        KT = es.enter_context(nc.sbuf_tensor("s_KT", [80, 8, S], BF16)); bKT = [Buf("KT%d" % g) for g in range(8)]
        VA = es.enter_context(nc.sbuf_tensor("s_VA", [128, NT, 8, 65], BF16)); bVA = [Buf("VA%d" % g) for g in range(8)]
        c.op("pool", lambda e: e.memset(VA[:], 1.0), wr=bVA)
        kmT, bkmT = T("kmT", [64, 8, 16], BF16); c.op("dve", lambda e: e.memset(kmT[:], 0.0), wr=[bkmT])
        km32, bkm32 = T("km32", [64, 8])
        qkvs = [T("qkv%d" % i, [128, 1536]) for i in range(2)]
        sq3, bsq3 = T("sq3", [128, 1024]); ss3, bss3 = T("ss3", [128, 16]); qn, bqn = T("qn", [128, 16, 64])
        rt, brt = T("rt", [128, 4, 16, 8])
        QA, bQA = T("QA", [128, 8, 80], BF16); KA, bKA = T("KA", [128, 8, 80], BF16)
        QT, bQT = T("QT", [64, 8, 128], BF16)
        QTAs = [T("QTA%d" % i, [80, 8, 512], BF16) for i in range(2)]
        gm, bgm = T("gm", [128, 8, 16]); mx, bmx = T("mx", [128, 8, 8]); sel, bsel = T("sel", [128, 8, 16])
        PTs = [T("PT%d" % i, [128, 512], BF16) for i in range(4)]
        OT, bOT = T("OT", [65, 512]); rr, brr = T("rr", [65, 512], BF16); r32, br32 = T("r32", [65, 512])
        c.op("dve", lambda e: e.memset(rr[:], 0.0), wr=[brr])
        yTs = [T("yT%d" % i, [64, 512], BF16) for i in range(2)]
        cnt3 = {"pt": 0, "ld": 0}

        def ld3(tt):
            c.dma("sp", qkvs[tt % 2][0][:], zT[tt * 128:(tt + 1) * 128, 512:2048], wr=[qkvs[tt % 2][1]])

        def pre3(tt):
            if tt + 1 < NT:
                ld3(tt + 1)
            qkv, bqkv = qkvs[tt % 2]
            t0 = tt * 128; qb = tt // 2; g = tt // 4; ti = tt % 4
            QTA, bQTA = QTAs[g % 2]
            c.op("act", lambda e: e.activation(sq3[:], qkv[:, 0:1024], AF.Square), rd=[bqkv], wr=[bsq3])
            c.op("dve", lambda e: e.tensor_reduce(ss3[:], sq3[:].rearrange("p (h d) -> p h d", h=16), AX.X, ALU.add), rd=[bsq3], wr=[bss3])
            c.op("dve", lambda e: e.tensor_scalar(ss3[:], ss3[:], 1.0 / 64, 1e-6, ALU.mult, ALU.add), rd=[bss3], wr=[bss3])
            c.op("act", lambda e: e.activation(ss3[:], ss3[:], AF.Ln), rd=[bss3], wr=[bss3])
            c.op("act", lambda e: e.activation(ss3[:], ss3[:], AF.Exp, scale=-0.5), rd=[bss3], wr=[bss3])
            c.op("dve", lambda e: e.tensor_tensor(qn[:], qkv[:, 0:1024].rearrange("p (h d) -> p h d", h=16), bcast(ss3[:], [128, 16, 64], 2), ALU.mult), rd=[bqkv, bss3], wr=[bqn])
            c.op("pool", lambda e: e.tensor_tensor(qn[:].rearrange("p h d -> p (h d)"), qn[:].rearrange("p h d -> p (h d)"), qkg[:], ALU.mult), rd=[bqn, bqkg], wr=[bqn])
            cs = cosT[:, tt:tt + 1, :].to_broadcast([128, 16, 8]); sn = sinT[:, tt:tt + 1, :].to_broadcast([128, 16, 8])
            c.op("dve", lambda e: e.tensor_tensor(rt[:, 0, :, :], qn[:, :, 0:8], cs, ALU.mult), rd=[bqn, bcosT], wr=[brt])
            c.op("dve", lambda e: e.tensor_tensor(rt[:, 1, :, :], qn[:, :, 8:16], sn, ALU.mult), rd=[bqn, bsinT], wr=[brt])
            c.op("pool", lambda e: e.tensor_tensor(rt[:, 2, :, :], qn[:, :, 8:16], cs, ALU.mult), rd=[bqn, bcosT], wr=[brt])
            c.op("pool", lambda e: e.tensor_tensor(rt[:, 3, :, :], qn[:, :, 0:8], sn, ALU.mult), rd=[bqn, bsinT], wr=[brt])
            c.op("dve", lambda e: e.tensor_tensor(qn[:, :, 0:8], rt[:, 0, :, :], rt[:, 1, :, :], ALU.subtract), rd=[brt], wr=[bqn])
            c.op("dve", lambda e: e.tensor_tensor(qn[:, :, 8:16], rt[:, 2, :, :], rt[:, 3, :, :], ALU.add), rd=[brt], wr=[bqn])
            c.op("dve", lambda e: e.tensor_copy(QA[:, :, 0:64], qn[:, 0:8, :]), rd=[bqn], wr=[bQA])
            c.op("pool", lambda e: e.tensor_copy(KA[:, :, 0:64], qn[:, 8:16, :]), rd=[bqn], wr=[bKA])
            c.op("pool", lambda e: e.memset(KA[:, :, 64:80], 0.0), wr=[bKA])
            c.op("pool", lambda e: e.memset(KA[:, :, 64 + qb:65 + qb], 1.0), wr=[bKA])
            p, bp = ps(); pb = p[:].bitcast(BF16)
            c.group("pe", [lambda e, h=h: e.transpose(pb[0:80, h * 128:(h + 1) * 128], KA[:, h, :], idb[:]) for h in range(8)], rd=[bKA, bidb], wr=[bp])
            c.op("dve", lambda e: e.tensor_copy(KT[:, :, t0:t0 + 128], pb[0:80, :].rearrange("p (h t) -> p h t", h=8)), rd=[bp], wr=[bKT[g]])
            c.op("pool", lambda e: e.tensor_copy(VA[:, tt, :, 0:64], qkv[:, 1024:1536].rearrange("p (h d) -> p h d", h=8)), rd=[bqkv], wr=[bVA[g]])
            if qb > 0:
                p, bp = ps(); pb = p[:].bitcast(BF16)
                c.group("pe", [lambda e, h=h: e.transpose(pb[0:64, h * 128:(h + 1) * 128], QA[:, h, 0:64], idb[:]) for h in range(8)], rd=[bQA, bidb], wr=[bp])
                c.op("dve", lambda e: e.tensor_copy(QT[:], pb[0:64, :].rearrange("p (h t) -> p h t", h=8)), rd=[bp], wr=[bQT])
                p, bp = ps()
                c.group("pe", [lambda e, h=h, p=p: e.matmul(p[:, h * 16:(h + 1) * 16], QT[:, h, :], kmT[:, h, :], start=True, stop=True) for h in range(8)],
                        rd=[bQT, bkmT], wr=[bp])
                c.op("dve", lambda e, p=p: e.tensor_tensor(gm[:], p[:, 0:128].rearrange("p (h j) -> p h j", h=8), pbi[:, qb:qb + 1, :].to_broadcast([128, 8, 16]), ALU.add),
                     rd=[bp, bpbi], wr=[bgm])
                for h in range(8):
                    c.op("dve", lambda e, h=h: e.max(mx[:, h, :], gm[:, h, :]), rd=[bgm], wr=[bmx])
                c.op("dve", lambda e: e.tensor_tensor(sel[:], gm[:], mx[:, :, 2:3].to_broadcast([128, 8, 16]), ALU.is_ge), rd=[bgm, bmx], wr=[bsel])
                c.op("dve", lambda e: e.tensor_tensor(sel[:], sel[:], pok[:, qb:qb + 1, :].to_broadcast([128, 8, 16]), ALU.mult), rd=[bsel, bpok], wr=[bsel])
                c.op("dve", lambda e: e.tensor_scalar(QA[:, :, 64:80], sel[:], BIG, -BIG, ALU.mult, ALU.add), rd=[bsel], wr=[bQA])
            else:
                c.op("dve", lambda e: e.memset(QA[:, :, 64:80], -BIG), wr=[bQA])
            p, bp = ps(); pb = p[:].bitcast(BF16)
            c.group("pe", [lambda e, h=h: e.transpose(pb[0:80, h * 128:(h + 1) * 128], QA[:, h, :], idb[:]) for h in range(8)], rd=[bQA, bidb], wr=[bp])
            c.op("dve", lambda e: e.tensor_copy(QTA[:, :, ti * 128:(ti + 1) * 128], pb[0:80, :].rearrange("p (h t) -> p h t", h=8)), rd=[bp], wr=[bQTA])
            if tt % 2 == 1:
                c.op("dve", lambda e: e.tensor_reduce(km32[:], KT[0:64, :, qb * 256:(qb + 1) * 256], AX.X, ALU.add), rd=[bKT[g]], wr=[bkm32])
                c.op("dve", lambda e: e.tensor_scalar(kmT[:, :, qb], km32[:], 1.0 / 256, None, ALU.mult), rd=[bkm32], wr=[bkmT])

        s65f, bs65f = T("s65f", [65, 64]); c.dma("sp", s65f[:], sel65_d, wr=[bs65f])

        def steps3(g, h):
            QTA, bQTA = QTAs[g % 2]
            pOh, bOh = pO[h % 2]
            kbufs = bKT[0:g + 1]; vbufs = bVA[0:g + 1]
            st = []
            nk = 4 * g + 2
            for kt in range(nk):
                d = {}
                def qk(d=d, kt=kt):
                    d["p"], d["bp"] = psq()
                    c.op("pe", lambda e: e.matmul(d["p"][:], KT[:, h, kt * 128:(kt + 1) * 128], QTA[:, h, :], start=True, stop=True), rd=kbufs + [bQTA], wr=[d["bp"]])
                def ex(d=d):
                    d["PT"], d["bPT"] = PTs[cnt3["pt"] % 4]; cnt3["pt"] += 1
                    c.op("act", lambda e: e.activation(d["PT"][:], d["p"][:], AF.Exp, scale=0.125), rd=[d["bp"]], wr=[d["bPT"]])
                def pv(d=d, kt=kt):
                    c.op("pe", lambda e: e.matmul(pOh[0:65, :], VA[:, kt, h, :], d["PT"][:], start=(kt == 0), stop=False), rd=vbufs + [d["bPT"]], wr=[bOh])
                st.append((qk, ex, pv))
            for half in range(2):
                for kti in range(2):
                    kt = 4 * g + 2 * half + kti
                    qs = slice(half * 256, (half + 1) * 256)
                    last = (half == 1 and kti == 1)
                    d = {}
                    def qk(d=d, kt=kt, qs=qs, kti=kti):
                        d["p"], d["bp"] = psq()
                        c.group("pe", [lambda e: e.matmul(d["p"][:, 0:256], KT[0:64, h, kt * 128:(kt + 1) * 128], QTA[0:64, h, qs], start=True, stop=False),
                                       lambda e: e.matmul(d["p"][:, 0:256], idb[:], CB[:, kti, :], start=False, stop=True)], rd=kbufs + [bQTA, bidb, bCB], wr=[d["bp"]])
                    def ex(d=d):
                        d["PT"], d["bPT"] = PTs[cnt3["pt"] % 4]; cnt3["pt"] += 1
                        c.op("act", lambda e: e.activation(d["PT"][:, 0:256], d["p"][:, 0:256], AF.Exp, scale=0.125), rd=[d["bp"]], wr=[d["bPT"]])
                    def pv(d=d, kt=kt, qs=qs, last=last):
                        c.op("pe", lambda e: e.matmul(pOh[0:65, qs], VA[:, kt, h, :], d["PT"][:, 0:256], start=False, stop=last), rd=vbufs + [d["bPT"]], wr=[bOh])
                    st.append((qk, ex, pv))

            def fin():
                c.op("dve", lambda e: e.tensor_copy(OT[:], pOh[0:65, :]), rd=[bOh], wr=[bOT])
                p, bp = ps()
                c.op("pe", lambda e: e.matmul(p[0:64, :], s65f[:], OT[:], start=True, stop=True), rd=[bs65f, bOT], wr=[bp])
                yT, byT = yTs[h % 2]
                c.op("dve", lambda e: e.reciprocal(r32[0:64, :], p[0:64, :]), rd=[bp], wr=[br32])
                c.op("dve", lambda e: e.tensor_tensor(yT[:], OT[0:64, :], r32[0:64, :], ALU.mult), rd=[bOT, br32], wr=[byT])
                c.dma("sp", ybF[h * 64:(h + 1) * 64, g * 512:(g + 1) * 512], yT[:], rd=[byT])
            return st, fin

        ld3(0)
        for tt in range(4):
            pre3(tt)
        LOOK = 2
        for g in range(8):
            allst = []
            for h in range(8):
                st, fin = steps3(g, h)
                for i, s in enumerate(st):
                    allst.append((s, fin if i == len(st) - 1 else None, h))
            n = len(allst)
            for i in range(min(LOOK, n)):
                allst[i][0][0]()
            for i in range(n):
                (qk, ex, pv), fin, h = allst[i]
                ex()
                if i + LOOK < n:
                    allst[i + LOOK][0][0]()
                pv()
                if fin is not None:
                    fin()
                    if g + 1 < 8 and h % 2 == 1:
                        pre3(4 * (g + 1) + h // 2)
        pst["n"] = 8; pst["off"] = 0
        c.barrier()

    if upto <= 3:
        print("ninst", c.ninst, "nwait", c.nwait); return nc
    with ExitStack() as es:
        def T(name, shape, dt=F32):
            return es.enter_context(nc.sbuf_tensor("s_" + name, list(shape), dt)), Buf(name)
        idb, bidb = T("idb4", [128, 128], BF16); c.dma("pool", idb[:], ident_d, wr=[bidb])
        wba, bwba = T("wba", [128, 4, DM], BF16); wbb, bwbb = T("wbb", [128, 4, DM], BF16); wo, bwo = T("wo", [128, 8, DM], BF16)
        for k in range(4):
            c.dma("pool", wba[:, k, :], wba_d[k * 128:(k + 1) * 128, :], wr=[bwba])
            c.dma("pool", wbb[:, k, :], wbb_d[k * 128:(k + 1) * 128, :], wr=[bwbb])
        for k in range(8):
            c.dma("pool", wo[:, k, :], wout_d[k * 128:(k + 1) * 128, :], wr=[bwo])
        xts = [T("x4%d" % i, [128, DM]) for i in range(2)]
        gps = [T("gp%d" % i, [128, 2048]) for i in range(2)]
        yas = [T("ya4%d" % i, [128, 4, 128], BF16) for i in range(2)]
        ybs = [T("yb4%d" % i, [128, 4, 128], BF16) for i in range(2)]
        m1, bm1 = T("m1", [128, DM]); m2, bm2 = T("m2", [128, DM]); mb, bmb = T("mb", [128, DM], BF16)
        mT, bmT = T("mT", [128, 8, 128], BF16)
        x1, bx1 = T("x1", [128, DM]); junk, bjunk = T("junk4", [128, DM]); h2, bh2 = T("h2", [128, DM], BF16)
        h2T, bh2T = T("h2T", [128, 8, 128], BF16)
        ss, bss = T("ss4", [128, NT]); c.op("dve", lambda e: e.memset(ss[:], 0.0), wr=[bss])
        rs, brs = T("rs4", [128, NT])
        yaFv = yaF.rearrange("(c p) t -> p c t", p=128); ybFv = ybF.rearrange("(c p) t -> p c t", p=128)
        h2Fv = h2F.rearrange("(c p) t -> p c t", p=128)

        def loads4(tt):
            i = tt % 2; t0 = tt * 128
            c.dma("sp", xts[i][0][:], x[t0:t0 + 128, :], wr=[xts[i][1]])
            c.dma("sp", gps[i][0][:], zT[t0:t0 + 128, 2048:4096], wr=[gps[i][1]])
            c.dma("sp", yas[i][0][:], yaFv[:, :, t0:t0 + 128], wr=[yas[i][1]])
            c.dma("sp", ybs[i][0][:], ybFv[:, :, t0:t0 + 128], wr=[ybs[i][1]])
        loads4(0)
        for tt in range(NT):
            if tt + 1 < NT:
                loads4(tt + 1)
            i = tt % 2; t0 = tt * 128
            xt, bxt = xts[i]; gp, bgp = gps[i]; ya, bya = yas[i]; yb, byb = ybs[i]
            c.op("act", lambda e: e.activation(gp[:], gp[:], AF.Sigmoid), rd=[bgp], wr=[bgp])
            for half in range(2):
                hs = slice(half * 512, (half + 1) * 512)
                pa, bpa = ps()
                c.group("pe", [lambda e, k=k: e.matmul(pa[:], ya[:, k, :], wba[:, k, hs], start=(k == 0), stop=(k == 3)) for k in range(4)], rd=[bya, bwba], wr=[bpa])
                c.op("dve", lambda e: e.tensor_tensor(m1[:, hs], pa[:], gp[:, half * 512:(half + 1) * 512], ALU.mult), rd=[bpa, bgp], wr=[bm1])
                pb_, bpb_ = ps()
                c.group("pe", [lambda e, k=k: e.matmul(pb_[:], yb[:, k, :], wbb[:, k, hs], start=(k == 0), stop=(k == 3)) for k in range(4)], rd=[byb, bwbb], wr=[bpb_])
                c.op("dve", lambda e: e.tensor_tensor(m2[:, hs], pb_[:], gp[:, 1024 + half * 512:1024 + (half + 1) * 512], ALU.mult), rd=[bpb_, bgp], wr=[bm2])
            c.op("pool", lambda e: e.tensor_tensor(mb[:], m1[:], m2[:], ALU.add), rd=[bm1, bm2], wr=[bmb])
            p, bp = ps(); pb = p[:].bitcast(BF16)
            c.group("pe", [lambda e, k=k: e.transpose(pb[:, k * 128:(k + 1) * 128], mb[:, k * 128:(k + 1) * 128], idb[:]) for k in range(8)], rd=[bmb, bidb], wr=[bp])
            c.op("act", lambda e: e.copy(mT[:].rearrange("p a b -> p (a b)"), pb), rd=[bp], wr=[bmT])
            for half in range(2):
                hs = slice(half * 512, (half + 1) * 512)
                po, bpo = ps()
                c.group("pe", [lambda e, k=k: e.matmul(po[:], mT[:, k, :], wo[:, k, hs], start=(k == 0), stop=(k == 7)) for k in range(8)], rd=[bmT, bwo], wr=[bpo])
                c.op("dve", lambda e: e.tensor_tensor(x1[:, hs], po[:], xt[:, hs], ALU.add), rd=[bpo, bxt], wr=[bx1])
            c.dma("sp", x1s[t0:t0 + 128, :], x1[:], rd=[bx1])
            c.op("act", lambda e: e.activation(junk[:], x1[:], AF.Square, accum_out=ss[:, tt:tt + 1]), rd=[bx1], wr=[bjunk, bss])
            c.op("dve", lambda e: e.tensor_scalar(rs[:, tt:tt + 1], ss[:, tt:tt + 1], 1.0 / DM, 1e-6, ALU.mult, ALU.add), rd=[bss], wr=[brs])
            c.op("act", lambda e: e.activation(rs[:, tt:tt + 1], rs[:, tt:tt + 1], AF.Ln), rd=[brs], wr=[brs])
            c.op("act", lambda e: e.activation(rs[:, tt:tt + 1], rs[:, tt:tt + 1], AF.Exp, scale=-0.5), rd=[brs], wr=[brs])
            c.op("dve", lambda e: e.tensor_scalar(h2[:], x1[:], rs[:, tt:tt + 1], None, ALU.mult), rd=[bx1, brs], wr=[bh2])
            p, bp = ps(); pb = p[:].bitcast(BF16)
            c.group("pe", [lambda e, k=k: e.transpose(pb[:, k * 128:(k + 1) * 128], h2[:, k * 128:(k + 1) * 128], idb[:]) for k in range(8)], rd=[bh2, bidb], wr=[bp])
            c.op("act", lambda e: e.copy(h2T[:].rearrange("p a b -> p (a b)"), pb), rd=[bp], wr=[bh2T])
            c.dma("sp", h2Fv[:, :, t0:t0 + 128], h2T[:], rd=[bh2T])
        c.barrier()

    if upto <= 4:
        print("ninst", c.ninst, "nwait", c.nwait); return nc
    with ExitStack() as es:
        def T(name, shape, dt=F32):
            return es.enter_context(nc.sbuf_tensor("s_" + name, list(shape), dt)), Buf(name)
        wup, bwup = T("wup", [128, 8, 2 * DFF], BF16); wdn, bwdn = T("wdn", [128, NFF, DM], BF16)
        g2, bg2 = T("g2", [128, 8]); c.dma("sp", g2[:], n2g, wr=[bg2])
        wub = [Buf("wup_%d" % k) for k in range(8)]
        for k in range(8):
            c.dma("pool", wup[:, k, :], wup_d[k * 128:(k + 1) * 128, :], wr=[wub[k]])
            c.op("dve", lambda e, k=k: e.tensor_scalar(wup[:, k, :], wup[:, k, :], g2[:, k:k + 1], None, ALU.mult), rd=[wub[k], bg2], wr=[wub[k]])
        for f in range(NFF):
            c.dma("pool", wdn[:, f, :], wdn_d[f * 128:(f + 1) * 128, :], wr=[bwdn])
        cw, bcw = T("cw", [128, NFF, 3]); c.dma("sp", cw[:].rearrange("p a b -> p (a b)"), cw_d, wr=[bcw])
        cbt, bcbt = T("cbt", [128, NFF]); c.dma("sp", cbt[:], cb_d, wr=[bcbt])
        cr, bcr = T("cr", [128, NFF, 2]); c.op("dve", lambda e: e.memset(cr[:], 0.0), wr=[bcr])
        h2s = [T("h2s%d" % i, [128, 8, 512], BF16) for i in range(2)]
        asb = [T("asb%d" % i, [128, 514]) for i in range(2)]
        acc = [T("acc%d" % i, [128, 512]) for i in range(2)]
        hg = es.enter_context(nc.sbuf_tensor("hg", [128, NFF, 512], BF16)); bhg = [Buf("hg%d" % f) for f in range(NFF)]
        x1t = [T("x1t%d" % i, [128, DM]) for i in range(2)]
        ost = [T("ost%d" % i, [128, DM]) for i in range(2)]
        h2Fv = h2F.rearrange("(c p) t -> p c t", p=128)
        c.dma("sp", h2s[0][0][:], h2Fv[:, :, 0:512], wr=[h2s[0][1]])
        nx = 0
        for st in range(8):
            if st + 1 < 8:
                c.dma("sp", h2s[(st + 1) % 2][0][:], h2Fv[:, :, (st + 1) * 512:(st + 2) * 512], wr=[h2s[(st + 1) % 2][1]])
            hT, bhT = h2s[st % 2]
            for f in range(NFF):
                pa, bpa = ps()
                c.group("pe", [lambda e, k=k: e.matmul(pa[:], wup[:, k, f * 128:(f + 1) * 128], hT[:, k, :], start=(k == 0), stop=(k == 7)) for k in range(8)], rd=[bhT] + wub, wr=[bpa])
                pg, bpg = ps()
                c.group("pe", [lambda e, k=k: e.matmul(pg[:], wup[:, k, DFF + f * 128:DFF + (f + 1) * 128], hT[:, k, :], start=(k == 0), stop=(k == 7)) for k in range(8)], rd=[bhT] + wub, wr=[bpg])
                a, ba = asb[f % 2]; ac, bac = acc[f % 2]
                c.op("act", lambda e: e.copy(a[:, 0:2], cr[:, f, :]), rd=[bcr], wr=[ba])
                c.op("act", lambda e: e.copy(a[:, 2:514], pa[:]), rd=[bpa], wr=[ba])
                c.op("act", lambda e: e.copy(cr[:, f, :], a[:, 512:514]), rd=[ba], wr=[bcr])
                c.op("dve", lambda e: e.tensor_scalar(ac[:], a[:, 0:512], cw[:, f, 0:1], None, ALU.mult), rd=[ba, bcw], wr=[bac])
                c.op("dve", lambda e: e.scalar_tensor_tensor(ac[:], a[:, 1:513], cw[:, f, 1:2], ac[:], ALU.mult, ALU.add), rd=[ba, bcw, bac], wr=[bac])
                c.op("dve", lambda e: e.scalar_tensor_tensor(ac[:], a[:, 2:514], cw[:, f, 2:3], ac[:], ALU.mult, ALU.add), rd=[ba, bcw, bac], wr=[bac])
                c.op("act", lambda e: e.activation(ac[:], ac[:], AF.Gelu, bias=cbt[:, f:f + 1]), rd=[bac, bcbt], wr=[bac])
                c.op("dve", lambda e: e.tensor_tensor(hg[:, f, :], ac[:], pg[:], ALU.mult), rd=[bac, bpg], wr=[bhg[f]])
            for sub in range(4):
                tt = st * 4 + sub; t0 = tt * 128
                xx, bxx = x1t[nx % 2]; oo, boo = ost[nx % 2]; nx += 1
                c.dma("sp", xx[:], x1s[t0:t0 + 128, :], wr=[bxx])
                for half in range(2):
                    hs = slice(half * 512, (half + 1) * 512)
                    po, bpo = ps()
                    c.group("pe", [lambda e, f=f: e.matmul(po[:], hg[:, f, sub * 128:(sub + 1) * 128], wdn[:, f, hs], start=(f == 0), stop=(f == NFF - 1)) for f in range(NFF)],
                            rd=bhg + [bwdn], wr=[bpo])
                    c.op("dve", lambda e: e.tensor_tensor(oo[:, hs], po[:], xx[:, hs], ALU.add), rd=[bpo, bxx], wr=[boo])
                c.dma("sp", out[t0:t0 + 128, :], oo[:], rd=[boo])
        c.barrier()
    print("ninst", c.ninst, "nwait", c.nwait, {e: c.cnt[e] for e in c.cnt})
    return nc


def _consts():
    i = np.arange(128)
    su = (i[:, None] < i[None, :]).astype(np.float32)
    ui = (i[:, None] <= i[None, :]).astype(np.float32)
    mu2 = np.concatenate([su, ui, su, ui], axis=1)
    sl = (i[None, :] < i[:, None]).astype(np.float32)
    sl2 = np.concatenate([sl, sl], axis=1)
    bones = ((i[:, None] // 64) == (i[None, :] // 64)).astype(np.float32)
    q2 = np.arange(256)
    cb0 = np.where(i[:, None] > q2[None, :], -BIG, 0.0).astype(np.float32)
    cb1 = np.where(i[:, None] + 128 > q2[None, :], -BIG, 0.0).astype(np.float32)
    cbias2 = np.concatenate([cb0, cb1], axis=1)
    sel65 = np.zeros((65, 64), np.float32); sel65[64, :] = 1.0
    j = np.arange(16)
    pok = (j[None, :] < j[:, None]).astype(np.float32)
    pbias = ((pok - 1.0) * 1e30).astype(np.float32)
    half = 8
    invf = (500000.0 ** (-np.arange(half, dtype=np.float32) / half)).astype(np.float32) / np.float32(2.0 * math.pi)
    return dict(ident=np.eye(128, dtype=np.float32), mu2=mu2, sl2=sl2, bones=bones, cbias2=cbias2, sel65=sel65,
                pok=pok.reshape(1, 256), pbias=pbias.reshape(1, 256), invf=invf.reshape(1, 8).astype(np.float32))


def _fm(v, n):
    return np.ascontiguousarray(np.asarray(v, np.float32).reshape(n, 128).T)


def _prep(inp):
    f = lambda a: np.ascontiguousarray(np.asarray(a, dtype=np.float32))
    w_in = f(inp["w_in"][0])
    fm_cols = np.r_[0:512, 512:1024, 1536:1600, 1600:1664, 1664:1824]
    tm_cols = np.r_[1024:1536, 1824:3360, 3360:5408]
    mu = f(inp["rwkv_mu"][0])
    mu_fm = np.zeros(1408, np.float32); mu_fm[:FMC] = mu[fm_cols]
    rk = f(inp["rwkv_r_k"][0])
    rkb = np.zeros((128, 8), np.float32)
    for h in range(8):
        rkb[(h % 2) * 64:(h % 2 + 1) * 64, h] = rk[h]
    cw = f(inp["ffn_conv_w"][0])
    cwl = np.ascontiguousarray(cw.reshape(3, NFF, 128).transpose(2, 1, 0)).reshape(128, NFF * 3)
    shared = dict(
        w_in=np.ascontiguousarray(w_in[:, np.r_[fm_cols, tm_cols]]),
        n1g=_fm(inp["norm1_g"][0], 8), mu_fm=_fm(mu_fm, 11), mu_v=f(mu[1024:1536]).reshape(1, 512),
        wdec=np.concatenate([f(inp["w_decay_up"][0]), f(inp["decay_bias"][0]).reshape(1, 512)], 0),
        waaa=np.concatenate([f(inp["w_aaa_up"][0]), f(inp["aaa_bias"][0]).reshape(1, 512)], 0),
        wgate=f(inp["w_gate_up"][0]), kk_fm=_fm(inp["rwkv_k_k"][0], 4), ka_fm=_fm(inp["rwkv_k_a"][0], 4), rkb=rkb,
        lng=f(inp["rwkv_ln_g"][0]).reshape(1, 512), lnb=f(inp["rwkv_ln_b"][0]).reshape(1, 512),
        qkg=np.concatenate([np.tile(f(inp["q_norm_g"][0]), 8), np.tile(f(inp["k_norm_g"][0]), 8)]).reshape(1, 1024),
        wba=f(inp["w_branch_a"][0]), wbb=f(inp["w_branch_b"][0]), wout=f(inp["w_out"][0]), n2g=_fm(inp["norm2_g"][0], 8),
        wup=f(inp["w_ffn_up"][0]), cw=cwl, cb=_fm(inp["ffn_conv_b"][0], NFF), wdn=f(inp["w_ffn_down"][0]),
    )
    shared.update(_consts())
    xs = np.asarray(inp["x"], np.float32); ps_ = np.asarray(inp["positions"], np.int32)
    maps = []
    for b in range(8):
        m = dict(shared)
        m["x"] = np.ascontiguousarray(xs[b])
        m["pos"] = np.ascontiguousarray(ps_[b].reshape(NT, 128).T)
        maps.append(m)
    return maps


def kernel(**inputs):
    maps = _prep(inputs)
    nc = build()
    res = run_bass_kernel_spmd(nc, maps, core_ids=list(range(8)))
    return np.stack([np.asarray(r["out"], np.float32) for r in res.results], axis=0)
```

```python
import numpy as np
import concourse.bass as bass
import concourse.mybir as mybir
from concourse.bass_utils import run_bass_kernel_spmd

F32 = mybir.dt.float32
BF16 = mybir.dt.bfloat16
I32 = mybir.dt.int32
ALU = mybir.AluOpType
AF = mybir.ActivationFunctionType
AX = mybir.AxisListType


class Buf:
    __slots__ = ("name", "lastw", "readers")

    def __init__(self, name):
        self.name = name
        self.lastw = None
        self.readers = {}


class Ctx:
    def __init__(self, nc, n_dma_sems=24):
        self.nc = nc
        self.eng = {"pe": nc.tensor, "act": nc.scalar, "dve": nc.vector,
                    "pool": nc.gpsimd, "sp": nc.sync}
        self.sem = {}
        self.cnt = {}
        self.waited = {e: {} for e in self.eng}
        self._stack = []
        for e in ("pe", "act", "dve", "pool"):
            cm = nc.semaphore("s_" + e)
            self.sem[e] = cm.__enter__()
            self._stack.append(cm)
            self.cnt[e] = 0
        self.dsem = []
        self.dpool = {"hw": [], "sw": []}
        for i in range(n_dma_sems):
            cm = nc.semaphore("d%d" % i)
            self.dsem.append([cm.__enter__(), 0])
            self._stack.append(cm)
            self.dpool["sw" if i < 8 else "hw"].append(i)
        self.dnext = {"hw": 0, "sw": 0}
        self.semh = {}
        for e in self.sem:
            self.semh[("e", e)] = self.sem[e]
        for i, (h, _) in enumerate(self.dsem):
            self.semh[("d", i)] = h
        self.nwait = 0
        self.ninst = 0

    def _wait(self, e, toks):
        w = self.waited[e]
        best = {}
        for t in toks:
            if t is None:
                continue
            k, v = t[0], t[1]
            if w.get(k, 0) >= v:
                continue
            if best.get(k, 0) < v:
                best[k] = v
        for k, v in best.items():
            self.eng[e].wait_ge(self.semh[k], v)
            w[k] = v
            self.nwait += 1

    def _deps(self, e, rd, wr):
        toks = []
        me = ("e", e)
        for b in rd:
            if b.lastw is not None:
                if not (e == "pe" and b.lastw[0] == me):
                    toks.append(b.lastw)
        for b in wr:
            if b.lastw is not None and not (e == "pe" and b.lastw[0] == me):
                toks.append(b.lastw)
            for k, t in b.readers.items():
                if not (e == "pe" and k == me):
                    toks.append(t)
        return toks

    def _mark(self, tok, rd, wr):
        for b in rd:
            b.readers[tok[0]] = tok
        for b in wr:
            b.lastw = tok
            b.readers = {}

    def op(self, e, fn, rd=(), wr=()):
        self._wait(e, self._deps(e, rd, wr))
        ins = fn(self.eng[e])
        self.cnt[e] += 1
        ins.then_inc(self.sem[e], 1)
        tok = (("e", e), self.cnt[e])
        self._mark(tok, rd, wr)
        self.ninst += 1
        return tok

    def group(self, e, fns, rd=(), wr=()):
        self._wait(e, self._deps(e, rd, wr))
        ins = None
        for fn in fns:
            ins = fn(self.eng[e])
            self.ninst += 1
        self.cnt[e] += 1
        ins.then_inc(self.sem[e], 1)
        tok = (("e", e), self.cnt[e])
        self._mark(tok, rd, wr)
        return tok

    def dma(self, q, out, in_, rd=(), wr=(), **kw):
        kind = "sw" if q == "pool" else "hw"
        pool = self.dpool[kind]
        i = pool[self.dnext[kind] % len(pool)]
        self.dnext[kind] += 1
        h, c = self.dsem[i]
        k = ("d", i)
        toks = self._deps(q, rd, wr)
        if c > 0:
            toks.append((k, 16 * c))
        self._wait(q, toks)
        self.eng[q].dma_start(out=out, in_=in_, **kw).then_inc(h, 16)
        self.dsem[i][1] = c + 1
        tok = (k, 16 * (c + 1))
        self._mark(tok, rd, wr)
        self.ninst += 1
        return tok

    def wait_all(self, e, bufs):
        toks = []
        for b in bufs:
            toks.append(b.lastw)
            toks.extend(b.readers.values())
        self._wait(e, toks)

    def barrier(self, bufs=()):
        toks = []
        for e in self.sem:
            if self.cnt[e] > 0:
                toks.append((("e", e), self.cnt[e]))
        for i, (h, c) in enumerate(self.dsem):
            if c > 0:
                toks.append((("d", i), 16 * c))
        for e in self.eng:
            self._wait(e, toks)

from contextlib import ExitStack
import math
import os

S = 4096
DM = 1024
NT = 32
FMC = 1312
TMC = 4096
DFF = 2816
NFF = 22
BIG = 30000.0


def build(debug=False, upto=99):
    nc = bass.Bass("TRN2", target_bir_lowering=False)
    okind = "ExternalOutput" if debug else "Internal"

    def DIN(name, shape, dt=F32):
        return nc.dram_tensor(name, list(shape), dt, kind="ExternalInput").ap()

    x = DIN("x", [S, DM]); pos = DIN("pos", [128, NT], I32)
    w_in = DIN("w_in", [DM, 5408]); n1g = DIN("n1g", [128, 8]); mu_fm = DIN("mu_fm", [128, 11]); mu_v = DIN("mu_v", [1, 512])
    wdec_d = DIN("wdec", [65, 512]); waaa_d = DIN("waaa", [65, 512]); wgate_d = DIN("wgate", [160, 512])
    kk_d = DIN("kk_fm", [128, 4]); ka_d = DIN("ka_fm", [128, 4]); rkb_d = DIN("rkb", [128, 8])
    lng_d = DIN("lng", [1, 512]); lnb_d = DIN("lnb", [1, 512]); qkg_d = DIN("qkg", [1, 1024])
    wba_d = DIN("wba", [512, DM]); wbb_d = DIN("wbb", [512, DM]); wout_d = DIN("wout", [DM, DM]); n2g = DIN("n2g", [128, 8])
    wup_d = DIN("wup", [DM, 2 * DFF]); cw_d = DIN("cw", [128, NFF * 3]); cb_d = DIN("cb", [128, NFF]); wdn_d = DIN("wdn", [DFF, DM])
    ident_d = DIN("ident", [128, 128]); mu2_d = DIN("mu2", [128, 512]); sl2_d = DIN("sl2", [128, 256]); bones_d = DIN("bones", [128, 128])
    cbias2_d = DIN("cbias2", [128, 512]); sel65_d = DIN("sel65", [65, 64]); pok_d = DIN("pok", [1, 256]); pbias_d = DIN("pbias", [1, 256]); invf_d = DIN("invf", [1, 8])
    out = nc.dram_tensor("out", [S, DM], F32, kind="ExternalOutput").ap()
    zF = nc.dram_tensor("zF", [1408, S], F32, kind=okind).ap()
    zT = nc.dram_tensor("zT", [S, TMC], F32, kind=okind).ap()
    yaF = nc.dram_tensor("yaF", [512, S], BF16, kind=okind).ap()
    ybF = nc.dram_tensor("ybF", [512, S], BF16, kind=okind).ap()
    x1s = nc.dram_tensor("x1s", [S, DM], F32, kind=okind).ap()
    h2F = nc.dram_tensor("h2F", [DM, S], BF16, kind=okind).ap()

    c = Ctx(nc)
    PSB = [(nc.alloc_psum_tensor("psb%d" % i, [128, 512], F32), Buf("psb%d" % i)) for i in range(8)]
    pst = {"i": 0, "n": 8, "off": 0}

    def ps():
        i = pst["off"] + pst["i"] % pst["n"]
        pst["i"] += 1
        return PSB[i]

    rr = {"i": 0}

    def ev():
        rr["i"] += 1
        return "act" if rr["i"] % 2 else "dve"

    def bcast(ap, shape, axis):
        return ap.unsqueeze(axis).to_broadcast(list(shape))

    with ExitStack() as es:
        def T(name, shape, dt=F32):
            return es.enter_context(nc.sbuf_tensor("s_" + name, list(shape), dt)), Buf(name)
        w, bw = T("w1", [128, 8, 5408], BF16)
        g1, bg1 = T("g1", [128, 8]); muf, bmuf = T("muf", [128, 11])
        idb, bidb = T("idb1", [128, 128], BF16)
        c.dma("sp", g1[:], n1g, wr=[bg1]); c.dma("sp", muf[:], mu_fm, wr=[bmuf])
        c.dma("pool", idb[:], ident_d, wr=[bidb])
        wb = [Buf("w1_%d" % k) for k in range(8)]
        for kc in range(8):
            c.dma("pool", w[:, kc, :], w_in[kc * 128:(kc + 1) * 128, :], wr=[wb[kc]])
            c.op("dve", lambda e, kc=kc: e.tensor_scalar(w[:, kc, :], w[:, kc, :], g1[:, kc:kc + 1], None, ALU.mult),
                 rd=[wb[kc], bg1], wr=[wb[kc]])
        ss, bss = T("ss1", [128, NT]); c.op("dve", lambda e: e.memset(ss[:], 0.0), wr=[bss])
        rs, brs = T("rs1", [128, NT])
        junk, bjunk = T("junk1", [128, DM])
        xts = [T("xt%d" % i, [128, DM]) for i in range(2)]
        hTs = [es.enter_context(nc.sbuf_tensor("hT%d" % i, [128, 8, 512], BF16)) for i in range(2)]
        hTb = [[Buf("hT%d_%d" % (i, s)) for s in range(4)] for i in range(2)]
        stgs = [T("stg%d" % i, [128, TMC]) for i in range(2)]
        zsb = [T("zsb%d" % j, [128, 513]) for j in range(11)]
        for j in range(11):
            c.op("pool", lambda e, j=j: e.memset(zsb[j][0][:, 0:1], 0.0), wr=[zsb[j][1]])
        tds = [T("td%d" % i, [128, 512]) for i in range(2)]
        ostg = [T("ostg%d" % i, [128, 512]) for i in range(3)]
        no = {"i": 0}
        NSTv = int(os.environ.get('NST', '8'))
        hbs = [T("hb1_%d" % i, [128, DM], BF16) for i in range(2)]

        def s1(tt):
            st, sub = tt // 4, tt % 4
            hT = hTs[st % 2]
            xt, bxt = xts[tt % 2]
            hb, bhb = hbs[tt % 2]
            c.dma("sp", xt[:], x[tt * 128:(tt + 1) * 128, :], wr=[bxt])
            c.op("act", lambda e: e.activation(junk[:], xt[:], AF.Square, accum_out=ss[:, tt:tt + 1]), rd=[bxt], wr=[bjunk, bss])
            c.op("dve", lambda e: e.tensor_scalar(rs[:, tt:tt + 1], ss[:, tt:tt + 1], 1.0 / DM, 1e-6, ALU.mult, ALU.add), rd=[bss], wr=[brs])
            c.op("act", lambda e: e.activation(rs[:, tt:tt + 1], rs[:, tt:tt + 1], AF.Ln), rd=[brs], wr=[brs])
            c.op("act", lambda e: e.activation(rs[:, tt:tt + 1], rs[:, tt:tt + 1], AF.Exp, scale=-0.5), rd=[brs], wr=[brs])
            c.op("dve", lambda e: e.tensor_scalar(hb[:], xt[:], rs[:, tt:tt + 1], None, ALU.mult), rd=[bxt, brs], wr=[bhb])
            p, bp = ps(); pb = p[:].bitcast(BF16)
            c.group("pe", [lambda e, k=k: e.transpose(pb[:, k * 128:(k + 1) * 128], hb[:, k * 128:(k + 1) * 128], idb[:]) for k in range(8)],
                    rd=[bhb, bidb], wr=[bp])
            c.op("act", lambda e: e.copy(hT[:, :, sub * 128:(sub + 1) * 128], pb.rearrange("p (k t) -> p k t", k=8)), rd=[bp], wr=[hTb[st % 2][sub]])

        def s2(tt):
            st, sub = tt // 4, tt % 4
            hT = hTs[st % 2]
            stg, bstg = stgs[tt % 2]
            for gi in range(8):
                p, bp = ps()
                c.group("pe", [lambda e, k=k, p=p: e.matmul(p[:], hT[:, k, sub * 128:(sub + 1) * 128], w[:, k, FMC + gi * 512:FMC + (gi + 1) * 512],
                                                           start=(k == 0), stop=(k == 7)) for k in range(8)],
                        rd=[hTb[st % 2][sub]] + wb, wr=[bp])
                en = ev()
                if en == "act":
                    c.op("act", lambda e, p=p: e.copy(stg[:, gi * 512:(gi + 1) * 512], p[:]), rd=[bp], wr=[bstg])
                else:
                    c.op("dve", lambda e, p=p: e.tensor_copy(stg[:, gi * 512:(gi + 1) * 512], p[:]), rd=[bp], wr=[bstg])
            c.dma("sp", zT[tt * 128:(tt + 1) * 128, :], stg[:], rd=[bstg])

        def fm(st):
            hT = hTs[st % 2]
            for j in range(11):
                ncol = 32 if j == 10 else 128
                z, bz = zsb[j]
                p, bp = ps()
                c.group("pe", [lambda e, k=k, p=p: e.matmul(p[0:ncol, :], w[:, k, j * 128:j * 128 + ncol], hT[:, k, :], start=(k == 0), stop=(k == 7)) for k in range(8)],
                        rd=hTb[st % 2] + wb, wr=[bp])
                c.op("act", lambda e, p=p: e.copy(z[0:ncol, 1:513], p[0:ncol, :]), rd=[bp], wr=[bz])
                td, btd = tds[j % 2]
                c.op("dve", lambda e: e.tensor_tensor(td[0:ncol, :], z[0:ncol, 0:512], z[0:ncol, 1:513], ALU.subtract), rd=[bz], wr=[btd])
                o, bo = ostg[no["i"] % 3]; no["i"] += 1
                c.op("dve", lambda e: e.scalar_tensor_tensor(o[0:ncol, :], td[0:ncol, :], muf[0:ncol, j:j + 1], z[0:ncol, 1:513], ALU.mult, ALU.add),
                     rd=[btd, bz, bmuf], wr=[bo])
                c.op("act", lambda e: e.copy(z[0:ncol, 0:1], z[0:ncol, 512:513]), rd=[bz], wr=[bz])
                c.dma("sp", zF[j * 128:j * 128 + ncol, st * 512:(st + 1) * 512], o[0:ncol, :], rd=[bo])

        ntl = NSTv * 4
        if ntl > 0:
            s1(0)
        for tt in range(ntl):
            if tt + 1 < ntl:
                s1(tt + 1)
            s2(tt)
            if tt % 4 == 3:
                fm(tt // 4)
        c.barrier()

    if upto <= 1:
        print("ninst", c.ninst, "nwait", c.nwait); return nc
    with ExitStack() as es:
        def T(name, shape, dt=F32):
            return es.enter_context(nc.sbuf_tensor("s_" + name, list(shape), dt)), Buf(name)
        idb, bidb = T("idb2", [128, 128], BF16); c.dma("pool", idb[:], ident_d, wr=[bidb])
        wdec, bwdec = T("wdec", [65, 512], BF16); c.dma("pool", wdec[:], wdec_d, wr=[bwdec])
        waaa, bwaaa = T("waaa", [65, 512], BF16); c.dma("pool", waaa[:], waaa_d, wr=[bwaaa])
        wgate, bwgate = T("wgate", [128, 2, 512], BF16)
        c.dma("pool", wgate[:, 0, :], wgate_d[0:128, :], wr=[bwgate]); c.dma("pool", wgate[0:32, 1, :], wgate_d[128:160, :], wr=[bwgate])
        kkf, bkkf = T("kkf", [128, 4]); c.dma("sp", kkf[:], kk_d, wr=[bkkf])
        kaf, bkaf = T("kaf", [128, 4]); c.dma("sp", kaf[:], ka_d, wr=[bkaf])
        c0f, bc0f = T("c0f", [128, 4])
        c.op("dve", lambda e: e.tensor_scalar(c0f[:], kaf[:], -1.0, 1.0, ALU.mult, ALU.add), rd=[bkaf], wr=[bc0f])
        rkb, brkb = T("rkb", [128, 8]); c.dma("sp", rkb[:], rkb_d, wr=[brkb])
        lng, blng = T("lng", [128, 512]); c.dma("sp", lng[:], lng_d.partition_broadcast(128), wr=[blng])
        lnb, blnb = T("lnb", [128, 512]); c.dma("sp", lnb[:], lnb_d.partition_broadcast(128), wr=[blnb])
        muv, bmuv = T("muv", [128, 512]); c.dma("sp", muv[:], mu_v.partition_broadcast(128), wr=[bmuv])
        MU2, bMU2 = T("MU2", [128, 512]); c.dma("sp", MU2[:], mu2_d, wr=[bMU2])
        SL2, bSL2 = T("SL2", [128, 256]); c.dma("sp", SL2[:], sl2_d, wr=[bSL2])
        bones, bbones = T("bones", [128, 128], BF16); c.dma("pool", bones[:], bones_d, wr=[bbones])
        S32, bS32 = T("S32", [128, 4, 64]); c.op("dve", lambda e: e.memset(S32[:], 0.0), wr=[bS32])
        Sb, bSb = T("Sb", [128, 4, 64], BF16); c.op("dve", lambda e: e.memset(Sb[:], 0.0), wr=[bSb])
        tha = [T("tha%d" % i, [65, 128], BF16) for i in range(2)]
        xaa = [T("xaa%d" % i, [65, 128], BF16) for i in range(2)]
        for i in range(2):
            c.op("dve", lambda e, i=i: e.memset(tha[i][0][:], 1.0), wr=[tha[i][1]])
            c.op("dve", lambda e, i=i: e.memset(xaa[i][0][:], 1.0), wr=[xaa[i][1]])
        rFs = [T("rF%d" % i, [128, 4, 128]) for i in range(2)]
        kFs = [T("kF%d" % i, [128, 4, 128]) for i in range(2)]
        xws = [T("xw%d" % i, [64, 128]) for i in range(2)]
        xas = [T("xa%d" % i, [64, 128]) for i in range(2)]
        xg0s = [T("xg0%d" % i, [128, 128]) for i in range(2)]
        xg1s = [T("xg1%d" % i, [32, 128]) for i in range(2)]
        vTs = [T("vT%d" % i, [128, 512]) for i in range(2)]
        vPs = [T("vP%d" % i, [128, 512]) for i in range(2)]
        for i in range(2):
            c.op("dve", lambda e, i=i: e.memset(vPs[i][0][:], 0.0), wr=[vPs[i][1]])
        v32s = [T("v32%d" % i, [128, 512]) for i in range(3)]; vbs = [T("vb%d" % i, [128, 512], BF16) for i in range(3)]; vtmp, bvtmp = T("vtmp", [128, 512])
        sg0, bsg0 = T("sg0", [128, 128], BF16); sg1, bsg1 = T("sg1", [32, 128], BF16)
        tg, btg = T("tg", [128, 512]); sgt, bsgt = T("sgt", [128, 128])
        logw, blogw = T("logw", [128, 4, 128]); lgi, blgi = T("lgi", [128, 4, 128]); lge, blge = T("lge", [128, 4, 128])
        alr, balr = T("alr", [128, 4, 128]); gTs = [T("gT%d" % i, [128, 512]) for i in range(3)]
        kkr, bkkr = T("kkr", [128, 4, 128]); sqb, bsqb = T("sqb", [128, 512], BF16); rn, brn = T("rn", [128, 512])
        kkn, bkkn = T("kkn", [128, 4, 128]); fF, bfF = T("fF", [128, 4, 128]); kM, bkM = T("kM", [128, 4, 128])
        gins = [T("gin%d" % i, [128, 4, 128]) for i in range(3)]; ginv, bginv = T("ginv", [128, 4, 128]); gex, bgex = T("gex", [128, 4, 128])
        ARs = [T("ARZ%d" % i, [128, 8, 2, 128], BF16) for i in range(3)]; bF, bbF = T("bF", [128, 4, 128])
        Bt, bBt = T("BtZ", [128, 8, 128], BF16); Kt, bKt = T("KtZ", [128, 8, 128], BF16)
        for (t_, b_) in (ARs[0], ARs[1], ARs[2], (Bt, bBt), (Kt, bKt)):
            c.op("pool", lambda e, t_=t_: e.memset(t_[:], 0.0), wr=[b_])
        Dd, bDd = T("Dd", [128, 4, 128]); Bh, bBh = T("Bh", [128, 4, 128], BF16); Kh, bKh = T("Kh", [128, 4, 128], BF16)
        BKhTs = [T("BKhT%d" % i, [128, 1024], BF16) for i in range(3)]
        rk, brk = T("rk", [128, 4, 128]); coefs = [T("coef%d" % i, [128, 8]) for i in range(3)]
        MABs = [T("MAB%d" % i, [128, 8, 2, 128], BF16) for i in range(3)]; MAKs = [T("MAK%d" % i, [128, 8, 2, 128], BF16) for i in range(3)]; MABTs = [T("MABT%d" % i, [128, 8, 128], BF16) for i in range(3)]
        Pk = [T("Pk%d" % i, [128, 8, 128], BF16) for i in range(2)]
        PTk = [T("PTk%d" % i, [128, 8, 128], BF16) for i in range(2)]
        ACk = [T("ACk%d" % i, [128, 8, 128], BF16) for i in range(2)]
        MinvS = [T("Minv%d" % i, [128, 8, 128], BF16) for i in range(2)]
        XT, bXT = T("XT", [128, 512], BF16); UT, bUT = T("UT", [128, 512], BF16)
        tmpS, btmpS = T("tmpS", [128, 4, 64])
        s1, bs1 = T("s1", [128, 8]); s2, bs2 = T("s2", [128, 8]); mean, bmean = T("mean", [128, 8]); var, bvar = T("var", [128, 8])
        sqt, bsqt = T("sqt", [128, 512]); yn, byn = T("yn", [128, 512]); bon, bbon = T("bon", [128, 512])
        yab, byab = T("yab", [128, 512], BF16); yaT, byaT = T("yaT", [128, 4, 128], BF16)

        zFr = zF[0:512, :].rearrange("(c p) t -> p c t", p=128)
        zFk = zF[512:1024, :].rearrange("(c p) t -> p c t", p=128)
        yaFv = yaF.rearrange("(c p) t -> p c t", p=128)

        def loads(ch):
            i = ch % 2; t0 = ch * 128
            c.dma("sp", rFs[i][0][:], zFr[:, :, t0:t0 + 128], wr=[rFs[i][1]])
            c.dma("sp", kFs[i][0][:], zFk[:, :, t0:t0 + 128], wr=[kFs[i][1]])
            c.dma("sp", xws[i][0][:], zF[1024:1088, t0:t0 + 128], wr=[xws[i][1]])
            c.dma("sp", xas[i][0][:], zF[1088:1152, t0:t0 + 128], wr=[xas[i][1]])
            c.dma("sp", xg0s[i][0][:], zF[1152:1280, t0:t0 + 128], wr=[xg0s[i][1]])
            c.dma("sp", xg1s[i][0][:], zF[1280:1312, t0:t0 + 128], wr=[xg1s[i][1]])
            c.dma("sp", vTs[i][0][:], zT[t0:t0 + 128, 0:512], wr=[vTs[i][1]])
            if ch == 0:
                c.dma("sp", vPs[i][0][1:128, :], zT[0:127, 0:512], wr=[vPs[i][1]])
            else:
                c.dma("sp", vPs[i][0][:], zT[t0 - 1:t0 + 127, 0:512], wr=[vPs[i][1]])

        loads(0)
        NCH = int(os.environ.get('NCH', str(NT)))
        STG = int(os.environ.get('STG', '99'))
        pqA = {"i": 0}; pqB = {"i": 0}
        pqC = {"i": 0}
        def psA():
            pqA["i"] += 1
            return PSB[pqA["i"] % 3]
        def psC():
            pqC["i"] += 1
            return PSB[3 + pqC["i"] % 3]
        def psB():
            pqB["i"] += 1
            return PSB[6 + pqB["i"] % 2]
        def stageA(ch):
            if ch + 1 < NCH:
                loads(ch + 1)
            i = ch % 2; t0 = ch * 128
            j3 = ch % 3
            v32, bv32 = v32s[j3]; vb, bvb = vbs[j3]; gT, bgT = gTs[j3]; gin, bgin = gins[j3]; AR, bAR = ARs[j3]
            BKhT, bBKhT = BKhTs[j3]; coef, bcoef = coefs[j3]; MAB, bMAB = MABs[j3]; MAK, bMAK = MAKs[j3]; MABT, bMABT = MABTs[j3]
            rF, brF = rFs[i]; kF, bkF = kFs[i]; xw, bxw = xws[i]; xa, bxa = xas[i]
            xg0, bxg0 = xg0s[i]; xg1, bxg1 = xg1s[i]; vT, bvT = vTs[i]; vP, bvP = vPs[i]
            th, bth = tha[i]; xab, bxab = xaa[i]
            c.op("pool", lambda e: e.tensor_tensor(vtmp[:], vP[:], vT[:], ALU.subtract), rd=[bvP, bvT], wr=[bvtmp])
            c.op("pool", lambda e: e.tensor_tensor(vtmp[:], vtmp[:], muv[:], ALU.mult), rd=[bvtmp, bmuv], wr=[bvtmp])
            c.op("pool", lambda e: e.tensor_tensor(v32[:], vtmp[:], vT[:], ALU.add), rd=[bvtmp, bvT], wr=[bv32])
            c.op("pool", lambda e: e.tensor_copy(vb[:], v32[:]), rd=[bv32], wr=[bvb])
            c.op("act", lambda e: e.activation(th[0:64, :], xw[:], AF.Tanh), rd=[bxw], wr=[bth])
            c.op("dve", lambda e: e.tensor_copy(xab[0:64, :], xa[:]), rd=[bxa], wr=[bxab])
            c.op("act", lambda e: e.activation(sgt[:], xg0[:], AF.Tanh, scale=0.5), rd=[bxg0], wr=[bsgt])
            c.op("dve", lambda e: e.tensor_scalar(sg0[:], sgt[:], 0.5, 0.5, ALU.mult, ALU.add), rd=[bsgt], wr=[bsg0])
            c.op("act", lambda e: e.activation(sgt[0:32, :], xg1[:], AF.Tanh, scale=0.5), rd=[bxg1], wr=[bsgt])
            c.op("dve", lambda e: e.tensor_scalar(sg1[:], sgt[0:32, :], 0.5, 0.5, ALU.mult, ALU.add), rd=[bsgt], wr=[bsg1])
            p, bp = psA()
            c.group("pe", [lambda e, q=q, p=p: e.matmul(p[:, q * 128:(q + 1) * 128], wdec[:, q * 128:(q + 1) * 128], th[:], start=True, stop=True) for q in range(4)],
                    rd=[bwdec, bth], wr=[bp])
            c.op("act", lambda e, p=p: e.activation(tg[:], p[:], AF.Tanh, scale=0.5), rd=[bp], wr=[btg])
            c.op("dve", lambda e: e.tensor_scalar(logw[:].rearrange("p a b -> p (a b)"), tg[:], -0.5 * math.exp(-0.5), -0.5 * math.exp(-0.5), ALU.mult, ALU.add),
                 rd=[btg], wr=[blogw])
            for q in range(4):
                c.op("dve", lambda e, q=q: e.tensor_tensor_scan(lgi[:, q, :], logw[:, q, :], logw[:, q, :], 0.0, ALU.add, ALU.bypass), rd=[blogw], wr=[blgi])
            c.op("pool", lambda e: e.tensor_tensor(lge[:], lgi[:], logw[:], ALU.subtract), rd=[blgi, blogw], wr=[blge])
            yield
            p, bp = psA()
            c.group("pe", [lambda e, q=q, p=p: e.matmul(p[:, q * 128:(q + 1) * 128], waaa[:, q * 128:(q + 1) * 128], xab[:], start=True, stop=True) for q in range(4)],
                    rd=[bwaaa, bxab], wr=[bp])
            c.op("act", lambda e, p=p: e.activation(tg[:], p[:], AF.Tanh, scale=0.5), rd=[bp], wr=[btg])
            c.op("dve", lambda e: e.tensor_scalar(alr[:].rearrange("p a b -> p (a b)"), tg[:], 0.5, 0.5, ALU.mult, ALU.add), rd=[btg], wr=[balr])
            p, bp = psA()
            c.group("pe", [lambda e, p=p: e.matmul(p[:], sg0[:], wgate[:, 0, :], start=True, stop=False),
                           lambda e, p=p: e.matmul(p[:], sg1[:], wgate[0:32, 1, :], start=False, stop=True)], rd=[bsg0, bsg1, bwgate], wr=[bp])
            c.op("act", lambda e, p=p: e.copy(gT[:], p[:]), rd=[bp], wr=[bgT])
            for q in range(4):
                c.op("dve", lambda e, q=q: e.tensor_scalar(kkr[:, q, :], kF[:, q, :], kkf[:, q:q + 1], None, ALU.mult), rd=[bkF, bkkf], wr=[bkkr])
            c.op("pool", lambda e: e.tensor_tensor(sqb[:], kkr[:].rearrange("p a b -> p (a b)"), kkr[:].rearrange("p a b -> p (a b)"), ALU.mult), rd=[bkkr], wr=[bsqb])
            p, bp = psA()
            c.op("pe", lambda e, p=p: e.matmul(p[:], bones[:], sqb[:], start=True, stop=True), rd=[bbones, bsqb], wr=[bp])
            c.op("act", lambda e, p=p: e.activation(rn[:], p[:], AF.Ln), rd=[bp], wr=[brn])
            c.op("act", lambda e: e.activation(rn[:], rn[:], AF.Exp, scale=-0.5), rd=[brn], wr=[brn])
            c.op("dve", lambda e: e.tensor_tensor(kkn[:].rearrange("p a b -> p (a b)"), kkr[:].rearrange("p a b -> p (a b)"), rn[:], ALU.mult), rd=[bkkr, brn], wr=[bkkn])
            yield
            for q in range(4):
                c.op("dve", lambda e, q=q: e.tensor_scalar(fF[:, q, :], alr[:, q, :], kaf[:, q:q + 1], c0f[:, q:q + 1], ALU.mult, ALU.add), rd=[balr, bkaf, bc0f], wr=[bfF])
            c.op("pool", lambda e: e.tensor_tensor(kM[:], kF[:], fF[:], ALU.mult), rd=[bkF, bfF], wr=[bkM])
            c.op("act", lambda e: e.activation(gin[:], lgi[:], AF.Exp), rd=[blgi], wr=[bgin])
            c.op("act", lambda e: e.activation(ginv[:], lgi[:], AF.Exp, scale=-1.0), rd=[blgi], wr=[bginv])
            c.op("act", lambda e: e.activation(gex[:], lge[:], AF.Exp), rd=[blge], wr=[bgex])
            for q in range(4):
                c.op("act", lambda e, q=q: e.activation(Dd[:, q, :], lgi[:, q, :], AF.Exp, bias=lgi[:, q, 127:128], scale=-1.0), rd=[blgi], wr=[bDd])
            c.op("pool", lambda e: e.tensor_tensor(bF[:], kkn[:], alr[:], ALU.mult), rd=[bkkn, balr], wr=[bbF])
            for hh in range(2):
                r0, r1 = hh * 64, (hh + 1) * 64
                ARv = AR[r0:r1, :, :, :].rearrange("p (q two) a t -> p q two a t", two=2)[:, :, hh, :, :]
                Btv = Bt[r0:r1, :, :].rearrange("p (q two) t -> p q two t", two=2)[:, :, hh, :]
                Ktv = Kt[r0:r1, :, :].rearrange("p (q two) t -> p q two t", two=2)[:, :, hh, :]
                c.op("dve", lambda e: e.tensor_tensor(ARv[:, :, 1, :], rF[r0:r1, :, :], gin[r0:r1, :, :], ALU.mult), rd=[brF, bgin], wr=[bAR])
                c.op("dve", lambda e: e.scalar_tensor_tensor(ARv[:, :, 0, :], kkn[r0:r1, :, :], -1.0, gex[r0:r1, :, :], ALU.mult, ALU.mult), rd=[bkkn, bgex], wr=[bAR])
                c.op("dve", lambda e: e.tensor_tensor(Btv, bF[r0:r1, :, :], ginv[r0:r1, :, :], ALU.mult), rd=[bbF, bginv], wr=[bBt])
                c.op("pool", lambda e: e.tensor_tensor(Ktv, kM[r0:r1, :, :], ginv[r0:r1, :, :], ALU.mult), rd=[bkM, bginv], wr=[bKt])
            c.op("pool", lambda e: e.tensor_tensor(Bh[:], bF[:], Dd[:], ALU.mult), rd=[bbF, bDd], wr=[bBh])
            c.op("pool", lambda e: e.tensor_tensor(Kh[:], kM[:], Dd[:], ALU.mult), rd=[bkM, bDd], wr=[bKh])
            yield
            p, bp = psA(); pb = p[:].bitcast(BF16)
            c.group("pe", [lambda e, q=q: e.transpose(pb[:, q * 128:(q + 1) * 128], Bh[:, q, :], idb[:]) for q in range(4)] +
                          [lambda e, q=q: e.transpose(pb[:, 512 + q * 128:512 + (q + 1) * 128], Kh[:, q, :], idb[:]) for q in range(4)],
                    rd=[bBh, bKh, bidb], wr=[bp])
            c.op("act", lambda e: e.copy(BKhT[:], pb), rd=[bp], wr=[bBKhT])
            c.op("pool", lambda e: e.tensor_tensor(rk[:], rF[:], kM[:], ALU.mult), rd=[brF, bkM], wr=[brk])
            p, bp = psA()
            c.group("pe", [lambda e, q=q, p=p: e.matmul(p[:, 2 * q:2 * q + 2], rk[:, q, :], rkb[:, 2 * q:2 * q + 2], start=True, stop=True) for q in range(4)],
                    rd=[brk, brkb], wr=[bp])
            c.op("dve", lambda e, p=p: e.tensor_copy(coef[:], p[:, 0:8]), rd=[bp], wr=[bcoef])
            for q in range(4):
                pA, bpA = psA(); pB, bpB = psA(); pC, bpC = psA()
                fa, fb, fc = [], [], []
                for hh in range(2):
                    h = 2 * q + hh
                    arr = AR[:, h, :, :].rearrange("p a b -> p (a b)")
                    fa.append(lambda e, hh=hh, h=h, arr=arr: e.matmul(pA[:, hh * 256:(hh + 1) * 256], Bt[:, h, :], arr, start=True, stop=True))
                    fb.append(lambda e, hh=hh, h=h, arr=arr: e.matmul(pB[:, hh * 256:(hh + 1) * 256], Kt[:, h, :], arr, start=True, stop=True))
                    fc.append(lambda e, hh=hh, h=h: e.matmul(pC[:, hh * 128:(hh + 1) * 128], AR[:, h, 0, :], Bt[:, h, :], start=True, stop=True))
                c.group("pe", fa, rd=[bBt, bAR], wr=[bpA])
                c.group("pe", fb, rd=[bKt, bAR], wr=[bpB])
                c.group("pe", fc, rd=[bBt, bAR], wr=[bpC])
                c.op("dve", lambda e: e.tensor_tensor(MAB[:, 2 * q:2 * q + 2, :, :].rearrange("p a b c -> p (a b c)"), pA[:], MU2[:], ALU.mult), rd=[bpA, bMU2], wr=[bMAB])
                c.op("dve", lambda e: e.tensor_tensor(MAK[:, 2 * q:2 * q + 2, :, :].rearrange("p a b c -> p (a b c)"), pB[:], MU2[:], ALU.mult), rd=[bpB, bMU2], wr=[bMAK])
                c.op("dve", lambda e: e.tensor_tensor(MABT[:, 2 * q:2 * q + 2, :].rearrange("p a b -> p (a b)"), pC[:, 0:256], SL2[:], ALU.mult), rd=[bpC, bSL2], wr=[bMABT])
            yield

        def stageA2(ch):
            i = ch % 2
            j3 = ch % 3
            MAB, bMAB = MABs[j3]; MABT, bMABT = MABTs[j3]
            AC0, bAC0 = ACk[0]
            c.op("pool", lambda e: e.tensor_tensor(AC0[:], MAB[:, :, 0, :], bcast(idb[:], [128, 8, 128], 1), ALU.add), rd=[bMAB, bidb], wr=[bAC0])
            Pp = lambda h: MAB[:, h, 0, :]
            PTp = lambda h: MABT[:, h, :]
            bPp, bPTp = bMAB, bMABT
            ACp, bACp = AC0, bAC0
            for lv in range(1, 7):
                Pn, bPn = Pk[lv % 2]; PTn, bPTn = PTk[lv % 2]; ACn, bACn = (ACk[lv % 2] if lv < 6 else MinvS[i])
                for grp in range(2):
                    hs = range(4 * grp, 4 * grp + 4)
                    if lv < 6:
                        p, bp = psC()
                        c.group("pe", [lambda e, h=h, p=p, Pp=Pp, PTp=PTp: e.matmul(p[:, (h % 4) * 128:(h % 4 + 1) * 128], PTp(h), Pp(h), start=True, stop=True) for h in hs],
                                rd=[bPp, bPTp], wr=[bp])
                        en = ev()
                        if en == "act":
                            c.op("act", lambda e, p=p: e.copy(Pn[:, 4 * grp:4 * grp + 4, :].rearrange("p a b -> p (a b)"), p[:]), rd=[bp], wr=[bPn])
                        else:
                            c.op("dve", lambda e, p=p: e.tensor_copy(Pn[:, 4 * grp:4 * grp + 4, :].rearrange("p a b -> p (a b)"), p[:]), rd=[bp], wr=[bPn])
                    p, bp = psC()
                    c.group("pe", [lambda e, h=h, p=p, Pp=Pp, PTp=PTp: e.matmul(p[:, (h % 4) * 128:(h % 4 + 1) * 128], Pp(h), PTp(h), start=True, stop=True) for h in hs],
                            rd=[bPp, bPTp], wr=[bp])
                    en = ev()
                    if en == "act":
                        c.op("act", lambda e, p=p: e.copy(PTn[:, 4 * grp:4 * grp + 4, :].rearrange("p a b -> p (a b)"), p[:]), rd=[bp], wr=[bPTn])
                    else:
                        c.op("dve", lambda e, p=p: e.tensor_copy(PTn[:, 4 * grp:4 * grp + 4, :].rearrange("p a b -> p (a b)"), p[:]), rd=[bp], wr=[bPTn])
                    p, bp = psC()
                    c.group("pe", [lambda e, h=h, p=p, ACp=ACp: e.matmul(p[:, (h % 4) * 128:(h % 4 + 1) * 128], PTn[:, h, :], ACp[:, h, :], start=True, stop=True) for h in hs],
                            rd=[bACp, bPTn], wr=[bp])
                    c.op("dve", lambda e, p=p, ACp=ACp: e.tensor_tensor(ACn[:, 4 * grp:4 * grp + 4, :].rearrange("p a b -> p (a b)"), p[:],
                                                                      ACp[:, 4 * grp:4 * grp + 4, :].rearrange("p a b -> p (a b)"), ALU.add), rd=[bp, bACp], wr=[bACn])
                yield
                Pp = (lambda Pn: (lambda h: Pn[:, h, :]))(Pn)
                PTp = (lambda PTn: (lambda h: PTn[:, h, :]))(PTn)
                bPp, bPTp = bPn, bPTn
                ACp, bACp = ACn, bACn
            yield

        def stageB(ch):
            i = ch % 2; t0 = ch * 128
            j3 = ch % 3
            v32, bv32 = v32s[j3]; vb, bvb = vbs[j3]; gT, bgT = gTs[j3]; gin, bgin = gins[j3]; AR, bAR = ARs[j3]
            Minv, bMinv = MinvS[i]
            BKhT, bBKhT = BKhTs[j3]; coef, bcoef = coefs[j3]; MAB, bMAB = MABs[j3]; MAK, bMAK = MAKs[j3]; MABT, bMABT = MABTs[j3]
            yield
            pX, bpX = psB()
            fs = []
            for h in range(8):
                q, r0 = h // 2, (h % 2) * 64
                fs.append(lambda e, h=h: e.matmul(pX[:, h * 64:(h + 1) * 64], MAK[:, h, 0, :], vb[:, h * 64:(h + 1) * 64], start=True, stop=False))
                fs.append(lambda e, h=h, q=q: e.matmul(pX[:, h * 64:(h + 1) * 64], AR[:, h, 0, :], Sb[:, q, :], start=False, stop=True))
            c.group("pe", fs, rd=[bMAK, bvb, bAR, bSb], wr=[bpX])
            c.op("act", lambda e: e.copy(XT[:], pX[:]), rd=[bpX], wr=[bXT])
            yield
            pU, bpU = psB()
            c.group("pe", [lambda e, h=h: e.matmul(pU[:, h * 64:(h + 1) * 64], Minv[:, h, :], XT[:, h * 64:(h + 1) * 64], start=True, stop=True) for h in range(8)],
                    rd=[bMinv, bXT], wr=[bpU])
            c.op("dve", lambda e: e.tensor_copy(UT[:], pU[:]), rd=[bpU], wr=[bUT])
            yield
            pY, bpY = psB()
            fs = []
            for h in range(8):
                q, r0 = h // 2, (h % 2) * 64
                fs.append(lambda e, h=h: e.matmul(pY[:, h * 64:(h + 1) * 64], MAK[:, h, 1, :], vb[:, h * 64:(h + 1) * 64], start=True, stop=False))
                fs.append(lambda e, h=h: e.matmul(pY[:, h * 64:(h + 1) * 64], MAB[:, h, 1, :], UT[:, h * 64:(h + 1) * 64], start=False, stop=False))
                fs.append(lambda e, h=h, q=q: e.matmul(pY[:, h * 64:(h + 1) * 64], AR[:, h, 1, :], Sb[:, q, :], start=False, stop=True))
            c.group("pe", fs, rd=[bMAK, bMAB, bvb, bUT, bAR, bSb], wr=[bpY])
            yield
            pS, bpS = psB()
            fs = []
            for q in range(4):
                fs.append(lambda e, q=q: e.matmul(pS[:, q * 128:(q + 1) * 128], BKhT[:, q * 128:(q + 1) * 128], UT[:, q * 128:(q + 1) * 128], start=True, stop=False))
                fs.append(lambda e, q=q: e.matmul(pS[:, q * 128:(q + 1) * 128], BKhT[:, 512 + q * 128:512 + (q + 1) * 128], vb[:, q * 128:(q + 1) * 128], start=False, stop=True))
            c.group("pe", fs, rd=[bBKhT, bUT, bvb], wr=[bpS])
            pSv = pS[:].rearrange("p (q c) -> p q c", q=4)
            c.op("dve", lambda e: e.tensor_tensor(tmpS[:], S32[:], gin[:, :, 127:128].to_broadcast([128, 4, 64]), ALU.mult), rd=[bS32, bgin], wr=[btmpS])
            c.op("dve", lambda e: e.tensor_tensor(S32[0:64, :, :], tmpS[0:64, :, :], pSv[0:64, :, 0:64], ALU.add), rd=[btmpS, bpS], wr=[bS32])
            c.op("dve", lambda e: e.tensor_tensor(S32[64:128, :, :], tmpS[64:128, :, :], pSv[64:128, :, 64:128], ALU.add), rd=[btmpS, bpS], wr=[bS32])
            c.op("dve", lambda e: e.tensor_copy(Sb[:], S32[:]), rd=[bS32], wr=[bSb])
            yield
            pYv = pY[:].rearrange("p (h d) -> p h d", h=8)
            c.op("dve", lambda e: e.tensor_reduce(s1[:], pYv, AX.X, ALU.add), rd=[bpY], wr=[bs1])
            c.op("act", lambda e: e.activation(sqt[:], pY[:], AF.Square), rd=[bpY], wr=[bsqt])
            c.op("dve", lambda e: e.tensor_reduce(s2[:], sqt[:].rearrange("p (h d) -> p h d", h=8), AX.X, ALU.add), rd=[bsqt], wr=[bs2])
            c.op("dve", lambda e: e.tensor_scalar(mean[:], s1[:], 1.0 / 64, None, ALU.mult), rd=[bs1], wr=[bmean])
            c.op("dve", lambda e: e.tensor_tensor(var[:], mean[:], mean[:], ALU.mult), rd=[bmean], wr=[bvar])
            c.op("dve", lambda e: e.scalar_tensor_tensor(var[:], s2[:], 1.0 / 64, var[:], ALU.mult, ALU.subtract), rd=[bs2, bvar], wr=[bvar])
            c.op("dve", lambda e: e.tensor_scalar(var[:], var[:], 64e-5, None, ALU.add), rd=[bvar], wr=[bvar])
            c.op("act", lambda e: e.activation(var[:], var[:], AF.Ln), rd=[bvar], wr=[bvar])
            c.op("act", lambda e: e.activation(var[:], var[:], AF.Exp, scale=-0.5), rd=[bvar], wr=[bvar])
            ynv = yn[:].rearrange("p (h d) -> p h d", h=8)
            c.op("dve", lambda e: e.tensor_tensor(ynv, pYv, bcast(mean[:], [128, 8, 64], 2), ALU.subtract), rd=[bpY, bmean], wr=[byn])
            c.op("pool", lambda e: e.tensor_tensor(ynv, ynv, bcast(var[:], [128, 8, 64], 2), ALU.mult), rd=[byn, bvar], wr=[byn])
            c.op("pool", lambda e: e.tensor_tensor(yn[:], yn[:], lng[:], ALU.mult), rd=[byn, blng], wr=[byn])
            c.op("dve", lambda e: e.tensor_tensor(yn[:], yn[:], lnb[:], ALU.add), rd=[byn, blnb], wr=[byn])
            c.op("pool", lambda e: e.tensor_tensor(bon[:].rearrange("p (h d) -> p h d", h=8), v32[:].rearrange("p (h d) -> p h d", h=8), bcast(coef[:], [128, 8, 64], 2), ALU.mult),
                 rd=[bv32, bcoef], wr=[bbon])
            c.op("dve", lambda e: e.tensor_tensor(yn[:], yn[:], bon[:], ALU.add), rd=[byn, bbon], wr=[byn])
            c.op("dve", lambda e: e.tensor_tensor(yab[:], yn[:], gT[:], ALU.mult), rd=[byn, bgT], wr=[byab])
            p, bp = psB(); pb = p[:].bitcast(BF16)
            c.group("pe", [lambda e, q=q: e.transpose(pb[:, q * 128:(q + 1) * 128], yab[:, q * 128:(q + 1) * 128], idb[:]) for q in range(4)], rd=[byab, bidb], wr=[bp])
            c.op("act", lambda e: e.copy(yaT[:].rearrange("p a b -> p (a b)"), pb[:, 0:512]), rd=[bp], wr=[byaT])
            c.dma("sp", yaFv[:, :, t0:t0 + 128], yaT[:], rd=[byaT])
        def run(gens):
            alive = [g for g in gens if g is not None]
            while alive:
                for g in list(alive):
                    try:
                        next(g)
                    except StopIteration:
                        alive.remove(g)
        run([stageA(0)])
        run([stageA2(0), stageA(1) if NCH > 1 else None])
        for ch in range(NCH):
            run([stageB(ch), stageA2(ch + 1) if ch + 1 < NCH else None, stageA(ch + 2) if ch + 2 < NCH else None])
        c.barrier()

    if upto <= 2:
        print("ninst", c.ninst, "nwait", c.nwait); return nc
    with ExitStack() as es:
        def T(name, shape, dt=F32):
            return es.enter_context(nc.sbuf_tensor("s_" + name, list(shape), dt)), Buf(name)
        pst["n"] = 2; pst["i"] = 0; pst["off"] = 4
        pO = [PSB[6], PSB[7]]
        qst = {"i": 0}
        def psq():
            qst["i"] += 1
            return PSB[qst["i"] % 4]
        idb, bidb = T("idb3", [128, 128], BF16); c.dma("pool", idb[:], ident_d, wr=[bidb])
        CB, bCB = T("CB", [128, 2, 256], BF16); c.dma("pool", CB[:].rearrange("p a b -> p (a b)"), cbias2_d, wr=[bCB])
        s65, bs65 = T("s65", [65, 64], BF16); c.dma("pool", s65[:], sel65_d, wr=[bs65])
        pok, bpok = T("pok", [128, 16, 16]); c.dma("sp", pok[:].rearrange("p a b -> p (a b)"), pok_d.partition_broadcast(128), wr=[bpok])
        pbi, bpbi = T("pbi", [128, 16, 16]); c.dma("sp", pbi[:].rearrange("p a b -> p (a b)"), pbias_d.partition_broadcast(128), wr=[bpbi])
        invf, binvf = T("invf", [128, 8]); c.dma("sp", invf[:], invf_d.partition_broadcast(128), wr=[binvf])
        qkg, bqkg = T("qkg", [128, 1024]); c.dma("sp", qkg[:], qkg_d.partition_broadcast(128), wr=[bqkg])
        posi, bposi = T("posi", [128, NT], I32); c.dma("sp", posi[:], pos, wr=[bposi])
        posf, bposf = T("posf", [128, NT])
        c.op("dve", lambda e: e.tensor_copy(posf[:], posi[:]), rd=[bposi], wr=[bposf])
        yy, byy = T("yy", [128, NT, 8]); yi, byi = T("yi", [128, NT, 8], I32); yf, byf = T("yf", [128, NT, 8])
        sinT, bsinT = T("sinT", [128, NT, 8]); cosT, bcosT = T("cosT", [128, NT, 8])
        c.op("dve", lambda e: e.tensor_copy(yy[:], bcast(invf[:], [128, NT, 8], 1)), rd=[binvf], wr=[byy])
        c.op("dve", lambda e: e.tensor_tensor(yy[:], yy[:], bcast(posf[:], [128, NT, 8], 2), ALU.mult), rd=[byy, bposf], wr=[byy])
        for (dst, bdst, off) in ((sinT, bsinT, 0.0), (cosT, bcosT, 0.25)):
            if off != 0.0:
                c.op("dve", lambda e: e.tensor_scalar(yy[:], yy[:], off, None, ALU.add), rd=[byy], wr=[byy])
            c.op("dve", lambda e: e.tensor_copy(yi[:], yy[:]), rd=[byy], wr=[byi])
            c.op("dve", lambda e: e.tensor_copy(yf[:], yi[:]), rd=[byi], wr=[byf])
            c.op("dve", lambda e: e.tensor_tensor(yf[:], yy[:], yf[:], ALU.subtract), rd=[byy, byf], wr=[byf])
            c.op("act", lambda e, dst=dst: e.activation(dst[:], yf[:], AF.Sin, scale=2.0 * math.pi), rd=[byf], wr=[bdst])
        KT = es.enter_context(nc.sbuf_tensor("s_KT", [80, 8, S], BF16)); bKT = [Buf("KT%d" % g) for g in range(8)]
        VA = es.enter_context(nc.sbuf_tensor("s_VA", [128, NT, 8, 65], BF16)); bVA = [Buf("VA%d" % g) for g in range(8)]
        c.op("pool", lambda e: e.memset(VA[:], 1.0), wr=bVA)
        kmT, bkmT = T("kmT", [64, 8, 16], BF16); c.op("dve", lambda e: e.memset(kmT[:], 0.0), wr=[bkmT])
        km32, bkm32 = T("km32", [64, 8])
        qkvs = [T("qkv%d" % i, [128, 1536]) for i in range(2)]
        sq3, bsq3 = T("sq3", [128, 1024]); ss3, bss3 = T("ss3", [128, 16]); qn, bqn = T("qn", [128, 16, 64])
        rt, brt = T("rt", [128, 4, 16, 8])
        QA, bQA = T("QA", [128, 8, 80], BF16); KA, bKA = T("KA", [128, 8, 80], BF16)
        QT, bQT = T("QT", [64, 8, 128], BF16)
        QTAs = [T("QTA%d" % i, [80, 8, 512], BF16) for i in range(2)]
        gm, bgm = T("gm", [128, 8, 16]); mx, bmx = T("mx", [128, 8, 8]); sel, bsel = T("sel", [128, 8, 16])
        PTs = [T("PT%d" % i, [128, 512], BF16) for i in range(4)]
        OT, bOT = T("OT", [65, 512]); rr, brr = T("rr", [65, 512], BF16); r32, br32 = T("r32", [65, 512])
        c.op("dve", lambda e: e.memset(rr[:], 0.0), wr=[brr])
        yTs = [T("yT%d" % i, [64, 512], BF16) for i in range(2)]
        cnt3 = {"pt": 0, "ld": 0}

        def ld3(tt):
            c.dma("sp", qkvs[tt % 2][0][:], zT[tt * 128:(tt + 1) * 128, 512:2048], wr=[qkvs[tt % 2][1]])

        def pre3(tt):
            if tt + 1 < NT:
                ld3(tt + 1)
            qkv, bqkv = qkvs[tt % 2]
            t0 = tt * 128; qb = tt // 2; g = tt // 4; ti = tt % 4
            QTA, bQTA = QTAs[g % 2]
            c.op("act", lambda e: e.activation(sq3[:], qkv[:, 0:1024], AF.Square), rd=[bqkv], wr=[bsq3])
            c.op("dve", lambda e: e.tensor_reduce(ss3[:], sq3[:].rearrange("p (h d) -> p h d", h=16), AX.X, ALU.add), rd=[bsq3], wr=[bss3])
            c.op("dve", lambda e: e.tensor_scalar(ss3[:], ss3[:], 1.0 / 64, 1e-6, ALU.mult, ALU.add), rd=[bss3], wr=[bss3])
            c.op("act", lambda e: e.activation(ss3[:], ss3[:], AF.Ln), rd=[bss3], wr=[bss3])
            c.op("act", lambda e: e.activation(ss3[:], ss3[:], AF.Exp, scale=-0.5), rd=[bss3], wr=[bss3])
            c.op("dve", lambda e: e.tensor_tensor(qn[:], qkv[:, 0:1024].rearrange("p (h d) -> p h d", h=16), bcast(ss3[:], [128, 16, 64], 2), ALU.mult), rd=[bqkv, bss3], wr=[bqn])
            c.op("pool", lambda e: e.tensor_tensor(qn[:].rearrange("p h d -> p (h d)"), qn[:].rearrange("p h d -> p (h d)"), qkg[:], ALU.mult), rd=[bqn, bqkg], wr=[bqn])
            cs = cosT[:, tt:tt + 1, :].to_broadcast([128, 16, 8]); sn = sinT[:, tt:tt + 1, :].to_broadcast([128, 16, 8])
            c.op("dve", lambda e: e.tensor_tensor(rt[:, 0, :, :], qn[:, :, 0:8], cs, ALU.mult), rd=[bqn, bcosT], wr=[brt])
            c.op("dve", lambda e: e.tensor_tensor(rt[:, 1, :, :], qn[:, :, 8:16], sn, ALU.mult), rd=[bqn, bsinT], wr=[brt])
            c.op("pool", lambda e: e.tensor_tensor(rt[:, 2, :, :], qn[:, :, 8:16], cs, ALU.mult), rd=[bqn, bcosT], wr=[brt])
            c.op("pool", lambda e: e.tensor_tensor(rt[:, 3, :, :], qn[:, :, 0:8], sn, ALU.mult), rd=[bqn, bsinT], wr=[brt])
            c.op("dve", lambda e: e.tensor_tensor(qn[:, :, 0:8], rt[:, 0, :, :], rt[:, 1, :, :], ALU.subtract), rd=[brt], wr=[bqn])
            c.op("dve", lambda e: e.tensor_tensor(qn[:, :, 8:16], rt[:, 2, :, :], rt[:, 3, :, :], ALU.add), rd=[brt], wr=[bqn])
            c.op("dve", lambda e: e.tensor_copy(QA[:, :, 0:64], qn[:, 0:8, :]), rd=[bqn], wr=[bQA])
            c.op("pool", lambda e: e.tensor_copy(KA[:, :, 0:64], qn[:, 8:16, :]), rd=[bqn], wr=[bKA])
            c.op("pool", lambda e: e.memset(KA[:, :, 64:80], 0.0), wr=[bKA])
            c.op("pool", lambda e: e.memset(KA[:, :, 64 + qb:65 + qb], 1.0), wr=[bKA])
            p, bp = ps(); pb = p[:].bitcast(BF16)
            c.group("pe", [lambda e, h=h: e.transpose(pb[0:80, h * 128:(h + 1) * 128], KA[:, h, :], idb[:]) for h in range(8)], rd=[bKA, bidb], wr=[bp])
            c.op("dve", lambda e: e.tensor_copy(KT[:, :, t0:t0 + 128], pb[0:80, :].rearrange("p (h t) -> p h t", h=8)), rd=[bp], wr=[bKT[g]])
            c.op("pool", lambda e: e.tensor_copy(VA[:, tt, :, 0:64], qkv[:, 1024:1536].rearrange("p (h d) -> p h d", h=8)), rd=[bqkv], wr=[bVA[g]])
            if qb > 0:
                p, bp = ps(); pb = p[:].bitcast(BF16)
                c.group("pe", [lambda e, h=h: e.transpose(pb[0:64, h * 128:(h + 1) * 128], QA[:, h, 0:64], idb[:]) for h in range(8)], rd=[bQA, bidb], wr=[bp])
                c.op("dve", lambda e: e.tensor_copy(QT[:], pb[0:64, :].rearrange("p (h t) -> p h t", h=8)), rd=[bp], wr=[bQT])
                p, bp = ps()
                c.group("pe", [lambda e, h=h, p=p: e.matmul(p[:, h * 16:(h + 1) * 16], QT[:, h, :], kmT[:, h, :], start=True, stop=True) for h in range(8)],
                        rd=[bQT, bkmT], wr=[bp])
                c.op("dve", lambda e, p=p: e.tensor_tensor(gm[:], p[:, 0:128].rearrange("p (h j) -> p h j", h=8), pbi[:, qb:qb + 1, :].to_broadcast([128, 8, 16]), ALU.add),
                     rd=[bp, bpbi], wr=[bgm])
                for h in range(8):
                    c.op("dve", lambda e, h=h: e.max(mx[:, h, :], gm[:, h, :]), rd=[bgm], wr=[bmx])
                c.op("dve", lambda e: e.tensor_tensor(sel[:], gm[:], mx[:, :, 2:3].to_broadcast([128, 8, 16]), ALU.is_ge), rd=[bgm, bmx], wr=[bsel])
                c.op("dve", lambda e: e.tensor_tensor(sel[:], sel[:], pok[:, qb:qb + 1, :].to_broadcast([128, 8, 16]), ALU.mult), rd=[bsel, bpok], wr=[bsel])
                c.op("dve", lambda e: e.tensor_scalar(QA[:, :, 64:80], sel[:], BIG, -BIG, ALU.mult, ALU.add), rd=[bsel], wr=[bQA])
            else:
                c.op("dve", lambda e: e.memset(QA[:, :, 64:80], -BIG), wr=[bQA])
            p, bp = ps(); pb = p[:].bitcast(BF16)
            c.group("pe", [lambda e, h=h: e.transpose(pb[0:80, h * 128:(h + 1) * 128], QA[:, h, :], idb[:]) for h in range(8)], rd=[bQA, bidb], wr=[bp])
            c.op("dve", lambda e: e.tensor_copy(QTA[:, :, ti * 128:(ti + 1) * 128], pb[0:80, :].rearrange("p (h t) -> p h t", h=8)), rd=[bp], wr=[bQTA])
            if tt % 2 == 1:
                c.op("dve", lambda e: e.tensor_reduce(km32[:], KT[0:64, :, qb * 256:(qb + 1) * 256], AX.X, ALU.add), rd=[bKT[g]], wr=[bkm32])
                c.op("dve", lambda e: e.tensor_scalar(kmT[:, :, qb], km32[:], 1.0 / 256, None, ALU.mult), rd=[bkm32], wr=[bkmT])

        s65f, bs65f = T("s65f", [65, 64]); c.dma("sp", s65f[:], sel65_d, wr=[bs65f])

        def steps3(g, h):
            QTA, bQTA = QTAs[g % 2]
            pOh, bOh = pO[h % 2]
            kbufs = bKT[0:g + 1]; vbufs = bVA[0:g + 1]
            st = []
            nk = 4 * g + 2
            for kt in range(nk):
                d = {}
                def qk(d=d, kt=kt):
                    d["p"], d["bp"] = psq()
                    c.op("pe", lambda e: e.matmul(d["p"][:], KT[:, h, kt * 128:(kt + 1) * 128], QTA[:, h, :], start=True, stop=True), rd=kbufs + [bQTA], wr=[d["bp"]])
                def ex(d=d):
                    d["PT"], d["bPT"] = PTs[cnt3["pt"] % 4]; cnt3["pt"] += 1
                    c.op("act", lambda e: e.activation(d["PT"][:], d["p"][:], AF.Exp, scale=0.125), rd=[d["bp"]], wr=[d["bPT"]])
                def pv(d=d, kt=kt):
                    c.op("pe", lambda e: e.matmul(pOh[0:65, :], VA[:, kt, h, :], d["PT"][:], start=(kt == 0), stop=False), rd=vbufs + [d["bPT"]], wr=[bOh])
                st.append((qk, ex, pv))
            for half in range(2):
                for kti in range(2):
                    kt = 4 * g + 2 * half + kti
                    qs = slice(half * 256, (half + 1) * 256)
                    last = (half == 1 and kti == 1)
                    d = {}
                    def qk(d=d, kt=kt, qs=qs, kti=kti):
                        d["p"], d["bp"] = psq()
                        c.group("pe", [lambda e: e.matmul(d["p"][:, 0:256], KT[0:64, h, kt * 128:(kt + 1) * 128], QTA[0:64, h, qs], start=True, stop=False),
                                       lambda e: e.matmul(d["p"][:, 0:256], idb[:], CB[:, kti, :], start=False, stop=True)], rd=kbufs + [bQTA, bidb, bCB], wr=[d["bp"]])
                    def ex(d=d):
                        d["PT"], d["bPT"] = PTs[cnt3["pt"] % 4]; cnt3["pt"] += 1
                        c.op("act", lambda e: e.activation(d["PT"][:, 0:256], d["p"][:, 0:256], AF.Exp, scale=0.125), rd=[d["bp"]], wr=[d["bPT"]])
                    def pv(d=d, kt=kt, qs=qs, last=last):
                        c.op("pe", lambda e: e.matmul(pOh[0:65, qs], VA[:, kt, h, :], d["PT"][:, 0:256], start=False, stop=last), rd=vbufs + [d["bPT"]], wr=[bOh])
                    st.append((qk, ex, pv))

            def fin():
                c.op("dve", lambda e: e.tensor_copy(OT[:], pOh[0:65, :]), rd=[bOh], wr=[bOT])
                p, bp = ps()
                c.op("pe", lambda e: e.matmul(p[0:64, :], s65f[:], OT[:], start=True, stop=True), rd=[bs65f, bOT], wr=[bp])
                yT, byT = yTs[h % 2]
                c.op("dve", lambda e: e.reciprocal(r32[0:64, :], p[0:64, :]), rd=[bp], wr=[br32])
                c.op("dve", lambda e: e.tensor_tensor(yT[:], OT[0:64, :], r32[0:64, :], ALU.mult), rd=[bOT, br32], wr=[byT])
                c.dma("sp", ybF[h * 64:(h + 1) * 64, g * 512:(g + 1) * 512], yT[:], rd=[byT])
            return st, fin

        ld3(0)
        for tt in range(4):
            pre3(tt)
        LOOK = 2
        for g in range(8):
            allst = []
            for h in range(8):
                st, fin = steps3(g, h)
                for i, s in enumerate(st):
                    allst.append((s, fin if i == len(st) - 1 else None, h))
            n = len(allst)
            for i in range(min(LOOK, n)):
                allst[i][0][0]()
            for i in range(n):
                (qk, ex, pv), fin, h = allst[i]
                ex()
                if i + LOOK < n:
                    allst[i + LOOK][0][0]()
                pv()
                if fin is not None:
                    fin()
                    if g + 1 < 8 and h % 2 == 1:
                        pre3(4 * (g + 1) + h // 2)
        pst["n"] = 8; pst["off"] = 0
        c.barrier()

    if upto <= 3:
        print("ninst", c.ninst, "nwait", c.nwait); return nc
    with ExitStack() as es:
        def T(name, shape, dt=F32):
            return es.enter_context(nc.sbuf_tensor("s_" + name, list(shape), dt)), Buf(name)
        idb, bidb = T("idb4", [128, 128], BF16); c.dma("pool", idb[:], ident_d, wr=[bidb])
        wba, bwba = T("wba", [128, 4, DM], BF16); wbb, bwbb = T("wbb", [128, 4, DM], BF16); wo, bwo = T("wo", [128, 8, DM], BF16)
        for k in range(4):
            c.dma("pool", wba[:, k, :], wba_d[k * 128:(k + 1) * 128, :], wr=[bwba])
            c.dma("pool", wbb[:, k, :], wbb_d[k * 128:(k + 1) * 128, :], wr=[bwbb])
        for k in range(8):
            c.dma("pool", wo[:, k, :], wout_d[k * 128:(k + 1) * 128, :], wr=[bwo])
        xts = [T("x4%d" % i, [128, DM]) for i in range(2)]
        gps = [T("gp%d" % i, [128, 2048]) for i in range(2)]
        yas = [T("ya4%d" % i, [128, 4, 128], BF16) for i in range(2)]
        ybs = [T("yb4%d" % i, [128, 4, 128], BF16) for i in range(2)]
        m1, bm1 = T("m1", [128, DM]); m2, bm2 = T("m2", [128, DM]); mb, bmb = T("mb", [128, DM], BF16)
        mT, bmT = T("mT", [128, 8, 128], BF16)
        x1, bx1 = T("x1", [128, DM]); junk, bjunk = T("junk4", [128, DM]); h2, bh2 = T("h2", [128, DM], BF16)
        h2T, bh2T = T("h2T", [128, 8, 128], BF16)
        ss, bss = T("ss4", [128, NT]); c.op("dve", lambda e: e.memset(ss[:], 0.0), wr=[bss])
        rs, brs = T("rs4", [128, NT])
        yaFv = yaF.rearrange("(c p) t -> p c t", p=128); ybFv = ybF.rearrange("(c p) t -> p c t", p=128)
        h2Fv = h2F.rearrange("(c p) t -> p c t", p=128)

        def loads4(tt):
            i = tt % 2; t0 = tt * 128
            c.dma("sp", xts[i][0][:], x[t0:t0 + 128, :], wr=[xts[i][1]])
            c.dma("sp", gps[i][0][:], zT[t0:t0 + 128, 2048:4096], wr=[gps[i][1]])
            c.dma("sp", yas[i][0][:], yaFv[:, :, t0:t0 + 128], wr=[yas[i][1]])
            c.dma("sp", ybs[i][0][:], ybFv[:, :, t0:t0 + 128], wr=[ybs[i][1]])
        loads4(0)
        for tt in range(NT):
            if tt + 1 < NT:
                loads4(tt + 1)
            i = tt % 2; t0 = tt * 128
            xt, bxt = xts[i]; gp, bgp = gps[i]; ya, bya = yas[i]; yb, byb = ybs[i]
            c.op("act", lambda e: e.activation(gp[:], gp[:], AF.Sigmoid), rd=[bgp], wr=[bgp])
            for half in range(2):
                hs = slice(half * 512, (half + 1) * 512)
                pa, bpa = ps()
                c.group("pe", [lambda e, k=k: e.matmul(pa[:], ya[:, k, :], wba[:, k, hs], start=(k == 0), stop=(k == 3)) for k in range(4)], rd=[bya, bwba], wr=[bpa])
                c.op("dve", lambda e: e.tensor_tensor(m1[:, hs], pa[:], gp[:, half * 512:(half + 1) * 512], ALU.mult), rd=[bpa, bgp], wr=[bm1])
                pb_, bpb_ = ps()
                c.group("pe", [lambda e, k=k: e.matmul(pb_[:], yb[:, k, :], wbb[:, k, hs], start=(k == 0), stop=(k == 3)) for k in range(4)], rd=[byb, bwbb], wr=[bpb_])
                c.op("dve", lambda e: e.tensor_tensor(m2[:, hs], pb_[:], gp[:, 1024 + half * 512:1024 + (half + 1) * 512], ALU.mult), rd=[bpb_, bgp], wr=[bm2])
            c.op("pool", lambda e: e.tensor_tensor(mb[:], m1[:], m2[:], ALU.add), rd=[bm1, bm2], wr=[bmb])
            p, bp = ps(); pb = p[:].bitcast(BF16)
            c.group("pe", [lambda e, k=k: e.transpose(pb[:, k * 128:(k + 1) * 128], mb[:, k * 128:(k + 1) * 128], idb[:]) for k in range(8)], rd=[bmb, bidb], wr=[bp])
            c.op("act", lambda e: e.copy(mT[:].rearrange("p a b -> p (a b)"), pb), rd=[bp], wr=[bmT])
            for half in range(2):
                hs = slice(half * 512, (half + 1) * 512)
                po, bpo = ps()
                c.group("pe", [lambda e, k=k: e.matmul(po[:], mT[:, k, :], wo[:, k, hs], start=(k == 0), stop=(k == 7)) for k in range(8)], rd=[bmT, bwo], wr=[bpo])
                c.op("dve", lambda e: e.tensor_tensor(x1[:, hs], po[:], xt[:, hs], ALU.add), rd=[bpo, bxt], wr=[bx1])
            c.dma("sp", x1s[t0:t0 + 128, :], x1[:], rd=[bx1])
            c.op("act", lambda e: e.activation(junk[:], x1[:], AF.Square, accum_out=ss[:, tt:tt + 1]), rd=[bx1], wr=[bjunk, bss])
            c.op("dve", lambda e: e.tensor_scalar(rs[:, tt:tt + 1], ss[:, tt:tt + 1], 1.0 / DM, 1e-6, ALU.mult, ALU.add), rd=[bss], wr=[brs])
            c.op("act", lambda e: e.activation(rs[:, tt:tt + 1], rs[:, tt:tt + 1], AF.Ln), rd=[brs], wr=[brs])
            c.op("act", lambda e: e.activation(rs[:, tt:tt + 1], rs[:, tt:tt + 1], AF.Exp, scale=-0.5), rd=[brs], wr=[brs])
            c.op("dve", lambda e: e.tensor_scalar(h2[:], x1[:], rs[:, tt:tt + 1], None, ALU.mult), rd=[bx1, brs], wr=[bh2])
            p, bp = ps(); pb = p[:].bitcast(BF16)
            c.group("pe", [lambda e, k=k: e.transpose(pb[:, k * 128:(k + 1) * 128], h2[:, k * 128:(k + 1) * 128], idb[:]) for k in range(8)], rd=[bh2, bidb], wr=[bp])
            c.op("act", lambda e: e.copy(h2T[:].rearrange("p a b -> p (a b)"), pb), rd=[bp], wr=[bh2T])
            c.dma("sp", h2Fv[:, :, t0:t0 + 128], h2T[:], rd=[bh2T])
        c.barrier()

    if upto <= 4:
        print("ninst", c.ninst, "nwait", c.nwait); return nc
    with ExitStack() as es:
        def T(name, shape, dt=F32):
            return es.enter_context(nc.sbuf_tensor("s_" + name, list(shape), dt)), Buf(name)
        wup, bwup = T("wup", [128, 8, 2 * DFF], BF16); wdn, bwdn = T("wdn", [128, NFF, DM], BF16)
        g2, bg2 = T("g2", [128, 8]); c.dma("sp", g2[:], n2g, wr=[bg2])
        wub = [Buf("wup_%d" % k) for k in range(8)]
        for k in range(8):
            c.dma("pool", wup[:, k, :], wup_d[k * 128:(k + 1) * 128, :], wr=[wub[k]])
            c.op("dve", lambda e, k=k: e.tensor_scalar(wup[:, k, :], wup[:, k, :], g2[:, k:k + 1], None, ALU.mult), rd=[wub[k], bg2], wr=[wub[k]])
        for f in range(NFF):
            c.dma("pool", wdn[:, f, :], wdn_d[f * 128:(f + 1) * 128, :], wr=[bwdn])
        cw, bcw = T("cw", [128, NFF, 3]); c.dma("sp", cw[:].rearrange("p a b -> p (a b)"), cw_d, wr=[bcw])
        cbt, bcbt = T("cbt", [128, NFF]); c.dma("sp", cbt[:], cb_d, wr=[bcbt])
        cr, bcr = T("cr", [128, NFF, 2]); c.op("dve", lambda e: e.memset(cr[:], 0.0), wr=[bcr])
        h2s = [T("h2s%d" % i, [128, 8, 512], BF16) for i in range(2)]
        asb = [T("asb%d" % i, [128, 514]) for i in range(2)]
        acc = [T("acc%d" % i, [128, 512]) for i in range(2)]
        hg = es.enter_context(nc.sbuf_tensor("hg", [128, NFF, 512], BF16)); bhg = [Buf("hg%d" % f) for f in range(NFF)]
        x1t = [T("x1t%d" % i, [128, DM]) for i in range(2)]
        ost = [T("ost%d" % i, [128, DM]) for i in range(2)]
        h2Fv = h2F.rearrange("(c p) t -> p c t", p=128)
        c.dma("sp", h2s[0][0][:], h2Fv[:, :, 0:512], wr=[h2s[0][1]])
        nx = 0
        for st in range(8):
            if st + 1 < 8:
                c.dma("sp", h2s[(st + 1) % 2][0][:], h2Fv[:, :, (st + 1) * 512:(st + 2) * 512], wr=[h2s[(st + 1) % 2][1]])
            hT, bhT = h2s[st % 2]
            for f in range(NFF):
                pa, bpa = ps()
                c.group("pe", [lambda e, k=k: e.matmul(pa[:], wup[:, k, f * 128:(f + 1) * 128], hT[:, k, :], start=(k == 0), stop=(k == 7)) for k in range(8)], rd=[bhT] + wub, wr=[bpa])
                pg, bpg = ps()
                c.group("pe", [lambda e, k=k: e.matmul(pg[:], wup[:, k, DFF + f * 128:DFF + (f + 1) * 128], hT[:, k, :], start=(k == 0), stop=(k == 7)) for k in range(8)], rd=[bhT] + wub, wr=[bpg])
                a, ba = asb[f % 2]; ac, bac = acc[f % 2]
                c.op("act", lambda e: e.copy(a[:, 0:2], cr[:, f, :]), rd=[bcr], wr=[ba])
                c.op("act", lambda e: e.copy(a[:, 2:514], pa[:]), rd=[bpa], wr=[ba])
                c.op("act", lambda e: e.copy(cr[:, f, :], a[:, 512:514]), rd=[ba], wr=[bcr])
                c.op("dve", lambda e: e.tensor_scalar(ac[:], a[:, 0:512], cw[:, f, 0:1], None, ALU.mult), rd=[ba, bcw], wr=[bac])
                c.op("dve", lambda e: e.scalar_tensor_tensor(ac[:], a[:, 1:513], cw[:, f, 1:2], ac[:], ALU.mult, ALU.add), rd=[ba, bcw, bac], wr=[bac])
                c.op("dve", lambda e: e.scalar_tensor_tensor(ac[:], a[:, 2:514], cw[:, f, 2:3], ac[:], ALU.mult, ALU.add), rd=[ba, bcw, bac], wr=[bac])
                c.op("act", lambda e: e.activation(ac[:], ac[:], AF.Gelu, bias=cbt[:, f:f + 1]), rd=[bac, bcbt], wr=[bac])
                c.op("dve", lambda e: e.tensor_tensor(hg[:, f, :], ac[:], pg[:], ALU.mult), rd=[bac, bpg], wr=[bhg[f]])
            for sub in range(4):
                tt = st * 4 + sub; t0 = tt * 128
                xx, bxx = x1t[nx % 2]; oo, boo = ost[nx % 2]; nx += 1
                c.dma("sp", xx[:], x1s[t0:t0 + 128, :], wr=[bxx])
                for half in range(2):
                    hs = slice(half * 512, (half + 1) * 512)
                    po, bpo = ps()
                    c.group("pe", [lambda e, f=f: e.matmul(po[:], hg[:, f, sub * 128:(sub + 1) * 128], wdn[:, f, hs], start=(f == 0), stop=(f == NFF - 1)) for f in range(NFF)],
                            rd=bhg + [bwdn], wr=[bpo])
                    c.op("dve", lambda e: e.tensor_tensor(oo[:, hs], po[:], xx[:, hs], ALU.add), rd=[bpo, bxx], wr=[boo])
                c.dma("sp", out[t0:t0 + 128, :], oo[:], rd=[boo])
        c.barrier()
    print("ninst", c.ninst, "nwait", c.nwait, {e: c.cnt[e] for e in c.cnt})
    return nc


def _consts():
    i = np.arange(128)
    su = (i[:, None] < i[None, :]).astype(np.float32)
    ui = (i[:, None] <= i[None, :]).astype(np.float32)
    mu2 = np.concatenate([su, ui, su, ui], axis=1)
    sl = (i[None, :] < i[:, None]).astype(np.float32)
    sl2 = np.concatenate([sl, sl], axis=1)
    bones = ((i[:, None] // 64) == (i[None, :] // 64)).astype(np.float32)
    q2 = np.arange(256)
    cb0 = np.where(i[:, None] > q2[None, :], -BIG, 0.0).astype(np.float32)
    cb1 = np.where(i[:, None] + 128 > q2[None, :], -BIG, 0.0).astype(np.float32)
    cbias2 = np.concatenate([cb0, cb1], axis=1)
    sel65 = np.zeros((65, 64), np.float32); sel65[64, :] = 1.0
    j = np.arange(16)
    pok = (j[None, :] < j[:, None]).astype(np.float32)
    pbias = ((pok - 1.0) * 1e30).astype(np.float32)
    half = 8
    invf = (500000.0 ** (-np.arange(half, dtype=np.float32) / half)).astype(np.float32) / np.float32(2.0 * math.pi)
    return dict(ident=np.eye(128, dtype=np.float32), mu2=mu2, sl2=sl2, bones=bones, cbias2=cbias2, sel65=sel65,
                pok=pok.reshape(1, 256), pbias=pbias.reshape(1, 256), invf=invf.reshape(1, 8).astype(np.float32))


def _fm(v, n):
    return np.ascontiguousarray(np.asarray(v, np.float32).reshape(n, 128).T)


def _prep(inp):
    f = lambda a: np.ascontiguousarray(np.asarray(a, dtype=np.float32))
    w_in = f(inp["w_in"][0])
    fm_cols = np.r_[0:512, 512:1024, 1536:1600, 1600:1664, 1664:1824]
    tm_cols = np.r_[1024:1536, 1824:3360, 3360:5408]
    mu = f(inp["rwkv_mu"][0])
    mu_fm = np.zeros(1408, np.float32); mu_fm[:FMC] = mu[fm_cols]
    rk = f(inp["rwkv_r_k"][0])
    rkb = np.zeros((128, 8), np.float32)
    for h in range(8):
        rkb[(h % 2) * 64:(h % 2 + 1) * 64, h] = rk[h]
    cw = f(inp["ffn_conv_w"][0])
    cwl = np.ascontiguousarray(cw.reshape(3, NFF, 128).transpose(2, 1, 0)).reshape(128, NFF * 3)
    shared = dict(
        w_in=np.ascontiguousarray(w_in[:, np.r_[fm_cols, tm_cols]]),
        n1g=_fm(inp["norm1_g"][0], 8), mu_fm=_fm(mu_fm, 11), mu_v=f(mu[1024:1536]).reshape(1, 512),
        wdec=np.concatenate([f(inp["w_decay_up"][0]), f(inp["decay_bias"][0]).reshape(1, 512)], 0),
        waaa=np.concatenate([f(inp["w_aaa_up"][0]), f(inp["aaa_bias"][0]).reshape(1, 512)], 0),
        wgate=f(inp["w_gate_up"][0]), kk_fm=_fm(inp["rwkv_k_k"][0], 4), ka_fm=_fm(inp["rwkv_k_a"][0], 4), rkb=rkb,
        lng=f(inp["rwkv_ln_g"][0]).reshape(1, 512), lnb=f(inp["rwkv_ln_b"][0]).reshape(1, 512),
        qkg=np.concatenate([np.tile(f(inp["q_norm_g"][0]), 8), np.tile(f(inp["k_norm_g"][0]), 8)]).reshape(1, 1024),
        wba=f(inp["w_branch_a"][0]), wbb=f(inp["w_branch_b"][0]), wout=f(inp["w_out"][0]), n2g=_fm(inp["norm2_g"][0], 8),
        wup=f(inp["w_ffn_up"][0]), cw=cwl, cb=_fm(inp["ffn_conv_b"][0], NFF), wdn=f(inp["w_ffn_down"][0]),
    )
    shared.update(_consts())
    xs = np.asarray(inp["x"], np.float32); ps_ = np.asarray(inp["positions"], np.int32)
    maps = []
    for b in range(8):
        m = dict(shared)
        m["x"] = np.ascontiguousarray(xs[b])
        m["pos"] = np.ascontiguousarray(ps_[b].reshape(NT, 128).T)
        maps.append(m)
    return maps


def kernel(**inputs):
    maps = _prep(inputs)
    nc = build()
    res = run_bass_kernel_spmd(nc, maps, core_ids=list(range(8)))
    return np.stack([np.asarray(r["out"], np.float32) for r in res.results], axis=0)
```

```python
import numpy as np
import concourse.bass as bass
import concourse.mybir as mybir
from concourse.bass_utils import run_bass_kernel_spmd

F32 = mybir.dt.float32
BF16 = mybir.dt.bfloat16
I32 = mybir.dt.int32
ALU = mybir.AluOpType
AF = mybir.ActivationFunctionType
AX = mybir.AxisListType


class Buf:
    __slots__ = ("name", "lastw", "readers")

    def __init__(self, name):
        self.name = name
        self.lastw = None
        self.readers = {}


class Ctx:
    def __init__(self, nc, n_dma_sems=24):
        self.nc = nc
        self.eng = {"pe": nc.tensor, "act": nc.scalar, "dve": nc.vector,
                    "pool": nc.gpsimd, "sp": nc.sync}
        self.sem = {}
        self.cnt = {}
        self.waited = {e: {} for e in self.eng}
        self._stack = []
        for e in ("pe", "act", "dve", "pool"):
            cm = nc.semaphore("s_" + e)
            self.sem[e] = cm.__enter__()
            self._stack.append(cm)
            self.cnt[e] = 0
        self.dsem = []
        self.dpool = {"hw": [], "sw": []}
        for i in range(n_dma_sems):
            cm = nc.semaphore("d%d" % i)
            self.dsem.append([cm.__enter__(), 0])
            self._stack.append(cm)
            self.dpool["sw" if i < 8 else "hw"].append(i)
        self.dnext = {"hw": 0, "sw": 0}
        self.semh = {}
        for e in self.sem:
            self.semh[("e", e)] = self.sem[e]
        for i, (h, _) in enumerate(self.dsem):
            self.semh[("d", i)] = h
        self.nwait = 0
        self.ninst = 0

    def _wait(self, e, toks):
        w = self.waited[e]
        best = {}
        for t in toks:
            if t is None:
                continue
            k, v = t[0], t[1]
            if w.get(k, 0) >= v:
                continue
            if best.get(k, 0) < v:
                best[k] = v
        for k, v in best.items():
            self.eng[e].wait_ge(self.semh[k], v)
            w[k] = v
            self.nwait += 1

    def _deps(self, e, rd, wr):
        toks = []
        me = ("e", e)
        for b in rd:
            if b.lastw is not None:
                if not (e == "pe" and b.lastw[0] == me):
                    toks.append(b.lastw)
        for b in wr:
            if b.lastw is not None and not (e == "pe" and b.lastw[0] == me):
                toks.append(b.lastw)
            for k, t in b.readers.items():
                if not (e == "pe" and k == me):
                    toks.append(t)
        return toks

    def _mark(self, tok, rd, wr):
        for b in rd:
            b.readers[tok[0]] = tok
        for b in wr:
            b.lastw = tok
            b.readers = {}

    def op(self, e, fn, rd=(), wr=()):
        self._wait(e, self._deps(e, rd, wr))
        ins = fn(self.eng[e])
        self.cnt[e] += 1
        ins.then_inc(self.sem[e], 1)
        tok = (("e", e), self.cnt[e])
        self._mark(tok, rd, wr)
        self.ninst += 1
        return tok

    def group(self, e, fns, rd=(), wr=()):
        self._wait(e, self._deps(e, rd, wr))
        ins = None
        for fn in fns:
            ins = fn(self.eng[e])
            self.ninst += 1
        self.cnt[e] += 1
        ins.then_inc(self.sem[e], 1)
        tok = (("e", e), self.cnt[e])
        self._mark(tok, rd, wr)
        return tok

    def dma(self, q, out, in_, rd=(), wr=(), **kw):
        kind = "sw" if q == "pool" else "hw"
        pool = self.dpool[kind]
        i = pool[self.dnext[kind] % len(pool)]
        self.dnext[kind] += 1
        h, c = self.dsem[i]
        k = ("d", i)
        toks = self._deps(q, rd, wr)
        if c > 0:
            toks.append((k, 16 * c))
        self._wait(q, toks)
        self.eng[q].dma_start(out=out, in_=in_, **kw).then_inc(h, 16)
        self.dsem[i][1] = c + 1
        tok = (k, 16 * (c + 1))
        self._mark(tok, rd, wr)
        self.ninst += 1
        return tok

    def wait_all(self, e, bufs):
        toks = []
        for b in bufs:
            toks.append(b.lastw)
            toks.extend(b.readers.values())
        self._wait(e, toks)

    def barrier(self, bufs=()):
        toks = []
        for e in self.sem:
            if self.cnt[e] > 0:
                toks.append((("e", e), self.cnt[e]))
        for i, (h, c) in enumerate(self.dsem):
            if c > 0:
                toks.append((("d", i), 16 * c))
        for e in self.eng:
            self._wait(e, toks)

from contextlib import ExitStack
import math
import os

S = 4096
DM = 1024
NT = 32
FMC = 1312
TMC = 4096
DFF = 2816
NFF = 22
BIG = 30000.0


def build(debug=False, upto=99):
    nc = bass.Bass("TRN2", target_bir_lowering=False)
    okind = "ExternalOutput" if debug else "Internal"

    def DIN(name, shape, dt=F32):
        return nc.dram_tensor(name, list(shape), dt, kind="ExternalInput").ap()

    x = DIN("x", [S, DM]); pos = DIN("pos", [128, NT], I32)
    w_in = DIN("w_in", [DM, 5408]); n1g = DIN("n1g", [128, 8]); mu_fm = DIN("mu_fm", [128, 11]); mu_v = DIN("mu_v", [1, 512])
    wdec_d = DIN("wdec", [65, 512]); waaa_d = DIN("waaa", [65, 512]); wgate_d = DIN("wgate", [160, 512])
    kk_d = DIN("kk_fm", [128, 4]); ka_d = DIN("ka_fm", [128, 4]); rkb_d = DIN("rkb", [128, 8])
    lng_d = DIN("lng", [1, 512]); lnb_d = DIN("lnb", [1, 512]); qkg_d = DIN("qkg", [1, 1024])
    wba_d = DIN("wba", [512, DM]); wbb_d = DIN("wbb", [512, DM]); wout_d = DIN("wout", [DM, DM]); n2g = DIN("n2g", [128, 8])
    wup_d = DIN("wup", [DM, 2 * DFF]); cw_d = DIN("cw", [128, NFF * 3]); cb_d = DIN("cb", [128, NFF]); wdn_d = DIN("wdn", [DFF, DM])
    ident_d = DIN("ident", [128, 128]); mu2_d = DIN("mu2", [128, 512]); sl2_d = DIN("sl2", [128, 256]); bones_d = DIN("bones", [128, 128])
    cbias2_d = DIN("cbias2", [128, 512]); sel65_d = DIN("sel65", [65, 64]); pok_d = DIN("pok", [1, 256]); pbias_d = DIN("pbias", [1, 256]); invf_d = DIN("invf", [1, 8])
    out = nc.dram_tensor("out", [S, DM], F32, kind="ExternalOutput").ap()
    zF = nc.dram_tensor("zF", [1408, S], F32, kind=okind).ap()
    zT = nc.dram_tensor("zT", [S, TMC], F32, kind=okind).ap()
    yaF = nc.dram_tensor("yaF", [512, S], BF16, kind=okind).ap()
    ybF = nc.dram_tensor("ybF", [512, S], BF16, kind=okind).ap()
    x1s = nc.dram_tensor("x1s", [S, DM], F32, kind=okind).ap()
    h2F = nc.dram_tensor("h2F", [DM, S], BF16, kind=okind).ap()

    c = Ctx(nc)
    PSB = [(nc.alloc_psum_tensor("psb%d" % i, [128, 512], F32), Buf("psb%d" % i)) for i in range(8)]
    pst = {"i": 0, "n": 8, "off": 0}

    def ps():
        i = pst["off"] + pst["i"] % pst["n"]
        pst["i"] += 1
        return PSB[i]

    rr = {"i": 0}

    def ev():
        rr["i"] += 1
        return "act" if rr["i"] % 2 else "dve"

    def bcast(ap, shape, axis):
        return ap.unsqueeze(axis).to_broadcast(list(shape))

    with ExitStack() as es:
        def T(name, shape, dt=F32):
            return es.enter_context(nc.sbuf_tensor("s_" + name, list(shape), dt)), Buf(name)
        w, bw = T("w1", [128, 8, 5408], BF16)
        g1, bg1 = T("g1", [128, 8]); muf, bmuf = T("muf", [128, 11])
        idb, bidb = T("idb1", [128, 128], BF16)
        c.dma("sp", g1[:], n1g, wr=[bg1]); c.dma("sp", muf[:], mu_fm, wr=[bmuf])
        c.dma("pool", idb[:], ident_d, wr=[bidb])
        wb = [Buf("w1_%d" % k) for k in range(8)]
        for kc in range(8):
            c.dma("pool", w[:, kc, :], w_in[kc * 128:(kc + 1) * 128, :], wr=[wb[kc]])
            c.op("dve", lambda e, kc=kc: e.tensor_scalar(w[:, kc, :], w[:, kc, :], g1[:, kc:kc + 1], None, ALU.mult),
                 rd=[wb[kc], bg1], wr=[wb[kc]])
        ss, bss = T("ss1", [128, NT]); c.op("dve", lambda e: e.memset(ss[:], 0.0), wr=[bss])
        rs, brs = T("rs1", [128, NT])
        junk, bjunk = T("junk1", [128, DM])
        xts = [T("xt%d" % i, [128, DM]) for i in range(2)]
        hTs = [es.enter_context(nc.sbuf_tensor("hT%d" % i, [128, 8, 512], BF16)) for i in range(2)]
        hTb = [[Buf("hT%d_%d" % (i, s)) for s in range(4)] for i in range(2)]
        stgs = [T("stg%d" % i, [128, TMC]) for i in range(2)]
        zsb = [T("zsb%d" % j, [128, 513]) for j in range(11)]
        for j in range(11):
            c.op("pool", lambda e, j=j: e.memset(zsb[j][0][:, 0:1], 0.0), wr=[zsb[j][1]])
        tds = [T("td%d" % i, [128, 512]) for i in range(2)]
        ostg = [T("ostg%d" % i, [128, 512]) for i in range(3)]
        no = {"i": 0}
        NSTv = int(os.environ.get('NST', '8'))
        hbs = [T("hb1_%d" % i, [128, DM], BF16) for i in range(2)]

        def s1(tt):
            st, sub = tt // 4, tt % 4
            hT = hTs[st % 2]
            xt, bxt = xts[tt % 2]
            hb, bhb = hbs[tt % 2]
            c.dma("sp", xt[:], x[tt * 128:(tt + 1) * 128, :], wr=[bxt])
            c.op("act", lambda e: e.activation(junk[:], xt[:], AF.Square, accum_out=ss[:, tt:tt + 1]), rd=[bxt], wr=[bjunk, bss])
            c.op("dve", lambda e: e.tensor_scalar(rs[:, tt:tt + 1], ss[:, tt:tt + 1], 1.0 / DM, 1e-6, ALU.mult, ALU.add), rd=[bss], wr=[brs])
            c.op("act", lambda e: e.activation(rs[:, tt:tt + 1], rs[:, tt:tt + 1], AF.Ln), rd=[brs], wr=[brs])
            c.op("act", lambda e: e.activation(rs[:, tt:tt + 1], rs[:, tt:tt + 1], AF.Exp, scale=-0.5), rd=[brs], wr=[brs])
            c.op("dve", lambda e: e.tensor_scalar(hb[:], xt[:], rs[:, tt:tt + 1], None, ALU.mult), rd=[bxt, brs], wr=[bhb])
            p, bp = ps(); pb = p[:].bitcast(BF16)
            c.group("pe", [lambda e, k=k: e.transpose(pb[:, k * 128:(k + 1) * 128], hb[:, k * 128:(k + 1) * 128], idb[:]) for k in range(8)],
                    rd=[bhb, bidb], wr=[bp])
            c.op("act", lambda e: e.copy(hT[:, :, sub * 128:(sub + 1) * 128], pb.rearrange("p (k t) -> p k t", k=8)), rd=[bp], wr=[hTb[st % 2][sub]])

        def s2(tt):
            st, sub = tt // 4, tt % 4
            hT = hTs[st % 2]
            stg, bstg = stgs[tt % 2]
            for gi in range(8):
                p, bp = ps()
                c.group("pe", [lambda e, k=k, p=p: e.matmul(p[:], hT[:, k, sub * 128:(sub + 1) * 128], w[:, k, FMC + gi * 512:FMC + (gi + 1) * 512],
                                                           start=(k == 0), stop=(k == 7)) for k in range(8)],
                        rd=[hTb[st % 2][sub]] + wb, wr=[bp])
                en = ev()
                if en == "act":
                    c.op("act", lambda e, p=p: e.copy(stg[:, gi * 512:(gi + 1) * 512], p[:]), rd=[bp], wr=[bstg])
                else:
                    c.op("dve", lambda e, p=p: e.tensor_copy(stg[:, gi * 512:(gi + 1) * 512], p[:]), rd=[bp], wr=[bstg])
            c.dma("pool", zT[tt * 128:(tt + 1) * 128, :], stg[:], rd=[bstg])

        def fm(st):
            hT = hTs[st % 2]
            for j in range(11):
                ncol = 32 if j == 10 else 128
                z, bz = zsb[j]
                p, bp = ps()
                c.group("pe", [lambda e, k=k, p=p: e.matmul(p[0:ncol, :], w[:, k, j * 128:j * 128 + ncol], hT[:, k, :], start=(k == 0), stop=(k == 7)) for k in range(8)],
                        rd=hTb[st % 2] + wb, wr=[bp])
                c.op("act", lambda e, p=p: e.copy(z[0:ncol, 1:513], p[0:ncol, :]), rd=[bp], wr=[bz])
                td, btd = tds[j % 2]
                c.op("dve", lambda e: e.tensor_tensor(td[0:ncol, :], z[0:ncol, 0:512], z[0:ncol, 1:513], ALU.subtract), rd=[bz], wr=[btd])
                o, bo = ostg[no["i"] % 3]; no["i"] += 1
                c.op("dve", lambda e: e.scalar_tensor_tensor(o[0:ncol, :], td[0:ncol, :], muf[0:ncol, j:j + 1], z[0:ncol, 1:513], ALU.mult, ALU.add),
                     rd=[btd, bz, bmuf], wr=[bo])
                c.op("act", lambda e: e.copy(z[0:ncol, 0:1], z[0:ncol, 512:513]), rd=[bz], wr=[bz])
                c.dma("pool", zF[j * 128:j * 128 + ncol, st * 512:(st + 1) * 512], o[0:ncol, :], rd=[bo])

        ntl = NSTv * 4
        if ntl > 0:
            s1(0)
        for tt in range(ntl):
            if tt + 1 < ntl:
                s1(tt + 1)
            s2(tt)
            if tt % 4 == 3:
                fm(tt // 4)
        c.barrier()

    if upto <= 1:
        print("ninst", c.ninst, "nwait", c.nwait); return nc
    with ExitStack() as es:
        def T(name, shape, dt=F32):
            return es.enter_context(nc.sbuf_tensor("s_" + name, list(shape), dt)), Buf(name)
        idb, bidb = T("idb2", [128, 128], BF16); c.dma("pool", idb[:], ident_d, wr=[bidb])
        wdec, bwdec = T("wdec", [65, 512], BF16); c.dma("pool", wdec[:], wdec_d, wr=[bwdec])
        waaa, bwaaa = T("waaa", [65, 512], BF16); c.dma("pool", waaa[:], waaa_d, wr=[bwaaa])
        wgate, bwgate = T("wgate", [128, 2, 512], BF16)
        c.dma("pool", wgate[:, 0, :], wgate_d[0:128, :], wr=[bwgate]); c.dma("pool", wgate[0:32, 1, :], wgate_d[128:160, :], wr=[bwgate])
        kkf, bkkf = T("kkf", [128, 4]); c.dma("sp", kkf[:], kk_d, wr=[bkkf])
        kaf, bkaf = T("kaf", [128, 4]); c.dma("sp", kaf[:], ka_d, wr=[bkaf])
        c0f, bc0f = T("c0f", [128, 4])
        c.op("dve", lambda e: e.tensor_scalar(c0f[:], kaf[:], -1.0, 1.0, ALU.mult, ALU.add), rd=[bkaf], wr=[bc0f])
        rkb, brkb = T("rkb", [128, 8]); c.dma("sp", rkb[:], rkb_d, wr=[brkb])
        lng, blng = T("lng", [128, 512]); c.dma("sp", lng[:], lng_d.partition_broadcast(128), wr=[blng])
        lnb, blnb = T("lnb", [128, 512]); c.dma("sp", lnb[:], lnb_d.partition_broadcast(128), wr=[blnb])
        muv, bmuv = T("muv", [128, 512]); c.dma("sp", muv[:], mu_v.partition_broadcast(128), wr=[bmuv])
        MU2, bMU2 = T("MU2", [128, 512]); c.dma("sp", MU2[:], mu2_d, wr=[bMU2])
        SL2, bSL2 = T("SL2", [128, 256]); c.dma("sp", SL2[:], sl2_d, wr=[bSL2])
        bones, bbones = T("bones", [128, 128], BF16); c.dma("pool", bones[:], bones_d, wr=[bbones])
        S32, bS32 = T("S32", [128, 4, 64]); c.op("dve", lambda e: e.memset(S32[:], 0.0), wr=[bS32])
        Sb, bSb = T("Sb", [128, 4, 64], BF16); c.op("dve", lambda e: e.memset(Sb[:], 0.0), wr=[bSb])
        tha = [T("tha%d" % i, [65, 128], BF16) for i in range(2)]
        xaa = [T("xaa%d" % i, [65, 128], BF16) for i in range(2)]
        for i in range(2):
            c.op("dve", lambda e, i=i: e.memset(tha[i][0][:], 1.0), wr=[tha[i][1]])
            c.op("dve", lambda e, i=i: e.memset(xaa[i][0][:], 1.0), wr=[xaa[i][1]])
        rFs = [T("rF%d" % i, [128, 4, 128]) for i in range(2)]
        kFs = [T("kF%d" % i, [128, 4, 128]) for i in range(2)]
        xws = [T("xw%d" % i, [64, 128]) for i in range(2)]
        xas = [T("xa%d" % i, [64, 128]) for i in range(2)]
        xg0s = [T("xg0%d" % i, [128, 128]) for i in range(2)]
        xg1s = [T("xg1%d" % i, [32, 128]) for i in range(2)]
        vTs = [T("vT%d" % i, [128, 512]) for i in range(2)]
        vPs = [T("vP%d" % i, [128, 512]) for i in range(2)]
        for i in range(2):
            c.op("dve", lambda e, i=i: e.memset(vPs[i][0][:], 0.0), wr=[vPs[i][1]])
        v32s = [T("v32%d" % i, [128, 512]) for i in range(3)]; vbs = [T("vb%d" % i, [128, 512], BF16) for i in range(3)]; vtmp, bvtmp = T("vtmp", [128, 512])
        sg0, bsg0 = T("sg0", [128, 128], BF16); sg1, bsg1 = T("sg1", [32, 128], BF16)
        tg, btg = T("tg", [128, 512]); sgt, bsgt = T("sgt", [128, 128])
        logw, blogw = T("logw", [128, 4, 128]); lgi, blgi = T("lgi", [128, 4, 128]); lge, blge = T("lge", [128, 4, 128])
        alr, balr = T("alr", [128, 4, 128]); gTs = [T("gT%d" % i, [128, 512]) for i in range(3)]
        kkr, bkkr = T("kkr", [128, 4, 128]); sqb, bsqb = T("sqb", [128, 512], BF16); rn, brn = T("rn", [128, 512])
        kkn, bkkn = T("kkn", [128, 4, 128]); fF, bfF = T("fF", [128, 4, 128]); kM, bkM = T("kM", [128, 4, 128])
        gins = [T("gin%d" % i, [128, 4, 128]) for i in range(3)]; ginv, bginv = T("ginv", [128, 4, 128]); gex, bgex = T("gex", [128, 4, 128])
        ARs = [T("ARZ%d" % i, [128, 8, 2, 128], BF16) for i in range(3)]; bF, bbF = T("bF", [128, 4, 128])
        Bt, bBt = T("BtZ", [128, 8, 128], BF16); Kt, bKt = T("KtZ", [128, 8, 128], BF16)
        for (t_, b_) in (ARs[0], ARs[1], ARs[2], (Bt, bBt), (Kt, bKt)):
            c.op("pool", lambda e, t_=t_: e.memset(t_[:], 0.0), wr=[b_])
        Dd, bDd = T("Dd", [128, 4, 128]); Bh, bBh = T("Bh", [128, 4, 128], BF16); Kh, bKh = T("Kh", [128, 4, 128], BF16)
        BKhTs = [T("BKhT%d" % i, [128, 1024], BF16) for i in range(3)]
        rk, brk = T("rk", [128, 4, 128]); coefs = [T("coef%d" % i, [128, 8]) for i in range(3)]
        MABs = [T("MAB%d" % i, [128, 8, 2, 128], BF16) for i in range(3)]; MAKs = [T("MAK%d" % i, [128, 8, 2, 128], BF16) for i in range(3)]; MABTs = [T("MABT%d" % i, [128, 8, 128], BF16) for i in range(3)]
        Pk = [T("Pk%d" % i, [128, 8, 128], BF16) for i in range(2)]
        PTk = [T("PTk%d" % i, [128, 8, 128], BF16) for i in range(2)]
        ACk = [T("ACk%d" % i, [128, 8, 128], BF16) for i in range(2)]
        MinvS = [T("Minv%d" % i, [128, 8, 128], BF16) for i in range(2)]
        XT, bXT = T("XT", [128, 512], BF16); UT, bUT = T("UT", [128, 512], BF16)
        tmpS, btmpS = T("tmpS", [128, 4, 64])
        s1, bs1 = T("s1", [128, 8]); s2, bs2 = T("s2", [128, 8]); mean, bmean = T("mean", [128, 8]); var, bvar = T("var", [128, 8])
        sqt, bsqt = T("sqt", [128, 512]); yn, byn = T("yn", [128, 512]); bon, bbon = T("bon", [128, 512])
        yab, byab = T("yab", [128, 512], BF16); yaT, byaT = T("yaT", [128, 4, 128], BF16)

        zFr = zF[0:512, :].rearrange("(c p) t -> p c t", p=128)
        zFk = zF[512:1024, :].rearrange("(c p) t -> p c t", p=128)
        yaFv = yaF.rearrange("(c p) t -> p c t", p=128)

        def loads(ch):
            i = ch % 2; t0 = ch * 128
            c.dma("sp", rFs[i][0][:], zFr[:, :, t0:t0 + 128], wr=[rFs[i][1]])
            c.dma("sp", kFs[i][0][:], zFk[:, :, t0:t0 + 128], wr=[kFs[i][1]])
            c.dma("sp", xws[i][0][:], zF[1024:1088, t0:t0 + 128], wr=[xws[i][1]])
            c.dma("sp", xas[i][0][:], zF[1088:1152, t0:t0 + 128], wr=[xas[i][1]])
            c.dma("sp", xg0s[i][0][:], zF[1152:1280, t0:t0 + 128], wr=[xg0s[i][1]])
            c.dma("sp", xg1s[i][0][:], zF[1280:1312, t0:t0 + 128], wr=[xg1s[i][1]])
            c.dma("sp", vTs[i][0][:], zT[t0:t0 + 128, 0:512], wr=[vTs[i][1]])
            if ch == 0:
                c.dma("sp", vPs[i][0][1:128, :], zT[0:127, 0:512], wr=[vPs[i][1]])
            else:
                c.dma("sp", vPs[i][0][:], zT[t0 - 1:t0 + 127, 0:512], wr=[vPs[i][1]])

        loads(0)
        NCH = int(os.environ.get('NCH', str(NT)))
        STG = int(os.environ.get('STG', '99'))
        pqA = {"i": 0}; pqB = {"i": 0}
        pqC = {"i": 0}
        def psA():
            pqA["i"] += 1
            return PSB[pqA["i"] % 3]
        def psC():
            pqC["i"] += 1
            return PSB[3 + pqC["i"] % 3]
        def psB():
            pqB["i"] += 1
            return PSB[6 + pqB["i"] % 2]
        def stageA(ch):
            if ch + 1 < NCH:
                loads(ch + 1)
            i = ch % 2; t0 = ch * 128
            j3 = ch % 3
            v32, bv32 = v32s[j3]; vb, bvb = vbs[j3]; gT, bgT = gTs[j3]; gin, bgin = gins[j3]; AR, bAR = ARs[j3]
            BKhT, bBKhT = BKhTs[j3]; coef, bcoef = coefs[j3]; MAB, bMAB = MABs[j3]; MAK, bMAK = MAKs[j3]; MABT, bMABT = MABTs[j3]
            rF, brF = rFs[i]; kF, bkF = kFs[i]; xw, bxw = xws[i]; xa, bxa = xas[i]
            xg0, bxg0 = xg0s[i]; xg1, bxg1 = xg1s[i]; vT, bvT = vTs[i]; vP, bvP = vPs[i]
            th, bth = tha[i]; xab, bxab = xaa[i]
            c.op("pool", lambda e: e.tensor_tensor(vtmp[:], vP[:], vT[:], ALU.subtract), rd=[bvP, bvT], wr=[bvtmp])
            c.op("pool", lambda e: e.tensor_tensor(vtmp[:], vtmp[:], muv[:], ALU.mult), rd=[bvtmp, bmuv], wr=[bvtmp])
            c.op("pool", lambda e: e.tensor_tensor(v32[:], vtmp[:], vT[:], ALU.add), rd=[bvtmp, bvT], wr=[bv32])
            c.op("pool", lambda e: e.tensor_copy(vb[:], v32[:]), rd=[bv32], wr=[bvb])
            c.op("act", lambda e: e.activation(th[0:64, :], xw[:], AF.Tanh), rd=[bxw], wr=[bth])
            c.op("dve", lambda e: e.tensor_copy(xab[0:64, :], xa[:]), rd=[bxa], wr=[bxab])
            c.op("act", lambda e: e.activation(sgt[:], xg0[:], AF.Tanh, scale=0.5), rd=[bxg0], wr=[bsgt])
            c.op("dve", lambda e: e.tensor_scalar(sg0[:], sgt[:], 0.5, 0.5, ALU.mult, ALU.add), rd=[bsgt], wr=[bsg0])
            c.op("act", lambda e: e.activation(sgt[0:32, :], xg1[:], AF.Tanh, scale=0.5), rd=[bxg1], wr=[bsgt])
            c.op("dve", lambda e: e.tensor_scalar(sg1[:], sgt[0:32, :], 0.5, 0.5, ALU.mult, ALU.add), rd=[bsgt], wr=[bsg1])
            p, bp = psA()
            c.group("pe", [lambda e, q=q, p=p: e.matmul(p[:, q * 128:(q + 1) * 128], wdec[:, q * 128:(q + 1) * 128], th[:], start=True, stop=True) for q in range(4)],
                    rd=[bwdec, bth], wr=[bp])
            c.op("act", lambda e, p=p: e.activation(tg[:], p[:], AF.Tanh, scale=0.5), rd=[bp], wr=[btg])
            c.op("dve", lambda e: e.tensor_scalar(logw[:].rearrange("p a b -> p (a b)"), tg[:], -0.5 * math.exp(-0.5), -0.5 * math.exp(-0.5), ALU.mult, ALU.add),
                 rd=[btg], wr=[blogw])
            for q in range(4):
                c.op("dve", lambda e, q=q: e.tensor_tensor_scan(lgi[:, q, :], logw[:, q, :], logw[:, q, :], 0.0, ALU.add, ALU.bypass), rd=[blogw], wr=[blgi])
            c.op("pool", lambda e: e.tensor_tensor(lge[:], lgi[:], logw[:], ALU.subtract), rd=[blgi, blogw], wr=[blge])
            yield
            p, bp = psA()
            c.group("pe", [lambda e, q=q, p=p: e.matmul(p[:, q * 128:(q + 1) * 128], waaa[:, q * 128:(q + 1) * 128], xab[:], start=True, stop=True) for q in range(4)],
                    rd=[bwaaa, bxab], wr=[bp])
            c.op("act", lambda e, p=p: e.activation(tg[:], p[:], AF.Tanh, scale=0.5), rd=[bp], wr=[btg])
            c.op("dve", lambda e: e.tensor_scalar(alr[:].rearrange("p a b -> p (a b)"), tg[:], 0.5, 0.5, ALU.mult, ALU.add), rd=[btg], wr=[balr])
            p, bp = psA()
            c.group("pe", [lambda e, p=p: e.matmul(p[:], sg0[:], wgate[:, 0, :], start=True, stop=False),
                           lambda e, p=p: e.matmul(p[:], sg1[:], wgate[0:32, 1, :], start=False, stop=True)], rd=[bsg0, bsg1, bwgate], wr=[bp])
            c.op("act", lambda e, p=p: e.copy(gT[:], p[:]), rd=[bp], wr=[bgT])
            for q in range(4):
                c.op("dve", lambda e, q=q: e.tensor_scalar(kkr[:, q, :], kF[:, q, :], kkf[:, q:q + 1], None, ALU.mult), rd=[bkF, bkkf], wr=[bkkr])
            c.op("pool", lambda e: e.tensor_tensor(sqb[:], kkr[:].rearrange("p a b -> p (a b)"), kkr[:].rearrange("p a b -> p (a b)"), ALU.mult), rd=[bkkr], wr=[bsqb])
            p, bp = psA()
            c.op("pe", lambda e, p=p: e.matmul(p[:], bones[:], sqb[:], start=True, stop=True), rd=[bbones, bsqb], wr=[bp])
            c.op("act", lambda e, p=p: e.activation(rn[:], p[:], AF.Ln), rd=[bp], wr=[brn])
            c.op("act", lambda e: e.activation(rn[:], rn[:], AF.Exp, scale=-0.5), rd=[brn], wr=[brn])
            c.op("dve", lambda e: e.tensor_tensor(kkn[:].rearrange("p a b -> p (a b)"), kkr[:].rearrange("p a b -> p (a b)"), rn[:], ALU.mult), rd=[bkkr, brn], wr=[bkkn])
            yield
            for q in range(4):
                c.op("dve", lambda e, q=q: e.tensor_scalar(fF[:, q, :], alr[:, q, :], kaf[:, q:q + 1], c0f[:, q:q + 1], ALU.mult, ALU.add), rd=[balr, bkaf, bc0f], wr=[bfF])
            c.op("pool", lambda e: e.tensor_tensor(kM[:], kF[:], fF[:], ALU.mult), rd=[bkF, bfF], wr=[bkM])
            c.op("act", lambda e: e.activation(gin[:], lgi[:], AF.Exp), rd=[blgi], wr=[bgin])
            c.op("act", lambda e: e.activation(ginv[:], lgi[:], AF.Exp, scale=-1.0), rd=[blgi], wr=[bginv])
            c.op("act", lambda e: e.activation(gex[:], lge[:], AF.Exp), rd=[blge], wr=[bgex])
            for q in range(4):
                c.op("act", lambda e, q=q: e.activation(Dd[:, q, :], lgi[:, q, :], AF.Exp, bias=lgi[:, q, 127:128], scale=-1.0), rd=[blgi], wr=[bDd])
            c.op("pool", lambda e: e.tensor_tensor(bF[:], kkn[:], alr[:], ALU.mult), rd=[bkkn, balr], wr=[bbF])
            for hh in range(2):
                r0, r1 = hh * 64, (hh + 1) * 64
                ARv = AR[r0:r1, :, :, :].rearrange("p (q two) a t -> p q two a t", two=2)[:, :, hh, :, :]
                Btv = Bt[r0:r1, :, :].rearrange("p (q two) t -> p q two t", two=2)[:, :, hh, :]
                Ktv = Kt[r0:r1, :, :].rearrange("p (q two) t -> p q two t", two=2)[:, :, hh, :]
                c.op("dve", lambda e: e.tensor_tensor(ARv[:, :, 1, :], rF[r0:r1, :, :], gin[r0:r1, :, :], ALU.mult), rd=[brF, bgin], wr=[bAR])
                c.op("dve", lambda e: e.scalar_tensor_tensor(ARv[:, :, 0, :], kkn[r0:r1, :, :], -1.0, gex[r0:r1, :, :], ALU.mult, ALU.mult), rd=[bkkn, bgex], wr=[bAR])
                c.op("dve", lambda e: e.tensor_tensor(Btv, bF[r0:r1, :, :], ginv[r0:r1, :, :], ALU.mult), rd=[bbF, bginv], wr=[bBt])
                c.op("pool", lambda e: e.tensor_tensor(Ktv, kM[r0:r1, :, :], ginv[r0:r1, :, :], ALU.mult), rd=[bkM, bginv], wr=[bKt])
            c.op("pool", lambda e: e.tensor_tensor(Bh[:], bF[:], Dd[:], ALU.mult), rd=[bbF, bDd], wr=[bBh])
            c.op("pool", lambda e: e.tensor_tensor(Kh[:], kM[:], Dd[:], ALU.mult), rd=[bkM, bDd], wr=[bKh])
            yield
            p, bp = psA(); pb = p[:].bitcast(BF16)
            c.group("pe", [lambda e, q=q: e.transpose(pb[:, q * 128:(q + 1) * 128], Bh[:, q, :], idb[:]) for q in range(4)] +
                          [lambda e, q=q: e.transpose(pb[:, 512 + q * 128:512 + (q + 1) * 128], Kh[:, q, :], idb[:]) for q in range(4)],
                    rd=[bBh, bKh, bidb], wr=[bp])
            c.op("act", lambda e: e.copy(BKhT[:], pb), rd=[bp], wr=[bBKhT])
            c.op("pool", lambda e: e.tensor_tensor(rk[:], rF[:], kM[:], ALU.mult), rd=[brF, bkM], wr=[brk])
            p, bp = psA()
            c.group("pe", [lambda e, q=q, p=p: e.matmul(p[:, 2 * q:2 * q + 2], rk[:, q, :], rkb[:, 2 * q:2 * q + 2], start=True, stop=True) for q in range(4)],
                    rd=[brk, brkb], wr=[bp])
            c.op("dve", lambda e, p=p: e.tensor_copy(coef[:], p[:, 0:8]), rd=[bp], wr=[bcoef])
            for q in range(4):
                pA, bpA = psA(); pB, bpB = psA(); pC, bpC = psA()
                fa, fb, fc = [], [], []
                for hh in range(2):
                    h = 2 * q + hh
                    arr = AR[:, h, :, :].rearrange("p a b -> p (a b)")
                    fa.append(lambda e, hh=hh, h=h, arr=arr: e.matmul(pA[:, hh * 256:(hh + 1) * 256], Bt[:, h, :], arr, start=True, stop=True))
                    fb.append(lambda e, hh=hh, h=h, arr=arr: e.matmul(pB[:, hh * 256:(hh + 1) * 256], Kt[:, h, :], arr, start=True, stop=True))
                    fc.append(lambda e, hh=hh, h=h: e.matmul(pC[:, hh * 128:(hh + 1) * 128], AR[:, h, 0, :], Bt[:, h, :], start=True, stop=True))
                c.group("pe", fa, rd=[bBt, bAR], wr=[bpA])
                c.group("pe", fb, rd=[bKt, bAR], wr=[bpB])
                c.group("pe", fc, rd=[bBt, bAR], wr=[bpC])
                c.op("dve", lambda e: e.tensor_tensor(MAB[:, 2 * q:2 * q + 2, :, :].rearrange("p a b c -> p (a b c)"), pA[:], MU2[:], ALU.mult), rd=[bpA, bMU2], wr=[bMAB])
                c.op("dve", lambda e: e.tensor_tensor(MAK[:, 2 * q:2 * q + 2, :, :].rearrange("p a b c -> p (a b c)"), pB[:], MU2[:], ALU.mult), rd=[bpB, bMU2], wr=[bMAK])
                c.op("dve", lambda e: e.tensor_tensor(MABT[:, 2 * q:2 * q + 2, :].rearrange("p a b -> p (a b)"), pC[:, 0:256], SL2[:], ALU.mult), rd=[bpC, bSL2], wr=[bMABT])
            yield

        def stageA2(ch):
            i = ch % 2
            j3 = ch % 3
            MAB, bMAB = MABs[j3]; MABT, bMABT = MABTs[j3]
            AC0, bAC0 = ACk[0]
            c.op("pool", lambda e: e.tensor_tensor(AC0[:], MAB[:, :, 0, :], bcast(idb[:], [128, 8, 128], 1), ALU.add), rd=[bMAB, bidb], wr=[bAC0])
            Pp = lambda h: MAB[:, h, 0, :]
            PTp = lambda h: MABT[:, h, :]
            bPp, bPTp = bMAB, bMABT
            ACp, bACp = AC0, bAC0
            for lv in range(1, 7):
                Pn, bPn = Pk[lv % 2]; PTn, bPTn = PTk[lv % 2]; ACn, bACn = (ACk[lv % 2] if lv < 6 else MinvS[i])
                for grp in range(2):
                    hs = range(4 * grp, 4 * grp + 4)
                    if lv < 6:
                        p, bp = psC()
                        c.group("pe", [lambda e, h=h, p=p, Pp=Pp, PTp=PTp: e.matmul(p[:, (h % 4) * 128:(h % 4 + 1) * 128], PTp(h), Pp(h), start=True, stop=True) for h in hs],
                                rd=[bPp, bPTp], wr=[bp])
                        en = ev()
                        if en == "act":
                            c.op("act", lambda e, p=p: e.copy(Pn[:, 4 * grp:4 * grp + 4, :].rearrange("p a b -> p (a b)"), p[:]), rd=[bp], wr=[bPn])
                        else:
                            c.op("dve", lambda e, p=p: e.tensor_copy(Pn[:, 4 * grp:4 * grp + 4, :].rearrange("p a b -> p (a b)"), p[:]), rd=[bp], wr=[bPn])
                    p, bp = psC()
                    c.group("pe", [lambda e, h=h, p=p, Pp=Pp, PTp=PTp: e.matmul(p[:, (h % 4) * 128:(h % 4 + 1) * 128], Pp(h), PTp(h), start=True, stop=True) for h in hs],
                            rd=[bPp, bPTp], wr=[bp])
                    en = ev()
                    if en == "act":
                        c.op("act", lambda e, p=p: e.copy(PTn[:, 4 * grp:4 * grp + 4, :].rearrange("p a b -> p (a b)"), p[:]), rd=[bp], wr=[bPTn])
                    else:
                        c.op("dve", lambda e, p=p: e.tensor_copy(PTn[:, 4 * grp:4 * grp + 4, :].rearrange("p a b -> p (a b)"), p[:]), rd=[bp], wr=[bPTn])
                    p, bp = psC()
                    c.group("pe", [lambda e, h=h, p=p, ACp=ACp: e.matmul(p[:, (h % 4) * 128:(h % 4 + 1) * 128], PTn[:, h, :], ACp[:, h, :], start=True, stop=True) for h in hs],
                            rd=[bACp, bPTn], wr=[bp])
                    c.op("dve", lambda e, p=p, ACp=ACp: e.tensor_tensor(ACn[:, 4 * grp:4 * grp + 4, :].rearrange("p a b -> p (a b)"), p[:],
                                                                      ACp[:, 4 * grp:4 * grp + 4, :].rearrange("p a b -> p (a b)"), ALU.add), rd=[bp, bACp], wr=[bACn])
                yield
                Pp = (lambda Pn: (lambda h: Pn[:, h, :]))(Pn)
                PTp = (lambda PTn: (lambda h: PTn[:, h, :]))(PTn)
                bPp, bPTp = bPn, bPTn
                ACp, bACp = ACn, bACn
            yield

        def stageB(ch):
            i = ch % 2; t0 = ch * 128
            j3 = ch % 3
            v32, bv32 = v32s[j3]; vb, bvb = vbs[j3]; gT, bgT = gTs[j3]; gin, bgin = gins[j3]; AR, bAR = ARs[j3]
            Minv, bMinv = MinvS[i]
            BKhT, bBKhT = BKhTs[j3]; coef, bcoef = coefs[j3]; MAB, bMAB = MABs[j3]; MAK, bMAK = MAKs[j3]; MABT, bMABT = MABTs[j3]
            yield
            pX, bpX = psB()
            fs = []
            for h in range(8):
                q, r0 = h // 2, (h % 2) * 64
                fs.append(lambda e, h=h: e.matmul(pX[:, h * 64:(h + 1) * 64], MAK[:, h, 0, :], vb[:, h * 64:(h + 1) * 64], start=True, stop=False))
                fs.append(lambda e, h=h, q=q: e.matmul(pX[:, h * 64:(h + 1) * 64], AR[:, h, 0, :], Sb[:, q, :], start=False, stop=True))
            c.group("pe", fs, rd=[bMAK, bvb, bAR, bSb], wr=[bpX])
            c.op("act", lambda e: e.copy(XT[:], pX[:]), rd=[bpX], wr=[bXT])
            yield
            pU, bpU = psB()
            c.group("pe", [lambda e, h=h: e.matmul(pU[:, h * 64:(h + 1) * 64], Minv[:, h, :], XT[:, h * 64:(h + 1) * 64], start=True, stop=True) for h in range(8)],
                    rd=[bMinv, bXT], wr=[bpU])
            c.op("dve", lambda e: e.tensor_copy(UT[:], pU[:]), rd=[bpU], wr=[bUT])
            yield
            pY, bpY = psB()
            fs = []
            for h in range(8):
                q, r0 = h // 2, (h % 2) * 64
                fs.append(lambda e, h=h: e.matmul(pY[:, h * 64:(h + 1) * 64], MAK[:, h, 1, :], vb[:, h * 64:(h + 1) * 64], start=True, stop=False))
                fs.append(lambda e, h=h: e.matmul(pY[:, h * 64:(h + 1) * 64], MAB[:, h, 1, :], UT[:, h * 64:(h + 1) * 64], start=False, stop=False))
                fs.append(lambda e, h=h, q=q: e.matmul(pY[:, h * 64:(h + 1) * 64], AR[:, h, 1, :], Sb[:, q, :], start=False, stop=True))
            c.group("pe", fs, rd=[bMAK, bMAB, bvb, bUT, bAR, bSb], wr=[bpY])
            yield
            pS, bpS = psB()
            fs = []
            for q in range(4):
                fs.append(lambda e, q=q: e.matmul(pS[:, q * 128:(q + 1) * 128], BKhT[:, q * 128:(q + 1) * 128], UT[:, q * 128:(q + 1) * 128], start=True, stop=False))
                fs.append(lambda e, q=q: e.matmul(pS[:, q * 128:(q + 1) * 128], BKhT[:, 512 + q * 128:512 + (q + 1) * 128], vb[:, q * 128:(q + 1) * 128], start=False, stop=True))
            c.group("pe", fs, rd=[bBKhT, bUT, bvb], wr=[bpS])
            pSv = pS[:].rearrange("p (q c) -> p q c", q=4)
            c.op("dve", lambda e: e.tensor_tensor(tmpS[:], S32[:], gin[:, :, 127:128].to_broadcast([128, 4, 64]), ALU.mult), rd=[bS32, bgin], wr=[btmpS])
            c.op("dve", lambda e: e.tensor_tensor(S32[0:64, :, :], tmpS[0:64, :, :], pSv[0:64, :, 0:64], ALU.add), rd=[btmpS, bpS], wr=[bS32])
            c.op("dve", lambda e: e.tensor_tensor(S32[64:128, :, :], tmpS[64:128, :, :], pSv[64:128, :, 64:128], ALU.add), rd=[btmpS, bpS], wr=[bS32])
            c.op("dve", lambda e: e.tensor_copy(Sb[:], S32[:]), rd=[bS32], wr=[bSb])
            yield
            pYv = pY[:].rearrange("p (h d) -> p h d", h=8)
            c.op("dve", lambda e: e.tensor_reduce(s1[:], pYv, AX.X, ALU.add), rd=[bpY], wr=[bs1])
            c.op("act", lambda e: e.activation(sqt[:], pY[:], AF.Square), rd=[bpY], wr=[bsqt])
            c.op("dve", lambda e: e.tensor_reduce(s2[:], sqt[:].rearrange("p (h d) -> p h d", h=8), AX.X, ALU.add), rd=[bsqt], wr=[bs2])
            c.op("dve", lambda e: e.tensor_scalar(mean[:], s1[:], 1.0 / 64, None, ALU.mult), rd=[bs1], wr=[bmean])
            c.op("dve", lambda e: e.tensor_tensor(var[:], mean[:], mean[:], ALU.mult), rd=[bmean], wr=[bvar])
            c.op("dve", lambda e: e.scalar_tensor_tensor(var[:], s2[:], 1.0 / 64, var[:], ALU.mult, ALU.subtract), rd=[bs2, bvar], wr=[bvar])
            c.op("dve", lambda e: e.tensor_scalar(var[:], var[:], 64e-5, None, ALU.add), rd=[bvar], wr=[bvar])
            c.op("act", lambda e: e.activation(var[:], var[:], AF.Ln), rd=[bvar], wr=[bvar])
            c.op("act", lambda e: e.activation(var[:], var[:], AF.Exp, scale=-0.5), rd=[bvar], wr=[bvar])
            ynv = yn[:].rearrange("p (h d) -> p h d", h=8)
            c.op("dve", lambda e: e.tensor_tensor(ynv, pYv, bcast(mean[:], [128, 8, 64], 2), ALU.subtract), rd=[bpY, bmean], wr=[byn])
            c.op("pool", lambda e: e.tensor_tensor(ynv, ynv, bcast(var[:], [128, 8, 64], 2), ALU.mult), rd=[byn, bvar], wr=[byn])
            c.op("pool", lambda e: e.tensor_tensor(yn[:], yn[:], lng[:], ALU.mult), rd=[byn, blng], wr=[byn])
            c.op("dve", lambda e: e.tensor_tensor(yn[:], yn[:], lnb[:], ALU.add), rd=[byn, blnb], wr=[byn])
            c.op("pool", lambda e: e.tensor_tensor(bon[:].rearrange("p (h d) -> p h d", h=8), v32[:].rearrange("p (h d) -> p h d", h=8), bcast(coef[:], [128, 8, 64], 2), ALU.mult),
                 rd=[bv32, bcoef], wr=[bbon])
            c.op("dve", lambda e: e.tensor_tensor(yn[:], yn[:], bon[:], ALU.add), rd=[byn, bbon], wr=[byn])
            c.op("dve", lambda e: e.tensor_tensor(yab[:], yn[:], gT[:], ALU.mult), rd=[byn, bgT], wr=[byab])
            p, bp = psB(); pb = p[:].bitcast(BF16)
            c.group("pe", [lambda e, q=q: e.transpose(pb[:, q * 128:(q + 1) * 128], yab[:, q * 128:(q + 1) * 128], idb[:]) for q in range(4)], rd=[byab, bidb], wr=[bp])
            c.op("act", lambda e: e.copy(yaT[:].rearrange("p a b -> p (a b)"), pb[:, 0:512]), rd=[bp], wr=[byaT])
            c.dma("act", yaFv[:, :, t0:t0 + 128], yaT[:], rd=[byaT])
        def run(gens):
            alive = [g for g in gens if g is not None]
            while alive:
                for g in list(alive):
                    try:
                        next(g)
                    except StopIteration:
                        alive.remove(g)
        run([stageA(0)])
        run([stageA2(0), stageA(1) if NCH > 1 else None])
        for ch in range(NCH):
            run([stageB(ch), stageA2(ch + 1) if ch + 1 < NCH else None, stageA(ch + 2) if ch + 2 < NCH else None])
        c.barrier()

    if upto <= 2:
        print("ninst", c.ninst, "nwait", c.nwait); return nc
    with ExitStack() as es:
        def T(name, shape, dt=F32):
            return es.enter_context(nc.sbuf_tensor("s_" + name, list(shape), dt)), Buf(name)
        pst["n"] = 2; pst["i"] = 0; pst["off"] = 4
        pO = [PSB[6], PSB[7]]
        qst = {"i": 0}
        def psq():
            qst["i"] += 1
            return PSB[qst["i"] % 4]
        idb, bidb = T("idb3", [128, 128], BF16); c.dma("pool", idb[:], ident_d, wr=[bidb])
        CB, bCB = T("CB", [128, 2, 256], BF16); c.dma("pool", CB[:].rearrange("p a b -> p (a b)"), cbias2_d, wr=[bCB])
        s65, bs65 = T("s65", [65, 64], BF16); c.dma("pool", s65[:], sel65_d, wr=[bs65])
        pok, bpok = T("pok", [128, 16, 16]); c.dma("sp", pok[:].rearrange("p a b -> p (a b)"), pok_d.partition_broadcast(128), wr=[bpok])
        pbi, bpbi = T("pbi", [128, 16, 16]); c.dma("sp", pbi[:].rearrange("p a b -> p (a b)"), pbias_d.partition_broadcast(128), wr=[bpbi])
        invf, binvf = T("invf", [128, 8]); c.dma("sp", invf[:], invf_d.partition_broadcast(128), wr=[binvf])
        qkg, bqkg = T("qkg", [128, 1024]); c.dma("sp", qkg[:], qkg_d.partition_broadcast(128), wr=[bqkg])
        posi, bposi = T("posi", [128, NT], I32); c.dma("sp", posi[:], pos, wr=[bposi])
        posf, bposf = T("posf", [128, NT])
        c.op("dve", lambda e: e.tensor_copy(posf[:], posi[:]), rd=[bposi], wr=[bposf])
        yy, byy = T("yy", [128, NT, 8]); yi, byi = T("yi", [128, NT, 8], I32); yf, byf = T("yf", [128, NT, 8])
        sinT, bsinT = T("sinT", [128, NT, 8]); cosT, bcosT = T("cosT", [128, NT, 8])
        c.op("dve", lambda e: e.tensor_copy(yy[:], bcast(invf[:], [128, NT, 8], 1)), rd=[binvf], wr=[byy])
        c.op("dve", lambda e: e.tensor_tensor(yy[:], yy[:], bcast(posf[:], [128, NT, 8], 2), ALU.mult), rd=[byy, bposf], wr=[byy])
        for (dst, bdst, off) in ((sinT, bsinT, 0.0), (cosT, bcosT, 0.25)):
            if off != 0.0:
                c.op("dve", lambda e: e.tensor_scalar(yy[:], yy[:], off, None, ALU.add), rd=[byy], wr=[byy])
            c.op("dve", lambda e: e.tensor_copy(yi[:], yy[:]), rd=[byy], wr=[byi])
            c.op("dve", lambda e: e.tensor_copy(yf[:], yi[:]), rd=[byi], wr=[byf])
            c.op("dve", lambda e: e.tensor_tensor(yf[:], yy[:], yf[:], ALU.subtract), rd=[byy, byf], wr=[byf])
            c.op("act", lambda e, dst=dst: e.activation(dst[:], yf[:], AF.Sin, scale=2.0 * math.pi), rd=[byf], wr=[bdst])
        KT = es.enter_context(nc.sbuf_tensor("s_KT", [80, 8, S], BF16)); bKT = [Buf("KT%d" % g) for g in range(8)]
        VA = es.enter_context(nc.sbuf_tensor("s_VA", [128, NT, 8, 65], BF16)); bVA = [Buf("VA%d" % g) for g in range(8)]
        c.op("pool", lambda e: e.memset(VA[:], 1.0), wr=bVA)
        kmT, bkmT = T("kmT", [64, 8, 16], BF16); c.op("dve", lambda e: e.memset(kmT[:], 0.0), wr=[bkmT])
        km32, bkm32 = T("km32", [64, 8])
        qkvs = [T("qkv%d" % i, [128, 1536]) for i in range(2)]
        sq3, bsq3 = T("sq3", [128, 1024]); ss3, bss3 = T("ss3", [128, 16]); qn, bqn = T("qn", [128, 16, 64])
        rt, brt = T("rt", [128, 4, 16, 8])
        QA, bQA = T("QA", [128, 8, 80], BF16); KA, bKA = T("KA", [128, 8, 80], BF16)
        QT, bQT = T("QT", [64, 8, 128], BF16)
        QTAs = [T("QTA%d" % i, [80, 8, 512], BF16) for i in range(2)]
        gm, bgm = T("gm", [128, 8, 16]); mx, bmx = T("mx", [128, 8, 8]); sel, bsel = T("sel", [128, 8, 16])
        PTs = [T("PT%d" % i, [128, 512], BF16) for i in range(4)]
        OT, bOT = T("OT", [65, 512]); rr, brr = T("rr", [65, 512], BF16); r32, br32 = T("r32", [65, 512])
        c.op("dve", lambda e: e.memset(rr[:], 0.0), wr=[brr])
        yTs = [T("yT%d" % i, [64, 512], BF16) for i in range(2)]
        cnt3 = {"pt": 0, "ld": 0}

        def ld3(tt):
            c.dma("act", qkvs[tt % 2][0][:], zT[tt * 128:(tt + 1) * 128, 512:2048], wr=[qkvs[tt % 2][1]])

        def pre3(tt):
            if tt + 1 < NT:
                ld3(tt + 1)
            qkv, bqkv = qkvs[tt % 2]
            t0 = tt * 128; qb = tt // 2; g = tt // 4; ti = tt % 4
            QTA, bQTA = QTAs[g % 2]
            c.op("act", lambda e: e.activation(sq3[:], qkv[:, 0:1024], AF.Square), rd=[bqkv], wr=[bsq3])
            c.op("dve", lambda e: e.tensor_reduce(ss3[:], sq3[:].rearrange("p (h d) -> p h d", h=16), AX.X, ALU.add), rd=[bsq3], wr=[bss3])
            c.op("dve", lambda e: e.tensor_scalar(ss3[:], ss3[:], 1.0 / 64, 1e-6, ALU.mult, ALU.add), rd=[bss3], wr=[bss3])
            c.op("act", lambda e: e.activation(ss3[:], ss3[:], AF.Ln), rd=[bss3], wr=[bss3])
            c.op("act", lambda e: e.activation(ss3[:], ss3[:], AF.Exp, scale=-0.5), rd=[bss3], wr=[bss3])
            c.op("dve", lambda e: e.tensor_tensor(qn[:], qkv[:, 0:1024].rearrange("p (h d) -> p h d", h=16), bcast(ss3[:], [128, 16, 64], 2), ALU.mult), rd=[bqkv, bss3], wr=[bqn])
            c.op("pool", lambda e: e.tensor_tensor(qn[:].rearrange("p h d -> p (h d)"), qn[:].rearrange("p h d -> p (h d)"), qkg[:], ALU.mult), rd=[bqn, bqkg], wr=[bqn])
            cs = cosT[:, tt:tt + 1, :].to_broadcast([128, 16, 8]); sn = sinT[:, tt:tt + 1, :].to_broadcast([128, 16, 8])
            c.op("dve", lambda e: e.tensor_tensor(rt[:, 0, :, :], qn[:, :, 0:8], cs, ALU.mult), rd=[bqn, bcosT], wr=[brt])
            c.op("dve", lambda e: e.tensor_tensor(rt[:, 1, :, :], qn[:, :, 8:16], sn, ALU.mult), rd=[bqn, bsinT], wr=[brt])
            c.op("pool", lambda e: e.tensor_tensor(rt[:, 2, :, :], qn[:, :, 8:16], cs, ALU.mult), rd=[bqn, bcosT], wr=[brt])
            c.op("pool", lambda e: e.tensor_tensor(rt[:, 3, :, :], qn[:, :, 0:8], sn, ALU.mult), rd=[bqn, bsinT], wr=[brt])
            c.op("dve", lambda e: e.tensor_tensor(qn[:, :, 0:8], rt[:, 0, :, :], rt[:, 1, :, :], ALU.subtract), rd=[brt], wr=[bqn])
            c.op("dve", lambda e: e.tensor_tensor(qn[:, :, 8:16], rt[:, 2, :, :], rt[:, 3, :, :], ALU.add), rd=[brt], wr=[bqn])
            c.op("dve", lambda e: e.tensor_copy(QA[:, :, 0:64], qn[:, 0:8, :]), rd=[bqn], wr=[bQA])
            c.op("pool", lambda e: e.tensor_copy(KA[:, :, 0:64], qn[:, 8:16, :]), rd=[bqn], wr=[bKA])
            c.op("pool", lambda e: e.memset(KA[:, :, 64:80], 0.0), wr=[bKA])
            c.op("pool", lambda e: e.memset(KA[:, :, 64 + qb:65 + qb], 1.0), wr=[bKA])
            p, bp = ps(); pb = p[:].bitcast(BF16)
            c.group("pe", [lambda e, h=h: e.transpose(pb[0:80, h * 128:(h + 1) * 128], KA[:, h, :], idb[:]) for h in range(8)], rd=[bKA, bidb], wr=[bp])
            c.op("dve", lambda e: e.tensor_copy(KT[:, :, t0:t0 + 128], pb[0:80, :].rearrange("p (h t) -> p h t", h=8)), rd=[bp], wr=[bKT[g]])
            c.op("pool", lambda e: e.tensor_copy(VA[:, tt, :, 0:64], qkv[:, 1024:1536].rearrange("p (h d) -> p h d", h=8)), rd=[bqkv], wr=[bVA[g]])
            if qb > 0:
                p, bp = ps(); pb = p[:].bitcast(BF16)
                c.group("pe", [lambda e, h=h: e.transpose(pb[0:64, h * 128:(h + 1) * 128], QA[:, h, 0:64], idb[:]) for h in range(8)], rd=[bQA, bidb], wr=[bp])
                c.op("dve", lambda e: e.tensor_copy(QT[:], pb[0:64, :].rearrange("p (h t) -> p h t", h=8)), rd=[bp], wr=[bQT])
                p, bp = ps()
                c.group("pe", [lambda e, h=h, p=p: e.matmul(p[:, h * 16:(h + 1) * 16], QT[:, h, :], kmT[:, h, :], start=True, stop=True) for h in range(8)],
                        rd=[bQT, bkmT], wr=[bp])
                c.op("dve", lambda e, p=p: e.tensor_tensor(gm[:], p[:, 0:128].rearrange("p (h j) -> p h j", h=8), pbi[:, qb:qb + 1, :].to_broadcast([128, 8, 16]), ALU.add),
                     rd=[bp, bpbi], wr=[bgm])
                for h in range(8):
                    c.op("dve", lambda e, h=h: e.max(mx[:, h, :], gm[:, h, :]), rd=[bgm], wr=[bmx])
                c.op("dve", lambda e: e.tensor_tensor(sel[:], gm[:], mx[:, :, 2:3].to_broadcast([128, 8, 16]), ALU.is_ge), rd=[bgm, bmx], wr=[bsel])
                c.op("dve", lambda e: e.tensor_tensor(sel[:], sel[:], pok[:, qb:qb + 1, :].to_broadcast([128, 8, 16]), ALU.mult), rd=[bsel, bpok], wr=[bsel])
                c.op("dve", lambda e: e.tensor_scalar(QA[:, :, 64:80], sel[:], BIG, -BIG, ALU.mult, ALU.add), rd=[bsel], wr=[bQA])
            else:
                c.op("dve", lambda e: e.memset(QA[:, :, 64:80], -BIG), wr=[bQA])
            p, bp = ps(); pb = p[:].bitcast(BF16)
            c.group("pe", [lambda e, h=h: e.transpose(pb[0:80, h * 128:(h + 1) * 128], QA[:, h, :], idb[:]) for h in range(8)], rd=[bQA, bidb], wr=[bp])
            c.op("dve", lambda e: e.tensor_copy(QTA[:, :, ti * 128:(ti + 1) * 128], pb[0:80, :].rearrange("p (h t) -> p h t", h=8)), rd=[bp], wr=[bQTA])
            if tt % 2 == 1:
                c.op("dve", lambda e: e.tensor_reduce(km32[:], KT[0:64, :, qb * 256:(qb + 1) * 256], AX.X, ALU.add), rd=[bKT[g]], wr=[bkm32])
                c.op("dve", lambda e: e.tensor_scalar(kmT[:, :, qb], km32[:], 1.0 / 256, None, ALU.mult), rd=[bkm32], wr=[bkmT])

        s65f, bs65f = T("s65f", [65, 64]); c.dma("sp", s65f[:], sel65_d, wr=[bs65f])

        def steps3(g, h):
            QTA, bQTA = QTAs[g % 2]
            pOh, bOh = pO[h % 2]
            kbufs = bKT[0:g + 1]; vbufs = bVA[0:g + 1]
            st = []
            nk = 4 * g + 2
            for kt in range(nk):
                d = {}
                def qk(d=d, kt=kt):
                    d["p"], d["bp"] = psq()
                    c.op("pe", lambda e: e.matmul(d["p"][:], KT[:, h, kt * 128:(kt + 1) * 128], QTA[:, h, :], start=True, stop=True), rd=kbufs + [bQTA], wr=[d["bp"]])
                def ex(d=d):
                    d["PT"], d["bPT"] = PTs[cnt3["pt"] % 4]; cnt3["pt"] += 1
                    c.op("act", lambda e: e.activation(d["PT"][:], d["p"][:], AF.Exp, scale=0.125), rd=[d["bp"]], wr=[d["bPT"]])
                def pv(d=d, kt=kt):
                    c.op("pe", lambda e: e.matmul(pOh[0:65, :], VA[:, kt, h, :], d["PT"][:], start=(kt == 0), stop=False), rd=vbufs + [d["bPT"]], wr=[bOh])
                st.append((qk, ex, pv))
            for half in range(2):
                for kti in range(2):
                    kt = 4 * g + 2 * half + kti
                    qs = slice(half * 256, (half + 1) * 256)
                    last = (half == 1 and kti == 1)
                    d = {}
                    def qk(d=d, kt=kt, qs=qs, kti=kti):
                        d["p"], d["bp"] = psq()
                        c.group("pe", [lambda e: e.matmul(d["p"][:, 0:256], KT[0:64, h, kt * 128:(kt + 1) * 128], QTA[0:64, h, qs], start=True, stop=False),
                                       lambda e: e.matmul(d["p"][:, 0:256], idb[:], CB[:, kti, :], start=False, stop=True)], rd=kbufs + [bQTA, bidb, bCB], wr=[d["bp"]])
                    def ex(d=d):
                        d["PT"], d["bPT"] = PTs[cnt3["pt"] % 4]; cnt3["pt"] += 1
                        c.op("act", lambda e: e.activation(d["PT"][:, 0:256], d["p"][:, 0:256], AF.Exp, scale=0.125), rd=[d["bp"]], wr=[d["bPT"]])
                    def pv(d=d, kt=kt, qs=qs, last=last):
                        c.op("pe", lambda e: e.matmul(pOh[0:65, qs], VA[:, kt, h, :], d["PT"][:, 0:256], start=False, stop=last), rd=vbufs + [d["bPT"]], wr=[bOh])
                    st.append((qk, ex, pv))

            def fin():
                c.op("dve", lambda e: e.tensor_copy(OT[:], pOh[0:65, :]), rd=[bOh], wr=[bOT])
                p, bp = ps()
                c.op("pe", lambda e: e.matmul(p[0:64, :], s65f[:], OT[:], start=True, stop=True), rd=[bs65f, bOT], wr=[bp])
                yT, byT = yTs[h % 2]
                c.op("dve", lambda e: e.reciprocal(r32[0:64, :], p[0:64, :]), rd=[bp], wr=[br32])
                c.op("dve", lambda e: e.tensor_tensor(yT[:], OT[0:64, :], r32[0:64, :], ALU.mult), rd=[bOT, br32], wr=[byT])
                c.dma("sp", ybF[h * 64:(h + 1) * 64, g * 512:(g + 1) * 512], yT[:], rd=[byT])
            return st, fin

        ld3(0)
        for tt in range(4):
            pre3(tt)
        LOOK = 2
        for g in range(8):
            allst = []
            for h in range(8):
                st, fin = steps3(g, h)
                for i, s in enumerate(st):
                    allst.append((s, fin if i == len(st) - 1 else None, h))
            n = len(allst)
            for i in range(min(LOOK, n)):
                allst[i][0][0]()
            for i in range(n):
                (qk, ex, pv), fin, h = allst[i]
                ex()
                if i + LOOK < n:
                    allst[i + LOOK][0][0]()
                pv()
                if fin is not None:
                    fin()
                    if g + 1 < 8 and h % 2 == 1:
                        pre3(4 * (g + 1) + h // 2)
        pst["n"] = 8; pst["off"] = 0
        c.barrier()

    if upto <= 3:
        print("ninst", c.ninst, "nwait", c.nwait); return nc
    with ExitStack() as es:
        def T(name, shape, dt=F32):
            return es.enter_context(nc.sbuf_tensor("s_" + name, list(shape), dt)), Buf(name)
        idb, bidb = T("idb4", [128, 128], BF16); c.dma("pool", idb[:], ident_d, wr=[bidb])
        wba, bwba = T("wba", [128, 4, DM], BF16); wbb, bwbb = T("wbb", [128, 4, DM], BF16); wo, bwo = T("wo", [128, 8, DM], BF16)
        for k in range(4):
            c.dma("pool", wba[:, k, :], wba_d[k * 128:(k + 1) * 128, :], wr=[bwba])
            c.dma("pool", wbb[:, k, :], wbb_d[k * 128:(k + 1) * 128, :], wr=[bwbb])
        for k in range(8):
            c.dma("pool", wo[:, k, :], wout_d[k * 128:(k + 1) * 128, :], wr=[bwo])
        xts = [T("x4%d" % i, [128, DM]) for i in range(2)]
        gps = [T("gp%d" % i, [128, 2048]) for i in range(2)]
        yas = [T("ya4%d" % i, [128, 4, 128], BF16) for i in range(2)]
        ybs = [T("yb4%d" % i, [128, 4, 128], BF16) for i in range(2)]
        m1, bm1 = T("m1", [128, DM]); m2, bm2 = T("m2", [128, DM]); mb, bmb = T("mb", [128, DM], BF16)
        mT, bmT = T("mT", [128, 8, 128], BF16)
        x1, bx1 = T("x1", [128, DM]); junk, bjunk = T("junk4", [128, DM]); h2, bh2 = T("h2", [128, DM], BF16)
        h2T, bh2T = T("h2T", [128, 8, 128], BF16)
        ss, bss = T("ss4", [128, NT]); c.op("dve", lambda e: e.memset(ss[:], 0.0), wr=[bss])
        rs, brs = T("rs4", [128, NT])
        yaFv = yaF.rearrange("(c p) t -> p c t", p=128); ybFv = ybF.rearrange("(c p) t -> p c t", p=128)
        h2Fv = h2F.rearrange("(c p) t -> p c t", p=128)

        def loads4(tt):
            i = tt % 2; t0 = tt * 128
            c.dma("sp", xts[i][0][:], x[t0:t0 + 128, :], wr=[xts[i][1]])
            c.dma("sp", gps[i][0][:], zT[t0:t0 + 128, 2048:4096], wr=[gps[i][1]])
            c.dma("sp", yas[i][0][:], yaFv[:, :, t0:t0 + 128], wr=[yas[i][1]])
            c.dma("sp", ybs[i][0][:], ybFv[:, :, t0:t0 + 128], wr=[ybs[i][1]])
        loads4(0)
        for tt in range(NT):
            if tt + 1 < NT:
                loads4(tt + 1)
            i = tt % 2; t0 = tt * 128
            xt, bxt = xts[i]; gp, bgp = gps[i]; ya, bya = yas[i]; yb, byb = ybs[i]
            c.op("act", lambda e: e.activation(gp[:], gp[:], AF.Sigmoid), rd=[bgp], wr=[bgp])
            for half in range(2):
                hs = slice(half * 512, (half + 1) * 512)
                pa, bpa = ps()
                c.group("pe", [lambda e, k=k: e.matmul(pa[:], ya[:, k, :], wba[:, k, hs], start=(k == 0), stop=(k == 3)) for k in range(4)], rd=[bya, bwba], wr=[bpa])
                c.op("dve", lambda e: e.tensor_tensor(m1[:, hs], pa[:], gp[:, half * 512:(half + 1) * 512], ALU.mult), rd=[bpa, bgp], wr=[bm1])
                pb_, bpb_ = ps()
                c.group("pe", [lambda e, k=k: e.matmul(pb_[:], yb[:, k, :], wbb[:, k, hs], start=(k == 0), stop=(k == 3)) for k in range(4)], rd=[byb, bwbb], wr=[bpb_])
                c.op("dve", lambda e: e.tensor_tensor(m2[:, hs], pb_[:], gp[:, 1024 + half * 512:1024 + (half + 1) * 512], ALU.mult), rd=[bpb_, bgp], wr=[bm2])
            c.op("dve", lambda e: e.tensor_tensor(mb[:], m1[:], m2[:], ALU.add), rd=[bm1, bm2], wr=[bmb])
            p, bp = ps(); pb = p[:].bitcast(BF16)
            c.group("pe", [lambda e, k=k: e.transpose(pb[:, k * 128:(k + 1) * 128], mb[:, k * 128:(k + 1) * 128], idb[:]) for k in range(8)], rd=[bmb, bidb], wr=[bp])
            c.op("act", lambda e: e.copy(mT[:].rearrange("p a b -> p (a b)"), pb), rd=[bp], wr=[bmT])
            for half in range(2):
                hs = slice(half * 512, (half + 1) * 512)
                po, bpo = ps()
                c.group("pe", [lambda e, k=k: e.matmul(po[:], mT[:, k, :], wo[:, k, hs], start=(k == 0), stop=(k == 7)) for k in range(8)], rd=[bmT, bwo], wr=[bpo])
                c.op("dve", lambda e: e.tensor_tensor(x1[:, hs], po[:], xt[:, hs], ALU.add), rd=[bpo, bxt], wr=[bx1])
            c.dma("pool", x1s[t0:t0 + 128, :], x1[:], rd=[bx1])
            c.op("act", lambda e: e.activation(junk[:], x1[:], AF.Square, accum_out=ss[:, tt:tt + 1]), rd=[bx1], wr=[bjunk, bss])
            c.op("dve", lambda e: e.tensor_scalar(rs[:, tt:tt + 1], ss[:, tt:tt + 1], 1.0 / DM, 1e-6, ALU.mult, ALU.add), rd=[bss], wr=[brs])
            c.op("act", lambda e: e.activation(rs[:, tt:tt + 1], rs[:, tt:tt + 1], AF.Ln), rd=[brs], wr=[brs])
            c.op("act", lambda e: e.activation(rs[:, tt:tt + 1], rs[:, tt:tt + 1], AF.Exp, scale=-0.5), rd=[brs], wr=[brs])
            c.op("dve", lambda e: e.tensor_scalar(h2[:], x1[:], rs[:, tt:tt + 1], None, ALU.mult), rd=[bx1, brs], wr=[bh2])
            p, bp = ps(); pb = p[:].bitcast(BF16)
            c.group("pe", [lambda e, k=k: e.transpose(pb[:, k * 128:(k + 1) * 128], h2[:, k * 128:(k + 1) * 128], idb[:]) for k in range(8)], rd=[bh2, bidb], wr=[bp])
            c.op("act", lambda e: e.copy(h2T[:].rearrange("p a b -> p (a b)"), pb), rd=[bp], wr=[bh2T])
            c.dma("pool", h2Fv[:, :, t0:t0 + 128], h2T[:], rd=[bh2T])
        c.barrier()

    if upto <= 4:
        print("ninst", c.ninst, "nwait", c.nwait); return nc
    with ExitStack() as es:
        def T(name, shape, dt=F32):
            return es.enter_context(nc.sbuf_tensor("s_" + name, list(shape), dt)), Buf(name)
        wup, bwup = T("wup", [128, 8, 2 * DFF], BF16); wdn, bwdn = T("wdn", [128, NFF, DM], BF16)
        g2, bg2 = T("g2", [128, 8]); c.dma("sp", g2[:], n2g, wr=[bg2])
        wub = [Buf("wup_%d" % k) for k in range(8)]
        for k in range(8):
            c.dma("pool", wup[:, k, :], wup_d[k * 128:(k + 1) * 128, :], wr=[wub[k]])
            c.op("dve", lambda e, k=k: e.tensor_scalar(wup[:, k, :], wup[:, k, :], g2[:, k:k + 1], None, ALU.mult), rd=[wub[k], bg2], wr=[wub[k]])
        for f in range(NFF):
            c.dma("pool", wdn[:, f, :], wdn_d[f * 128:(f + 1) * 128, :], wr=[bwdn])
        cw, bcw = T("cw", [128, NFF, 3]); c.dma("sp", cw[:].rearrange("p a b -> p (a b)"), cw_d, wr=[bcw])
        cbt, bcbt = T("cbt", [128, NFF]); c.dma("sp", cbt[:], cb_d, wr=[bcbt])
        cr, bcr = T("cr", [128, NFF, 2]); c.op("dve", lambda e: e.memset(cr[:], 0.0), wr=[bcr])
        h2s = [T("h2s%d" % i, [128, 8, 512], BF16) for i in range(2)]
        asb = [T("asb%d" % i, [128, 514]) for i in range(2)]
        acc = [T("acc%d" % i, [128, 512]) for i in range(2)]
        hg = es.enter_context(nc.sbuf_tensor("hg", [128, NFF, 512], BF16)); bhg = [Buf("hg%d" % f) for f in range(NFF)]
        x1t = [T("x1t%d" % i, [128, DM]) for i in range(2)]
        ost = [T("ost%d" % i, [128, DM]) for i in range(2)]
        h2Fv = h2F.rearrange("(c p) t -> p c t", p=128)
        c.dma("sp", h2s[0][0][:], h2Fv[:, :, 0:512], wr=[h2s[0][1]])
        nx = 0
        for st in range(8):
            if st + 1 < 8:
                c.dma("sp", h2s[(st + 1) % 2][0][:], h2Fv[:, :, (st + 1) * 512:(st + 2) * 512], wr=[h2s[(st + 1) % 2][1]])
            hT, bhT = h2s[st % 2]
            for f in range(NFF):
                pa, bpa = ps()
                c.group("pe", [lambda e, k=k: e.matmul(pa[:], wup[:, k, f * 128:(f + 1) * 128], hT[:, k, :], start=(k == 0), stop=(k == 7)) for k in range(8)], rd=[bhT] + wub, wr=[bpa])
                pg, bpg = ps()
                c.group("pe", [lambda e, k=k: e.matmul(pg[:], wup[:, k, DFF + f * 128:DFF + (f + 1) * 128], hT[:, k, :], start=(k == 0), stop=(k == 7)) for k in range(8)], rd=[bhT] + wub, wr=[bpg])
                a, ba = asb[f % 2]; ac, bac = acc[f % 2]
                c.op("act", lambda e: e.copy(a[:, 0:2], cr[:, f, :]), rd=[bcr], wr=[ba])
                c.op("act", lambda e: e.copy(a[:, 2:514], pa[:]), rd=[bpa], wr=[ba])
                c.op("act", lambda e: e.copy(cr[:, f, :], a[:, 512:514]), rd=[ba], wr=[bcr])
                c.op("dve", lambda e: e.tensor_scalar(ac[:], a[:, 0:512], cw[:, f, 0:1], None, ALU.mult), rd=[ba, bcw], wr=[bac])
                c.op("dve", lambda e: e.scalar_tensor_tensor(ac[:], a[:, 1:513], cw[:, f, 1:2], ac[:], ALU.mult, ALU.add), rd=[ba, bcw, bac], wr=[bac])
                c.op("dve", lambda e: e.scalar_tensor_tensor(ac[:], a[:, 2:514], cw[:, f, 2:3], ac[:], ALU.mult, ALU.add), rd=[ba, bcw, bac], wr=[bac])
                c.op("act", lambda e: e.activation(ac[:], ac[:], AF.Gelu, bias=cbt[:, f:f + 1]), rd=[bac, bcbt], wr=[bac])
                c.op("dve", lambda e: e.tensor_tensor(hg[:, f, :], ac[:], pg[:], ALU.mult), rd=[bac, bpg], wr=[bhg[f]])
            for sub in range(4):
                tt = st * 4 + sub; t0 = tt * 128
                xx, bxx = x1t[nx % 2]; oo, boo = ost[nx % 2]; nx += 1
                c.dma("sp", xx[:], x1s[t0:t0 + 128, :], wr=[bxx])
                for half in range(2):
                    hs = slice(half * 512, (half + 1) * 512)
                    po, bpo = ps()
                    c.group("pe", [lambda e, f=f: e.matmul(po[:], hg[:, f, sub * 128:(sub + 1) * 128], wdn[:, f, hs], start=(f == 0), stop=(f == NFF - 1)) for f in range(NFF)],
                            rd=bhg + [bwdn], wr=[bpo])
                    c.op("dve", lambda e: e.tensor_tensor(oo[:, hs], po[:], xx[:, hs], ALU.add), rd=[bpo, bxx], wr=[boo])
                c.dma("pool", out[t0:t0 + 128, :], oo[:], rd=[boo])
        c.barrier()
    print("ninst", c.ninst, "nwait", c.nwait, {e: c.cnt[e] for e in c.cnt})
    return nc


def _consts():
    i = np.arange(128)
    su = (i[:, None] < i[None, :]).astype(np.float32)
    ui = (i[:, None] <= i[None, :]).astype(np.float32)
    mu2 = np.concatenate([su, ui, su, ui], axis=1)
    sl = (i[None, :] < i[:, None]).astype(np.float32)
    sl2 = np.concatenate([sl, sl], axis=1)
    bones = ((i[:, None] // 64) == (i[None, :] // 64)).astype(np.float32)
    q2 = np.arange(256)
    cb0 = np.where(i[:, None] > q2[None, :], -BIG, 0.0).astype(np.float32)
    cb1 = np.where(i[:, None] + 128 > q2[None, :], -BIG, 0.0).astype(np.float32)
    cbias2 = np.concatenate([cb0, cb1], axis=1)
    sel65 = np.zeros((65, 64), np.float32); sel65[64, :] = 1.0
    j = np.arange(16)
    pok = (j[None, :] < j[:, None]).astype(np.float32)
    pbias = ((pok - 1.0) * 1e30).astype(np.float32)
    half = 8
    invf = (500000.0 ** (-np.arange(half, dtype=np.float32) / half)).astype(np.float32) / np.float32(2.0 * math.pi)
    return dict(ident=np.eye(128, dtype=np.float32), mu2=mu2, sl2=sl2, bones=bones, cbias2=cbias2, sel65=sel65,
                pok=pok.reshape(1, 256), pbias=pbias.reshape(1, 256), invf=invf.reshape(1, 8).astype(np.float32))


def _fm(v, n):
    return np.ascontiguousarray(np.asarray(v, np.float32).reshape(n, 128).T)


def _prep(inp):
    f = lambda a: np.ascontiguousarray(np.asarray(a, dtype=np.float32))
    w_in = f(inp["w_in"][0])
    fm_cols = np.r_[0:512, 512:1024, 1536:1600, 1600:1664, 1664:1824]
    tm_cols = np.r_[1024:1536, 1824:3360, 3360:5408]
    mu = f(inp["rwkv_mu"][0])
    mu_fm = np.zeros(1408, np.float32); mu_fm[:FMC] = mu[fm_cols]
    rk = f(inp["rwkv_r_k"][0])
    rkb = np.zeros((128, 8), np.float32)
    for h in range(8):
        rkb[(h % 2) * 64:(h % 2 + 1) * 64, h] = rk[h]
    cw = f(inp["ffn_conv_w"][0])
    cwl = np.ascontiguousarray(cw.reshape(3, NFF, 128).transpose(2, 1, 0)).reshape(128, NFF * 3)
    shared = dict(
        w_in=np.ascontiguousarray(w_in[:, np.r_[fm_cols, tm_cols]]),
        n1g=_fm(inp["norm1_g"][0], 8), mu_fm=_fm(mu_fm, 11), mu_v=f(mu[1024:1536]).reshape(1, 512),
        wdec=np.concatenate([f(inp["w_decay_up"][0]), f(inp["decay_bias"][0]).reshape(1, 512)], 0),
        waaa=np.concatenate([f(inp["w_aaa_up"][0]), f(inp["aaa_bias"][0]).reshape(1, 512)], 0),
        wgate=f(inp["w_gate_up"][0]), kk_fm=_fm(inp["rwkv_k_k"][0], 4), ka_fm=_fm(inp["rwkv_k_a"][0], 4), rkb=rkb,
        lng=f(inp["rwkv_ln_g"][0]).reshape(1, 512), lnb=f(inp["rwkv_ln_b"][0]).reshape(1, 512),
        qkg=np.concatenate([np.tile(f(inp["q_norm_g"][0]), 8), np.tile(f(inp["k_norm_g"][0]), 8)]).reshape(1, 1024),
        wba=f(inp["w_branch_a"][0]), wbb=f(inp["w_branch_b"][0]), wout=f(inp["w_out"][0]), n2g=_fm(inp["norm2_g"][0], 8),
        wup=f(inp["w_ffn_up"][0]), cw=cwl, cb=_fm(inp["ffn_conv_b"][0], NFF), wdn=f(inp["w_ffn_down"][0]),
    )
    shared.update(_consts())
    xs = np.asarray(inp["x"], np.float32); ps_ = np.asarray(inp["positions"], np.int32)
    maps = []
    for b in range(8):
        m = dict(shared)
        m["x"] = np.ascontiguousarray(xs[b])
        m["pos"] = np.ascontiguousarray(ps_[b].reshape(NT, 128).T)
        maps.append(m)
    return maps


def kernel(**inputs):
    maps = _prep(inputs)
    nc = build()
    res = run_bass_kernel_spmd(nc, maps, core_ids=list(range(8)))
    return np.stack([np.asarray(r["out"], np.float32) for r in res.results], axis=0)
```

```python
import numpy as np
import concourse.bass as bass
import concourse.mybir as mybir
from concourse.bass_utils import run_bass_kernel_spmd

F32 = mybir.dt.float32
BF16 = mybir.dt.bfloat16
I32 = mybir.dt.int32
ALU = mybir.AluOpType
AF = mybir.ActivationFunctionType
AX = mybir.AxisListType


class Buf:
    __slots__ = ("name", "lastw", "readers")

    def __init__(self, name):
        self.name = name
        self.lastw = None
        self.readers = {}


class Ctx:
    def __init__(self, nc, n_dma_sems=24):
        self.nc = nc
        self.eng = {"pe": nc.tensor, "act": nc.scalar, "dve": nc.vector,
                    "pool": nc.gpsimd, "sp": nc.sync}
        self.sem = {}
        self.cnt = {}
        self.waited = {e: {} for e in self.eng}
        self._stack = []
        for e in ("pe", "act", "dve", "pool"):
            cm = nc.semaphore("s_" + e)
            self.sem[e] = cm.__enter__()
            self._stack.append(cm)
            self.cnt[e] = 0
        self.dsem = []
        self.dpool = {"hw": [], "sw": []}
        for i in range(n_dma_sems):
            cm = nc.semaphore("d%d" % i)
            self.dsem.append([cm.__enter__(), 0])
            self._stack.append(cm)
            self.dpool["sw" if i < 8 else "hw"].append(i)
        self.dnext = {"hw": 0, "sw": 0}
        self.semh = {}
        for e in self.sem:
            self.semh[("e", e)] = self.sem[e]
        for i, (h, _) in enumerate(self.dsem):
            self.semh[("d", i)] = h
        self.nwait = 0
        self.ninst = 0

    def _wait(self, e, toks):
        w = self.waited[e]
        best = {}
        for t in toks:
            if t is None:
                continue
            k, v = t[0], t[1]
            if w.get(k, 0) >= v:
                continue
            if best.get(k, 0) < v:
                best[k] = v
        for k, v in best.items():
            self.eng[e].wait_ge(self.semh[k], v)
            w[k] = v
            self.nwait += 1

    def _deps(self, e, rd, wr):
        toks = []
        me = ("e", e)
        for b in rd:
            if b.lastw is not None:
                if not (e == "pe" and b.lastw[0] == me):
                    toks.append(b.lastw)
        for b in wr:
            if b.lastw is not None and not (e == "pe" and b.lastw[0] == me):
                toks.append(b.lastw)
            for k, t in b.readers.items():
                if not (e == "pe" and k == me):
                    toks.append(t)
        return toks

    def _mark(self, tok, rd, wr):
        for b in rd:
            b.readers[tok[0]] = tok
        for b in wr:
            b.lastw = tok
            b.readers = {}

    def op(self, e, fn, rd=(), wr=()):
        self._wait(e, self._deps(e, rd, wr))
        ins = fn(self.eng[e])
        self.cnt[e] += 1
        ins.then_inc(self.sem[e], 1)
        tok = (("e", e), self.cnt[e])
        self._mark(tok, rd, wr)
        self.ninst += 1
        return tok

    def group(self, e, fns, rd=(), wr=()):
        self._wait(e, self._deps(e, rd, wr))
        ins = None
        for fn in fns:
            ins = fn(self.eng[e])
            self.ninst += 1
        self.cnt[e] += 1
        ins.then_inc(self.sem[e], 1)
        tok = (("e", e), self.cnt[e])
        self._mark(tok, rd, wr)
        return tok

    def dma(self, q, out, in_, rd=(), wr=(), **kw):
        kind = "sw" if q == "pool" else "hw"
        pool = self.dpool[kind]
        i = pool[self.dnext[kind] % len(pool)]
        self.dnext[kind] += 1
        h, c = self.dsem[i]
        k = ("d", i)
        toks = self._deps(q, rd, wr)
        if c > 0:
            toks.append((k, 16 * c))
        self._wait(q, toks)
        self.eng[q].dma_start(out=out, in_=in_, **kw).then_inc(h, 16)
        self.dsem[i][1] = c + 1
        tok = (k, 16 * (c + 1))
        self._mark(tok, rd, wr)
        self.ninst += 1
        return tok

    def wait_all(self, e, bufs):
        toks = []
        for b in bufs:
            toks.append(b.lastw)
            toks.extend(b.readers.values())
        self._wait(e, toks)

    def barrier(self, bufs=()):
        toks = []
        for e in self.sem:
            if self.cnt[e] > 0:
                toks.append((("e", e), self.cnt[e]))
        for i, (h, c) in enumerate(self.dsem):
            if c > 0:
                toks.append((("d", i), 16 * c))
        for e in self.eng:
            self._wait(e, toks)

from contextlib import ExitStack
import math
import os

S = 4096
DM = 1024
NT = 32
FMC = 1312
TMC = 4096
DFF = 2816
NFF = 22
BIG = 30000.0


def build(debug=False, upto=99):
    nc = bass.Bass("TRN2", target_bir_lowering=False)
    okind = "ExternalOutput" if debug else "Internal"

    def DIN(name, shape, dt=F32):
        return nc.dram_tensor(name, list(shape), dt, kind="ExternalInput").ap()

    x = DIN("x", [S, DM]); pos = DIN("pos", [128, NT], I32)
    w_in = DIN("w_in", [DM, 5408]); n1g = DIN("n1g", [128, 8]); mu_fm = DIN("mu_fm", [128, 11]); mu_v = DIN("mu_v", [1, 512])
    wdec_d = DIN("wdec", [65, 512]); waaa_d = DIN("waaa", [65, 512]); wgate_d = DIN("wgate", [160, 512])
    kk_d = DIN("kk_fm", [128, 4]); ka_d = DIN("ka_fm", [128, 4]); rkb_d = DIN("rkb", [128, 8])
    lng_d = DIN("lng", [1, 512]); lnb_d = DIN("lnb", [1, 512]); qkg_d = DIN("qkg", [1, 1024])
    wba_d = DIN("wba", [512, DM]); wbb_d = DIN("wbb", [512, DM]); wout_d = DIN("wout", [DM, DM]); n2g = DIN("n2g", [128, 8])
    wup_d = DIN("wup", [DM, 2 * DFF]); cw_d = DIN("cw", [128, NFF * 3]); cb_d = DIN("cb", [128, NFF]); wdn_d = DIN("wdn", [DFF, DM])
    ident_d = DIN("ident", [128, 128]); mu2_d = DIN("mu2", [128, 512]); sl2_d = DIN("sl2", [128, 256]); bones_d = DIN("bones", [128, 128])
    cbias2_d = DIN("cbias2", [128, 512]); sel65_d = DIN("sel65", [65, 64]); pok_d = DIN("pok", [1, 256]); pbias_d = DIN("pbias", [1, 256]); invf_d = DIN("invf", [1, 8])
    out = nc.dram_tensor("out", [S, DM], F32, kind="ExternalOutput").ap()
    zF = nc.dram_tensor("zF", [1408, S], F32, kind=okind).ap()
    zT = nc.dram_tensor("zT", [S, TMC], F32, kind=okind).ap()
    yaF = nc.dram_tensor("yaF", [512, S], BF16, kind=okind).ap()
    ybF = nc.dram_tensor("ybF", [512, S], BF16, kind=okind).ap()
    x1s = nc.dram_tensor("x1s", [S, DM], F32, kind=okind).ap()
    h2F = nc.dram_tensor("h2F", [DM, S], BF16, kind=okind).ap()

    c = Ctx(nc)
    PSB = [(nc.alloc_psum_tensor("psb%d" % i, [128, 512], F32), Buf("psb%d" % i)) for i in range(8)]
    pst = {"i": 0, "n": 8, "off": 0}

    def ps():
        i = pst["off"] + pst["i"] % pst["n"]
        pst["i"] += 1
        return PSB[i]

    rr = {"i": 0}

    def ev():
        rr["i"] += 1
        return "act" if rr["i"] % 2 else "dve"

    def bcast(ap, shape, axis):
        return ap.unsqueeze(axis).to_broadcast(list(shape))

    with ExitStack() as es:
        def T(name, shape, dt=F32):
            return es.enter_context(nc.sbuf_tensor("s_" + name, list(shape), dt)), Buf(name)
        w, bw = T("w1", [128, 8, 5408], BF16)
        g1, bg1 = T("g1", [128, 8]); muf, bmuf = T("muf", [128, 11])
        idb, bidb = T("idb1", [128, 128], BF16)
        c.dma("sp", g1[:], n1g, wr=[bg1]); c.dma("sp", muf[:], mu_fm, wr=[bmuf])
        c.dma("pool", idb[:], ident_d, wr=[bidb])
        wb = [Buf("w1_%d" % k) for k in range(8)]
        for kc in range(8):
            c.dma("pool", w[:, kc, :], w_in[kc * 128:(kc + 1) * 128, :], wr=[wb[kc]])
            c.op("dve", lambda e, kc=kc: e.tensor_scalar(w[:, kc, :], w[:, kc, :], g1[:, kc:kc + 1], None, ALU.mult),
                 rd=[wb[kc], bg1], wr=[wb[kc]])
        ss, bss = T("ss1", [128, NT]); c.op("dve", lambda e: e.memset(ss[:], 0.0), wr=[bss])
        rs, brs = T("rs1", [128, NT])
        junk, bjunk = T("junk1", [128, DM])
        xts = [T("xt%d" % i, [128, DM]) for i in range(2)]
        hTs = [es.enter_context(nc.sbuf_tensor("hT%d" % i, [128, 8, 512], BF16)) for i in range(2)]
        hTb = [[Buf("hT%d_%d" % (i, s)) for s in range(4)] for i in range(2)]
        stgs = [T("stg%d" % i, [128, TMC]) for i in range(2)]
        zsb = [T("zsb%d" % j, [128, 513]) for j in range(11)]
        for j in range(11):
            c.op("pool", lambda e, j=j: e.memset(zsb[j][0][:, 0:1], 0.0), wr=[zsb[j][1]])
        tds = [T("td%d" % i, [128, 512]) for i in range(2)]
        ostg = [T("ostg%d" % i, [128, 512]) for i in range(3)]
        no = {"i": 0}
        NSTv = int(os.environ.get('NST', '8'))
        hbs = [T("hb1_%d" % i, [128, DM], BF16) for i in range(2)]

        def s1(tt):
            st, sub = tt // 4, tt % 4
            hT = hTs[st % 2]
            xt, bxt = xts[tt % 2]
            hb, bhb = hbs[tt % 2]
            c.dma("sp", xt[:], x[tt * 128:(tt + 1) * 128, :], wr=[bxt])
            c.op("act", lambda e: e.activation(junk[:], xt[:], AF.Square, accum_out=ss[:, tt:tt + 1]), rd=[bxt], wr=[bjunk, bss])
            c.op("dve", lambda e: e.tensor_scalar(rs[:, tt:tt + 1], ss[:, tt:tt + 1], 1.0 / DM, 1e-6, ALU.mult, ALU.add), rd=[bss], wr=[brs])
            c.op("act", lambda e: e.activation(rs[:, tt:tt + 1], rs[:, tt:tt + 1], AF.Ln), rd=[brs], wr=[brs])
            c.op("act", lambda e: e.activation(rs[:, tt:tt + 1], rs[:, tt:tt + 1], AF.Exp, scale=-0.5), rd=[brs], wr=[brs])
            c.op("dve", lambda e: e.tensor_scalar(hb[:], xt[:], rs[:, tt:tt + 1], None, ALU.mult), rd=[bxt, brs], wr=[bhb])
            p, bp = ps(); pb = p[:].bitcast(BF16)
            c.group("pe", [lambda e, k=k: e.transpose(pb[:, k * 128:(k + 1) * 128], hb[:, k * 128:(k + 1) * 128], idb[:]) for k in range(8)],
                    rd=[bhb, bidb], wr=[bp])
            c.op("act", lambda e: e.copy(hT[:, :, sub * 128:(sub + 1) * 128], pb.rearrange("p (k t) -> p k t", k=8)), rd=[bp], wr=[hTb[st % 2][sub]])

        def s2(tt):
            st, sub = tt // 4, tt % 4
            hT = hTs[st % 2]
            stg, bstg = stgs[tt % 2]
            for gi in range(8):
                p, bp = ps()
                c.group("pe", [lambda e, k=k, p=p: e.matmul(p[:], hT[:, k, sub * 128:(sub + 1) * 128], w[:, k, FMC + gi * 512:FMC + (gi + 1) * 512],
                                                           start=(k == 0), stop=(k == 7)) for k in range(8)],
                        rd=[hTb[st % 2][sub]] + wb, wr=[bp])
                en = ev()
                if en == "act":
                    c.op("act", lambda e, p=p: e.copy(stg[:, gi * 512:(gi + 1) * 512], p[:]), rd=[bp], wr=[bstg])
                else:
                    c.op("dve", lambda e, p=p: e.tensor_copy(stg[:, gi * 512:(gi + 1) * 512], p[:]), rd=[bp], wr=[bstg])
            c.dma("pool", zT[tt * 128:(tt + 1) * 128, :], stg[:], rd=[bstg])

        def fm(st):
            hT = hTs[st % 2]
            for j in range(11):
                ncol = 32 if j == 10 else 128
                z, bz = zsb[j]
                p, bp = ps()
                c.group("pe", [lambda e, k=k, p=p: e.matmul(p[0:ncol, :], w[:, k, j * 128:j * 128 + ncol], hT[:, k, :], start=(k == 0), stop=(k == 7)) for k in range(8)],
                        rd=hTb[st % 2] + wb, wr=[bp])
                c.op("act", lambda e, p=p: e.copy(z[0:ncol, 1:513], p[0:ncol, :]), rd=[bp], wr=[bz])
                td, btd = tds[j % 2]
                c.op("dve", lambda e: e.tensor_tensor(td[0:ncol, :], z[0:ncol, 0:512], z[0:ncol, 1:513], ALU.subtract), rd=[bz], wr=[btd])
                o, bo = ostg[no["i"] % 3]; no["i"] += 1
                c.op("dve", lambda e: e.scalar_tensor_tensor(o[0:ncol, :], td[0:ncol, :], muf[0:ncol, j:j + 1], z[0:ncol, 1:513], ALU.mult, ALU.add),
                     rd=[btd, bz, bmuf], wr=[bo])
                c.op("act", lambda e: e.copy(z[0:ncol, 0:1], z[0:ncol, 512:513]), rd=[bz], wr=[bz])
                c.dma("pool", zF[j * 128:j * 128 + ncol, st * 512:(st + 1) * 512], o[0:ncol, :], rd=[bo])

        ntl = NSTv * 4
        if ntl > 0:
            s1(0)
        for tt in range(ntl):
            if tt + 1 < ntl:
                s1(tt + 1)
            s2(tt)
            if tt % 4 == 3:
                fm(tt // 4)
        c.barrier()

    if upto <= 1:
        print("ninst", c.ninst, "nwait", c.nwait); return nc
    with ExitStack() as es:
        def T(name, shape, dt=F32):
            return es.enter_context(nc.sbuf_tensor("s_" + name, list(shape), dt)), Buf(name)
        idb, bidb = T("idb2", [128, 128], BF16); c.dma("pool", idb[:], ident_d, wr=[bidb])
        wdec, bwdec = T("wdec", [65, 512], BF16); c.dma("pool", wdec[:], wdec_d, wr=[bwdec])
        waaa, bwaaa = T("waaa", [65, 512], BF16); c.dma("pool", waaa[:], waaa_d, wr=[bwaaa])
        wgate, bwgate = T("wgate", [128, 2, 512], BF16)
        c.dma("pool", wgate[:, 0, :], wgate_d[0:128, :], wr=[bwgate]); c.dma("pool", wgate[0:32, 1, :], wgate_d[128:160, :], wr=[bwgate])
        kkf, bkkf = T("kkf", [128, 4]); c.dma("sp", kkf[:], kk_d, wr=[bkkf])
        kaf, bkaf = T("kaf", [128, 4]); c.dma("sp", kaf[:], ka_d, wr=[bkaf])
        c0f, bc0f = T("c0f", [128, 4])
        c.op("dve", lambda e: e.tensor_scalar(c0f[:], kaf[:], -1.0, 1.0, ALU.mult, ALU.add), rd=[bkaf], wr=[bc0f])
        rkb, brkb = T("rkb", [128, 8]); c.dma("sp", rkb[:], rkb_d, wr=[brkb])
        lng, blng = T("lng", [128, 512]); c.dma("sp", lng[:], lng_d.partition_broadcast(128), wr=[blng])
        lnb, blnb = T("lnb", [128, 512]); c.dma("sp", lnb[:], lnb_d.partition_broadcast(128), wr=[blnb])
        muv, bmuv = T("muv", [128, 512]); c.dma("sp", muv[:], mu_v.partition_broadcast(128), wr=[bmuv])
        MU2, bMU2 = T("MU2", [128, 512]); c.dma("sp", MU2[:], mu2_d, wr=[bMU2])
        SL2, bSL2 = T("SL2", [128, 256]); c.dma("sp", SL2[:], sl2_d, wr=[bSL2])
        bones, bbones = T("bones", [128, 128], BF16); c.dma("pool", bones[:], bones_d, wr=[bbones])
        S32, bS32 = T("S32", [128, 4, 64]); c.op("dve", lambda e: e.memset(S32[:], 0.0), wr=[bS32])
        Sb, bSb = T("Sb", [128, 4, 64], BF16); c.op("dve", lambda e: e.memset(Sb[:], 0.0), wr=[bSb])
        tha = [T("tha%d" % i, [65, 128], BF16) for i in range(2)]
        xaa = [T("xaa%d" % i, [65, 128], BF16) for i in range(2)]
        for i in range(2):
            c.op("dve", lambda e, i=i: e.memset(tha[i][0][:], 1.0), wr=[tha[i][1]])
            c.op("dve", lambda e, i=i: e.memset(xaa[i][0][:], 1.0), wr=[xaa[i][1]])
        rFs = [T("rF%d" % i, [128, 4, 128]) for i in range(2)]
        kFs = [T("kF%d" % i, [128, 4, 128]) for i in range(2)]
        xws = [T("xw%d" % i, [64, 128]) for i in range(2)]
        xas = [T("xa%d" % i, [64, 128]) for i in range(2)]
        xg0s = [T("xg0%d" % i, [128, 128]) for i in range(2)]
        xg1s = [T("xg1%d" % i, [32, 128]) for i in range(2)]
        vTs = [T("vT%d" % i, [128, 512]) for i in range(2)]
        vPs = [T("vP%d" % i, [128, 512]) for i in range(2)]
        for i in range(2):
            c.op("dve", lambda e, i=i: e.memset(vPs[i][0][:], 0.0), wr=[vPs[i][1]])
        v32s = [T("v32%d" % i, [128, 512]) for i in range(3)]; vbs = [T("vb%d" % i, [128, 512], BF16) for i in range(3)]; vtmp, bvtmp = T("vtmp", [128, 512])
        sg0, bsg0 = T("sg0", [128, 128], BF16); sg1, bsg1 = T("sg1", [32, 128], BF16)
        tg, btg = T("tg", [128, 512]); sgt, bsgt = T("sgt", [128, 128])
        logw, blogw = T("logw", [128, 4, 128]); lgi, blgi = T("lgi", [128, 4, 128]); lge, blge = T("lge", [128, 4, 128])
        alr, balr = T("alr", [128, 4, 128]); gTs = [T("gT%d" % i, [128, 512]) for i in range(3)]
        kkr, bkkr = T("kkr", [128, 4, 128]); sqb, bsqb = T("sqb", [128, 512], BF16); rn, brn = T("rn", [128, 512])
        kkn, bkkn = T("kkn", [128, 4, 128]); fF, bfF = T("fF", [128, 4, 128]); kM, bkM = T("kM", [128, 4, 128])
        gins = [T("gin%d" % i, [128, 4, 128]) for i in range(3)]; ginv, bginv = T("ginv", [128, 4, 128]); gex, bgex = T("gex", [128, 4, 128])
        ARs = [T("ARZ%d" % i, [128, 8, 2, 128], BF16) for i in range(3)]; bF, bbF = T("bF", [128, 4, 128])
        Bt, bBt = T("BtZ", [128, 8, 128], BF16); Kt, bKt = T("KtZ", [128, 8, 128], BF16)
        for (t_, b_) in (ARs[0], ARs[1], ARs[2], (Bt, bBt), (Kt, bKt)):
            c.op("pool", lambda e, t_=t_: e.memset(t_[:], 0.0), wr=[b_])
        Dd, bDd = T("Dd", [128, 4, 128]); Bh, bBh = T("Bh", [128, 4, 128], BF16); Kh, bKh = T("Kh", [128, 4, 128], BF16)
        BKhTs = [T("BKhT%d" % i, [128, 1024], BF16) for i in range(3)]
        rk, brk = T("rk", [128, 4, 128]); coefs = [T("coef%d" % i, [128, 8]) for i in range(3)]
        MABs = [T("MAB%d" % i, [128, 8, 2, 128], BF16) for i in range(3)]; MAKs = [T("MAK%d" % i, [128, 8, 2, 128], BF16) for i in range(3)]; MABTs = [T("MABT%d" % i, [128, 8, 128], BF16) for i in range(3)]
        Pk = [T("Pk%d" % i, [128, 8, 128], BF16) for i in range(2)]
        PTk = [T("PTk%d" % i, [128, 8, 128], BF16) for i in range(2)]
        ACk = [T("ACk%d" % i, [128, 8, 128], BF16) for i in range(2)]
        MinvS = [T("Minv%d" % i, [128, 8, 128], BF16) for i in range(2)]
        XT, bXT = T("XT", [128, 512], BF16); UT, bUT = T("UT", [128, 512], BF16)
        tmpS, btmpS = T("tmpS", [128, 4, 64])
        s1, bs1 = T("s1", [128, 8]); s2, bs2 = T("s2", [128, 8]); mean, bmean = T("mean", [128, 8]); var, bvar = T("var", [128, 8])
        sqt, bsqt = T("sqt", [128, 512]); yn, byn = T("yn", [128, 512]); bon, bbon = T("bon", [128, 512])
        yab, byab = T("yab", [128, 512], BF16); yaT, byaT = T("yaT", [128, 4, 128], BF16)

        zFr = zF[0:512, :].rearrange("(c p) t -> p c t", p=128)
        zFk = zF[512:1024, :].rearrange("(c p) t -> p c t", p=128)
        yaFv = yaF.rearrange("(c p) t -> p c t", p=128)

        def loads(ch):
            i = ch % 2; t0 = ch * 128
            c.dma("sp", rFs[i][0][:], zFr[:, :, t0:t0 + 128], wr=[rFs[i][1]])
            c.dma("sp", kFs[i][0][:], zFk[:, :, t0:t0 + 128], wr=[kFs[i][1]])
            c.dma("sp", xws[i][0][:], zF[1024:1088, t0:t0 + 128], wr=[xws[i][1]])
            c.dma("sp", xas[i][0][:], zF[1088:1152, t0:t0 + 128], wr=[xas[i][1]])
            c.dma("sp", xg0s[i][0][:], zF[1152:1280, t0:t0 + 128], wr=[xg0s[i][1]])
            c.dma("sp", xg1s[i][0][:], zF[1280:1312, t0:t0 + 128], wr=[xg1s[i][1]])
            c.dma("sp", vTs[i][0][:], zT[t0:t0 + 128, 0:512], wr=[vTs[i][1]])
            if ch == 0:
                c.dma("sp", vPs[i][0][1:128, :], zT[0:127, 0:512], wr=[vPs[i][1]])
            else:
                c.dma("sp", vPs[i][0][:], zT[t0 - 1:t0 + 127, 0:512], wr=[vPs[i][1]])

        loads(0)
        NCH = int(os.environ.get('NCH', str(NT)))
        STG = int(os.environ.get('STG', '99'))
        pqA = {"i": 0}; pqB = {"i": 0}
        pqC = {"i": 0}
        def psA():
            pqA["i"] += 1
            return PSB[pqA["i"] % 3]
        def psC():
            pqC["i"] += 1
            return PSB[3 + pqC["i"] % 3]
        def psB():
            pqB["i"] += 1
            return PSB[6 + pqB["i"] % 2]
        def stageA(ch):
            if ch + 1 < NCH:
                loads(ch + 1)
            i = ch % 2; t0 = ch * 128
            j3 = ch % 3
            v32, bv32 = v32s[j3]; vb, bvb = vbs[j3]; gT, bgT = gTs[j3]; gin, bgin = gins[j3]; AR, bAR = ARs[j3]
            BKhT, bBKhT = BKhTs[j3]; coef, bcoef = coefs[j3]; MAB, bMAB = MABs[j3]; MAK, bMAK = MAKs[j3]; MABT, bMABT = MABTs[j3]
            rF, brF = rFs[i]; kF, bkF = kFs[i]; xw, bxw = xws[i]; xa, bxa = xas[i]
            xg0, bxg0 = xg0s[i]; xg1, bxg1 = xg1s[i]; vT, bvT = vTs[i]; vP, bvP = vPs[i]
            th, bth = tha[i]; xab, bxab = xaa[i]
            c.op("pool", lambda e: e.tensor_tensor(vtmp[:], vP[:], vT[:], ALU.subtract), rd=[bvP, bvT], wr=[bvtmp])
            c.op("pool", lambda e: e.tensor_tensor(vtmp[:], vtmp[:], muv[:], ALU.mult), rd=[bvtmp, bmuv], wr=[bvtmp])
            c.op("pool", lambda e: e.tensor_tensor(v32[:], vtmp[:], vT[:], ALU.add), rd=[bvtmp, bvT], wr=[bv32])
            c.op("pool", lambda e: e.tensor_copy(vb[:], v32[:]), rd=[bv32], wr=[bvb])
            c.op("act", lambda e: e.activation(th[0:64, :], xw[:], AF.Tanh), rd=[bxw], wr=[bth])
            c.op("dve", lambda e: e.tensor_copy(xab[0:64, :], xa[:]), rd=[bxa], wr=[bxab])
            c.op("act", lambda e: e.activation(sgt[:], xg0[:], AF.Tanh, scale=0.5), rd=[bxg0], wr=[bsgt])
            c.op("dve", lambda e: e.tensor_scalar(sg0[:], sgt[:], 0.5, 0.5, ALU.mult, ALU.add), rd=[bsgt], wr=[bsg0])
            c.op("act", lambda e: e.activation(sgt[0:32, :], xg1[:], AF.Tanh, scale=0.5), rd=[bxg1], wr=[bsgt])
            c.op("dve", lambda e: e.tensor_scalar(sg1[:], sgt[0:32, :], 0.5, 0.5, ALU.mult, ALU.add), rd=[bsgt], wr=[bsg1])
            p, bp = psA()
            c.group("pe", [lambda e, q=q, p=p: e.matmul(p[:, q * 128:(q + 1) * 128], wdec[:, q * 128:(q + 1) * 128], th[:], start=True, stop=True) for q in range(4)],
                    rd=[bwdec, bth], wr=[bp])
            c.op("act", lambda e, p=p: e.activation(tg[:], p[:], AF.Tanh, scale=0.5), rd=[bp], wr=[btg])
            c.op("dve", lambda e: e.tensor_scalar(logw[:].rearrange("p a b -> p (a b)"), tg[:], -0.5 * math.exp(-0.5), -0.5 * math.exp(-0.5), ALU.mult, ALU.add),
                 rd=[btg], wr=[blogw])
            for q in range(4):
                c.op("dve", lambda e, q=q: e.tensor_tensor_scan(lgi[:, q, :], logw[:, q, :], logw[:, q, :], 0.0, ALU.add, ALU.bypass), rd=[blogw], wr=[blgi])
            c.op("pool", lambda e: e.tensor_tensor(lge[:], lgi[:], logw[:], ALU.subtract), rd=[blgi, blogw], wr=[blge])
            yield
            p, bp = psA()
            c.group("pe", [lambda e, q=q, p=p: e.matmul(p[:, q * 128:(q + 1) * 128], waaa[:, q * 128:(q + 1) * 128], xab[:], start=True, stop=True) for q in range(4)],
                    rd=[bwaaa, bxab], wr=[bp])
            c.op("act", lambda e, p=p: e.activation(tg[:], p[:], AF.Tanh, scale=0.5), rd=[bp], wr=[btg])
            c.op("dve", lambda e: e.tensor_scalar(alr[:].rearrange("p a b -> p (a b)"), tg[:], 0.5, 0.5, ALU.mult, ALU.add), rd=[btg], wr=[balr])
            p, bp = psA()
            c.group("pe", [lambda e, p=p: e.matmul(p[:], sg0[:], wgate[:, 0, :], start=True, stop=False),
                           lambda e, p=p: e.matmul(p[:], sg1[:], wgate[0:32, 1, :], start=False, stop=True)], rd=[bsg0, bsg1, bwgate], wr=[bp])
            c.op("act", lambda e, p=p: e.copy(gT[:], p[:]), rd=[bp], wr=[bgT])
            for q in range(4):
                c.op("dve", lambda e, q=q: e.tensor_scalar(kkr[:, q, :], kF[:, q, :], kkf[:, q:q + 1], None, ALU.mult), rd=[bkF, bkkf], wr=[bkkr])
            c.op("pool", lambda e: e.tensor_tensor(sqb[:], kkr[:].rearrange("p a b -> p (a b)"), kkr[:].rearrange("p a b -> p (a b)"), ALU.mult), rd=[bkkr], wr=[bsqb])
            p, bp = psA()
            c.op("pe", lambda e, p=p: e.matmul(p[:], bones[:], sqb[:], start=True, stop=True), rd=[bbones, bsqb], wr=[bp])
            c.op("act", lambda e, p=p: e.activation(rn[:], p[:], AF.Ln), rd=[bp], wr=[brn])
            c.op("act", lambda e: e.activation(rn[:], rn[:], AF.Exp, scale=-0.5), rd=[brn], wr=[brn])
            c.op("dve", lambda e: e.tensor_tensor(kkn[:].rearrange("p a b -> p (a b)"), kkr[:].rearrange("p a b -> p (a b)"), rn[:], ALU.mult), rd=[bkkr, brn], wr=[bkkn])
            yield
            for q in range(4):
                c.op("dve", lambda e, q=q: e.tensor_scalar(fF[:, q, :], alr[:, q, :], kaf[:, q:q + 1], c0f[:, q:q + 1], ALU.mult, ALU.add), rd=[balr, bkaf, bc0f], wr=[bfF])
            c.op("pool", lambda e: e.tensor_tensor(kM[:], kF[:], fF[:], ALU.mult), rd=[bkF, bfF], wr=[bkM])
            c.op("act", lambda e: e.activation(gin[:], lgi[:], AF.Exp), rd=[blgi], wr=[bgin])
            c.op("act", lambda e: e.activation(ginv[:], lgi[:], AF.Exp, scale=-1.0), rd=[blgi], wr=[bginv])
            c.op("act", lambda e: e.activation(gex[:], lge[:], AF.Exp), rd=[blge], wr=[bgex])
            for q in range(4):
                c.op("act", lambda e, q=q: e.activation(Dd[:, q, :], lgi[:, q, :], AF.Exp, bias=lgi[:, q, 127:128], scale=-1.0), rd=[blgi], wr=[bDd])
            c.op("pool", lambda e: e.tensor_tensor(bF[:], kkn[:], alr[:], ALU.mult), rd=[bkkn, balr], wr=[bbF])
            for hh in range(2):
                r0, r1 = hh * 64, (hh + 1) * 64
                ARv = AR[r0:r1, :, :, :].rearrange("p (q two) a t -> p q two a t", two=2)[:, :, hh, :, :]
                Btv = Bt[r0:r1, :, :].rearrange("p (q two) t -> p q two t", two=2)[:, :, hh, :]
                Ktv = Kt[r0:r1, :, :].rearrange("p (q two) t -> p q two t", two=2)[:, :, hh, :]
                c.op("dve", lambda e: e.tensor_tensor(ARv[:, :, 1, :], rF[r0:r1, :, :], gin[r0:r1, :, :], ALU.mult), rd=[brF, bgin], wr=[bAR])
                c.op("dve", lambda e: e.scalar_tensor_tensor(ARv[:, :, 0, :], kkn[r0:r1, :, :], -1.0, gex[r0:r1, :, :], ALU.mult, ALU.mult), rd=[bkkn, bgex], wr=[bAR])
                c.op("dve", lambda e: e.tensor_tensor(Btv, bF[r0:r1, :, :], ginv[r0:r1, :, :], ALU.mult), rd=[bbF, bginv], wr=[bBt])
                c.op("pool", lambda e: e.tensor_tensor(Ktv, kM[r0:r1, :, :], ginv[r0:r1, :, :], ALU.mult), rd=[bkM, bginv], wr=[bKt])
            c.op("pool", lambda e: e.tensor_tensor(Bh[:], bF[:], Dd[:], ALU.mult), rd=[bbF, bDd], wr=[bBh])
            c.op("pool", lambda e: e.tensor_tensor(Kh[:], kM[:], Dd[:], ALU.mult), rd=[bkM, bDd], wr=[bKh])
            yield
            p, bp = psA(); pb = p[:].bitcast(BF16)
            c.group("pe", [lambda e, q=q: e.transpose(pb[:, q * 128:(q + 1) * 128], Bh[:, q, :], idb[:]) for q in range(4)] +
                          [lambda e, q=q: e.transpose(pb[:, 512 + q * 128:512 + (q + 1) * 128], Kh[:, q, :], idb[:]) for q in range(4)],
                    rd=[bBh, bKh, bidb], wr=[bp])
            c.op("act", lambda e: e.copy(BKhT[:], pb), rd=[bp], wr=[bBKhT])
            c.op("pool", lambda e: e.tensor_tensor(rk[:], rF[:], kM[:], ALU.mult), rd=[brF, bkM], wr=[brk])
            p, bp = psA()
            c.group("pe", [lambda e, q=q, p=p: e.matmul(p[:, 2 * q:2 * q + 2], rk[:, q, :], rkb[:, 2 * q:2 * q + 2], start=True, stop=True) for q in range(4)],
                    rd=[brk, brkb], wr=[bp])
            c.op("dve", lambda e, p=p: e.tensor_copy(coef[:], p[:, 0:8]), rd=[bp], wr=[bcoef])
            for q in range(4):
                pA, bpA = psA(); pB, bpB = psA(); pC, bpC = psA()
                fa, fb, fc = [], [], []
                for hh in range(2):
                    h = 2 * q + hh
                    arr = AR[:, h, :, :].rearrange("p a b -> p (a b)")
                    fa.append(lambda e, hh=hh, h=h, arr=arr: e.matmul(pA[:, hh * 256:(hh + 1) * 256], Bt[:, h, :], arr, start=True, stop=True))
                    fb.append(lambda e, hh=hh, h=h, arr=arr: e.matmul(pB[:, hh * 256:(hh + 1) * 256], Kt[:, h, :], arr, start=True, stop=True))
                    fc.append(lambda e, hh=hh, h=h: e.matmul(pC[:, hh * 128:(hh + 1) * 128], AR[:, h, 0, :], Bt[:, h, :], start=True, stop=True))
                c.group("pe", fa, rd=[bBt, bAR], wr=[bpA])
                c.group("pe", fb, rd=[bKt, bAR], wr=[bpB])
                c.group("pe", fc, rd=[bBt, bAR], wr=[bpC])
                c.op("dve", lambda e: e.tensor_tensor(MAB[:, 2 * q:2 * q + 2, :, :].rearrange("p a b c -> p (a b c)"), pA[:], MU2[:], ALU.mult), rd=[bpA, bMU2], wr=[bMAB])
                c.op("dve", lambda e: e.tensor_tensor(MAK[:, 2 * q:2 * q + 2, :, :].rearrange("p a b c -> p (a b c)"), pB[:], MU2[:], ALU.mult), rd=[bpB, bMU2], wr=[bMAK])
                c.op("dve", lambda e: e.tensor_tensor(MABT[:, 2 * q:2 * q + 2, :].rearrange("p a b -> p (a b)"), pC[:, 0:256], SL2[:], ALU.mult), rd=[bpC, bSL2], wr=[bMABT])
            yield

        def stageA2(ch):
            i = ch % 2
            j3 = ch % 3
            MAB, bMAB = MABs[j3]; MABT, bMABT = MABTs[j3]
            AC0, bAC0 = ACk[0]
            c.op("pool", lambda e: e.tensor_tensor(AC0[:], MAB[:, :, 0, :], bcast(idb[:], [128, 8, 128], 1), ALU.add), rd=[bMAB, bidb], wr=[bAC0])
            Pp = lambda h: MAB[:, h, 0, :]
            PTp = lambda h: MABT[:, h, :]
            bPp, bPTp = bMAB, bMABT
            ACp, bACp = AC0, bAC0
            for lv in range(1, 7):
                Pn, bPn = Pk[lv % 2]; PTn, bPTn = PTk[lv % 2]; ACn, bACn = (ACk[lv % 2] if lv < 6 else MinvS[i])
                for grp in range(2):
                    hs = range(4 * grp, 4 * grp + 4)
                    if lv < 6:
                        p, bp = psC()
                        c.group("pe", [lambda e, h=h, p=p, Pp=Pp, PTp=PTp: e.matmul(p[:, (h % 4) * 128:(h % 4 + 1) * 128], PTp(h), Pp(h), start=True, stop=True) for h in hs],
                                rd=[bPp, bPTp], wr=[bp])
                        en = ev()
                        if en == "act":
                            c.op("act", lambda e, p=p: e.copy(Pn[:, 4 * grp:4 * grp + 4, :].rearrange("p a b -> p (a b)"), p[:]), rd=[bp], wr=[bPn])
                        else:
                            c.op("dve", lambda e, p=p: e.tensor_copy(Pn[:, 4 * grp:4 * grp + 4, :].rearrange("p a b -> p (a b)"), p[:]), rd=[bp], wr=[bPn])
                    p, bp = psC()
                    c.group("pe", [lambda e, h=h, p=p, Pp=Pp, PTp=PTp: e.matmul(p[:, (h % 4) * 128:(h % 4 + 1) * 128], Pp(h), PTp(h), start=True, stop=True) for h in hs],
                            rd=[bPp, bPTp], wr=[bp])
                    en = ev()
                    if en == "act":
                        c.op("act", lambda e, p=p: e.copy(PTn[:, 4 * grp:4 * grp + 4, :].rearrange("p a b -> p (a b)"), p[:]), rd=[bp], wr=[bPTn])
                    else:
                        c.op("dve", lambda e, p=p: e.tensor_copy(PTn[:, 4 * grp:4 * grp + 4, :].rearrange("p a b -> p (a b)"), p[:]), rd=[bp], wr=[bPTn])
                    p, bp = psC()
                    c.group("pe", [lambda e, h=h, p=p, ACp=ACp: e.matmul(p[:, (h % 4) * 128:(h % 4 + 1) * 128], PTn[:, h, :], ACp[:, h, :], start=True, stop=True) for h in hs],
                            rd=[bACp, bPTn], wr=[bp])
                    c.op("dve", lambda e, p=p, ACp=ACp: e.tensor_tensor(ACn[:, 4 * grp:4 * grp + 4, :].rearrange("p a b -> p (a b)"), p[:],
                                                                      ACp[:, 4 * grp:4 * grp + 4, :].rearrange("p a b -> p (a b)"), ALU.add), rd=[bp, bACp], wr=[bACn])
                yield
                Pp = (lambda Pn: (lambda h: Pn[:, h, :]))(Pn)
                PTp = (lambda PTn: (lambda h: PTn[:, h, :]))(PTn)
                bPp, bPTp = bPn, bPTn
                ACp, bACp = ACn, bACn
            yield

        def stageB(ch):
            i = ch % 2; t0 = ch * 128
            j3 = ch % 3
            v32, bv32 = v32s[j3]; vb, bvb = vbs[j3]; gT, bgT = gTs[j3]; gin, bgin = gins[j3]; AR, bAR = ARs[j3]
            Minv, bMinv = MinvS[i]
            BKhT, bBKhT = BKhTs[j3]; coef, bcoef = coefs[j3]; MAB, bMAB = MABs[j3]; MAK, bMAK = MAKs[j3]; MABT, bMABT = MABTs[j3]
            yield
            pX, bpX = psB()
            fs = []
            for h in range(8):
                q, r0 = h // 2, (h % 2) * 64
                fs.append(lambda e, h=h: e.matmul(pX[:, h * 64:(h + 1) * 64], MAK[:, h, 0, :], vb[:, h * 64:(h + 1) * 64], start=True, stop=False))
                fs.append(lambda e, h=h, q=q: e.matmul(pX[:, h * 64:(h + 1) * 64], AR[:, h, 0, :], Sb[:, q, :], start=False, stop=True))
            c.group("pe", fs, rd=[bMAK, bvb, bAR, bSb], wr=[bpX])
            c.op("act", lambda e: e.copy(XT[:], pX[:]), rd=[bpX], wr=[bXT])
            yield
            pU, bpU = psB()
            c.group("pe", [lambda e, h=h: e.matmul(pU[:, h * 64:(h + 1) * 64], Minv[:, h, :], XT[:, h * 64:(h + 1) * 64], start=True, stop=True) for h in range(8)],
                    rd=[bMinv, bXT], wr=[bpU])
            c.op("dve", lambda e: e.tensor_copy(UT[:], pU[:]), rd=[bpU], wr=[bUT])
            yield
            pY, bpY = psB()
            fs = []
            for h in range(8):
                q, r0 = h // 2, (h % 2) * 64
                fs.append(lambda e, h=h: e.matmul(pY[:, h * 64:(h + 1) * 64], MAK[:, h, 1, :], vb[:, h * 64:(h + 1) * 64], start=True, stop=False))
                fs.append(lambda e, h=h: e.matmul(pY[:, h * 64:(h + 1) * 64], MAB[:, h, 1, :], UT[:, h * 64:(h + 1) * 64], start=False, stop=False))
                fs.append(lambda e, h=h, q=q: e.matmul(pY[:, h * 64:(h + 1) * 64], AR[:, h, 1, :], Sb[:, q, :], start=False, stop=True))
            c.group("pe", fs, rd=[bMAK, bMAB, bvb, bUT, bAR, bSb], wr=[bpY])
            yield
            pS, bpS = psB()
            fs = []
            for q in range(4):
                fs.append(lambda e, q=q: e.matmul(pS[:, q * 128:(q + 1) * 128], BKhT[:, q * 128:(q + 1) * 128], UT[:, q * 128:(q + 1) * 128], start=True, stop=False))
                fs.append(lambda e, q=q: e.matmul(pS[:, q * 128:(q + 1) * 128], BKhT[:, 512 + q * 128:512 + (q + 1) * 128], vb[:, q * 128:(q + 1) * 128], start=False, stop=True))
            c.group("pe", fs, rd=[bBKhT, bUT, bvb], wr=[bpS])
            pSv = pS[:].rearrange("p (q c) -> p q c", q=4)
            c.op("dve", lambda e: e.tensor_tensor(tmpS[:], S32[:], gin[:, :, 127:128].to_broadcast([128, 4, 64]), ALU.mult), rd=[bS32, bgin], wr=[btmpS])
            c.op("dve", lambda e: e.tensor_tensor(S32[0:64, :, :], tmpS[0:64, :, :], pSv[0:64, :, 0:64], ALU.add), rd=[btmpS, bpS], wr=[bS32])
            c.op("dve", lambda e: e.tensor_tensor(S32[64:128, :, :], tmpS[64:128, :, :], pSv[64:128, :, 64:128], ALU.add), rd=[btmpS, bpS], wr=[bS32])
            c.op("dve", lambda e: e.tensor_copy(Sb[:], S32[:]), rd=[bS32], wr=[bSb])
            yield
            pYv = pY[:].rearrange("p (h d) -> p h d", h=8)
            c.op("dve", lambda e: e.tensor_reduce(s1[:], pYv, AX.X, ALU.add), rd=[bpY], wr=[bs1])
            c.op("act", lambda e: e.activation(sqt[:], pY[:], AF.Square), rd=[bpY], wr=[bsqt])
            c.op("dve", lambda e: e.tensor_reduce(s2[:], sqt[:].rearrange("p (h d) -> p h d", h=8), AX.X, ALU.add), rd=[bsqt], wr=[bs2])
            c.op("dve", lambda e: e.tensor_scalar(mean[:], s1[:], 1.0 / 64, None, ALU.mult), rd=[bs1], wr=[bmean])
            c.op("dve", lambda e: e.tensor_tensor(var[:], mean[:], mean[:], ALU.mult), rd=[bmean], wr=[bvar])
            c.op("dve", lambda e: e.scalar_tensor_tensor(var[:], s2[:], 1.0 / 64, var[:], ALU.mult, ALU.subtract), rd=[bs2, bvar], wr=[bvar])
            c.op("dve", lambda e: e.tensor_scalar(var[:], var[:], 64e-5, None, ALU.add), rd=[bvar], wr=[bvar])
            c.op("act", lambda e: e.activation(var[:], var[:], AF.Ln), rd=[bvar], wr=[bvar])
            c.op("act", lambda e: e.activation(var[:], var[:], AF.Exp, scale=-0.5), rd=[bvar], wr=[bvar])
            ynv = yn[:].rearrange("p (h d) -> p h d", h=8)
            c.op("dve", lambda e: e.tensor_tensor(ynv, pYv, bcast(mean[:], [128, 8, 64], 2), ALU.subtract), rd=[bpY, bmean], wr=[byn])
            c.op("pool", lambda e: e.tensor_tensor(ynv, ynv, bcast(var[:], [128, 8, 64], 2), ALU.mult), rd=[byn, bvar], wr=[byn])
            c.op("pool", lambda e: e.tensor_tensor(yn[:], yn[:], lng[:], ALU.mult), rd=[byn, blng], wr=[byn])
            c.op("dve", lambda e: e.tensor_tensor(yn[:], yn[:], lnb[:], ALU.add), rd=[byn, blnb], wr=[byn])
            c.op("pool", lambda e: e.tensor_tensor(bon[:].rearrange("p (h d) -> p h d", h=8), v32[:].rearrange("p (h d) -> p h d", h=8), bcast(coef[:], [128, 8, 64], 2), ALU.mult),
                 rd=[bv32, bcoef], wr=[bbon])
            c.op("dve", lambda e: e.tensor_tensor(yn[:], yn[:], bon[:], ALU.add), rd=[byn, bbon], wr=[byn])
            c.op("dve", lambda e: e.tensor_tensor(yab[:], yn[:], gT[:], ALU.mult), rd=[byn, bgT], wr=[byab])
            p, bp = psB(); pb = p[:].bitcast(BF16)
            c.group("pe", [lambda e, q=q: e.transpose(pb[:, q * 128:(q + 1) * 128], yab[:, q * 128:(q + 1) * 128], idb[:]) for q in range(4)], rd=[byab, bidb], wr=[bp])
            c.op("act", lambda e: e.copy(yaT[:].rearrange("p a b -> p (a b)"), pb[:, 0:512]), rd=[bp], wr=[byaT])
            c.dma("act", yaFv[:, :, t0:t0 + 128], yaT[:], rd=[byaT])
        def run(gens):
            alive = [g for g in gens if g is not None]
            while alive:
                for g in list(alive):
                    try:
                        next(g)
                    except StopIteration:
                        alive.remove(g)
        run([stageA(0)])
        run([stageA2(0), stageA(1) if NCH > 1 else None])
        for ch in range(NCH):
            run([stageB(ch), stageA2(ch + 1) if ch + 1 < NCH else None, stageA(ch + 2) if ch + 2 < NCH else None])
        c.barrier()

    if upto <= 2:
        print("ninst", c.ninst, "nwait", c.nwait); return nc
    with ExitStack() as es:
        def T(name, shape, dt=F32):
            return es.enter_context(nc.sbuf_tensor("s_" + name, list(shape), dt)), Buf(name)
        pst["n"] = 2; pst["i"] = 0; pst["off"] = 4
        pO = [PSB[6], PSB[7]]
        qst = {"i": 0}
        def psq():
            qst["i"] += 1
            return PSB[qst["i"] % 4]
        idb, bidb = T("idb3", [128, 128], BF16); c.dma("pool", idb[:], ident_d, wr=[bidb])
        CB, bCB = T("CB", [128, 2, 256], BF16); c.dma("pool", CB[:].rearrange("p a b -> p (a b)"), cbias2_d, wr=[bCB])
        s65, bs65 = T("s65", [65, 64], BF16); c.dma("pool", s65[:], sel65_d, wr=[bs65])
        pok, bpok = T("pok", [128, 16, 16]); c.dma("sp", pok[:].rearrange("p a b -> p (a b)"), pok_d.partition_broadcast(128), wr=[bpok])
        pbi, bpbi = T("pbi", [128, 16, 16]); c.dma("sp", pbi[:].rearrange("p a b -> p (a b)"), pbias_d.partition_broadcast(128), wr=[bpbi])
        invf, binvf = T("invf", [128, 8]); c.dma("sp", invf[:], invf_d.partition_broadcast(128), wr=[binvf])
        qkg, bqkg = T("qkg", [128, 1024]); c.dma("sp", qkg[:], qkg_d.partition_broadcast(128), wr=[bqkg])
        posi, bposi = T("posi", [128, NT], I32); c.dma("sp", posi[:], pos, wr=[bposi])
        posf, bposf = T("posf", [128, NT])
        c.op("dve", lambda e: e.tensor_copy(posf[:], posi[:]), rd=[bposi], wr=[bposf])
        yy, byy = T("yy", [128, NT, 8]); yi, byi = T("yi", [128, NT, 8], I32); yf, byf = T("yf", [128, NT, 8])
        sinT, bsinT = T("sinT", [128, NT, 8]); cosT, bcosT = T("cosT", [128, NT, 8])
        c.op("dve", lambda e: e.tensor_copy(yy[:], bcast(invf[:], [128, NT, 8], 1)), rd=[binvf], wr=[byy])
        c.op("dve", lambda e: e.tensor_tensor(yy[:], yy[:], bcast(posf[:], [128, NT, 8], 2), ALU.mult), rd=[byy, bposf], wr=[byy])
        for (dst, bdst, off) in ((sinT, bsinT, 0.0), (cosT, bcosT, 0.25)):
            if off != 0.0:
                c.op("dve", lambda e: e.tensor_scalar(yy[:], yy[:], off, None, ALU.add), rd=[byy], wr=[byy])
            c.op("dve", lambda e: e.tensor_copy(yi[:], yy[:]), rd=[byy], wr=[byi])
            c.op("dve", lambda e: e.tensor_copy(yf[:], yi[:]), rd=[byi], wr=[byf])
            c.op("dve", lambda e: e.tensor_tensor(yf[:], yy[:], yf[:], ALU.subtract), rd=[byy, byf], wr=[byf])
            c.op("act", lambda e, dst=dst: e.activation(dst[:], yf[:], AF.Sin, scale=2.0 * math.pi), rd=[byf], wr=[bdst])
        KT = es.enter_context(nc.sbuf_tensor("s_KT", [80, 8, S], BF16)); bKT = [Buf("KT%d" % g) for g in range(8)]
        VA = es.enter_context(nc.sbuf_tensor("s_VA", [128, NT, 8, 65], BF16)); bVA = [Buf("VA%d" % g) for g in range(8)]
        c.op("pool", lambda e: e.memset(VA[:], 1.0), wr=bVA)
        kmT, bkmT = T("kmT", [64, 8, 16], BF16); c.op("dve", lambda e: e.memset(kmT[:], 0.0), wr=[bkmT])
        km32, bkm32 = T("km32", [64, 8])
        qkvs = [T("qkv%d" % i, [128, 1536]) for i in range(2)]
        sq3, bsq3 = T("sq3", [128, 1024]); ss3, bss3 = T("ss3", [128, 16]); qn, bqn = T("qn", [128, 16, 64])
        rt, brt = T("rt", [128, 4, 16, 8])
        QA, bQA = T("QA", [128, 8, 80], BF16); KA, bKA = T("KA", [128, 8, 80], BF16)
        QT, bQT = T("QT", [64, 8, 128], BF16)
        QTAs = [T("QTA%d" % i, [80, 8, 512], BF16) for i in range(2)]
        gm, bgm = T("gm", [128, 8, 16]); mx, bmx = T("mx", [128, 8, 8]); sel, bsel = T("sel", [128, 8, 16])
        PTs = [T("PT%d" % i, [128, 512], BF16) for i in range(4)]
        OT, bOT = T("OT", [65, 512]); rr, brr = T("rr", [65, 512], BF16); r32, br32 = T("r32", [65, 512])
        c.op("dve", lambda e: e.memset(rr[:], 0.0), wr=[brr])
        yTs = [T("yT%d" % i, [64, 512], BF16) for i in range(2)]
        cnt3 = {"pt": 0, "ld": 0}

        def ld3(tt):
            c.dma("act", qkvs[tt % 2][0][:], zT[tt * 128:(tt + 1) * 128, 512:2048], wr=[qkvs[tt % 2][1]])

        def pre3(tt):
            if tt + 1 < NT:
                ld3(tt + 1)
            qkv, bqkv = qkvs[tt % 2]
            t0 = tt * 128; qb = tt // 2; g = tt // 4; ti = tt % 4
            QTA, bQTA = QTAs[g % 2]
            c.op("act", lambda e: e.activation(sq3[:], qkv[:, 0:1024], AF.Square), rd=[bqkv], wr=[bsq3])
            c.op("dve", lambda e: e.tensor_reduce(ss3[:], sq3[:].rearrange("p (h d) -> p h d", h=16), AX.X, ALU.add), rd=[bsq3], wr=[bss3])
            c.op("dve", lambda e: e.tensor_scalar(ss3[:], ss3[:], 1.0 / 64, 1e-6, ALU.mult, ALU.add), rd=[bss3], wr=[bss3])
            c.op("act", lambda e: e.activation(ss3[:], ss3[:], AF.Ln), rd=[bss3], wr=[bss3])
            c.op("act", lambda e: e.activation(ss3[:], ss3[:], AF.Exp, scale=-0.5), rd=[bss3], wr=[bss3])
            c.op("dve", lambda e: e.tensor_tensor(qn[:], qkv[:, 0:1024].rearrange("p (h d) -> p h d", h=16), bcast(ss3[:], [128, 16, 64], 2), ALU.mult), rd=[bqkv, bss3], wr=[bqn])
            c.op("pool", lambda e: e.tensor_tensor(qn[:].rearrange("p h d -> p (h d)"), qn[:].rearrange("p h d -> p (h d)"), qkg[:], ALU.mult), rd=[bqn, bqkg], wr=[bqn])
            cs = cosT[:, tt:tt + 1, :].to_broadcast([128, 16, 8]); sn = sinT[:, tt:tt + 1, :].to_broadcast([128, 16, 8])
            c.op("dve", lambda e: e.tensor_tensor(rt[:, 0, :, :], qn[:, :, 0:8], cs, ALU.mult), rd=[bqn, bcosT], wr=[brt])
            c.op("dve", lambda e: e.tensor_tensor(rt[:, 1, :, :], qn[:, :, 8:16], sn, ALU.mult), rd=[bqn, bsinT], wr=[brt])
            c.op("pool", lambda e: e.tensor_tensor(rt[:, 2, :, :], qn[:, :, 8:16], cs, ALU.mult), rd=[bqn, bcosT], wr=[brt])
            c.op("pool", lambda e: e.tensor_tensor(rt[:, 3, :, :], qn[:, :, 0:8], sn, ALU.mult), rd=[bqn, bsinT], wr=[brt])
            c.op("dve", lambda e: e.tensor_tensor(qn[:, :, 0:8], rt[:, 0, :, :], rt[:, 1, :, :], ALU.subtract), rd=[brt], wr=[bqn])
            c.op("dve", lambda e: e.tensor_tensor(qn[:, :, 8:16], rt[:, 2, :, :], rt[:, 3, :, :], ALU.add), rd=[brt], wr=[bqn])
            c.op("dve", lambda e: e.tensor_copy(QA[:, :, 0:64], qn[:, 0:8, :]), rd=[bqn], wr=[bQA])
            c.op("pool", lambda e: e.tensor_copy(KA[:, :, 0:64], qn[:, 8:16, :]), rd=[bqn], wr=[bKA])
            c.op("pool", lambda e: e.memset(KA[:, :, 64:80], 0.0), wr=[bKA])
            c.op("pool", lambda e: e.memset(KA[:, :, 64 + qb:65 + qb], 1.0), wr=[bKA])
            yield
            p, bp = ps(); pb = p[:].bitcast(BF16)
            c.group("pe", [lambda e, h=h: e.transpose(pb[0:80, h * 128:(h + 1) * 128], KA[:, h, :], idb[:]) for h in range(8)], rd=[bKA, bidb], wr=[bp])
            c.op("dve", lambda e: e.tensor_copy(KT[:, :, t0:t0 + 128], pb[0:80, :].rearrange("p (h t) -> p h t", h=8)), rd=[bp], wr=[bKT[g]])
            c.op("pool", lambda e: e.tensor_copy(VA[:, tt, :, 0:64], qkv[:, 1024:1536].rearrange("p (h d) -> p h d", h=8)), rd=[bqkv], wr=[bVA[g]])
            if qb > 0:
                p, bp = ps(); pb = p[:].bitcast(BF16)
                c.group("pe", [lambda e, h=h: e.transpose(pb[0:64, h * 128:(h + 1) * 128], QA[:, h, 0:64], idb[:]) for h in range(8)], rd=[bQA, bidb], wr=[bp])
                c.op("dve", lambda e: e.tensor_copy(QT[:], pb[0:64, :].rearrange("p (h t) -> p h t", h=8)), rd=[bp], wr=[bQT])
                yield
                p, bp = ps()
                c.group("pe", [lambda e, h=h, p=p: e.matmul(p[:, h * 16:(h + 1) * 16], QT[:, h, :], kmT[:, h, :], start=True, stop=True) for h in range(8)],
                        rd=[bQT, bkmT], wr=[bp])
                c.op("dve", lambda e, p=p: e.tensor_tensor(gm[:], p[:, 0:128].rearrange("p (h j) -> p h j", h=8), pbi[:, qb:qb + 1, :].to_broadcast([128, 8, 16]), ALU.add),
                     rd=[bp, bpbi], wr=[bgm])
                for h in range(8):
                    c.op("dve", lambda e, h=h: e.max(mx[:, h, :], gm[:, h, :]), rd=[bgm], wr=[bmx])
                c.op("dve", lambda e: e.tensor_tensor(sel[:], gm[:], mx[:, :, 2:3].to_broadcast([128, 8, 16]), ALU.is_ge), rd=[bgm, bmx], wr=[bsel])
                c.op("dve", lambda e: e.tensor_tensor(sel[:], sel[:], pok[:, qb:qb + 1, :].to_broadcast([128, 8, 16]), ALU.mult), rd=[bsel, bpok], wr=[bsel])
                c.op("dve", lambda e: e.tensor_scalar(QA[:, :, 64:80], sel[:], BIG, -BIG, ALU.mult, ALU.add), rd=[bsel], wr=[bQA])
            else:
                c.op("dve", lambda e: e.memset(QA[:, :, 64:80], -BIG), wr=[bQA])
            yield
            p, bp = ps(); pb = p[:].bitcast(BF16)
            c.group("pe", [lambda e, h=h: e.transpose(pb[0:80, h * 128:(h + 1) * 128], QA[:, h, :], idb[:]) for h in range(8)], rd=[bQA, bidb], wr=[bp])
            c.op("dve", lambda e: e.tensor_copy(QTA[:, :, ti * 128:(ti + 1) * 128], pb[0:80, :].rearrange("p (h t) -> p h t", h=8)), rd=[bp], wr=[bQTA])
            if tt % 2 == 1:
                c.op("dve", lambda e: e.tensor_reduce(km32[:], KT[0:64, :, qb * 256:(qb + 1) * 256], AX.X, ALU.add), rd=[bKT[g]], wr=[bkm32])
                c.op("dve", lambda e: e.tensor_scalar(kmT[:, :, qb], km32[:], 1.0 / 256, None, ALU.mult), rd=[bkm32], wr=[bkmT])

        s65f, bs65f = T("s65f", [65, 64]); c.dma("sp", s65f[:], sel65_d, wr=[bs65f])

        def steps3(g, h):
            QTA, bQTA = QTAs[g % 2]
            pOh, bOh = pO[h % 2]
            kbufs = bKT[0:g + 1]; vbufs = bVA[0:g + 1]
            st = []
            nk = 4 * g + 2
            for kt in range(nk):
                d = {}
                def qk(d=d, kt=kt):
                    d["p"], d["bp"] = psq()
                    c.op("pe", lambda e: e.matmul(d["p"][:], KT[:, h, kt * 128:(kt + 1) * 128], QTA[:, h, :], start=True, stop=True), rd=kbufs + [bQTA], wr=[d["bp"]])
                def ex(d=d):
                    d["PT"], d["bPT"] = PTs[cnt3["pt"] % 4]; cnt3["pt"] += 1
                    c.op("act", lambda e: e.activation(d["PT"][:], d["p"][:], AF.Exp, scale=0.125), rd=[d["bp"]], wr=[d["bPT"]])
                def pv(d=d, kt=kt):
                    c.op("pe", lambda e: e.matmul(pOh[0:65, :], VA[:, kt, h, :], d["PT"][:], start=(kt == 0), stop=False), rd=vbufs + [d["bPT"]], wr=[bOh])
                st.append((qk, ex, pv))
            for half in range(2):
                for kti in range(2):
                    kt = 4 * g + 2 * half + kti
                    qs = slice(half * 256, (half + 1) * 256)
                    last = (half == 1 and kti == 1)
                    d = {}
                    def qk(d=d, kt=kt, qs=qs, kti=kti):
                        d["p"], d["bp"] = psq()
                        c.group("pe", [lambda e: e.matmul(d["p"][:, 0:256], KT[0:64, h, kt * 128:(kt + 1) * 128], QTA[0:64, h, qs], start=True, stop=False),
                                       lambda e: e.matmul(d["p"][:, 0:256], idb[:], CB[:, kti, :], start=False, stop=True)], rd=kbufs + [bQTA, bidb, bCB], wr=[d["bp"]])
                    def ex(d=d):
                        d["PT"], d["bPT"] = PTs[cnt3["pt"] % 4]; cnt3["pt"] += 1
                        c.op("act", lambda e: e.activation(d["PT"][:, 0:256], d["p"][:, 0:256], AF.Exp, scale=0.125), rd=[d["bp"]], wr=[d["bPT"]])
                    def pv(d=d, kt=kt, qs=qs, last=last):
                        c.op("pe", lambda e: e.matmul(pOh[0:65, qs], VA[:, kt, h, :], d["PT"][:, 0:256], start=False, stop=last), rd=vbufs + [d["bPT"]], wr=[bOh])
                    st.append((qk, ex, pv))

            def fin():
                c.op("dve", lambda e: e.tensor_copy(OT[:], pOh[0:65, :]), rd=[bOh], wr=[bOT])
                p, bp = ps()
                c.op("pe", lambda e: e.matmul(p[0:64, :], s65f[:], OT[:], start=True, stop=True), rd=[bs65f, bOT], wr=[bp])
                yT, byT = yTs[h % 2]
                c.op("dve", lambda e: e.reciprocal(r32[0:64, :], p[0:64, :]), rd=[bp], wr=[br32])
                c.op("dve", lambda e: e.tensor_tensor(yT[:], OT[0:64, :], r32[0:64, :], ALU.mult), rd=[bOT, br32], wr=[byT])
                c.dma("sp", ybF[h * 64:(h + 1) * 64, g * 512:(g + 1) * 512], yT[:], rd=[byT])
            return st, fin

        ld3(0)
        for tt in range(4):
            for _ in pre3(tt):
                pass
        LOOK = 2
        for g in range(8):
            allst = []
            for h in range(8):
                st, fin = steps3(g, h)
                for i, s in enumerate(st):
                    allst.append((s, fin if i == len(st) - 1 else None, h))
            n = len(allst)
            pend = [pre3(4 * (g + 1) + i) for i in range(4)] if g + 1 < 8 else []
            every = max(1, n // 18)

            def advance():
                while pend:
                    try:
                        next(pend[0])
                        return
                    except StopIteration:
                        pend.pop(0)
            for i in range(min(LOOK, n)):
                allst[i][0][0]()
            for i in range(n):
                (qk, ex, pv), fin, h = allst[i]
                ex()
                if i + LOOK < n:
                    allst[i + LOOK][0][0]()
                pv()
                if fin is not None:
                    fin()
                if i % every == every - 1:
                    advance()
            while pend:
                advance()
        pst["n"] = 8; pst["off"] = 0
        c.barrier()

    if upto <= 3:
        print("ninst", c.ninst, "nwait", c.nwait); return nc
    with ExitStack() as es:
        def T(name, shape, dt=F32):
            return es.enter_context(nc.sbuf_tensor("s_" + name, list(shape), dt)), Buf(name)
        idb, bidb = T("idb4", [128, 128], BF16); c.dma("pool", idb[:], ident_d, wr=[bidb])
        wba, bwba = T("wba", [128, 4, DM], BF16); wbb, bwbb = T("wbb", [128, 4, DM], BF16); wo, bwo = T("wo", [128, 8, DM], BF16)
        for k in range(4):
            c.dma("pool", wba[:, k, :], wba_d[k * 128:(k + 1) * 128, :], wr=[bwba])
            c.dma("pool", wbb[:, k, :], wbb_d[k * 128:(k + 1) * 128, :], wr=[bwbb])
        for k in range(8):
            c.dma("pool", wo[:, k, :], wout_d[k * 128:(k + 1) * 128, :], wr=[bwo])
        xts = [T("x4%d" % i, [128, DM]) for i in range(3)]
        gps = [T("gp%d" % i, [128, 2048]) for i in range(2)]
        yas = [T("ya4%d" % i, [128, 4, 128], BF16) for i in range(2)]
        ybs = [T("yb4%d" % i, [128, 4, 128], BF16) for i in range(2)]
        x1, bx1 = T("x1", [128, DM]); junk, bjunk = T("junk4", [128, DM]); h2, bh2 = T("h2", [128, DM], BF16)
        h2T, bh2T = T("h2T", [128, 8, 128], BF16)
        ss, bss = T("ss4", [128, NT]); c.op("dve", lambda e: e.memset(ss[:], 0.0), wr=[bss])
        rs, brs = T("rs4", [128, NT])
        yaFv = yaF.rearrange("(c p) t -> p c t", p=128); ybFv = ybF.rearrange("(c p) t -> p c t", p=128)
        h2Fv = h2F.rearrange("(c p) t -> p c t", p=128)

        def loads4(tt):
            i = tt % 2; t0 = tt * 128
            c.dma("sp", xts[tt % 3][0][:], x[t0:t0 + 128, :], wr=[xts[tt % 3][1]])
            c.dma("sp", gps[i][0][:], zT[t0:t0 + 128, 2048:4096], wr=[gps[i][1]])
            c.dma("sp", yas[i][0][:], yaFv[:, :, t0:t0 + 128], wr=[yas[i][1]])
            c.dma("sp", ybs[i][0][:], ybFv[:, :, t0:t0 + 128], wr=[ybs[i][1]])
        m1s = [T("m1_%d" % i, [128, DM]) for i in range(2)]; m2s = [T("m2_%d" % i, [128, DM]) for i in range(2)]
        mbs = [T("mb_%d" % i, [128, DM], BF16) for i in range(2)]; mTs = [T("mT_%d" % i, [128, 8, 128], BF16) for i in range(2)]

        def g1(tt):
            if tt + 1 < NT:
                loads4(tt + 1)
            i = tt % 2
            gp, bgp = gps[i]; ya, bya = yas[i]; yb, byb = ybs[i]
            m1, bm1 = m1s[i]; m2, bm2 = m2s[i]; mb, bmb = mbs[i]; mT, bmT = mTs[i]
            c.op("act", lambda e: e.activation(gp[:], gp[:], AF.Sigmoid), rd=[bgp], wr=[bgp])
            for half in range(2):
                hs = slice(half * 512, (half + 1) * 512)
                pa, bpa = ps()
                c.group("pe", [lambda e, k=k: e.matmul(pa[:], ya[:, k, :], wba[:, k, hs], start=(k == 0), stop=(k == 3)) for k in range(4)], rd=[bya, bwba], wr=[bpa])
                c.op("dve", lambda e: e.tensor_tensor(m1[:, hs], pa[:], gp[:, half * 512:(half + 1) * 512], ALU.mult), rd=[bpa, bgp], wr=[bm1])
                pb_, bpb_ = ps()
                c.group("pe", [lambda e, k=k: e.matmul(pb_[:], yb[:, k, :], wbb[:, k, hs], start=(k == 0), stop=(k == 3)) for k in range(4)], rd=[byb, bwbb], wr=[bpb_])
                c.op("dve", lambda e: e.tensor_tensor(m2[:, hs], pb_[:], gp[:, 1024 + half * 512:1024 + (half + 1) * 512], ALU.mult), rd=[bpb_, bgp], wr=[bm2])
            c.op("dve", lambda e: e.tensor_tensor(mb[:], m1[:], m2[:], ALU.add), rd=[bm1, bm2], wr=[bmb])
            yield
            p, bp = ps(); pb = p[:].bitcast(BF16)
            c.group("pe", [lambda e, k=k: e.transpose(pb[:, k * 128:(k + 1) * 128], mb[:, k * 128:(k + 1) * 128], idb[:]) for k in range(8)], rd=[bmb, bidb], wr=[bp])
            c.op("act", lambda e: e.copy(mT[:].rearrange("p a b -> p (a b)"), pb), rd=[bp], wr=[bmT])

        def g2(tt):
            i = tt % 2; t0 = tt * 128
            xt, bxt = xts[tt % 3]; mT, bmT = mTs[i]
            for half in range(2):
                hs = slice(half * 512, (half + 1) * 512)
                po, bpo = ps()
                c.group("pe", [lambda e, k=k: e.matmul(po[:], mT[:, k, :], wo[:, k, hs], start=(k == 0), stop=(k == 7)) for k in range(8)], rd=[bmT, bwo], wr=[bpo])
                c.op("dve", lambda e: e.tensor_tensor(x1[:, hs], po[:], xt[:, hs], ALU.add), rd=[bpo, bxt], wr=[bx1])
            c.dma("pool", x1s[t0:t0 + 128, :], x1[:], rd=[bx1])
            c.op("act", lambda e: e.activation(junk[:], x1[:], AF.Square, accum_out=ss[:, tt:tt + 1]), rd=[bx1], wr=[bjunk, bss])
            c.op("dve", lambda e: e.tensor_scalar(rs[:, tt:tt + 1], ss[:, tt:tt + 1], 1.0 / DM, 1e-6, ALU.mult, ALU.add), rd=[bss], wr=[brs])
            c.op("act", lambda e: e.activation(rs[:, tt:tt + 1], rs[:, tt:tt + 1], AF.Ln), rd=[brs], wr=[brs])
            c.op("act", lambda e: e.activation(rs[:, tt:tt + 1], rs[:, tt:tt + 1], AF.Exp, scale=-0.5), rd=[brs], wr=[brs])
            c.op("dve", lambda e: e.tensor_scalar(h2[:], x1[:], rs[:, tt:tt + 1], None, ALU.mult), rd=[bx1, brs], wr=[bh2])
            yield
            p, bp = ps(); pb = p[:].bitcast(BF16)
            c.group("pe", [lambda e, k=k: e.transpose(pb[:, k * 128:(k + 1) * 128], h2[:, k * 128:(k + 1) * 128], idb[:]) for k in range(8)], rd=[bh2, bidb], wr=[bp])
            c.op("act", lambda e: e.copy(h2T[:].rearrange("p a b -> p (a b)"), pb), rd=[bp], wr=[bh2T])
            c.dma("pool", h2Fv[:, :, t0:t0 + 128], h2T[:], rd=[bh2T])

        def run4(gens):
            alive = [g for g in gens if g is not None]
            while alive:
                for g in list(alive):
                    try:
                        next(g)
                    except StopIteration:
                        alive.remove(g)
        loads4(0)
        run4([g1(0)])
        for tt in range(NT):
            run4([g1(tt + 1) if tt + 1 < NT else None, g2(tt)])
        c.barrier()

    if upto <= 4:
        print("ninst", c.ninst, "nwait", c.nwait); return nc
    with ExitStack() as es:
        def T(name, shape, dt=F32):
            return es.enter_context(nc.sbuf_tensor("s_" + name, list(shape), dt)), Buf(name)
        wup, bwup = T("wup", [128, 8, 2 * DFF], BF16); wdn, bwdn = T("wdn", [128, NFF, DM], BF16)
        g2, bg2 = T("g2", [128, 8]); c.dma("sp", g2[:], n2g, wr=[bg2])
        wub = [Buf("wup_%d" % k) for k in range(8)]
        for k in range(8):
            c.dma("pool", wup[:, k, :], wup_d[k * 128:(k + 1) * 128, :], wr=[wub[k]])
            c.op("dve", lambda e, k=k: e.tensor_scalar(wup[:, k, :], wup[:, k, :], g2[:, k:k + 1], None, ALU.mult), rd=[wub[k], bg2], wr=[wub[k]])
        for f in range(NFF):
            c.dma("pool", wdn[:, f, :], wdn_d[f * 128:(f + 1) * 128, :], wr=[bwdn])
        cw, bcw = T("cw", [128, NFF, 3]); c.dma("sp", cw[:].rearrange("p a b -> p (a b)"), cw_d, wr=[bcw])
        cbt, bcbt = T("cbt", [128, NFF]); c.dma("sp", cbt[:], cb_d, wr=[bcbt])
        cr, bcr = T("cr", [128, NFF, 2]); c.op("dve", lambda e: e.memset(cr[:], 0.0), wr=[bcr])
        h2s = [T("h2s%d" % i, [128, 8, 512], BF16) for i in range(2)]
        asb = [T("asb%d" % i, [128, 514]) for i in range(2)]
        acc = [T("acc%d" % i, [128, 512]) for i in range(2)]
        hg = es.enter_context(nc.sbuf_tensor("hg", [128, NFF, 512], BF16)); bhg = [Buf("hg%d" % f) for f in range(NFF)]
        x1t = [T("x1t%d" % i, [128, DM]) for i in range(2)]
        ost = [T("ost%d" % i, [128, DM]) for i in range(2)]
        h2Fv = h2F.rearrange("(c p) t -> p c t", p=128)
        c.dma("sp", h2s[0][0][:], h2Fv[:, :, 0:512], wr=[h2s[0][1]])
        nx = 0
        for st in range(8):
            if st + 1 < 8:
                c.dma("sp", h2s[(st + 1) % 2][0][:], h2Fv[:, :, (st + 1) * 512:(st + 2) * 512], wr=[h2s[(st + 1) % 2][1]])
            hT, bhT = h2s[st % 2]
            for f in range(NFF):
                pa, bpa = ps()
                c.group("pe", [lambda e, k=k: e.matmul(pa[:], wup[:, k, f * 128:(f + 1) * 128], hT[:, k, :], start=(k == 0), stop=(k == 7)) for k in range(8)], rd=[bhT] + wub, wr=[bpa])
                pg, bpg = ps()
                c.group("pe", [lambda e, k=k: e.matmul(pg[:], wup[:, k, DFF + f * 128:DFF + (f + 1) * 128], hT[:, k, :], start=(k == 0), stop=(k == 7)) for k in range(8)], rd=[bhT] + wub, wr=[bpg])
                a, ba = asb[f % 2]; ac, bac = acc[f % 2]
                c.op("act", lambda e: e.copy(a[:, 0:2], cr[:, f, :]), rd=[bcr], wr=[ba])
                c.op("act", lambda e: e.copy(a[:, 2:514], pa[:]), rd=[bpa], wr=[ba])
                c.op("act", lambda e: e.copy(cr[:, f, :], a[:, 512:514]), rd=[ba], wr=[bcr])
                c.op("dve", lambda e: e.tensor_scalar(ac[:], a[:, 0:512], cw[:, f, 0:1], None, ALU.mult), rd=[ba, bcw], wr=[bac])
                c.op("dve", lambda e: e.scalar_tensor_tensor(ac[:], a[:, 1:513], cw[:, f, 1:2], ac[:], ALU.mult, ALU.add), rd=[ba, bcw, bac], wr=[bac])
                c.op("dve", lambda e: e.scalar_tensor_tensor(ac[:], a[:, 2:514], cw[:, f, 2:3], ac[:], ALU.mult, ALU.add), rd=[ba, bcw, bac], wr=[bac])
                c.op("act", lambda e: e.activation(ac[:], ac[:], AF.Gelu, bias=cbt[:, f:f + 1]), rd=[bac, bcbt], wr=[bac])
                c.op("dve", lambda e: e.tensor_tensor(hg[:, f, :], ac[:], pg[:], ALU.mult), rd=[bac, bpg], wr=[bhg[f]])
            for sub in range(4):
                tt = st * 4 + sub; t0 = tt * 128
                xx, bxx = x1t[nx % 2]; oo, boo = ost[nx % 2]; nx += 1
                c.dma("sp", xx[:], x1s[t0:t0 + 128, :], wr=[bxx])
                for half in range(2):
                    hs = slice(half * 512, (half + 1) * 512)
                    po, bpo = ps()
                    c.group("pe", [lambda e, f=f: e.matmul(po[:], hg[:, f, sub * 128:(sub + 1) * 128], wdn[:, f, hs], start=(f == 0), stop=(f == NFF - 1)) for f in range(NFF)],
                            rd=bhg + [bwdn], wr=[bpo])
                    c.op("dve", lambda e: e.tensor_tensor(oo[:, hs], po[:], xx[:, hs], ALU.add), rd=[bpo, bxx], wr=[boo])
                c.dma("pool", out[t0:t0 + 128, :], oo[:], rd=[boo])
        c.barrier()
    print("ninst", c.ninst, "nwait", c.nwait, {e: c.cnt[e] for e in c.cnt})
    return nc


def _consts():
    i = np.arange(128)
    su = (i[:, None] < i[None, :]).astype(np.float32)
    ui = (i[:, None] <= i[None, :]).astype(np.float32)
    mu2 = np.concatenate([su, ui, su, ui], axis=1)
    sl = (i[None, :] < i[:, None]).astype(np.float32)
    sl2 = np.concatenate([sl, sl], axis=1)
    bones = ((i[:, None] // 64) == (i[None, :] // 64)).astype(np.float32)
    q2 = np.arange(256)
    cb0 = np.where(i[:, None] > q2[None, :], -BIG, 0.0).astype(np.float32)
    cb1 = np.where(i[:, None] + 128 > q2[None, :], -BIG, 0.0).astype(np.float32)
    cbias2 = np.concatenate([cb0, cb1], axis=1)
    sel65 = np.zeros((65, 64), np.float32); sel65[64, :] = 1.0
    j = np.arange(16)
    pok = (j[None, :] < j[:, None]).astype(np.float32)
    pbias = ((pok - 1.0) * 1e30).astype(np.float32)
    half = 8
    invf = (500000.0 ** (-np.arange(half, dtype=np.float32) / half)).astype(np.float32) / np.float32(2.0 * math.pi)
    return dict(ident=np.eye(128, dtype=np.float32), mu2=mu2, sl2=sl2, bones=bones, cbias2=cbias2, sel65=sel65,
                pok=pok.reshape(1, 256), pbias=pbias.reshape(1, 256), invf=invf.reshape(1, 8).astype(np.float32))


def _fm(v, n):
    return np.ascontiguousarray(np.asarray(v, np.float32).reshape(n, 128).T)


def _prep(inp):
    f = lambda a: np.ascontiguousarray(np.asarray(a, dtype=np.float32))
    w_in = f(inp["w_in"][0])
    fm_cols = np.r_[0:512, 512:1024, 1536:1600, 1600:1664, 1664:1824]
    tm_cols = np.r_[1024:1536, 1824:3360, 3360:5408]
    mu = f(inp["rwkv_mu"][0])
    mu_fm = np.zeros(1408, np.float32); mu_fm[:FMC] = mu[fm_cols]
    rk = f(inp["rwkv_r_k"][0])
    rkb = np.zeros((128, 8), np.float32)
    for h in range(8):
        rkb[(h % 2) * 64:(h % 2 + 1) * 64, h] = rk[h]
    cw = f(inp["ffn_conv_w"][0])
    cwl = np.ascontiguousarray(cw.reshape(3, NFF, 128).transpose(2, 1, 0)).reshape(128, NFF * 3)
    shared = dict(
        w_in=np.ascontiguousarray(w_in[:, np.r_[fm_cols, tm_cols]]),
        n1g=_fm(inp["norm1_g"][0], 8), mu_fm=_fm(mu_fm, 11), mu_v=f(mu[1024:1536]).reshape(1, 512),
        wdec=np.concatenate([f(inp["w_decay_up"][0]), f(inp["decay_bias"][0]).reshape(1, 512)], 0),
        waaa=np.concatenate([f(inp["w_aaa_up"][0]), f(inp["aaa_bias"][0]).reshape(1, 512)], 0),
        wgate=f(inp["w_gate_up"][0]), kk_fm=_fm(inp["rwkv_k_k"][0], 4), ka_fm=_fm(inp["rwkv_k_a"][0], 4), rkb=rkb,
        lng=f(inp["rwkv_ln_g"][0]).reshape(1, 512), lnb=f(inp["rwkv_ln_b"][0]).reshape(1, 512),
        qkg=np.concatenate([np.tile(f(inp["q_norm_g"][0]), 8), np.tile(f(inp["k_norm_g"][0]), 8)]).reshape(1, 1024),
        wba=f(inp["w_branch_a"][0]), wbb=f(inp["w_branch_b"][0]), wout=f(inp["w_out"][0]), n2g=_fm(inp["norm2_g"][0], 8),
        wup=f(inp["w_ffn_up"][0]), cw=cwl, cb=_fm(inp["ffn_conv_b"][0], NFF), wdn=f(inp["w_ffn_down"][0]),
    )
    shared.update(_consts())
    xs = np.asarray(inp["x"], np.float32); ps_ = np.asarray(inp["positions"], np.int32)
    maps = []
    for b in range(8):
        m = dict(shared)
        m["x"] = np.ascontiguousarray(xs[b])
        m["pos"] = np.ascontiguousarray(ps_[b].reshape(NT, 128).T)
        maps.append(m)
    return maps


def kernel(**inputs):
    maps = _prep(inputs)
    nc = build()
    res = run_bass_kernel_spmd(nc, maps, core_ids=list(range(8)))
    return np.stack([np.asarray(r["out"], np.float32) for r in res.results], axis=0)
```

```python
import numpy as np
import concourse.bass as bass
import concourse.mybir as mybir
from concourse.bass_utils import run_bass_kernel_spmd

F32 = mybir.dt.float32
BF16 = mybir.dt.bfloat16
I32 = mybir.dt.int32
ALU = mybir.AluOpType
AF = mybir.ActivationFunctionType
AX = mybir.AxisListType


class Buf:
    __slots__ = ("name", "lastw", "readers")

    def __init__(self, name):
        self.name = name
        self.lastw = None
        self.readers = {}


class Ctx:
    def __init__(self, nc, n_dma_sems=24):
        self.nc = nc
        self.eng = {"pe": nc.tensor, "act": nc.scalar, "dve": nc.vector,
                    "pool": nc.gpsimd, "sp": nc.sync}
        self.sem = {}
        self.cnt = {}
        self.waited = {e: {} for e in self.eng}
        self._stack = []
        for e in ("pe", "act", "dve", "pool"):
            cm = nc.semaphore("s_" + e)
            self.sem[e] = cm.__enter__()
            self._stack.append(cm)
            self.cnt[e] = 0
        self.dsem = []
        self.dpool = {"hw": [], "sw": []}
        for i in range(n_dma_sems):
            cm = nc.semaphore("d%d" % i)
            self.dsem.append([cm.__enter__(), 0])
            self._stack.append(cm)
            self.dpool["sw" if i < 8 else "hw"].append(i)
        self.dnext = {"hw": 0, "sw": 0}
        self.semh = {}
        for e in self.sem:
            self.semh[("e", e)] = self.sem[e]
        for i, (h, _) in enumerate(self.dsem):
            self.semh[("d", i)] = h
        self.nwait = 0
        self.ninst = 0

    def _wait(self, e, toks):
        w = self.waited[e]
        best = {}
        for t in toks:
            if t is None:
                continue
            k, v = t[0], t[1]
            if w.get(k, 0) >= v:
                continue
            if best.get(k, 0) < v:
                best[k] = v
        for k, v in best.items():
            self.eng[e].wait_ge(self.semh[k], v)
            w[k] = v
            self.nwait += 1

    def _deps(self, e, rd, wr):
        toks = []
        me = ("e", e)
        for b in rd:
            if b.lastw is not None:
                if not (e == "pe" and b.lastw[0] == me):
                    toks.append(b.lastw)
        for b in wr:
            if b.lastw is not None and not (e == "pe" and b.lastw[0] == me):
                toks.append(b.lastw)
            for k, t in b.readers.items():
                if not (e == "pe" and k == me):
                    toks.append(t)
        return toks

    def _mark(self, tok, rd, wr):
        for b in rd:
            b.readers[tok[0]] = tok
        for b in wr:
            b.lastw = tok
            b.readers = {}

    def op(self, e, fn, rd=(), wr=()):
        self._wait(e, self._deps(e, rd, wr))
        ins = fn(self.eng[e])
        self.cnt[e] += 1
        ins.then_inc(self.sem[e], 1)
        tok = (("e", e), self.cnt[e])
        self._mark(tok, rd, wr)
        self.ninst += 1
        return tok

    def group(self, e, fns, rd=(), wr=()):
        self._wait(e, self._deps(e, rd, wr))
        ins = None
        for fn in fns:
            ins = fn(self.eng[e])
            self.ninst += 1
        self.cnt[e] += 1
        ins.then_inc(self.sem[e], 1)
        tok = (("e", e), self.cnt[e])
        self._mark(tok, rd, wr)
        return tok

    def dma(self, q, out, in_, rd=(), wr=(), **kw):
        kind = "sw" if q == "pool" else "hw"
        pool = self.dpool[kind]
        i = pool[self.dnext[kind] % len(pool)]
        self.dnext[kind] += 1
        h, c = self.dsem[i]
        k = ("d", i)
        toks = self._deps(q, rd, wr)
        if c > 0:
            toks.append((k, 16 * c))
        self._wait(q, toks)
        self.eng[q].dma_start(out=out, in_=in_, **kw).then_inc(h, 16)
        self.dsem[i][1] = c + 1
        tok = (k, 16 * (c + 1))
        self._mark(tok, rd, wr)
        self.ninst += 1
        return tok

    def wait_all(self, e, bufs):
        toks = []
        for b in bufs:
            toks.append(b.lastw)
            toks.extend(b.readers.values())
        self._wait(e, toks)

    def barrier(self, bufs=()):
        toks = []
        for e in self.sem:
            if self.cnt[e] > 0:
                toks.append((("e", e), self.cnt[e]))
        for i, (h, c) in enumerate(self.dsem):
            if c > 0:
                toks.append((("d", i), 16 * c))
        for e in self.eng:
            self._wait(e, toks)

from contextlib import ExitStack
import math
import os

S = 4096
DM = 1024
NT = 32
FMC = 1312
TMC = 4096
DFF = 2816
NFF = 22
BIG = 30000.0


def build(debug=False, upto=99):
    nc = bass.Bass("TRN2", target_bir_lowering=False)
    okind = "ExternalOutput" if debug else "Internal"

    def DIN(name, shape, dt=F32):
        return nc.dram_tensor(name, list(shape), dt, kind="ExternalInput").ap()

    x = DIN("x", [S, DM]); pos = DIN("pos", [128, NT], I32)
    w_in = DIN("w_in", [DM, 5408]); n1g = DIN("n1g", [128, 8]); mu_fm = DIN("mu_fm", [128, 11]); mu_v = DIN("mu_v", [1, 512])
    wdec_d = DIN("wdec", [65, 512]); waaa_d = DIN("waaa", [65, 512]); wgate_d = DIN("wgate", [160, 512])
    kk_d = DIN("kk_fm", [128, 4]); ka_d = DIN("ka_fm", [128, 4]); rkb_d = DIN("rkb", [128, 8])
    lng_d = DIN("lng", [1, 512]); lnb_d = DIN("lnb", [1, 512]); qkg_d = DIN("qkg", [1, 1024])
    wba_d = DIN("wba", [512, DM]); wbb_d = DIN("wbb", [512, DM]); wout_d = DIN("wout", [DM, DM]); n2g = DIN("n2g", [128, 8])
    wup_d = DIN("wup", [DM, 2 * DFF]); cw_d = DIN("cw", [128, NFF * 3]); cb_d = DIN("cb", [128, NFF]); wdn_d = DIN("wdn", [DFF, DM])
    ident_d = DIN("ident", [128, 128]); mu2_d = DIN("mu2", [128, 512]); sl2_d = DIN("sl2", [128, 256]); bones_d = DIN("bones", [128, 128])
    cbias2_d = DIN("cbias2", [128, 512]); sel65_d = DIN("sel65", [65, 64]); pok_d = DIN("pok", [1, 256]); pbias_d = DIN("pbias", [1, 256]); invf_d = DIN("invf", [1, 8])
    out = nc.dram_tensor("out", [S, DM], F32, kind="ExternalOutput").ap()
    zF = nc.dram_tensor("zF", [1408, S], F32, kind=okind).ap()
    zT = nc.dram_tensor("zT", [S, TMC], F32, kind=okind).ap()
    yaF = nc.dram_tensor("yaF", [512, S], BF16, kind=okind).ap()
    ybF = nc.dram_tensor("ybF", [512, S], BF16, kind=okind).ap()
    x1s = nc.dram_tensor("x1s", [S, DM], F32, kind=okind).ap()
    h2F = nc.dram_tensor("h2F", [DM, S], BF16, kind=okind).ap()

    c = Ctx(nc)
    PSB = [(nc.alloc_psum_tensor("psb%d" % i, [128, 512], F32), Buf("psb%d" % i)) for i in range(8)]
    pst = {"i": 0, "n": 8, "off": 0}

    def ps():
        i = pst["off"] + pst["i"] % pst["n"]
        pst["i"] += 1
        return PSB[i]

    rr = {"i": 0}

    def ev():
        rr["i"] += 1
        return "act" if rr["i"] % 2 else "dve"

    def bcast(ap, shape, axis):
        return ap.unsqueeze(axis).to_broadcast(list(shape))

    with ExitStack() as es:
        def T(name, shape, dt=F32):
            return es.enter_context(nc.sbuf_tensor("s_" + name, list(shape), dt)), Buf(name)
        w, bw = T("w1", [128, 8, 5408], BF16)
        g1, bg1 = T("g1", [128, 8]); muf, bmuf = T("muf", [128, 11])
        idb, bidb = T("idb1", [128, 128], BF16)
        c.dma("sp", g1[:], n1g, wr=[bg1]); c.dma("sp", muf[:], mu_fm, wr=[bmuf])
        c.dma("pool", idb[:], ident_d, wr=[bidb])
        wb = [Buf("w1_%d" % k) for k in range(8)]
        for kc in range(8):
            c.dma("pool", w[:, kc, :], w_in[kc * 128:(kc + 1) * 128, :], wr=[wb[kc]])
            c.op("dve", lambda e, kc=kc: e.tensor_scalar(w[:, kc, :], w[:, kc, :], g1[:, kc:kc + 1], None, ALU.mult),
                 rd=[wb[kc], bg1], wr=[wb[kc]])
        ss, bss = T("ss1", [128, NT]); c.op("dve", lambda e: e.memset(ss[:], 0.0), wr=[bss])
        rs, brs = T("rs1", [128, NT])
        junk, bjunk = T("junk1", [128, DM])
        xts = [T("xt%d" % i, [128, DM]) for i in range(2)]
        hTs = [es.enter_context(nc.sbuf_tensor("hT%d" % i, [128, 8, 512], BF16)) for i in range(2)]
        hTb = [[Buf("hT%d_%d" % (i, s)) for s in range(4)] for i in range(2)]
        stgs = [T("stg%d" % i, [128, TMC]) for i in range(2)]
        zsb = [T("zsb%d" % j, [128, 513]) for j in range(11)]
        for j in range(11):
            c.op("pool", lambda e, j=j: e.memset(zsb[j][0][:, 0:1], 0.0), wr=[zsb[j][1]])
        tds = [T("td%d" % i, [128, 512]) for i in range(2)]
        ostg = [T("ostg%d" % i, [128, 512]) for i in range(3)]
        no = {"i": 0}
        NSTv = int(os.environ.get('NST', '8'))
        hbs = [T("hb1_%d" % i, [128, DM], BF16) for i in range(2)]

        def s1(tt):
            st, sub = tt // 4, tt % 4
            hT = hTs[st % 2]
            xt, bxt = xts[tt % 2]
            hb, bhb = hbs[tt % 2]
            c.dma("sp", xt[:], x[tt * 128:(tt + 1) * 128, :], wr=[bxt])
            c.op("act", lambda e: e.activation(junk[:], xt[:], AF.Square, accum_out=ss[:, tt:tt + 1]), rd=[bxt], wr=[bjunk, bss])
            c.op("dve", lambda e: e.tensor_scalar(rs[:, tt:tt + 1], ss[:, tt:tt + 1], 1.0 / DM, 1e-6, ALU.mult, ALU.add), rd=[bss], wr=[brs])
            c.op("act", lambda e: e.activation(rs[:, tt:tt + 1], rs[:, tt:tt + 1], AF.Ln), rd=[brs], wr=[brs])
            c.op("act", lambda e: e.activation(rs[:, tt:tt + 1], rs[:, tt:tt + 1], AF.Exp, scale=-0.5), rd=[brs], wr=[brs])
            c.op("dve", lambda e: e.tensor_scalar(hb[:], xt[:], rs[:, tt:tt + 1], None, ALU.mult), rd=[bxt, brs], wr=[bhb])
            p, bp = ps(); pb = p[:].bitcast(BF16)
            c.group("pe", [lambda e, k=k: e.transpose(pb[:, k * 128:(k + 1) * 128], hb[:, k * 128:(k + 1) * 128], idb[:]) for k in range(8)],
                    rd=[bhb, bidb], wr=[bp])
            c.op("act", lambda e: e.copy(hT[:, :, sub * 128:(sub + 1) * 128], pb.rearrange("p (k t) -> p k t", k=8)), rd=[bp], wr=[hTb[st % 2][sub]])

        def s2(tt):
            st, sub = tt // 4, tt % 4
            hT = hTs[st % 2]
            stg, bstg = stgs[tt % 2]
            for gi in range(8):
                p, bp = ps()
                c.group("pe", [lambda e, k=k, p=p: e.matmul(p[:], hT[:, k, sub * 128:(sub + 1) * 128], w[:, k, FMC + gi * 512:FMC + (gi + 1) * 512],
                                                           start=(k == 0), stop=(k == 7)) for k in range(8)],
                        rd=[hTb[st % 2][sub]] + wb, wr=[bp])
                en = ev()
                if en == "act":
                    c.op("act", lambda e, p=p: e.copy(stg[:, gi * 512:(gi + 1) * 512], p[:]), rd=[bp], wr=[bstg])
                else:
                    c.op("dve", lambda e, p=p: e.tensor_copy(stg[:, gi * 512:(gi + 1) * 512], p[:]), rd=[bp], wr=[bstg])
            c.dma("pool", zT[tt * 128:(tt + 1) * 128, :], stg[:], rd=[bstg])

        def fm(st):
            hT = hTs[st % 2]
            for j in range(11):
                ncol = 32 if j == 10 else 128
                z, bz = zsb[j]
                p, bp = ps()
                c.group("pe", [lambda e, k=k, p=p: e.matmul(p[0:ncol, :], w[:, k, j * 128:j * 128 + ncol], hT[:, k, :], start=(k == 0), stop=(k == 7)) for k in range(8)],
                        rd=hTb[st % 2] + wb, wr=[bp])
                c.op("act", lambda e, p=p: e.copy(z[0:ncol, 1:513], p[0:ncol, :]), rd=[bp], wr=[bz])
                td, btd = tds[j % 2]
                c.op("dve", lambda e: e.tensor_tensor(td[0:ncol, :], z[0:ncol, 0:512], z[0:ncol, 1:513], ALU.subtract), rd=[bz], wr=[btd])
                o, bo = ostg[no["i"] % 3]; no["i"] += 1
                c.op("dve", lambda e: e.scalar_tensor_tensor(o[0:ncol, :], td[0:ncol, :], muf[0:ncol, j:j + 1], z[0:ncol, 1:513], ALU.mult, ALU.add),
                     rd=[btd, bz, bmuf], wr=[bo])
                c.op("act", lambda e: e.copy(z[0:ncol, 0:1], z[0:ncol, 512:513]), rd=[bz], wr=[bz])
                c.dma("pool", zF[j * 128:j * 128 + ncol, st * 512:(st + 1) * 512], o[0:ncol, :], rd=[bo])

        ntl = NSTv * 4
        if ntl > 0:
            s1(0)
        for tt in range(ntl):
            if tt + 1 < ntl:
                s1(tt + 1)
            s2(tt)
            if tt % 4 == 3:
                fm(tt // 4)
        c.barrier()

    if upto <= 1:
        print("ninst", c.ninst, "nwait", c.nwait); return nc
    with ExitStack() as es:
        def T(name, shape, dt=F32):
            return es.enter_context(nc.sbuf_tensor("s_" + name, list(shape), dt)), Buf(name)
        idb, bidb = T("idb2", [128, 128], BF16); c.dma("pool", idb[:], ident_d, wr=[bidb])
        wdec, bwdec = T("wdec", [65, 512], BF16); c.dma("pool", wdec[:], wdec_d, wr=[bwdec])
        waaa, bwaaa = T("waaa", [65, 512], BF16); c.dma("pool", waaa[:], waaa_d, wr=[bwaaa])
        wgate, bwgate = T("wgate", [128, 2, 512], BF16)
        c.dma("pool", wgate[:, 0, :], wgate_d[0:128, :], wr=[bwgate]); c.dma("pool", wgate[0:32, 1, :], wgate_d[128:160, :], wr=[bwgate])
        kkf, bkkf = T("kkf", [128, 4]); c.dma("sp", kkf[:], kk_d, wr=[bkkf])
        kaf, bkaf = T("kaf", [128, 4]); c.dma("sp", kaf[:], ka_d, wr=[bkaf])
        c0f, bc0f = T("c0f", [128, 4])
        c.op("dve", lambda e: e.tensor_scalar(c0f[:], kaf[:], -1.0, 1.0, ALU.mult, ALU.add), rd=[bkaf], wr=[bc0f])
        rkb, brkb = T("rkb", [128, 8]); c.dma("sp", rkb[:], rkb_d, wr=[brkb])
        lng, blng = T("lng", [128, 512]); c.dma("sp", lng[:], lng_d.partition_broadcast(128), wr=[blng])
        lnb, blnb = T("lnb", [128, 512]); c.dma("sp", lnb[:], lnb_d.partition_broadcast(128), wr=[blnb])
        muv, bmuv = T("muv", [128, 512]); c.dma("sp", muv[:], mu_v.partition_broadcast(128), wr=[bmuv])
        MU2, bMU2 = T("MU2", [128, 512]); c.dma("sp", MU2[:], mu2_d, wr=[bMU2])
        SL2, bSL2 = T("SL2", [128, 256]); c.dma("sp", SL2[:], sl2_d, wr=[bSL2])
        bones, bbones = T("bones", [128, 128], BF16); c.dma("pool", bones[:], bones_d, wr=[bbones])
        S32, bS32 = T("S32", [128, 4, 64]); c.op("dve", lambda e: e.memset(S32[:], 0.0), wr=[bS32])
        Sb, bSb = T("Sb", [128, 4, 64], BF16); c.op("dve", lambda e: e.memset(Sb[:], 0.0), wr=[bSb])
        tha = [T("tha%d" % i, [65, 128], BF16) for i in range(2)]
        xaa = [T("xaa%d" % i, [65, 128], BF16) for i in range(2)]
        for i in range(2):
            c.op("dve", lambda e, i=i: e.memset(tha[i][0][:], 1.0), wr=[tha[i][1]])
            c.op("dve", lambda e, i=i: e.memset(xaa[i][0][:], 1.0), wr=[xaa[i][1]])
        rFs = [T("rF%d" % i, [128, 4, 128]) for i in range(2)]
        kFs = [T("kF%d" % i, [128, 4, 128]) for i in range(2)]
        xws = [T("xw%d" % i, [64, 128]) for i in range(2)]
        xas = [T("xa%d" % i, [64, 128]) for i in range(2)]
        xg0s = [T("xg0%d" % i, [128, 128]) for i in range(2)]
        xg1s = [T("xg1%d" % i, [32, 128]) for i in range(2)]
        vTs = [T("vT%d" % i, [128, 512]) for i in range(2)]
        vPs = [T("vP%d" % i, [128, 512]) for i in range(2)]
        for i in range(2):
            c.op("dve", lambda e, i=i: e.memset(vPs[i][0][:], 0.0), wr=[vPs[i][1]])
        v32s = [T("v32%d" % i, [128, 512]) for i in range(3)]; vbs = [T("vb%d" % i, [128, 512], BF16) for i in range(3)]; vtmp, bvtmp = T("vtmp", [128, 512])
        sg0, bsg0 = T("sg0", [128, 128], BF16); sg1, bsg1 = T("sg1", [32, 128], BF16)
        tg, btg = T("tg", [128, 512]); sgt, bsgt = T("sgt", [128, 128])
        logw, blogw = T("logw", [128, 4, 128]); lgi, blgi = T("lgi", [128, 4, 128]); lge, blge = T("lge", [128, 4, 128])
        alr, balr = T("alr", [128, 4, 128]); gTs = [T("gT%d" % i, [128, 512]) for i in range(3)]
        kkr, bkkr = T("kkr", [128, 4, 128]); sqb, bsqb = T("sqb", [128, 512], BF16); rn, brn = T("rn", [128, 512])
        kkn, bkkn = T("kkn", [128, 4, 128]); fF, bfF = T("fF", [128, 4, 128]); kM, bkM = T("kM", [128, 4, 128])
        gins = [T("gin%d" % i, [128, 4, 128]) for i in range(3)]; ginv, bginv = T("ginv", [128, 4, 128]); gex, bgex = T("gex", [128, 4, 128])
        ARs = [T("ARZ%d" % i, [128, 8, 2, 128], BF16) for i in range(3)]; bF, bbF = T("bF", [128, 4, 128])
        Bt, bBt = T("BtZ", [128, 8, 128], BF16); Kt, bKt = T("KtZ", [128, 8, 128], BF16)
        for (t_, b_) in (ARs[0], ARs[1], ARs[2], (Bt, bBt), (Kt, bKt)):
            c.op("pool", lambda e, t_=t_: e.memset(t_[:], 0.0), wr=[b_])
        Dd, bDd = T("Dd", [128, 4, 128]); Bh, bBh = T("Bh", [128, 4, 128], BF16); Kh, bKh = T("Kh", [128, 4, 128], BF16)
        BKhTs = [T("BKhT%d" % i, [128, 1024], BF16) for i in range(3)]
        rk, brk = T("rk", [128, 4, 128]); coefs = [T("coef%d" % i, [128, 8]) for i in range(3)]
        MABs = [T("MAB%d" % i, [128, 8, 2, 128], BF16) for i in range(3)]; MAKs = [T("MAK%d" % i, [128, 8, 2, 128], BF16) for i in range(3)]; MABTs = [T("MABT%d" % i, [128, 8, 128], BF16) for i in range(3)]
        Pk = [T("Pk%d" % i, [128, 8, 128], BF16) for i in range(2)]
        PTk = [T("PTk%d" % i, [128, 8, 128], BF16) for i in range(2)]
        ACk = [T("ACk%d" % i, [128, 8, 128], BF16) for i in range(2)]
        MinvS = [T("Minv%d" % i, [128, 8, 128], BF16) for i in range(2)]
        XT, bXT = T("XT", [128, 512], BF16); UT, bUT = T("UT", [128, 512], BF16)
        tmpS, btmpS = T("tmpS", [128, 4, 64])
        s1, bs1 = T("s1", [128, 8]); s2, bs2 = T("s2", [128, 8]); mean, bmean = T("mean", [128, 8]); var, bvar = T("var", [128, 8])
        sqt, bsqt = T("sqt", [128, 512]); yn, byn = T("yn", [128, 512]); bon, bbon = T("bon", [128, 512])
        yab, byab = T("yab", [128, 512], BF16); yaT, byaT = T("yaT", [128, 4, 128], BF16)

        zFr = zF[0:512, :].rearrange("(c p) t -> p c t", p=128)
        zFk = zF[512:1024, :].rearrange("(c p) t -> p c t", p=128)
        yaFv = yaF.rearrange("(c p) t -> p c t", p=128)

        def loads(ch):
            i = ch % 2; t0 = ch * 128
            c.dma("sp", rFs[i][0][:], zFr[:, :, t0:t0 + 128], wr=[rFs[i][1]])
            c.dma("sp", kFs[i][0][:], zFk[:, :, t0:t0 + 128], wr=[kFs[i][1]])
            c.dma("sp", xws[i][0][:], zF[1024:1088, t0:t0 + 128], wr=[xws[i][1]])
            c.dma("sp", xas[i][0][:], zF[1088:1152, t0:t0 + 128], wr=[xas[i][1]])
            c.dma("sp", xg0s[i][0][:], zF[1152:1280, t0:t0 + 128], wr=[xg0s[i][1]])
            c.dma("sp", xg1s[i][0][:], zF[1280:1312, t0:t0 + 128], wr=[xg1s[i][1]])
            c.dma("sp", vTs[i][0][:], zT[t0:t0 + 128, 0:512], wr=[vTs[i][1]])
            if ch == 0:
                c.dma("sp", vPs[i][0][1:128, :], zT[0:127, 0:512], wr=[vPs[i][1]])
            else:
                c.dma("sp", vPs[i][0][:], zT[t0 - 1:t0 + 127, 0:512], wr=[vPs[i][1]])

        loads(0)
        NCH = int(os.environ.get('NCH', str(NT)))
        STG = int(os.environ.get('STG', '99'))
        pqA = {"i": 0}; pqB = {"i": 0}
        pqC = {"i": 0}
        def psA():
            pqA["i"] += 1
            return PSB[pqA["i"] % 3]
        def psC():
            pqC["i"] += 1
            return PSB[3 + pqC["i"] % 3]
        def psB():
            pqB["i"] += 1
            return PSB[6 + pqB["i"] % 2]
        def stageA(ch):
            if ch + 1 < NCH:
                loads(ch + 1)
            i = ch % 2; t0 = ch * 128
            j3 = ch % 3
            v32, bv32 = v32s[j3]; vb, bvb = vbs[j3]; gT, bgT = gTs[j3]; gin, bgin = gins[j3]; AR, bAR = ARs[j3]
            BKhT, bBKhT = BKhTs[j3]; coef, bcoef = coefs[j3]; MAB, bMAB = MABs[j3]; MAK, bMAK = MAKs[j3]; MABT, bMABT = MABTs[j3]
            rF, brF = rFs[i]; kF, bkF = kFs[i]; xw, bxw = xws[i]; xa, bxa = xas[i]
            xg0, bxg0 = xg0s[i]; xg1, bxg1 = xg1s[i]; vT, bvT = vTs[i]; vP, bvP = vPs[i]
            th, bth = tha[i]; xab, bxab = xaa[i]
            c.op("pool", lambda e: e.tensor_tensor(vtmp[:], vP[:], vT[:], ALU.subtract), rd=[bvP, bvT], wr=[bvtmp])
            c.op("pool", lambda e: e.tensor_tensor(vtmp[:], vtmp[:], muv[:], ALU.mult), rd=[bvtmp, bmuv], wr=[bvtmp])
            c.op("pool", lambda e: e.tensor_tensor(v32[:], vtmp[:], vT[:], ALU.add), rd=[bvtmp, bvT], wr=[bv32])
            c.op("pool", lambda e: e.tensor_copy(vb[:], v32[:]), rd=[bv32], wr=[bvb])
            c.op("act", lambda e: e.activation(th[0:64, :], xw[:], AF.Tanh), rd=[bxw], wr=[bth])
            c.op("dve", lambda e: e.tensor_copy(xab[0:64, :], xa[:]), rd=[bxa], wr=[bxab])
            c.op("act", lambda e: e.activation(sgt[:], xg0[:], AF.Tanh, scale=0.5), rd=[bxg0], wr=[bsgt])
            c.op("dve", lambda e: e.tensor_scalar(sg0[:], sgt[:], 0.5, 0.5, ALU.mult, ALU.add), rd=[bsgt], wr=[bsg0])
            c.op("act", lambda e: e.activation(sgt[0:32, :], xg1[:], AF.Tanh, scale=0.5), rd=[bxg1], wr=[bsgt])
            c.op("dve", lambda e: e.tensor_scalar(sg1[:], sgt[0:32, :], 0.5, 0.5, ALU.mult, ALU.add), rd=[bsgt], wr=[bsg1])
            p, bp = psA()
            c.group("pe", [lambda e, q=q, p=p: e.matmul(p[:, q * 128:(q + 1) * 128], wdec[:, q * 128:(q + 1) * 128], th[:], start=True, stop=True) for q in range(4)],
                    rd=[bwdec, bth], wr=[bp])
            c.op("act", lambda e, p=p: e.activation(tg[:], p[:], AF.Tanh, scale=0.5), rd=[bp], wr=[btg])
            c.op("dve", lambda e: e.tensor_scalar(logw[:].rearrange("p a b -> p (a b)"), tg[:], -0.5 * math.exp(-0.5), -0.5 * math.exp(-0.5), ALU.mult, ALU.add),
                 rd=[btg], wr=[blogw])
            for q in range(4):
                c.op("dve", lambda e, q=q: e.tensor_tensor_scan(lgi[:, q, :], logw[:, q, :], logw[:, q, :], 0.0, ALU.add, ALU.bypass), rd=[blogw], wr=[blgi])
            c.op("pool", lambda e: e.tensor_tensor(lge[:], lgi[:], logw[:], ALU.subtract), rd=[blgi, blogw], wr=[blge])
            yield
            p, bp = psA()
            c.group("pe", [lambda e, q=q, p=p: e.matmul(p[:, q * 128:(q + 1) * 128], waaa[:, q * 128:(q + 1) * 128], xab[:], start=True, stop=True) for q in range(4)],
                    rd=[bwaaa, bxab], wr=[bp])
            c.op("act", lambda e, p=p: e.activation(tg[:], p[:], AF.Tanh, scale=0.5), rd=[bp], wr=[btg])
            c.op("dve", lambda e: e.tensor_scalar(alr[:].rearrange("p a b -> p (a b)"), tg[:], 0.5, 0.5, ALU.mult, ALU.add), rd=[btg], wr=[balr])
            p, bp = psA()
            c.group("pe", [lambda e, p=p: e.matmul(p[:], sg0[:], wgate[:, 0, :], start=True, stop=False),
                           lambda e, p=p: e.matmul(p[:], sg1[:], wgate[0:32, 1, :], start=False, stop=True)], rd=[bsg0, bsg1, bwgate], wr=[bp])
            c.op("act", lambda e, p=p: e.copy(gT[:], p[:]), rd=[bp], wr=[bgT])
            for q in range(4):
                c.op("dve", lambda e, q=q: e.tensor_scalar(kkr[:, q, :], kF[:, q, :], kkf[:, q:q + 1], None, ALU.mult), rd=[bkF, bkkf], wr=[bkkr])
            c.op("pool", lambda e: e.tensor_tensor(sqb[:], kkr[:].rearrange("p a b -> p (a b)"), kkr[:].rearrange("p a b -> p (a b)"), ALU.mult), rd=[bkkr], wr=[bsqb])
            p, bp = psA()
            c.op("pe", lambda e, p=p: e.matmul(p[:], bones[:], sqb[:], start=True, stop=True), rd=[bbones, bsqb], wr=[bp])
            c.op("act", lambda e, p=p: e.activation(rn[:], p[:], AF.Ln), rd=[bp], wr=[brn])
            c.op("act", lambda e: e.activation(rn[:], rn[:], AF.Exp, scale=-0.5), rd=[brn], wr=[brn])
            c.op("dve", lambda e: e.tensor_tensor(kkn[:].rearrange("p a b -> p (a b)"), kkr[:].rearrange("p a b -> p (a b)"), rn[:], ALU.mult), rd=[bkkr, brn], wr=[bkkn])
            yield
            for q in range(4):
                c.op("dve", lambda e, q=q: e.tensor_scalar(fF[:, q, :], alr[:, q, :], kaf[:, q:q + 1], c0f[:, q:q + 1], ALU.mult, ALU.add), rd=[balr, bkaf, bc0f], wr=[bfF])
            c.op("pool", lambda e: e.tensor_tensor(kM[:], kF[:], fF[:], ALU.mult), rd=[bkF, bfF], wr=[bkM])
            c.op("act", lambda e: e.activation(gin[:], lgi[:], AF.Exp), rd=[blgi], wr=[bgin])
            c.op("act", lambda e: e.activation(ginv[:], lgi[:], AF.Exp, scale=-1.0), rd=[blgi], wr=[bginv])
            c.op("act", lambda e: e.activation(gex[:], lge[:], AF.Exp), rd=[blge], wr=[bgex])
            for q in range(4):
                c.op("act", lambda e, q=q: e.activation(Dd[:, q, :], lgi[:, q, :], AF.Exp, bias=lgi[:, q, 127:128], scale=-1.0), rd=[blgi], wr=[bDd])
            c.op("pool", lambda e: e.tensor_tensor(bF[:], kkn[:], alr[:], ALU.mult), rd=[bkkn, balr], wr=[bbF])
            for hh in range(2):
                r0, r1 = hh * 64, (hh + 1) * 64
                ARv = AR[r0:r1, :, :, :].rearrange("p (q two) a t -> p q two a t", two=2)[:, :, hh, :, :]
                Btv = Bt[r0:r1, :, :].rearrange("p (q two) t -> p q two t", two=2)[:, :, hh, :]
                Ktv = Kt[r0:r1, :, :].rearrange("p (q two) t -> p q two t", two=2)[:, :, hh, :]
                c.op("dve", lambda e: e.tensor_tensor(ARv[:, :, 1, :], rF[r0:r1, :, :], gin[r0:r1, :, :], ALU.mult), rd=[brF, bgin], wr=[bAR])
                c.op("dve", lambda e: e.scalar_tensor_tensor(ARv[:, :, 0, :], kkn[r0:r1, :, :], -1.0, gex[r0:r1, :, :], ALU.mult, ALU.mult), rd=[bkkn, bgex], wr=[bAR])
                c.op("dve", lambda e: e.tensor_tensor(Btv, bF[r0:r1, :, :], ginv[r0:r1, :, :], ALU.mult), rd=[bbF, bginv], wr=[bBt])
                c.op("pool", lambda e: e.tensor_tensor(Ktv, kM[r0:r1, :, :], ginv[r0:r1, :, :], ALU.mult), rd=[bkM, bginv], wr=[bKt])
            c.op("pool", lambda e: e.tensor_tensor(Bh[:], bF[:], Dd[:], ALU.mult), rd=[bbF, bDd], wr=[bBh])
            c.op("pool", lambda e: e.tensor_tensor(Kh[:], kM[:], Dd[:], ALU.mult), rd=[bkM, bDd], wr=[bKh])
            yield
            p, bp = psA(); pb = p[:].bitcast(BF16)
            c.group("pe", [lambda e, q=q: e.transpose(pb[:, q * 128:(q + 1) * 128], Bh[:, q, :], idb[:]) for q in range(4)] +
                          [lambda e, q=q: e.transpose(pb[:, 512 + q * 128:512 + (q + 1) * 128], Kh[:, q, :], idb[:]) for q in range(4)],
                    rd=[bBh, bKh, bidb], wr=[bp])
            c.op("act", lambda e: e.copy(BKhT[:], pb), rd=[bp], wr=[bBKhT])
            c.op("pool", lambda e: e.tensor_tensor(rk[:], rF[:], kM[:], ALU.mult), rd=[brF, bkM], wr=[brk])
            p, bp = psA()
            c.group("pe", [lambda e, q=q, p=p: e.matmul(p[:, 2 * q:2 * q + 2], rk[:, q, :], rkb[:, 2 * q:2 * q + 2], start=True, stop=True) for q in range(4)],
                    rd=[brk, brkb], wr=[bp])
            c.op("dve", lambda e, p=p: e.tensor_copy(coef[:], p[:, 0:8]), rd=[bp], wr=[bcoef])
            for q in range(4):
                pA, bpA = psA(); pB, bpB = psA(); pC, bpC = psA()
                fa, fb, fc = [], [], []
                for hh in range(2):
                    h = 2 * q + hh
                    arr = AR[:, h, :, :].rearrange("p a b -> p (a b)")
                    fa.append(lambda e, hh=hh, h=h, arr=arr: e.matmul(pA[:, hh * 256:(hh + 1) * 256], Bt[:, h, :], arr, start=True, stop=True))
                    fb.append(lambda e, hh=hh, h=h, arr=arr: e.matmul(pB[:, hh * 256:(hh + 1) * 256], Kt[:, h, :], arr, start=True, stop=True))
                    fc.append(lambda e, hh=hh, h=h: e.matmul(pC[:, hh * 128:(hh + 1) * 128], AR[:, h, 0, :], Bt[:, h, :], start=True, stop=True))
                c.group("pe", fa, rd=[bBt, bAR], wr=[bpA])
                c.group("pe", fb, rd=[bKt, bAR], wr=[bpB])
                c.group("pe", fc, rd=[bBt, bAR], wr=[bpC])
                c.op("dve", lambda e: e.tensor_tensor(MAB[:, 2 * q:2 * q + 2, :, :].rearrange("p a b c -> p (a b c)"), pA[:], MU2[:], ALU.mult), rd=[bpA, bMU2], wr=[bMAB])
                c.op("dve", lambda e: e.tensor_tensor(MAK[:, 2 * q:2 * q + 2, :, :].rearrange("p a b c -> p (a b c)"), pB[:], MU2[:], ALU.mult), rd=[bpB, bMU2], wr=[bMAK])
                c.op("dve", lambda e: e.tensor_tensor(MABT[:, 2 * q:2 * q + 2, :].rearrange("p a b -> p (a b)"), pC[:, 0:256], SL2[:], ALU.mult), rd=[bpC, bSL2], wr=[bMABT])
            yield

        def stageA2(ch):
            i = ch % 2
            j3 = ch % 3
            MAB, bMAB = MABs[j3]; MABT, bMABT = MABTs[j3]
            AC0, bAC0 = ACk[0]
            c.op("pool", lambda e: e.tensor_tensor(AC0[:], MAB[:, :, 0, :], bcast(idb[:], [128, 8, 128], 1), ALU.add), rd=[bMAB, bidb], wr=[bAC0])
            Pp = lambda h: MAB[:, h, 0, :]
            PTp = lambda h: MABT[:, h, :]
            bPp, bPTp = bMAB, bMABT
            ACp, bACp = AC0, bAC0
            for lv in range(1, 7):
                Pn, bPn = Pk[lv % 2]; PTn, bPTn = PTk[lv % 2]; ACn, bACn = (ACk[lv % 2] if lv < 6 else MinvS[i])
                for grp in range(2):
                    hs = range(4 * grp, 4 * grp + 4)
                    if lv < 6:
                        p, bp = psC()
                        c.group("pe", [lambda e, h=h, p=p, Pp=Pp, PTp=PTp: e.matmul(p[:, (h % 4) * 128:(h % 4 + 1) * 128], PTp(h), Pp(h), start=True, stop=True) for h in hs],
                                rd=[bPp, bPTp], wr=[bp])
                        en = ev()
                        if en == "act":
                            c.op("act", lambda e, p=p: e.copy(Pn[:, 4 * grp:4 * grp + 4, :].rearrange("p a b -> p (a b)"), p[:]), rd=[bp], wr=[bPn])
                        else:
                            c.op("dve", lambda e, p=p: e.tensor_copy(Pn[:, 4 * grp:4 * grp + 4, :].rearrange("p a b -> p (a b)"), p[:]), rd=[bp], wr=[bPn])
                    p, bp = psC()
                    c.group("pe", [lambda e, h=h, p=p, Pp=Pp, PTp=PTp: e.matmul(p[:, (h % 4) * 128:(h % 4 + 1) * 128], Pp(h), PTp(h), start=True, stop=True) for h in hs],
                            rd=[bPp, bPTp], wr=[bp])
                    en = ev()
                    if en == "act":
                        c.op("act", lambda e, p=p: e.copy(PTn[:, 4 * grp:4 * grp + 4, :].rearrange("p a b -> p (a b)"), p[:]), rd=[bp], wr=[bPTn])
                    else:
                        c.op("dve", lambda e, p=p: e.tensor_copy(PTn[:, 4 * grp:4 * grp + 4, :].rearrange("p a b -> p (a b)"), p[:]), rd=[bp], wr=[bPTn])
                    p, bp = psC()
                    c.group("pe", [lambda e, h=h, p=p, ACp=ACp: e.matmul(p[:, (h % 4) * 128:(h % 4 + 1) * 128], PTn[:, h, :], ACp[:, h, :], start=True, stop=True) for h in hs],
                            rd=[bACp, bPTn], wr=[bp])
                    c.op("dve", lambda e, p=p, ACp=ACp: e.tensor_tensor(ACn[:, 4 * grp:4 * grp + 4, :].rearrange("p a b -> p (a b)"), p[:],
                                                                      ACp[:, 4 * grp:4 * grp + 4, :].rearrange("p a b -> p (a b)"), ALU.add), rd=[bp, bACp], wr=[bACn])
                yield
                Pp = (lambda Pn: (lambda h: Pn[:, h, :]))(Pn)
                PTp = (lambda PTn: (lambda h: PTn[:, h, :]))(PTn)
                bPp, bPTp = bPn, bPTn
                ACp, bACp = ACn, bACn
            yield

        def stageB(ch):
            i = ch % 2; t0 = ch * 128
            j3 = ch % 3
            v32, bv32 = v32s[j3]; vb, bvb = vbs[j3]; gT, bgT = gTs[j3]; gin, bgin = gins[j3]; AR, bAR = ARs[j3]
            Minv, bMinv = MinvS[i]
            BKhT, bBKhT = BKhTs[j3]; coef, bcoef = coefs[j3]; MAB, bMAB = MABs[j3]; MAK, bMAK = MAKs[j3]; MABT, bMABT = MABTs[j3]
            yield
            pX, bpX = psB()
            fs = []
            for h in range(8):
                q, r0 = h // 2, (h % 2) * 64
                fs.append(lambda e, h=h: e.matmul(pX[:, h * 64:(h + 1) * 64], MAK[:, h, 0, :], vb[:, h * 64:(h + 1) * 64], start=True, stop=False))
                fs.append(lambda e, h=h, q=q: e.matmul(pX[:, h * 64:(h + 1) * 64], AR[:, h, 0, :], Sb[:, q, :], start=False, stop=True))
            c.group("pe", fs, rd=[bMAK, bvb, bAR, bSb], wr=[bpX])
            c.op("act", lambda e: e.copy(XT[:], pX[:]), rd=[bpX], wr=[bXT])
            yield
            pU, bpU = psB()
            c.group("pe", [lambda e, h=h: e.matmul(pU[:, h * 64:(h + 1) * 64], Minv[:, h, :], XT[:, h * 64:(h + 1) * 64], start=True, stop=True) for h in range(8)],
                    rd=[bMinv, bXT], wr=[bpU])
            c.op("dve", lambda e: e.tensor_copy(UT[:], pU[:]), rd=[bpU], wr=[bUT])
            yield
            pY, bpY = psB()
            fs = []
            for h in range(8):
                q, r0 = h // 2, (h % 2) * 64
                fs.append(lambda e, h=h: e.matmul(pY[:, h * 64:(h + 1) * 64], MAK[:, h, 1, :], vb[:, h * 64:(h + 1) * 64], start=True, stop=False))
                fs.append(lambda e, h=h: e.matmul(pY[:, h * 64:(h + 1) * 64], MAB[:, h, 1, :], UT[:, h * 64:(h + 1) * 64], start=False, stop=False))
                fs.append(lambda e, h=h, q=q: e.matmul(pY[:, h * 64:(h + 1) * 64], AR[:, h, 1, :], Sb[:, q, :], start=False, stop=True))
            c.group("pe", fs, rd=[bMAK, bMAB, bvb, bUT, bAR, bSb], wr=[bpY])
            yield
            pS, bpS = psB()
            fs = []
            for q in range(4):
                fs.append(lambda e, q=q: e.matmul(pS[:, q * 128:(q + 1) * 128], BKhT[:, q * 128:(q + 1) * 128], UT[:, q * 128:(q + 1) * 128], start=True, stop=False))
                fs.append(lambda e, q=q: e.matmul(pS[:, q * 128:(q + 1) * 128], BKhT[:, 512 + q * 128:512 + (q + 1) * 128], vb[:, q * 128:(q + 1) * 128], start=False, stop=True))
            c.group("pe", fs, rd=[bBKhT, bUT, bvb], wr=[bpS])
            pSv = pS[:].rearrange("p (q c) -> p q c", q=4)
            c.op("dve", lambda e: e.tensor_tensor(tmpS[:], S32[:], gin[:, :, 127:128].to_broadcast([128, 4, 64]), ALU.mult), rd=[bS32, bgin], wr=[btmpS])
            c.op("dve", lambda e: e.tensor_tensor(S32[0:64, :, :], tmpS[0:64, :, :], pSv[0:64, :, 0:64], ALU.add), rd=[btmpS, bpS], wr=[bS32])
            c.op("dve", lambda e: e.tensor_tensor(S32[64:128, :, :], tmpS[64:128, :, :], pSv[64:128, :, 64:128], ALU.add), rd=[btmpS, bpS], wr=[bS32])
            c.op("dve", lambda e: e.tensor_copy(Sb[:], S32[:]), rd=[bS32], wr=[bSb])
            yield
            pYv = pY[:].rearrange("p (h d) -> p h d", h=8)
            c.op("dve", lambda e: e.tensor_reduce(s1[:], pYv, AX.X, ALU.add), rd=[bpY], wr=[bs1])
            c.op("act", lambda e: e.activation(sqt[:], pY[:], AF.Square), rd=[bpY], wr=[bsqt])
            c.op("dve", lambda e: e.tensor_reduce(s2[:], sqt[:].rearrange("p (h d) -> p h d", h=8), AX.X, ALU.add), rd=[bsqt], wr=[bs2])
            c.op("dve", lambda e: e.tensor_scalar(mean[:], s1[:], 1.0 / 64, None, ALU.mult), rd=[bs1], wr=[bmean])
            c.op("dve", lambda e: e.tensor_tensor(var[:], mean[:], mean[:], ALU.mult), rd=[bmean], wr=[bvar])
            c.op("dve", lambda e: e.scalar_tensor_tensor(var[:], s2[:], 1.0 / 64, var[:], ALU.mult, ALU.subtract), rd=[bs2, bvar], wr=[bvar])
            c.op("dve", lambda e: e.tensor_scalar(var[:], var[:], 64e-5, None, ALU.add), rd=[bvar], wr=[bvar])
            c.op("act", lambda e: e.activation(var[:], var[:], AF.Ln), rd=[bvar], wr=[bvar])
            c.op("act", lambda e: e.activation(var[:], var[:], AF.Exp, scale=-0.5), rd=[bvar], wr=[bvar])
            ynv = yn[:].rearrange("p (h d) -> p h d", h=8)
            c.op("dve", lambda e: e.tensor_tensor(ynv, pYv, bcast(mean[:], [128, 8, 64], 2), ALU.subtract), rd=[bpY, bmean], wr=[byn])
            c.op("pool", lambda e: e.tensor_tensor(ynv, ynv, bcast(var[:], [128, 8, 64], 2), ALU.mult), rd=[byn, bvar], wr=[byn])
            c.op("pool", lambda e: e.tensor_tensor(yn[:], yn[:], lng[:], ALU.mult), rd=[byn, blng], wr=[byn])
            c.op("dve", lambda e: e.tensor_tensor(yn[:], yn[:], lnb[:], ALU.add), rd=[byn, blnb], wr=[byn])
            c.op("pool", lambda e: e.tensor_tensor(bon[:].rearrange("p (h d) -> p h d", h=8), v32[:].rearrange("p (h d) -> p h d", h=8), bcast(coef[:], [128, 8, 64], 2), ALU.mult),
                 rd=[bv32, bcoef], wr=[bbon])
            c.op("dve", lambda e: e.tensor_tensor(yn[:], yn[:], bon[:], ALU.add), rd=[byn, bbon], wr=[byn])
            c.op("dve", lambda e: e.tensor_tensor(yab[:], yn[:], gT[:], ALU.mult), rd=[byn, bgT], wr=[byab])
            p, bp = psB(); pb = p[:].bitcast(BF16)
            c.group("pe", [lambda e, q=q: e.transpose(pb[:, q * 128:(q + 1) * 128], yab[:, q * 128:(q + 1) * 128], idb[:]) for q in range(4)], rd=[byab, bidb], wr=[bp])
            c.op("act", lambda e: e.copy(yaT[:].rearrange("p a b -> p (a b)"), pb[:, 0:512]), rd=[bp], wr=[byaT])
            c.dma("act", yaFv[:, :, t0:t0 + 128], yaT[:], rd=[byaT])
        RUNW = [int(v) for v in os.environ.get('RUNW', '1,1,1').split(',')]
        def run(gens):
            alive = [(g, RUNW[k % 3]) for k, g in enumerate(gens) if g is not None]
            while alive:
                for (g, wgt) in list(alive):
                    for _ in range(wgt):
                        try:
                            next(g)
                        except StopIteration:
                            alive.remove((g, wgt))
                            break
        run([stageA(0)])
        run([stageA2(0), stageA(1) if NCH > 1 else None])
        for ch in range(NCH):
            run([stageB(ch), stageA2(ch + 1) if ch + 1 < NCH else None, stageA(ch + 2) if ch + 2 < NCH else None])
        c.barrier()

    if upto <= 2:
        print("ninst", c.ninst, "nwait", c.nwait); return nc
    es_w = ExitStack()
    def TW(name, shape, dt=F32):
        return es_w.enter_context(nc.sbuf_tensor("s_" + name, list(shape), dt)), Buf(name)
    with ExitStack() as es:
        def T(name, shape, dt=F32):
            return es.enter_context(nc.sbuf_tensor("s_" + name, list(shape), dt)), Buf(name)
        pst["n"] = 2; pst["i"] = 0; pst["off"] = 4
        pO = [PSB[6], PSB[7]]
        qst = {"i": 0}
        def psq():
            qst["i"] += 1
            return PSB[qst["i"] % 4]
        idb, bidb = T("idb3", [128, 128], BF16); c.dma("pool", idb[:], ident_d, wr=[bidb])
        CB, bCB = T("CB", [128, 2, 256], BF16); c.dma("pool", CB[:].rearrange("p a b -> p (a b)"), cbias2_d, wr=[bCB])
        s65, bs65 = T("s65", [65, 64], BF16); c.dma("pool", s65[:], sel65_d, wr=[bs65])
        pok, bpok = T("pok", [128, 16, 16]); c.dma("sp", pok[:].rearrange("p a b -> p (a b)"), pok_d.partition_broadcast(128), wr=[bpok])
        pbi, bpbi = T("pbi", [128, 16, 16]); c.dma("sp", pbi[:].rearrange("p a b -> p (a b)"), pbias_d.partition_broadcast(128), wr=[bpbi])
        invf, binvf = T("invf", [128, 8]); c.dma("sp", invf[:], invf_d.partition_broadcast(128), wr=[binvf])
        qkg, bqkg = T("qkg", [128, 1024]); c.dma("sp", qkg[:], qkg_d.partition_broadcast(128), wr=[bqkg])
        posi, bposi = T("posi", [128, NT], I32); c.dma("sp", posi[:], pos, wr=[bposi])
        posf, bposf = T("posf", [128, NT])
        c.op("dve", lambda e: e.tensor_copy(posf[:], posi[:]), rd=[bposi], wr=[bposf])
        yy, byy = T("yy", [128, NT, 8]); yi, byi = T("yi", [128, NT, 8], I32); yf, byf = T("yf", [128, NT, 8])
        sinT, bsinT = T("sinT", [128, NT, 8]); cosT, bcosT = T("cosT", [128, NT, 8])
        c.op("dve", lambda e: e.tensor_copy(yy[:], bcast(invf[:], [128, NT, 8], 1)), rd=[binvf], wr=[byy])
        c.op("dve", lambda e: e.tensor_tensor(yy[:], yy[:], bcast(posf[:], [128, NT, 8], 2), ALU.mult), rd=[byy, bposf], wr=[byy])
        for (dst, bdst, off) in ((sinT, bsinT, 0.0), (cosT, bcosT, 0.25)):
            if off != 0.0:
                c.op("dve", lambda e: e.tensor_scalar(yy[:], yy[:], off, None, ALU.add), rd=[byy], wr=[byy])
            c.op("dve", lambda e: e.tensor_copy(yi[:], yy[:]), rd=[byy], wr=[byi])
            c.op("dve", lambda e: e.tensor_copy(yf[:], yi[:]), rd=[byi], wr=[byf])
            c.op("dve", lambda e: e.tensor_tensor(yf[:], yy[:], yf[:], ALU.subtract), rd=[byy, byf], wr=[byf])
            c.op("act", lambda e, dst=dst: e.activation(dst[:], yf[:], AF.Sin, scale=2.0 * math.pi), rd=[byf], wr=[bdst])
        KT = es.enter_context(nc.sbuf_tensor("s_KT", [80, 8, S], BF16)); bKT = [Buf("KT%d" % g) for g in range(8)]
        VA = es.enter_context(nc.sbuf_tensor("s_VA", [128, NT, 8, 65], BF16)); bVA = [Buf("VA%d" % g) for g in range(8)]
        c.op("pool", lambda e: e.memset(VA[:], 1.0), wr=bVA)
        kmT, bkmT = T("kmT", [64, 8, 16], BF16); c.op("dve", lambda e: e.memset(kmT[:], 0.0), wr=[bkmT])
        km32, bkm32 = T("km32", [64, 8])
        qkvs = [T("qkv%d" % i, [128, 1536]) for i in range(2)]
        sq3, bsq3 = T("sq3", [128, 1024]); ss3, bss3 = T("ss3", [128, 16]); qn, bqn = T("qn", [128, 16, 64])
        rt, brt = T("rt", [128, 4, 16, 8])
        QA, bQA = T("QA", [128, 8, 80], BF16); KA, bKA = T("KA", [128, 8, 80], BF16)
        QT, bQT = T("QT", [64, 8, 128], BF16)
        QTAs = [T("QTA%d" % i, [80, 8, 512], BF16) for i in range(2)]
        gm, bgm = T("gm", [128, 8, 16]); mx, bmx = T("mx", [128, 8, 8]); sel, bsel = T("sel", [128, 8, 16])
        PTs = [T("PT%d" % i, [128, 512], BF16) for i in range(4)]
        OT, bOT = T("OT", [65, 512]); rr, brr = T("rr", [65, 512], BF16); r32, br32 = T("r32", [65, 512])
        c.op("dve", lambda e: e.memset(rr[:], 0.0), wr=[brr])
        yTs = [T("yT%d" % i, [64, 512], BF16) for i in range(2)]
        cnt3 = {"pt": 0, "ld": 0}

        def ld3(tt):
            c.dma("act", qkvs[tt % 2][0][:], zT[tt * 128:(tt + 1) * 128, 512:2048], wr=[qkvs[tt % 2][1]])

        def pre3(tt):
            if tt + 1 < NT:
                ld3(tt + 1)
            qkv, bqkv = qkvs[tt % 2]
            t0 = tt * 128; qb = tt // 2; g = tt // 4; ti = tt % 4
            QTA, bQTA = QTAs[g % 2]
            c.op("act", lambda e: e.activation(sq3[:], qkv[:, 0:1024], AF.Square), rd=[bqkv], wr=[bsq3])
            c.op("dve", lambda e: e.tensor_reduce(ss3[:], sq3[:].rearrange("p (h d) -> p h d", h=16), AX.X, ALU.add), rd=[bsq3], wr=[bss3])
            c.op("dve", lambda e: e.tensor_scalar(ss3[:], ss3[:], 1.0 / 64, 1e-6, ALU.mult, ALU.add), rd=[bss3], wr=[bss3])
            c.op("act", lambda e: e.activation(ss3[:], ss3[:], AF.Ln), rd=[bss3], wr=[bss3])
            c.op("act", lambda e: e.activation(ss3[:], ss3[:], AF.Exp, scale=-0.5), rd=[bss3], wr=[bss3])
            c.op("dve", lambda e: e.tensor_tensor(qn[:], qkv[:, 0:1024].rearrange("p (h d) -> p h d", h=16), bcast(ss3[:], [128, 16, 64], 2), ALU.mult), rd=[bqkv, bss3], wr=[bqn])
            c.op("pool", lambda e: e.tensor_tensor(qn[:].rearrange("p h d -> p (h d)"), qn[:].rearrange("p h d -> p (h d)"), qkg[:], ALU.mult), rd=[bqn, bqkg], wr=[bqn])
            cs = cosT[:, tt:tt + 1, :].to_broadcast([128, 16, 8]); sn = sinT[:, tt:tt + 1, :].to_broadcast([128, 16, 8])
            c.op("dve", lambda e: e.tensor_tensor(rt[:, 0, :, :], qn[:, :, 0:8], cs, ALU.mult), rd=[bqn, bcosT], wr=[brt])
            c.op("dve", lambda e: e.tensor_tensor(rt[:, 1, :, :], qn[:, :, 8:16], sn, ALU.mult), rd=[bqn, bsinT], wr=[brt])
            c.op("pool", lambda e: e.tensor_tensor(rt[:, 2, :, :], qn[:, :, 8:16], cs, ALU.mult), rd=[bqn, bcosT], wr=[brt])
            c.op("pool", lambda e: e.tensor_tensor(rt[:, 3, :, :], qn[:, :, 0:8], sn, ALU.mult), rd=[bqn, bsinT], wr=[brt])
            c.op("dve", lambda e: e.tensor_tensor(qn[:, :, 0:8], rt[:, 0, :, :], rt[:, 1, :, :], ALU.subtract), rd=[brt], wr=[bqn])
            c.op("dve", lambda e: e.tensor_tensor(qn[:, :, 8:16], rt[:, 2, :, :], rt[:, 3, :, :], ALU.add), rd=[brt], wr=[bqn])
            c.op("dve", lambda e: e.tensor_copy(QA[:, :, 0:64], qn[:, 0:8, :]), rd=[bqn], wr=[bQA])
            c.op("pool", lambda e: e.tensor_copy(KA[:, :, 0:64], qn[:, 8:16, :]), rd=[bqn], wr=[bKA])
            c.op("pool", lambda e: e.memset(KA[:, :, 64:80], 0.0), wr=[bKA])
            c.op("pool", lambda e: e.memset(KA[:, :, 64 + qb:65 + qb], 1.0), wr=[bKA])
            yield
            p, bp = ps(); pb = p[:].bitcast(BF16)
            c.group("pe", [lambda e, h=h: e.transpose(pb[0:80, h * 128:(h + 1) * 128], KA[:, h, :], idb[:]) for h in range(8)], rd=[bKA, bidb], wr=[bp])
            c.op("dve", lambda e: e.tensor_copy(KT[:, :, t0:t0 + 128], pb[0:80, :].rearrange("p (h t) -> p h t", h=8)), rd=[bp], wr=[bKT[g]])
            c.op("pool", lambda e: e.tensor_copy(VA[:, tt, :, 0:64], qkv[:, 1024:1536].rearrange("p (h d) -> p h d", h=8)), rd=[bqkv], wr=[bVA[g]])
            if qb > 0:
                p, bp = ps(); pb = p[:].bitcast(BF16)
                c.group("pe", [lambda e, h=h: e.transpose(pb[0:64, h * 128:(h + 1) * 128], QA[:, h, 0:64], idb[:]) for h in range(8)], rd=[bQA, bidb], wr=[bp])
                c.op("dve", lambda e: e.tensor_copy(QT[:], pb[0:64, :].rearrange("p (h t) -> p h t", h=8)), rd=[bp], wr=[bQT])
                yield
                p, bp = ps()
                c.group("pe", [lambda e, h=h, p=p: e.matmul(p[:, h * 16:(h + 1) * 16], QT[:, h, :], kmT[:, h, :], start=True, stop=True) for h in range(8)],
                        rd=[bQT, bkmT], wr=[bp])
                c.op("dve", lambda e, p=p: e.tensor_tensor(gm[:], p[:, 0:128].rearrange("p (h j) -> p h j", h=8), pbi[:, qb:qb + 1, :].to_broadcast([128, 8, 16]), ALU.add),
                     rd=[bp, bpbi], wr=[bgm])
                for h in range(8):
                    c.op("dve", lambda e, h=h: e.max(mx[:, h, :], gm[:, h, :]), rd=[bgm], wr=[bmx])
                c.op("dve", lambda e: e.tensor_tensor(sel[:], gm[:], mx[:, :, 2:3].to_broadcast([128, 8, 16]), ALU.is_ge), rd=[bgm, bmx], wr=[bsel])
                c.op("dve", lambda e: e.tensor_tensor(sel[:], sel[:], pok[:, qb:qb + 1, :].to_broadcast([128, 8, 16]), ALU.mult), rd=[bsel, bpok], wr=[bsel])
                c.op("dve", lambda e: e.tensor_scalar(QA[:, :, 64:80], sel[:], BIG, -BIG, ALU.mult, ALU.add), rd=[bsel], wr=[bQA])
            else:
                c.op("dve", lambda e: e.memset(QA[:, :, 64:80], -BIG), wr=[bQA])
            yield
            p, bp = ps(); pb = p[:].bitcast(BF16)
            c.group("pe", [lambda e, h=h: e.transpose(pb[0:80, h * 128:(h + 1) * 128], QA[:, h, :], idb[:]) for h in range(8)], rd=[bQA, bidb], wr=[bp])
            c.op("dve", lambda e: e.tensor_copy(QTA[:, :, ti * 128:(ti + 1) * 128], pb[0:80, :].rearrange("p (h t) -> p h t", h=8)), rd=[bp], wr=[bQTA])
            if tt % 2 == 1:
                c.op("dve", lambda e: e.tensor_reduce(km32[:], KT[0:64, :, qb * 256:(qb + 1) * 256], AX.X, ALU.add), rd=[bKT[g]], wr=[bkm32])
                c.op("dve", lambda e: e.tensor_scalar(kmT[:, :, qb], km32[:], 1.0 / 256, None, ALU.mult), rd=[bkm32], wr=[bkmT])

        s65f, bs65f = T("s65f", [65, 64]); c.dma("sp", s65f[:], sel65_d, wr=[bs65f])

        def steps3(g, h):
            QTA, bQTA = QTAs[g % 2]
            pOh, bOh = pO[h % 2]
            kbufs = bKT[0:g + 1]; vbufs = bVA[0:g + 1]
            st = []
            nk = 4 * g + 2
            for kt in range(nk):
                d = {}
                def qk(d=d, kt=kt):
                    d["p"], d["bp"] = psq()
                    c.op("pe", lambda e: e.matmul(d["p"][:], KT[:, h, kt * 128:(kt + 1) * 128], QTA[:, h, :], start=True, stop=True), rd=kbufs + [bQTA], wr=[d["bp"]])
                def ex(d=d):
                    d["PT"], d["bPT"] = PTs[cnt3["pt"] % 4]; cnt3["pt"] += 1
                    c.op("act", lambda e: e.activation(d["PT"][:], d["p"][:], AF.Exp, scale=0.125), rd=[d["bp"]], wr=[d["bPT"]])
                def pv(d=d, kt=kt):
                    c.op("pe", lambda e: e.matmul(pOh[0:65, :], VA[:, kt, h, :], d["PT"][:], start=(kt == 0), stop=False), rd=vbufs + [d["bPT"]], wr=[bOh])
                st.append((qk, ex, pv))
            for half in range(2):
                for kti in range(2):
                    kt = 4 * g + 2 * half + kti
                    qs = slice(half * 256, (half + 1) * 256)
                    last = (half == 1 and kti == 1)
                    d = {}
                    def qk(d=d, kt=kt, qs=qs, kti=kti):
                        d["p"], d["bp"] = psq()
                        c.group("pe", [lambda e: e.matmul(d["p"][:, 0:256], KT[0:64, h, kt * 128:(kt + 1) * 128], QTA[0:64, h, qs], start=True, stop=False),
                                       lambda e: e.matmul(d["p"][:, 0:256], idb[:], CB[:, kti, :], start=False, stop=True)], rd=kbufs + [bQTA, bidb, bCB], wr=[d["bp"]])
                    def ex(d=d):
                        d["PT"], d["bPT"] = PTs[cnt3["pt"] % 4]; cnt3["pt"] += 1
                        c.op("act", lambda e: e.activation(d["PT"][:, 0:256], d["p"][:, 0:256], AF.Exp, scale=0.125), rd=[d["bp"]], wr=[d["bPT"]])
                    def pv(d=d, kt=kt, qs=qs, last=last):
                        c.op("pe", lambda e: e.matmul(pOh[0:65, qs], VA[:, kt, h, :], d["PT"][:, 0:256], start=False, stop=last), rd=vbufs + [d["bPT"]], wr=[bOh])
                    st.append((qk, ex, pv))

            def fin():
                c.op("dve", lambda e: e.tensor_copy(OT[:], pOh[0:65, :]), rd=[bOh], wr=[bOT])
                p, bp = ps()
                c.op("pe", lambda e: e.matmul(p[0:64, :], s65f[:], OT[:], start=True, stop=True), rd=[bs65f, bOT], wr=[bp])
                yT, byT = yTs[h % 2]
                c.op("dve", lambda e: e.reciprocal(r32[0:64, :], p[0:64, :]), rd=[bp], wr=[br32])
                c.op("dve", lambda e: e.tensor_tensor(yT[:], OT[0:64, :], r32[0:64, :], ALU.mult), rd=[bOT, br32], wr=[byT])
                c.dma("sp", ybF[h * 64:(h + 1) * 64, g * 512:(g + 1) * 512], yT[:], rd=[byT])
            return st, fin

        ld3(0)
        for tt in range(4):
            for _ in pre3(tt):
                pass
        LOOK = int(os.environ.get('LOOK', '3'))
        for g in range(8):
            allst = []
            for h in range(8):
                st, fin = steps3(g, h)
                for i, s in enumerate(st):
                    allst.append((s, fin if i == len(st) - 1 else None, h))
            n = len(allst)
            pend = [pre3(4 * (g + 1) + i) for i in range(4)] if g + 1 < 8 else []
            every = max(1, n // 18)

            def advance():
                while pend:
                    try:
                        next(pend[0])
                        return
                    except StopIteration:
                        pend.pop(0)
            for i in range(min(LOOK, n)):
                allst[i][0][0]()
            for i in range(n):
                (qk, ex, pv), fin, h = allst[i]
                ex()
                if i + LOOK < n:
                    allst[i + LOOK][0][0]()
                pv()
                if fin is not None:
                    fin()
                if i % every == every - 1:
                    advance()
            while pend:
                advance()
        pst["n"] = 8; pst["off"] = 0
        c.barrier()

    if upto <= 3:
        print("ninst", c.ninst, "nwait", c.nwait); return nc
    wup, bwup = TW("wup", [128, 8, 2 * DFF], BF16)
    g2, bg2 = TW("g2", [128, 8]); c.dma("sp", g2[:], n2g, wr=[bg2])
    wub = [Buf("wup_%d" % k) for k in range(8)]
    for k in range(8):
        c.dma("pool", wup[:, k, :], wup_d[k * 128:(k + 1) * 128, :], wr=[wub[k]])
        c.op("dve", lambda e, k=k: e.tensor_scalar(wup[:, k, :], wup[:, k, :], g2[:, k:k + 1], None, ALU.mult), rd=[wub[k], bg2], wr=[wub[k]])
    with ExitStack() as es:
        def T(name, shape, dt=F32):
            return es.enter_context(nc.sbuf_tensor("s_" + name, list(shape), dt)), Buf(name)
        idb, bidb = T("idb4", [128, 128], BF16); c.dma("pool", idb[:], ident_d, wr=[bidb])
        wba, bwba = T("wba", [128, 4, DM], BF16); wbb, bwbb = T("wbb", [128, 4, DM], BF16); wo, bwo = T("wo", [128, 8, DM], BF16)
        for k in range(4):
            c.dma("pool", wba[:, k, :], wba_d[k * 128:(k + 1) * 128, :], wr=[bwba])
            c.dma("pool", wbb[:, k, :], wbb_d[k * 128:(k + 1) * 128, :], wr=[bwbb])
        for k in range(8):
            c.dma("pool", wo[:, k, :], wout_d[k * 128:(k + 1) * 128, :], wr=[bwo])
        xts = [T("x4%d" % i, [128, DM]) for i in range(3)]
        gps = [T("gp%d" % i, [128, 2048]) for i in range(2)]
        yas = [T("ya4%d" % i, [128, 4, 128], BF16) for i in range(2)]
        ybs = [T("yb4%d" % i, [128, 4, 128], BF16) for i in range(2)]
        x1, bx1 = T("x1", [128, DM]); junk, bjunk = T("junk4", [128, DM]); h2, bh2 = T("h2", [128, DM], BF16)
        h2T, bh2T = T("h2T", [128, 8, 128], BF16)
        ss, bss = T("ss4", [128, NT]); c.op("dve", lambda e: e.memset(ss[:], 0.0), wr=[bss])
        rs, brs = T("rs4", [128, NT])
        yaFv = yaF.rearrange("(c p) t -> p c t", p=128); ybFv = ybF.rearrange("(c p) t -> p c t", p=128)
        h2Fv = h2F.rearrange("(c p) t -> p c t", p=128)

        def loads4(tt):
            i = tt % 2; t0 = tt * 128
            c.dma("sp", xts[tt % 3][0][:], x[t0:t0 + 128, :], wr=[xts[tt % 3][1]])
            c.dma("sp", gps[i][0][:], zT[t0:t0 + 128, 2048:4096], wr=[gps[i][1]])
            c.dma("sp", yas[i][0][:], yaFv[:, :, t0:t0 + 128], wr=[yas[i][1]])
            c.dma("sp", ybs[i][0][:], ybFv[:, :, t0:t0 + 128], wr=[ybs[i][1]])
        m1s = [T("m1_%d" % i, [128, DM]) for i in range(2)]; m2s = [T("m2_%d" % i, [128, DM]) for i in range(2)]
        mbs = [T("mb_%d" % i, [128, DM], BF16) for i in range(2)]; mTs = [T("mT_%d" % i, [128, 8, 128], BF16) for i in range(2)]

        def g1(tt):
            if tt + 1 < NT:
                loads4(tt + 1)
            i = tt % 2
            gp, bgp = gps[i]; ya, bya = yas[i]; yb, byb = ybs[i]
            m1, bm1 = m1s[i]; m2, bm2 = m2s[i]; mb, bmb = mbs[i]; mT, bmT = mTs[i]
            c.op("act", lambda e: e.activation(gp[:], gp[:], AF.Sigmoid), rd=[bgp], wr=[bgp])
            for half in range(2):
                hs = slice(half * 512, (half + 1) * 512)
                pa, bpa = ps()
                c.group("pe", [lambda e, k=k: e.matmul(pa[:], ya[:, k, :], wba[:, k, hs], start=(k == 0), stop=(k == 3)) for k in range(4)], rd=[bya, bwba], wr=[bpa])
                c.op("dve", lambda e: e.tensor_tensor(m1[:, hs], pa[:], gp[:, half * 512:(half + 1) * 512], ALU.mult), rd=[bpa, bgp], wr=[bm1])
                pb_, bpb_ = ps()
                c.group("pe", [lambda e, k=k: e.matmul(pb_[:], yb[:, k, :], wbb[:, k, hs], start=(k == 0), stop=(k == 3)) for k in range(4)], rd=[byb, bwbb], wr=[bpb_])
                c.op("dve", lambda e: e.tensor_tensor(m2[:, hs], pb_[:], gp[:, 1024 + half * 512:1024 + (half + 1) * 512], ALU.mult), rd=[bpb_, bgp], wr=[bm2])
            c.op("dve", lambda e: e.tensor_tensor(mb[:], m1[:], m2[:], ALU.add), rd=[bm1, bm2], wr=[bmb])
            yield
            p, bp = ps(); pb = p[:].bitcast(BF16)
            c.group("pe", [lambda e, k=k: e.transpose(pb[:, k * 128:(k + 1) * 128], mb[:, k * 128:(k + 1) * 128], idb[:]) for k in range(8)], rd=[bmb, bidb], wr=[bp])
            c.op("act", lambda e: e.copy(mT[:].rearrange("p a b -> p (a b)"), pb), rd=[bp], wr=[bmT])

        def g2(tt):
            i = tt % 2; t0 = tt * 128
            xt, bxt = xts[tt % 3]; mT, bmT = mTs[i]
            for half in range(2):
                hs = slice(half * 512, (half + 1) * 512)
                po, bpo = ps()
                c.group("pe", [lambda e, k=k: e.matmul(po[:], mT[:, k, :], wo[:, k, hs], start=(k == 0), stop=(k == 7)) for k in range(8)], rd=[bmT, bwo], wr=[bpo])
                c.op("dve", lambda e: e.tensor_tensor(x1[:, hs], po[:], xt[:, hs], ALU.add), rd=[bpo, bxt], wr=[bx1])
            c.dma("pool", x1s[t0:t0 + 128, :], x1[:], rd=[bx1])
            c.op("act", lambda e: e.activation(junk[:], x1[:], AF.Square, accum_out=ss[:, tt:tt + 1]), rd=[bx1], wr=[bjunk, bss])
            c.op("dve", lambda e: e.tensor_scalar(rs[:, tt:tt + 1], ss[:, tt:tt + 1], 1.0 / DM, 1e-6, ALU.mult, ALU.add), rd=[bss], wr=[brs])
            c.op("act", lambda e: e.activation(rs[:, tt:tt + 1], rs[:, tt:tt + 1], AF.Ln), rd=[brs], wr=[brs])
            c.op("act", lambda e: e.activation(rs[:, tt:tt + 1], rs[:, tt:tt + 1], AF.Exp, scale=-0.5), rd=[brs], wr=[brs])
            c.op("dve", lambda e: e.tensor_scalar(h2[:], x1[:], rs[:, tt:tt + 1], None, ALU.mult), rd=[bx1, brs], wr=[bh2])
            yield
            p, bp = ps(); pb = p[:].bitcast(BF16)
            c.group("pe", [lambda e, k=k: e.transpose(pb[:, k * 128:(k + 1) * 128], h2[:, k * 128:(k + 1) * 128], idb[:]) for k in range(8)], rd=[bh2, bidb], wr=[bp])
            c.op("act", lambda e: e.copy(h2T[:].rearrange("p a b -> p (a b)"), pb), rd=[bp], wr=[bh2T])
            c.dma("pool", h2Fv[:, :, t0:t0 + 128], h2T[:], rd=[bh2T])

        def run4(gens):
            alive = [g for g in gens if g is not None]
            while alive:
                for g in list(alive):
                    try:
                        next(g)
                    except StopIteration:
                        alive.remove(g)
        loads4(0)
        run4([g1(0)])
        for tt in range(NT):
            run4([g1(tt + 1) if tt + 1 < NT else None, g2(tt)])
        c.barrier()

    if upto <= 4:
        print("ninst", c.ninst, "nwait", c.nwait); return nc
    with ExitStack() as es:
        def T(name, shape, dt=F32):
            return es.enter_context(nc.sbuf_tensor("s_" + name, list(shape), dt)), Buf(name)
        wdn, bwdn = T("wdn", [128, NFF, DM], BF16)
        for f in range(NFF):
            c.dma("pool", wdn[:, f, :], wdn_d[f * 128:(f + 1) * 128, :], wr=[bwdn])
        cw, bcw = T("cw", [128, NFF, 3]); c.dma("sp", cw[:].rearrange("p a b -> p (a b)"), cw_d, wr=[bcw])
        cbt, bcbt = T("cbt", [128, NFF]); c.dma("sp", cbt[:], cb_d, wr=[bcbt])
        cr, bcr = T("cr", [128, NFF, 2]); c.op("dve", lambda e: e.memset(cr[:], 0.0), wr=[bcr])
        h2s = [T("h2s%d" % i, [128, 8, 512], BF16) for i in range(2)]
        asb = [T("asb%d" % i, [128, 514]) for i in range(2)]
        acc = [T("acc%d" % i, [128, 512]) for i in range(2)]
        hg = es.enter_context(nc.sbuf_tensor("hg", [128, NFF, 512], BF16)); bhg = [Buf("hg%d" % f) for f in range(NFF)]
        x1t = [T("x1t%d" % i, [128, DM]) for i in range(2)]
        ost = [T("ost%d" % i, [128, DM]) for i in range(2)]
        h2Fv = h2F.rearrange("(c p) t -> p c t", p=128)
        c.dma("sp", h2s[0][0][:], h2Fv[:, :, 0:512], wr=[h2s[0][1]])
        nx = 0
        for st in range(8):
            if st + 1 < 8:
                c.dma("sp", h2s[(st + 1) % 2][0][:], h2Fv[:, :, (st + 1) * 512:(st + 2) * 512], wr=[h2s[(st + 1) % 2][1]])
            hT, bhT = h2s[st % 2]
            for f in range(NFF):
                pa, bpa = ps()
                c.group("pe", [lambda e, k=k: e.matmul(pa[:], wup[:, k, f * 128:(f + 1) * 128], hT[:, k, :], start=(k == 0), stop=(k == 7)) for k in range(8)], rd=[bhT] + wub, wr=[bpa])
                pg, bpg = ps()
                c.group("pe", [lambda e, k=k: e.matmul(pg[:], wup[:, k, DFF + f * 128:DFF + (f + 1) * 128], hT[:, k, :], start=(k == 0), stop=(k == 7)) for k in range(8)], rd=[bhT] + wub, wr=[bpg])
                a, ba = asb[f % 2]; ac, bac = acc[f % 2]
                c.op("act", lambda e: e.copy(a[:, 0:2], cr[:, f, :]), rd=[bcr], wr=[ba])
                c.op("act", lambda e: e.copy(a[:, 2:514], pa[:]), rd=[bpa], wr=[ba])
                c.op("act", lambda e: e.copy(cr[:, f, :], a[:, 512:514]), rd=[ba], wr=[bcr])
                c.op("dve", lambda e: e.tensor_scalar(ac[:], a[:, 0:512], cw[:, f, 0:1], None, ALU.mult), rd=[ba, bcw], wr=[bac])
                c.op("dve", lambda e: e.scalar_tensor_tensor(ac[:], a[:, 1:513], cw[:, f, 1:2], ac[:], ALU.mult, ALU.add), rd=[ba, bcw, bac], wr=[bac])
                c.op("dve", lambda e: e.scalar_tensor_tensor(ac[:], a[:, 2:514], cw[:, f, 2:3], ac[:], ALU.mult, ALU.add), rd=[ba, bcw, bac], wr=[bac])
                c.op("act", lambda e: e.activation(ac[:], ac[:], AF.Gelu, bias=cbt[:, f:f + 1]), rd=[bac, bcbt], wr=[bac])
                c.op("dve", lambda e: e.tensor_tensor(hg[:, f, :], ac[:], pg[:], ALU.mult), rd=[bac, bpg], wr=[bhg[f]])
            for sub in range(4):
                tt = st * 4 + sub; t0 = tt * 128
                xx, bxx = x1t[nx % 2]; oo, boo = ost[nx % 2]; nx += 1
                c.dma("sp", xx[:], x1s[t0:t0 + 128, :], wr=[bxx])
                for half in range(2):
                    hs = slice(half * 512, (half + 1) * 512)
                    po, bpo = ps()
                    c.group("pe", [lambda e, f=f: e.matmul(po[:], hg[:, f, sub * 128:(sub + 1) * 128], wdn[:, f, hs], start=(f == 0), stop=(f == NFF - 1)) for f in range(NFF)],
                            rd=bhg + [bwdn], wr=[bpo])
                    c.op("dve", lambda e: e.tensor_tensor(oo[:, hs], po[:], xx[:, hs], ALU.add), rd=[bpo, bxx], wr=[boo])
                c.dma("pool", out[t0:t0 + 128, :], oo[:], rd=[boo])
        c.barrier()
    es_w.close()
    print("ninst", c.ninst, "nwait", c.nwait, {e: c.cnt[e] for e in c.cnt})
    return nc


def _consts():
    i = np.arange(128)
    su = (i[:, None] < i[None, :]).astype(np.float32)
    ui = (i[:, None] <= i[None, :]).astype(np.float32)
    mu2 = np.concatenate([su, ui, su, ui], axis=1)
    sl = (i[None, :] < i[:, None]).astype(np.float32)
    sl2 = np.concatenate([sl, sl], axis=1)
    bones = ((i[:, None] // 64) == (i[None, :] // 64)).astype(np.float32)
    q2 = np.arange(256)
    cb0 = np.where(i[:, None] > q2[None, :], -BIG, 0.0).astype(np.float32)
    cb1 = np.where(i[:, None] + 128 > q2[None, :], -BIG, 0.0).astype(np.float32)
    cbias2 = np.concatenate([cb0, cb1], axis=1)
    sel65 = np.zeros((65, 64), np.float32); sel65[64, :] = 1.0
    j = np.arange(16)
    pok = (j[None, :] < j[:, None]).astype(np.float32)
    pbias = ((pok - 1.0) * 1e30).astype(np.float32)
    half = 8
    invf = (500000.0 ** (-np.arange(half, dtype=np.float32) / half)).astype(np.float32) / np.float32(2.0 * math.pi)
    return dict(ident=np.eye(128, dtype=np.float32), mu2=mu2, sl2=sl2, bones=bones, cbias2=cbias2, sel65=sel65,
                pok=pok.reshape(1, 256), pbias=pbias.reshape(1, 256), invf=invf.reshape(1, 8).astype(np.float32))


def _fm(v, n):
    return np.ascontiguousarray(np.asarray(v, np.float32).reshape(n, 128).T)


def _prep(inp):
    f = lambda a: np.ascontiguousarray(np.asarray(a, dtype=np.float32))
    w_in = f(inp["w_in"][0])
    fm_cols = np.r_[0:512, 512:1024, 1536:1600, 1600:1664, 1664:1824]
    tm_cols = np.r_[1024:1536, 1824:3360, 3360:5408]
    mu = f(inp["rwkv_mu"][0])
    mu_fm = np.zeros(1408, np.float32); mu_fm[:FMC] = mu[fm_cols]
    rk = f(inp["rwkv_r_k"][0])
    rkb = np.zeros((128, 8), np.float32)
    for h in range(8):
        rkb[(h % 2) * 64:(h % 2 + 1) * 64, h] = rk[h]
    cw = f(inp["ffn_conv_w"][0])
    cwl = np.ascontiguousarray(cw.reshape(3, NFF, 128).transpose(2, 1, 0)).reshape(128, NFF * 3)
    shared = dict(
        w_in=np.ascontiguousarray(w_in[:, np.r_[fm_cols, tm_cols]]),
        n1g=_fm(inp["norm1_g"][0], 8), mu_fm=_fm(mu_fm, 11), mu_v=f(mu[1024:1536]).reshape(1, 512),
        wdec=np.concatenate([f(inp["w_decay_up"][0]), f(inp["decay_bias"][0]).reshape(1, 512)], 0),
        waaa=np.concatenate([f(inp["w_aaa_up"][0]), f(inp["aaa_bias"][0]).reshape(1, 512)], 0),
        wgate=f(inp["w_gate_up"][0]), kk_fm=_fm(inp["rwkv_k_k"][0], 4), ka_fm=_fm(inp["rwkv_k_a"][0], 4), rkb=rkb,
        lng=f(inp["rwkv_ln_g"][0]).reshape(1, 512), lnb=f(inp["rwkv_ln_b"][0]).reshape(1, 512),
        qkg=np.concatenate([np.tile(f(inp["q_norm_g"][0]), 8), np.tile(f(inp["k_norm_g"][0]), 8)]).reshape(1, 1024),
        wba=f(inp["w_branch_a"][0]), wbb=f(inp["w_branch_b"][0]), wout=f(inp["w_out"][0]), n2g=_fm(inp["norm2_g"][0], 8),
        wup=f(inp["w_ffn_up"][0]), cw=cwl, cb=_fm(inp["ffn_conv_b"][0], NFF), wdn=f(inp["w_ffn_down"][0]),
    )
    shared.update(_consts())
    xs = np.asarray(inp["x"], np.float32); ps_ = np.asarray(inp["positions"], np.int32)
    maps = []
    for b in range(8):
        m = dict(shared)
        m["x"] = np.ascontiguousarray(xs[b])
        m["pos"] = np.ascontiguousarray(ps_[b].reshape(NT, 128).T)
        maps.append(m)
    return maps


def kernel(**inputs):
    maps = _prep(inputs)
    nc = build()
    res = run_bass_kernel_spmd(nc, maps, core_ids=list(range(8)))
    return np.stack([np.asarray(r["out"], np.float32) for r in res.results], axis=0)
```

```python
import numpy as np
import concourse.bass as bass
import concourse.mybir as mybir
from concourse.bass_utils import run_bass_kernel_spmd

F32 = mybir.dt.float32
BF16 = mybir.dt.bfloat16
I32 = mybir.dt.int32
ALU = mybir.AluOpType
AF = mybir.ActivationFunctionType
AX = mybir.AxisListType


class Buf:
    __slots__ = ("name", "lastw", "readers")

    def __init__(self, name):
        self.name = name
        self.lastw = None
        self.readers = {}


class Ctx:
    def __init__(self, nc, n_dma_sems=24):
        self.nc = nc
        self.eng = {"pe": nc.tensor, "act": nc.scalar, "dve": nc.vector,
                    "pool": nc.gpsimd, "sp": nc.sync}
        self.sem = {}
        self.cnt = {}
        self.waited = {e: {} for e in self.eng}
        self._stack = []
        for e in ("pe", "act", "dve", "pool"):
            cm = nc.semaphore("s_" + e)
            self.sem[e] = cm.__enter__()
            self._stack.append(cm)
            self.cnt[e] = 0
        self.dsem = []
        self.dpool = {"hw": [], "sw": []}
        for i in range(n_dma_sems):
            cm = nc.semaphore("d%d" % i)
            self.dsem.append([cm.__enter__(), 0])
            self._stack.append(cm)
            self.dpool["sw" if i < 8 else "hw"].append(i)
        self.dnext = {"hw": 0, "sw": 0}
        self.semh = {}
        for e in self.sem:
            self.semh[("e", e)] = self.sem[e]
        for i, (h, _) in enumerate(self.dsem):
            self.semh[("d", i)] = h
        self.nwait = 0
        self.ninst = 0

    def _wait(self, e, toks):
        w = self.waited[e]
        best = {}
        for t in toks:
            if t is None:
                continue
            k, v = t[0], t[1]
            if w.get(k, 0) >= v:
                continue
            if best.get(k, 0) < v:
                best[k] = v
        for k, v in best.items():
            self.eng[e].wait_ge(self.semh[k], v)
            w[k] = v
            self.nwait += 1

    def _deps(self, e, rd, wr):
        toks = []
        me = ("e", e)
        for b in rd:
            if b.lastw is not None:
                if not (e == "pe" and b.lastw[0] == me):
                    toks.append(b.lastw)
        for b in wr:
            if b.lastw is not None and not (e == "pe" and b.lastw[0] == me):
                toks.append(b.lastw)
            for k, t in b.readers.items():
                if not (e == "pe" and k == me):
                    toks.append(t)
        return toks

    def _mark(self, tok, rd, wr):
        for b in rd:
            b.readers[tok[0]] = tok
        for b in wr:
            b.lastw = tok
            b.readers = {}

    def op(self, e, fn, rd=(), wr=()):
        self._wait(e, self._deps(e, rd, wr))
        ins = fn(self.eng[e])
        self.cnt[e] += 1
        ins.then_inc(self.sem[e], 1)
        tok = (("e", e), self.cnt[e])
        self._mark(tok, rd, wr)
        self.ninst += 1
        return tok

    def group(self, e, fns, rd=(), wr=()):
        self._wait(e, self._deps(e, rd, wr))
        ins = None
        for fn in fns:
            ins = fn(self.eng[e])
            self.ninst += 1
        self.cnt[e] += 1
        ins.then_inc(self.sem[e], 1)
        tok = (("e", e), self.cnt[e])
        self._mark(tok, rd, wr)
        return tok

    def dma(self, q, out, in_, rd=(), wr=(), **kw):
        kind = "sw" if q == "pool" else "hw"
        pool = self.dpool[kind]
        i = pool[self.dnext[kind] % len(pool)]
        self.dnext[kind] += 1
        h, c = self.dsem[i]
        k = ("d", i)
        toks = self._deps(q, rd, wr)
        if c > 0:
            toks.append((k, 16 * c))
        self._wait(q, toks)
        self.eng[q].dma_start(out=out, in_=in_, **kw).then_inc(h, 16)
        self.dsem[i][1] = c + 1
        tok = (k, 16 * (c + 1))
        self._mark(tok, rd, wr)
        self.ninst += 1
        return tok

    def wait_all(self, e, bufs):
        toks = []
        for b in bufs:
            toks.append(b.lastw)
            toks.extend(b.readers.values())
        self._wait(e, toks)

    def barrier(self, bufs=()):
        toks = []
        for e in self.sem:
            if self.cnt[e] > 0:
                toks.append((("e", e), self.cnt[e]))
        for i, (h, c) in enumerate(self.dsem):
            if c > 0:
                toks.append((("d", i), 16 * c))
        for e in self.eng:
            self._wait(e, toks)

from contextlib import ExitStack
import math
import os

S = 4096
DM = 1024
NT = 32
FMC = 1312
TMC = 4096
DFF = 2816
NFF = 22
BIG = 30000.0


def build(debug=False, upto=99):
    nc = bass.Bass("TRN2", target_bir_lowering=False)
    okind = "ExternalOutput" if debug else "Internal"

    def DIN(name, shape, dt=F32):
        return nc.dram_tensor(name, list(shape), dt, kind="ExternalInput").ap()

    x = DIN("x", [S, DM]); pos = DIN("pos", [128, NT], I32)
    w_in = DIN("w_in", [DM, 5408]); n1g = DIN("n1g", [128, 8]); mu_fm = DIN("mu_fm", [128, 11]); mu_v = DIN("mu_v", [1, 512])
    wdec_d = DIN("wdec", [65, 512]); waaa_d = DIN("waaa", [65, 512]); wgate_d = DIN("wgate", [160, 512])
    kk_d = DIN("kk_fm", [128, 4]); ka_d = DIN("ka_fm", [128, 4]); rkb_d = DIN("rkb", [128, 8])
    lng_d = DIN("lng", [1, 512]); lnb_d = DIN("lnb", [1, 512]); qkg_d = DIN("qkg", [1, 1024])
    wba_d = DIN("wba", [512, DM]); wbb_d = DIN("wbb", [512, DM]); wout_d = DIN("wout", [DM, DM]); n2g = DIN("n2g", [128, 8])
    wup_d = DIN("wup", [DM, 2 * DFF]); cw_d = DIN("cw", [128, NFF * 3]); cb_d = DIN("cb", [128, NFF]); wdn_d = DIN("wdn", [DFF, DM])
    ident_d = DIN("ident", [128, 128]); mu2_d = DIN("mu2", [128, 512]); sl2_d = DIN("sl2", [128, 256]); bones_d = DIN("bones", [128, 128])
    cbias2_d = DIN("cbias2", [128, 512]); sel65_d = DIN("sel65", [65, 64]); pok_d = DIN("pok", [1, 256]); pbias_d = DIN("pbias", [1, 256]); invf_d = DIN("invf", [1, 8])
    out = nc.dram_tensor("out", [S, DM], F32, kind="ExternalOutput").ap()
    zF = nc.dram_tensor("zF", [1408, S], F32, kind=okind).ap()
    zT = nc.dram_tensor("zT", [S, TMC], F32, kind=okind).ap()
    yaF = nc.dram_tensor("yaF", [512, S], BF16, kind=okind).ap()
    ybF = nc.dram_tensor("ybF", [512, S], BF16, kind=okind).ap()
    x1s = nc.dram_tensor("x1s", [S, DM], F32, kind=okind).ap()
    h2F = nc.dram_tensor("h2F", [DM, S], BF16, kind=okind).ap()

    c = Ctx(nc)
    PSB = [(nc.alloc_psum_tensor("psb%d" % i, [128, 512], F32), Buf("psb%d" % i)) for i in range(8)]
    pst = {"i": 0, "n": 8, "off": 0}

    def ps():
        i = pst["off"] + pst["i"] % pst["n"]
        pst["i"] += 1
        return PSB[i]

    rr = {"i": 0}

    def ev():
        rr["i"] += 1
        return "act" if rr["i"] % 2 else "dve"

    def bcast(ap, shape, axis):
        return ap.unsqueeze(axis).to_broadcast(list(shape))

    with ExitStack() as es:
        def T(name, shape, dt=F32):
            return es.enter_context(nc.sbuf_tensor("s_" + name, list(shape), dt)), Buf(name)
        w, bw = T("w1", [128, 8, 5408], BF16)
        g1, bg1 = T("g1", [128, 8]); muf, bmuf = T("muf", [128, 11])
        idb, bidb = T("idb1", [128, 128], BF16)
        c.dma("sp", g1[:], n1g, wr=[bg1]); c.dma("sp", muf[:], mu_fm, wr=[bmuf])
        c.dma("pool", idb[:], ident_d, wr=[bidb])
        wb = [Buf("w1_%d" % k) for k in range(8)]
        for kc in range(8):
            c.dma("pool", w[:, kc, :], w_in[kc * 128:(kc + 1) * 128, :], wr=[wb[kc]])
            c.op("dve", lambda e, kc=kc: e.tensor_scalar(w[:, kc, :], w[:, kc, :], g1[:, kc:kc + 1], None, ALU.mult),
                 rd=[wb[kc], bg1], wr=[wb[kc]])
        ss, bss = T("ss1", [128, NT]); c.op("dve", lambda e: e.memset(ss[:], 0.0), wr=[bss])
        rs, brs = T("rs1", [128, NT])
        junk, bjunk = T("junk1", [128, DM])
        xts = [T("xt%d" % i, [128, DM]) for i in range(2)]
        hTs = [es.enter_context(nc.sbuf_tensor("hT%d" % i, [128, 8, 512], BF16)) for i in range(2)]
        hTb = [[Buf("hT%d_%d" % (i, s)) for s in range(4)] for i in range(2)]
        stgs = [T("stg%d" % i, [128, TMC]) for i in range(2)]
        zsb = [T("zsb%d" % j, [128, 513]) for j in range(11)]
        for j in range(11):
            c.op("pool", lambda e, j=j: e.memset(zsb[j][0][:, 0:1], 0.0), wr=[zsb[j][1]])
        tds = [T("td%d" % i, [128, 512]) for i in range(2)]
        ostg = [T("ostg%d" % i, [128, 512]) for i in range(3)]
        no = {"i": 0}
        NSTv = int(os.environ.get('NST', '8'))
        hbs = [T("hb1_%d" % i, [128, DM], BF16) for i in range(2)]

        def s1(tt):
            st, sub = tt // 4, tt % 4
            hT = hTs[st % 2]
            xt, bxt = xts[tt % 2]
            hb, bhb = hbs[tt % 2]
            c.dma("sp", xt[:], x[tt * 128:(tt + 1) * 128, :], wr=[bxt])
            c.op("act", lambda e: e.activation(junk[:], xt[:], AF.Square, accum_out=ss[:, tt:tt + 1]), rd=[bxt], wr=[bjunk, bss])
            c.op("dve", lambda e: e.tensor_scalar(rs[:, tt:tt + 1], ss[:, tt:tt + 1], 1.0 / DM, 1e-6, ALU.mult, ALU.add), rd=[bss], wr=[brs])
            c.op("act", lambda e: e.activation(rs[:, tt:tt + 1], rs[:, tt:tt + 1], AF.Ln), rd=[brs], wr=[brs])
            c.op("act", lambda e: e.activation(rs[:, tt:tt + 1], rs[:, tt:tt + 1], AF.Exp, scale=-0.5), rd=[brs], wr=[brs])
            c.op("dve", lambda e: e.tensor_scalar(hb[:], xt[:], rs[:, tt:tt + 1], None, ALU.mult), rd=[bxt, brs], wr=[bhb])
            p, bp = ps(); pb = p[:].bitcast(BF16)
            c.group("pe", [lambda e, k=k: e.transpose(pb[:, k * 128:(k + 1) * 128], hb[:, k * 128:(k + 1) * 128], idb[:]) for k in range(8)],
                    rd=[bhb, bidb], wr=[bp])
            c.op("act", lambda e: e.copy(hT[:, :, sub * 128:(sub + 1) * 128], pb.rearrange("p (k t) -> p k t", k=8)), rd=[bp], wr=[hTb[st % 2][sub]])

        def s2(tt):
            st, sub = tt // 4, tt % 4
            hT = hTs[st % 2]
            stg, bstg = stgs[tt % 2]
            for gi in range(8):
                p, bp = ps()
                c.group("pe", [lambda e, k=k, p=p: e.matmul(p[:], hT[:, k, sub * 128:(sub + 1) * 128], w[:, k, FMC + gi * 512:FMC + (gi + 1) * 512],
                                                           start=(k == 0), stop=(k == 7)) for k in range(8)],
                        rd=[hTb[st % 2][sub]] + wb, wr=[bp])
                en = ev()
                if en == "act":
                    c.op("act", lambda e, p=p: e.copy(stg[:, gi * 512:(gi + 1) * 512], p[:]), rd=[bp], wr=[bstg])
                else:
                    c.op("dve", lambda e, p=p: e.tensor_copy(stg[:, gi * 512:(gi + 1) * 512], p[:]), rd=[bp], wr=[bstg])
            c.dma("pool", zT[tt * 128:(tt + 1) * 128, :], stg[:], rd=[bstg])

        def fm(st):
            hT = hTs[st % 2]
            for j in range(11):
                ncol = 32 if j == 10 else 128
                z, bz = zsb[j]
                p, bp = ps()
                c.group("pe", [lambda e, k=k, p=p: e.matmul(p[0:ncol, :], w[:, k, j * 128:j * 128 + ncol], hT[:, k, :], start=(k == 0), stop=(k == 7)) for k in range(8)],
                        rd=hTb[st % 2] + wb, wr=[bp])
                c.op("act", lambda e, p=p: e.copy(z[0:ncol, 1:513], p[0:ncol, :]), rd=[bp], wr=[bz])
                td, btd = tds[j % 2]
                c.op("dve", lambda e: e.tensor_tensor(td[0:ncol, :], z[0:ncol, 0:512], z[0:ncol, 1:513], ALU.subtract), rd=[bz], wr=[btd])
                o, bo = ostg[no["i"] % 3]; no["i"] += 1
                c.op("dve", lambda e: e.scalar_tensor_tensor(o[0:ncol, :], td[0:ncol, :], muf[0:ncol, j:j + 1], z[0:ncol, 1:513], ALU.mult, ALU.add),
                     rd=[btd, bz, bmuf], wr=[bo])
                c.op("act", lambda e: e.copy(z[0:ncol, 0:1], z[0:ncol, 512:513]), rd=[bz], wr=[bz])
                c.dma("pool", zF[j * 128:j * 128 + ncol, st * 512:(st + 1) * 512], o[0:ncol, :], rd=[bo])

        ntl = NSTv * 4
        if ntl > 0:
            s1(0)
        for tt in range(ntl):
            if tt + 1 < ntl:
                s1(tt + 1)
            s2(tt)
            if tt % 4 == 3:
                fm(tt // 4)
        c.barrier()

    if upto <= 1:
        print("ninst", c.ninst, "nwait", c.nwait); return nc
    with ExitStack() as es:
        def T(name, shape, dt=F32):
            return es.enter_context(nc.sbuf_tensor("s_" + name, list(shape), dt)), Buf(name)
        idb, bidb = T("idb2", [128, 128], BF16); c.dma("pool", idb[:], ident_d, wr=[bidb])
        wdec, bwdec = T("wdec", [65, 512], BF16); c.dma("pool", wdec[:], wdec_d, wr=[bwdec])
        waaa, bwaaa = T("waaa", [65, 512], BF16); c.dma("pool", waaa[:], waaa_d, wr=[bwaaa])
        wgate, bwgate = T("wgate", [128, 2, 512], BF16)
        c.dma("pool", wgate[:, 0, :], wgate_d[0:128, :], wr=[bwgate]); c.dma("pool", wgate[0:32, 1, :], wgate_d[128:160, :], wr=[bwgate])
        kkf, bkkf = T("kkf", [128, 4]); c.dma("sp", kkf[:], kk_d, wr=[bkkf])
        kaf, bkaf = T("kaf", [128, 4]); c.dma("sp", kaf[:], ka_d, wr=[bkaf])
        c0f, bc0f = T("c0f", [128, 4])
        c.op("dve", lambda e: e.tensor_scalar(c0f[:], kaf[:], -1.0, 1.0, ALU.mult, ALU.add), rd=[bkaf], wr=[bc0f])
        rkb, brkb = T("rkb", [128, 8]); c.dma("sp", rkb[:], rkb_d, wr=[brkb])
        lng, blng = T("lng", [128, 512]); c.dma("sp", lng[:], lng_d.partition_broadcast(128), wr=[blng])
        lnb, blnb = T("lnb", [128, 512]); c.dma("sp", lnb[:], lnb_d.partition_broadcast(128), wr=[blnb])
        muv, bmuv = T("muv", [128, 512]); c.dma("sp", muv[:], mu_v.partition_broadcast(128), wr=[bmuv])
        MU2, bMU2 = T("MU2", [128, 512]); c.dma("sp", MU2[:], mu2_d, wr=[bMU2])
        SL2, bSL2 = T("SL2", [128, 256]); c.dma("sp", SL2[:], sl2_d, wr=[bSL2])
        bones, bbones = T("bones", [128, 128], BF16); c.dma("pool", bones[:], bones_d, wr=[bbones])
        S32, bS32 = T("S32", [128, 4, 64]); c.op("dve", lambda e: e.memset(S32[:], 0.0), wr=[bS32])
        Sb, bSb = T("Sb", [128, 4, 64], BF16); c.op("dve", lambda e: e.memset(Sb[:], 0.0), wr=[bSb])
        tha = [T("tha%d" % i, [65, 128], BF16) for i in range(2)]
        xaa = [T("xaa%d" % i, [65, 128], BF16) for i in range(2)]
        for i in range(2):
            c.op("dve", lambda e, i=i: e.memset(tha[i][0][:], 1.0), wr=[tha[i][1]])
            c.op("dve", lambda e, i=i: e.memset(xaa[i][0][:], 1.0), wr=[xaa[i][1]])
        rFs = [T("rF%d" % i, [128, 4, 128]) for i in range(2)]
        kFs = [T("kF%d" % i, [128, 4, 128]) for i in range(2)]
        xws = [T("xw%d" % i, [64, 128]) for i in range(2)]
        xas = [T("xa%d" % i, [64, 128]) for i in range(2)]
        xg0s = [T("xg0%d" % i, [128, 128]) for i in range(2)]
        xg1s = [T("xg1%d" % i, [32, 128]) for i in range(2)]
        vTs = [T("vT%d" % i, [128, 512]) for i in range(2)]
        vPs = [T("vP%d" % i, [128, 512]) for i in range(2)]
        for i in range(2):
            c.op("dve", lambda e, i=i: e.memset(vPs[i][0][:], 0.0), wr=[vPs[i][1]])
        v32s = [T("v32%d" % i, [128, 512]) for i in range(3)]; vbs = [T("vb%d" % i, [128, 512], BF16) for i in range(3)]; vtmp, bvtmp = T("vtmp", [128, 512])
        sg0, bsg0 = T("sg0", [128, 128], BF16); sg1, bsg1 = T("sg1", [32, 128], BF16)
        tg, btg = T("tg", [128, 512]); sgt, bsgt = T("sgt", [128, 128])
        logw, blogw = T("logw", [128, 4, 128]); lgi, blgi = T("lgi", [128, 4, 128]); lge, blge = T("lge", [128, 4, 128])
        alr, balr = T("alr", [128, 4, 128]); gTs = [T("gT%d" % i, [128, 512]) for i in range(3)]
        kkr, bkkr = T("kkr", [128, 4, 128]); sqb, bsqb = T("sqb", [128, 512], BF16); rn, brn = T("rn", [128, 512])
        kkn, bkkn = T("kkn", [128, 4, 128]); fF, bfF = T("fF", [128, 4, 128]); kM, bkM = T("kM", [128, 4, 128])
        gins = [T("gin%d" % i, [128, 4, 128]) for i in range(3)]; ginv, bginv = T("ginv", [128, 4, 128]); gex, bgex = T("gex", [128, 4, 128])
        ARs = [T("ARZ%d" % i, [128, 8, 2, 128], BF16) for i in range(3)]; bF, bbF = T("bF", [128, 4, 128])
        Bt, bBt = T("BtZ", [128, 8, 128], BF16); Kt, bKt = T("KtZ", [128, 8, 128], BF16)
        for (t_, b_) in (ARs[0], ARs[1], ARs[2], (Bt, bBt), (Kt, bKt)):
            c.op("pool", lambda e, t_=t_: e.memset(t_[:], 0.0), wr=[b_])
        Dd, bDd = T("Dd", [128, 4, 128]); Bh, bBh = T("Bh", [128, 4, 128], BF16); Kh, bKh = T("Kh", [128, 4, 128], BF16)
        BKhTs = [T("BKhT%d" % i, [128, 1024], BF16) for i in range(3)]
        rk, brk = T("rk", [128, 4, 128]); coefs = [T("coef%d" % i, [128, 8]) for i in range(3)]
        MABs = [T("MAB%d" % i, [128, 8, 2, 128], BF16) for i in range(3)]; MAKs = [T("MAK%d" % i, [128, 8, 2, 128], BF16) for i in range(3)]; MABTs = [T("MABT%d" % i, [128, 8, 128], BF16) for i in range(3)]
        Pk = [T("Pk%d" % i, [128, 8, 128], BF16) for i in range(2)]
        PTk = [T("PTk%d" % i, [128, 8, 128], BF16) for i in range(2)]
        ACk = [T("ACk%d" % i, [128, 8, 128], BF16) for i in range(2)]
        MinvS = [T("Minv%d" % i, [128, 8, 128], BF16) for i in range(2)]
        gPk = [[Buf("gPk%d_%d" % (i, g)) for g in range(2)] for i in range(2)]; gPTk = [[Buf("gPTk%d_%d" % (i, g)) for g in range(2)] for i in range(2)]
        gACk = [[Buf("gACk%d_%d" % (i, g)) for g in range(2)] for i in range(2)]; gMinv = [[Buf("gMinv%d_%d" % (i, g)) for g in range(2)] for i in range(2)]
        XT, bXT = T("XT", [128, 512], BF16); UT, bUT = T("UT", [128, 512], BF16)
        tmpS, btmpS = T("tmpS", [128, 4, 64])
        s1, bs1 = T("s1", [128, 8]); s2, bs2 = T("s2", [128, 8]); mean, bmean = T("mean", [128, 8]); var, bvar = T("var", [128, 8])
        sqt, bsqt = T("sqt", [128, 512]); yn, byn = T("yn", [128, 512]); bon, bbon = T("bon", [128, 512])
        yab, byab = T("yab", [128, 512], BF16); yaT, byaT = T("yaT", [128, 4, 128], BF16)

        zFr = zF[0:512, :].rearrange("(c p) t -> p c t", p=128)
        zFk = zF[512:1024, :].rearrange("(c p) t -> p c t", p=128)
        yaFv = yaF.rearrange("(c p) t -> p c t", p=128)

        def loads(ch):
            i = ch % 2; t0 = ch * 128
            c.dma("sp", rFs[i][0][:], zFr[:, :, t0:t0 + 128], wr=[rFs[i][1]])
            c.dma("sp", kFs[i][0][:], zFk[:, :, t0:t0 + 128], wr=[kFs[i][1]])
            c.dma("sp", xws[i][0][:], zF[1024:1088, t0:t0 + 128], wr=[xws[i][1]])
            c.dma("sp", xas[i][0][:], zF[1088:1152, t0:t0 + 128], wr=[xas[i][1]])
            c.dma("sp", xg0s[i][0][:], zF[1152:1280, t0:t0 + 128], wr=[xg0s[i][1]])
            c.dma("sp", xg1s[i][0][:], zF[1280:1312, t0:t0 + 128], wr=[xg1s[i][1]])
            c.dma("sp", vTs[i][0][:], zT[t0:t0 + 128, 0:512], wr=[vTs[i][1]])
            if ch == 0:
                c.dma("sp", vPs[i][0][1:128, :], zT[0:127, 0:512], wr=[vPs[i][1]])
            else:
                c.dma("sp", vPs[i][0][:], zT[t0 - 1:t0 + 127, 0:512], wr=[vPs[i][1]])

        loads(0)
        NCH = int(os.environ.get('NCH', str(NT)))
        STG = int(os.environ.get('STG', '99'))
        pqA = {"i": 0}; pqB = {"i": 0}
        pqC = {"i": 0}
        def psA():
            pqA["i"] += 1
            return PSB[pqA["i"] % 3]
        def psC():
            pqC["i"] += 1
            return PSB[3 + pqC["i"] % 3]
        def psB():
            pqB["i"] += 1
            return PSB[6 + pqB["i"] % 2]
        def stageA(ch):
            if ch + 1 < NCH:
                loads(ch + 1)
            i = ch % 2; t0 = ch * 128
            j3 = ch % 3
            v32, bv32 = v32s[j3]; vb, bvb = vbs[j3]; gT, bgT = gTs[j3]; gin, bgin = gins[j3]; AR, bAR = ARs[j3]
            BKhT, bBKhT = BKhTs[j3]; coef, bcoef = coefs[j3]; MAB, bMAB = MABs[j3]; MAK, bMAK = MAKs[j3]; MABT, bMABT = MABTs[j3]
            rF, brF = rFs[i]; kF, bkF = kFs[i]; xw, bxw = xws[i]; xa, bxa = xas[i]
            xg0, bxg0 = xg0s[i]; xg1, bxg1 = xg1s[i]; vT, bvT = vTs[i]; vP, bvP = vPs[i]
            th, bth = tha[i]; xab, bxab = xaa[i]
            c.op("pool", lambda e: e.tensor_tensor(vtmp[:], vP[:], vT[:], ALU.subtract), rd=[bvP, bvT], wr=[bvtmp])
            c.op("pool", lambda e: e.tensor_tensor(vtmp[:], vtmp[:], muv[:], ALU.mult), rd=[bvtmp, bmuv], wr=[bvtmp])
            c.op("pool", lambda e: e.tensor_tensor(v32[:], vtmp[:], vT[:], ALU.add), rd=[bvtmp, bvT], wr=[bv32])
            c.op("pool", lambda e: e.tensor_copy(vb[:], v32[:]), rd=[bv32], wr=[bvb])
            c.op("act", lambda e: e.activation(th[0:64, :], xw[:], AF.Tanh), rd=[bxw], wr=[bth])
            c.op("dve", lambda e: e.tensor_copy(xab[0:64, :], xa[:]), rd=[bxa], wr=[bxab])
            c.op("act", lambda e: e.activation(sgt[:], xg0[:], AF.Tanh, scale=0.5), rd=[bxg0], wr=[bsgt])
            c.op("dve", lambda e: e.tensor_scalar(sg0[:], sgt[:], 0.5, 0.5, ALU.mult, ALU.add), rd=[bsgt], wr=[bsg0])
            c.op("act", lambda e: e.activation(sgt[0:32, :], xg1[:], AF.Tanh, scale=0.5), rd=[bxg1], wr=[bsgt])
            c.op("dve", lambda e: e.tensor_scalar(sg1[:], sgt[0:32, :], 0.5, 0.5, ALU.mult, ALU.add), rd=[bsgt], wr=[bsg1])
            p, bp = psA()
            c.group("pe", [lambda e, q=q, p=p: e.matmul(p[:, q * 128:(q + 1) * 128], wdec[:, q * 128:(q + 1) * 128], th[:], start=True, stop=True) for q in range(4)],
                    rd=[bwdec, bth], wr=[bp])
            c.op("act", lambda e, p=p: e.activation(tg[:], p[:], AF.Tanh, scale=0.5), rd=[bp], wr=[btg])
            c.op("dve", lambda e: e.tensor_scalar(logw[:].rearrange("p a b -> p (a b)"), tg[:], -0.5 * math.exp(-0.5), -0.5 * math.exp(-0.5), ALU.mult, ALU.add),
                 rd=[btg], wr=[blogw])
            for q in range(4):
                c.op("dve", lambda e, q=q: e.tensor_tensor_scan(lgi[:, q, :], logw[:, q, :], logw[:, q, :], 0.0, ALU.add, ALU.bypass), rd=[blogw], wr=[blgi])
            c.op("pool", lambda e: e.tensor_tensor(lge[:], lgi[:], logw[:], ALU.subtract), rd=[blgi, blogw], wr=[blge])
            yield
            p, bp = psA()
            c.group("pe", [lambda e, q=q, p=p: e.matmul(p[:, q * 128:(q + 1) * 128], waaa[:, q * 128:(q + 1) * 128], xab[:], start=True, stop=True) for q in range(4)],
                    rd=[bwaaa, bxab], wr=[bp])
            c.op("act", lambda e, p=p: e.activation(tg[:], p[:], AF.Tanh, scale=0.5), rd=[bp], wr=[btg])
            c.op("dve", lambda e: e.tensor_scalar(alr[:].rearrange("p a b -> p (a b)"), tg[:], 0.5, 0.5, ALU.mult, ALU.add), rd=[btg], wr=[balr])
            p, bp = psA()
            c.group("pe", [lambda e, p=p: e.matmul(p[:], sg0[:], wgate[:, 0, :], start=True, stop=False),
                           lambda e, p=p: e.matmul(p[:], sg1[:], wgate[0:32, 1, :], start=False, stop=True)], rd=[bsg0, bsg1, bwgate], wr=[bp])
            c.op("dve", lambda e, p=p: e.tensor_copy(gT[:], p[:]), rd=[bp], wr=[bgT])
            for q in range(4):
                c.op("dve", lambda e, q=q: e.tensor_scalar(kkr[:, q, :], kF[:, q, :], kkf[:, q:q + 1], None, ALU.mult), rd=[bkF, bkkf], wr=[bkkr])
            c.op("pool", lambda e: e.tensor_tensor(sqb[:], kkr[:].rearrange("p a b -> p (a b)"), kkr[:].rearrange("p a b -> p (a b)"), ALU.mult), rd=[bkkr], wr=[bsqb])
            p, bp = psA()
            c.op("pe", lambda e, p=p: e.matmul(p[:], bones[:], sqb[:], start=True, stop=True), rd=[bbones, bsqb], wr=[bp])
            c.op("act", lambda e, p=p: e.activation(rn[:], p[:], AF.Ln), rd=[bp], wr=[brn])
            c.op("act", lambda e: e.activation(rn[:], rn[:], AF.Exp, scale=-0.5), rd=[brn], wr=[brn])
            c.op("dve", lambda e: e.tensor_tensor(kkn[:].rearrange("p a b -> p (a b)"), kkr[:].rearrange("p a b -> p (a b)"), rn[:], ALU.mult), rd=[bkkr, brn], wr=[bkkn])
            yield
            for q in range(4):
                c.op("dve", lambda e, q=q: e.tensor_scalar(fF[:, q, :], alr[:, q, :], kaf[:, q:q + 1], c0f[:, q:q + 1], ALU.mult, ALU.add), rd=[balr, bkaf, bc0f], wr=[bfF])
            c.op("pool", lambda e: e.tensor_tensor(kM[:], kF[:], fF[:], ALU.mult), rd=[bkF, bfF], wr=[bkM])
            c.op("act", lambda e: e.activation(gin[:], lgi[:], AF.Exp), rd=[blgi], wr=[bgin])
            c.op("act", lambda e: e.activation(ginv[:], lgi[:], AF.Exp, scale=-1.0), rd=[blgi], wr=[bginv])
            c.op("act", lambda e: e.activation(gex[:], lge[:], AF.Exp), rd=[blge], wr=[bgex])
            for q in range(4):
                c.op("act", lambda e, q=q: e.activation(Dd[:, q, :], lgi[:, q, :], AF.Exp, bias=lgi[:, q, 127:128], scale=-1.0), rd=[blgi], wr=[bDd])
            c.op("pool", lambda e: e.tensor_tensor(bF[:], kkn[:], alr[:], ALU.mult), rd=[bkkn, balr], wr=[bbF])
            for hh in range(2):
                r0, r1 = hh * 64, (hh + 1) * 64
                ARv = AR[r0:r1, :, :, :].rearrange("p (q two) a t -> p q two a t", two=2)[:, :, hh, :, :]
                Btv = Bt[r0:r1, :, :].rearrange("p (q two) t -> p q two t", two=2)[:, :, hh, :]
                Ktv = Kt[r0:r1, :, :].rearrange("p (q two) t -> p q two t", two=2)[:, :, hh, :]
                c.op("dve", lambda e: e.tensor_tensor(ARv[:, :, 1, :], rF[r0:r1, :, :], gin[r0:r1, :, :], ALU.mult), rd=[brF, bgin], wr=[bAR])
                c.op("dve", lambda e: e.scalar_tensor_tensor(ARv[:, :, 0, :], kkn[r0:r1, :, :], -1.0, gex[r0:r1, :, :], ALU.mult, ALU.mult), rd=[bkkn, bgex], wr=[bAR])
                c.op("dve", lambda e: e.tensor_tensor(Btv, bF[r0:r1, :, :], ginv[r0:r1, :, :], ALU.mult), rd=[bbF, bginv], wr=[bBt])
                c.op("pool", lambda e: e.tensor_tensor(Ktv, kM[r0:r1, :, :], ginv[r0:r1, :, :], ALU.mult), rd=[bkM, bginv], wr=[bKt])
            c.op("pool", lambda e: e.tensor_tensor(Bh[:], bF[:], Dd[:], ALU.mult), rd=[bbF, bDd], wr=[bBh])
            c.op("pool", lambda e: e.tensor_tensor(Kh[:], kM[:], Dd[:], ALU.mult), rd=[bkM, bDd], wr=[bKh])
            yield
            p, bp = psA(); pb = p[:].bitcast(BF16)
            c.group("pe", [lambda e, q=q: e.transpose(pb[:, q * 128:(q + 1) * 128], Bh[:, q, :], idb[:]) for q in range(4)] +
                          [lambda e, q=q: e.transpose(pb[:, 512 + q * 128:512 + (q + 1) * 128], Kh[:, q, :], idb[:]) for q in range(4)],
                    rd=[bBh, bKh, bidb], wr=[bp])
            c.op("dve", lambda e: e.tensor_copy(BKhT[:], pb), rd=[bp], wr=[bBKhT])
            c.op("pool", lambda e: e.tensor_tensor(rk[:], rF[:], kM[:], ALU.mult), rd=[brF, bkM], wr=[brk])
            p, bp = psA()
            c.group("pe", [lambda e, q=q, p=p: e.matmul(p[:, 2 * q:2 * q + 2], rk[:, q, :], rkb[:, 2 * q:2 * q + 2], start=True, stop=True) for q in range(4)],
                    rd=[brk, brkb], wr=[bp])
            c.op("dve", lambda e, p=p: e.tensor_copy(coef[:], p[:, 0:8]), rd=[bp], wr=[bcoef])
            for q in range(4):
                pA, bpA = psA(); pB, bpB = psA(); pC, bpC = psA()
                fa, fb, fc = [], [], []
                for hh in range(2):
                    h = 2 * q + hh
                    arr = AR[:, h, :, :].rearrange("p a b -> p (a b)")
                    fa.append(lambda e, hh=hh, h=h, arr=arr: e.matmul(pA[:, hh * 256:(hh + 1) * 256], Bt[:, h, :], arr, start=True, stop=True))
                    fb.append(lambda e, hh=hh, h=h, arr=arr: e.matmul(pB[:, hh * 256:(hh + 1) * 256], Kt[:, h, :], arr, start=True, stop=True))
                    fc.append(lambda e, hh=hh, h=h: e.matmul(pC[:, hh * 128:(hh + 1) * 128], AR[:, h, 0, :], Bt[:, h, :], start=True, stop=True))
                c.group("pe", fa, rd=[bBt, bAR], wr=[bpA])
                c.group("pe", fb, rd=[bKt, bAR], wr=[bpB])
                c.group("pe", fc, rd=[bBt, bAR], wr=[bpC])
                c.op("dve", lambda e: e.tensor_tensor(MAB[:, 2 * q:2 * q + 2, :, :].rearrange("p a b c -> p (a b c)"), pA[:], MU2[:], ALU.mult), rd=[bpA, bMU2], wr=[bMAB])
                c.op("dve", lambda e: e.tensor_tensor(MAK[:, 2 * q:2 * q + 2, :, :].rearrange("p a b c -> p (a b c)"), pB[:], MU2[:], ALU.mult), rd=[bpB, bMU2], wr=[bMAK])
                c.op("dve", lambda e: e.tensor_tensor(MABT[:, 2 * q:2 * q + 2, :].rearrange("p a b -> p (a b)"), pC[:, 0:256], SL2[:], ALU.mult), rd=[bpC, bSL2], wr=[bMABT])
            yield

        def stageA2(ch):
            i = ch % 2
            j3 = ch % 3
            MAB, bMAB = MABs[j3]; MABT, bMABT = MABTs[j3]
            AC0, _ = ACk[0]
            c.op("pool", lambda e: e.tensor_tensor(AC0[:], MAB[:, :, 0, :], bcast(idb[:], [128, 8, 128], 1), ALU.add), rd=[bMAB, bidb], wr=gACk[0])
            Pp = lambda h: MAB[:, h, 0, :]
            PTp = lambda h: MABT[:, h, :]
            bPp = [bMAB, bMAB]; bPTp = [bMABT, bMABT]
            ACp = AC0; bACp = gACk[0]
            for lv in range(1, 7):
                Pn = Pk[lv % 2][0]; PTn = PTk[lv % 2][0]
                ACn = ACk[lv % 2][0] if lv < 6 else MinvS[i][0]
                bPn = gPk[lv % 2]; bPTn = gPTk[lv % 2]; bACn = gACk[lv % 2] if lv < 6 else gMinv[i]
                for grp in range(2):
                    hs = range(4 * grp, 4 * grp + 4)
                    gsl = slice(4 * grp, 4 * grp + 4)
                    if lv < 6:
                        p, bp = psC()
                        c.group("pe", [lambda e, h=h, p=p, Pp=Pp, PTp=PTp: e.matmul(p[:, (h % 4) * 128:(h % 4 + 1) * 128], PTp(h), Pp(h), start=True, stop=True) for h in hs],
                                rd=[bPp[grp], bPTp[grp]], wr=[bp])
                        c.op("act", lambda e, p=p: e.copy(Pn[:, gsl, :].rearrange("p a b -> p (a b)"), p[:]), rd=[bp], wr=[bPn[grp]])
                    p, bp = psC()
                    c.group("pe", [lambda e, h=h, p=p, Pp=Pp, PTp=PTp: e.matmul(p[:, (h % 4) * 128:(h % 4 + 1) * 128], Pp(h), PTp(h), start=True, stop=True) for h in hs],
                            rd=[bPp[grp], bPTp[grp]], wr=[bp])
                    c.op("act", lambda e, p=p: e.copy(PTn[:, gsl, :].rearrange("p a b -> p (a b)"), p[:]), rd=[bp], wr=[bPTn[grp]])
                for grp in range(2):
                    hs = range(4 * grp, 4 * grp + 4)
                    gsl = slice(4 * grp, 4 * grp + 4)
                    p, bp = psC()
                    fs = []
                    for h in hs:
                        fs.append(lambda e, h=h, p=p, ACp=ACp: e.matmul(p[:, (h % 4) * 128:(h % 4 + 1) * 128], idb[:], ACp[:, h, :], start=True, stop=False))
                        fs.append(lambda e, h=h, p=p, ACp=ACp: e.matmul(p[:, (h % 4) * 128:(h % 4 + 1) * 128], PTn[:, h, :], ACp[:, h, :], start=False, stop=True))
                    c.group("pe", fs, rd=[bidb, bACp[grp], bPTn[grp]], wr=[bp])
                    c.op("act", lambda e, p=p: e.copy(ACn[:, gsl, :].rearrange("p a b -> p (a b)"), p[:]), rd=[bp], wr=[bACn[grp]])
                yield
                Pp = (lambda Pn: (lambda h: Pn[:, h, :]))(Pn)
                PTp = (lambda PTn: (lambda h: PTn[:, h, :]))(PTn)
                bPp, bPTp = bPn, bPTn
                ACp, bACp = ACn, bACn
            yield

        def stageB(ch):
            i = ch % 2; t0 = ch * 128
            j3 = ch % 3
            v32, bv32 = v32s[j3]; vb, bvb = vbs[j3]; gT, bgT = gTs[j3]; gin, bgin = gins[j3]; AR, bAR = ARs[j3]
            Minv = MinvS[i][0]; bMinvL = gMinv[i]
            BKhT, bBKhT = BKhTs[j3]; coef, bcoef = coefs[j3]; MAB, bMAB = MABs[j3]; MAK, bMAK = MAKs[j3]; MABT, bMABT = MABTs[j3]
            yield
            pX, bpX = psB()
            fs = []
            for h in range(8):
                q, r0 = h // 2, (h % 2) * 64
                fs.append(lambda e, h=h: e.matmul(pX[:, h * 64:(h + 1) * 64], MAK[:, h, 0, :], vb[:, h * 64:(h + 1) * 64], start=True, stop=False))
                fs.append(lambda e, h=h, q=q: e.matmul(pX[:, h * 64:(h + 1) * 64], AR[:, h, 0, :], Sb[:, q, :], start=False, stop=True))
            c.group("pe", fs, rd=[bMAK, bvb, bAR, bSb], wr=[bpX])
            c.op("dve", lambda e: e.tensor_copy(XT[:], pX[:]), rd=[bpX], wr=[bXT])
            yield
            pU, bpU = psB()
            c.group("pe", [lambda e, h=h: e.matmul(pU[:, h * 64:(h + 1) * 64], Minv[:, h, :], XT[:, h * 64:(h + 1) * 64], start=True, stop=True) for h in range(8)],
                    rd=bMinvL + [bXT], wr=[bpU])
            c.op("dve", lambda e: e.tensor_copy(UT[:], pU[:]), rd=[bpU], wr=[bUT])
            yield
            pY, bpY = psB()
            fs = []
            for h in range(8):
                q, r0 = h // 2, (h % 2) * 64
                fs.append(lambda e, h=h: e.matmul(pY[:, h * 64:(h + 1) * 64], MAK[:, h, 1, :], vb[:, h * 64:(h + 1) * 64], start=True, stop=False))
                fs.append(lambda e, h=h: e.matmul(pY[:, h * 64:(h + 1) * 64], MAB[:, h, 1, :], UT[:, h * 64:(h + 1) * 64], start=False, stop=False))
                fs.append(lambda e, h=h, q=q: e.matmul(pY[:, h * 64:(h + 1) * 64], AR[:, h, 1, :], Sb[:, q, :], start=False, stop=True))
            c.group("pe", fs, rd=[bMAK, bMAB, bvb, bUT, bAR, bSb], wr=[bpY])
            yield
            pS, bpS = psB()
            fs = []
            for q in range(4):
                fs.append(lambda e, q=q: e.matmul(pS[:, q * 128:(q + 1) * 128], BKhT[:, q * 128:(q + 1) * 128], UT[:, q * 128:(q + 1) * 128], start=True, stop=False))
                fs.append(lambda e, q=q: e.matmul(pS[:, q * 128:(q + 1) * 128], BKhT[:, 512 + q * 128:512 + (q + 1) * 128], vb[:, q * 128:(q + 1) * 128], start=False, stop=True))
            c.group("pe", fs, rd=[bBKhT, bUT, bvb], wr=[bpS])
            pSv = pS[:].rearrange("p (q c) -> p q c", q=4)
            c.op("dve", lambda e: e.tensor_tensor(tmpS[:], S32[:], gin[:, :, 127:128].to_broadcast([128, 4, 64]), ALU.mult), rd=[bS32, bgin], wr=[btmpS])
            c.op("dve", lambda e: e.tensor_tensor(S32[0:64, :, :], tmpS[0:64, :, :], pSv[0:64, :, 0:64], ALU.add), rd=[btmpS, bpS], wr=[bS32])
            c.op("dve", lambda e: e.tensor_tensor(S32[64:128, :, :], tmpS[64:128, :, :], pSv[64:128, :, 64:128], ALU.add), rd=[btmpS, bpS], wr=[bS32])
            c.op("dve", lambda e: e.tensor_copy(Sb[:], S32[:]), rd=[bS32], wr=[bSb])
            yield
            pYv = pY[:].rearrange("p (h d) -> p h d", h=8)
            c.op("dve", lambda e: e.tensor_reduce(s1[:], pYv, AX.X, ALU.add), rd=[bpY], wr=[bs1])
            c.op("act", lambda e: e.activation(sqt[:], pY[:], AF.Square), rd=[bpY], wr=[bsqt])
            c.op("dve", lambda e: e.tensor_reduce(s2[:], sqt[:].rearrange("p (h d) -> p h d", h=8), AX.X, ALU.add), rd=[bsqt], wr=[bs2])
            c.op("dve", lambda e: e.tensor_scalar(mean[:], s1[:], 1.0 / 64, None, ALU.mult), rd=[bs1], wr=[bmean])
            c.op("dve", lambda e: e.tensor_tensor(var[:], mean[:], mean[:], ALU.mult), rd=[bmean], wr=[bvar])
            c.op("dve", lambda e: e.scalar_tensor_tensor(var[:], s2[:], 1.0 / 64, var[:], ALU.mult, ALU.subtract), rd=[bs2, bvar], wr=[bvar])
            c.op("dve", lambda e: e.tensor_scalar(var[:], var[:], 64e-5, None, ALU.add), rd=[bvar], wr=[bvar])
            c.op("act", lambda e: e.activation(var[:], var[:], AF.Ln), rd=[bvar], wr=[bvar])
            c.op("act", lambda e: e.activation(var[:], var[:], AF.Exp, scale=-0.5), rd=[bvar], wr=[bvar])
            ynv = yn[:].rearrange("p (h d) -> p h d", h=8)
            c.op("dve", lambda e: e.tensor_tensor(ynv, pYv, bcast(mean[:], [128, 8, 64], 2), ALU.subtract), rd=[bpY, bmean], wr=[byn])
            c.op("pool", lambda e: e.tensor_tensor(ynv, ynv, bcast(var[:], [128, 8, 64], 2), ALU.mult), rd=[byn, bvar], wr=[byn])
            c.op("pool", lambda e: e.tensor_tensor(yn[:], yn[:], lng[:], ALU.mult), rd=[byn, blng], wr=[byn])
            c.op("dve", lambda e: e.tensor_tensor(yn[:], yn[:], lnb[:], ALU.add), rd=[byn, blnb], wr=[byn])
            c.op("pool", lambda e: e.tensor_tensor(bon[:].rearrange("p (h d) -> p h d", h=8), v32[:].rearrange("p (h d) -> p h d", h=8), bcast(coef[:], [128, 8, 64], 2), ALU.mult),
                 rd=[bv32, bcoef], wr=[bbon])
            c.op("dve", lambda e: e.tensor_tensor(yn[:], yn[:], bon[:], ALU.add), rd=[byn, bbon], wr=[byn])
            c.op("dve", lambda e: e.tensor_tensor(yab[:], yn[:], gT[:], ALU.mult), rd=[byn, bgT], wr=[byab])
            p, bp = psB(); pb = p[:].bitcast(BF16)
            c.group("pe", [lambda e, q=q: e.transpose(pb[:, q * 128:(q + 1) * 128], yab[:, q * 128:(q + 1) * 128], idb[:]) for q in range(4)], rd=[byab, bidb], wr=[bp])
            c.op("act", lambda e: e.copy(yaT[:].rearrange("p a b -> p (a b)"), pb[:, 0:512]), rd=[bp], wr=[byaT])
            c.dma("act", yaFv[:, :, t0:t0 + 128], yaT[:], rd=[byaT])
        RUNW = [int(v) for v in os.environ.get('RUNW', '1,1,1').split(',')]
        def run(gens):
            alive = [(g, RUNW[k % 3]) for k, g in enumerate(gens) if g is not None]
            while alive:
                for (g, wgt) in list(alive):
                    for _ in range(wgt):
                        try:
                            next(g)
                        except StopIteration:
                            alive.remove((g, wgt))
                            break
        run([stageA(0)])
        run([stageA2(0), stageA(1) if NCH > 1 else None])
        for ch in range(NCH):
            run([stageB(ch), stageA2(ch + 1) if ch + 1 < NCH else None, stageA(ch + 2) if ch + 2 < NCH else None])
        c.barrier()

    if upto <= 2:
        print("ninst", c.ninst, "nwait", c.nwait); return nc
    es_w = ExitStack()
    def TW(name, shape, dt=F32):
        return es_w.enter_context(nc.sbuf_tensor("s_" + name, list(shape), dt)), Buf(name)
    with ExitStack() as es:
        def T(name, shape, dt=F32):
            return es.enter_context(nc.sbuf_tensor("s_" + name, list(shape), dt)), Buf(name)
        pst["n"] = 2; pst["i"] = 0; pst["off"] = 4
        pO = [PSB[6], PSB[7]]
        qst = {"i": 0}
        def psq():
            qst["i"] += 1
            return PSB[qst["i"] % 4]
        idb, bidb = T("idb3", [128, 128], BF16); c.dma("pool", idb[:], ident_d, wr=[bidb])
        CB, bCB = T("CB", [128, 2, 256], BF16); c.dma("pool", CB[:].rearrange("p a b -> p (a b)"), cbias2_d, wr=[bCB])
        s65, bs65 = T("s65", [65, 64], BF16); c.dma("pool", s65[:], sel65_d, wr=[bs65])
        pok, bpok = T("pok", [128, 16, 16]); c.dma("sp", pok[:].rearrange("p a b -> p (a b)"), pok_d.partition_broadcast(128), wr=[bpok])
        pbi, bpbi = T("pbi", [128, 16, 16]); c.dma("sp", pbi[:].rearrange("p a b -> p (a b)"), pbias_d.partition_broadcast(128), wr=[bpbi])
        invf, binvf = T("invf", [128, 8]); c.dma("sp", invf[:], invf_d.partition_broadcast(128), wr=[binvf])
        qkg, bqkg = T("qkg", [128, 1024]); c.dma("sp", qkg[:], qkg_d.partition_broadcast(128), wr=[bqkg])
        posi, bposi = T("posi", [128, NT], I32); c.dma("sp", posi[:], pos, wr=[bposi])
        posf, bposf = T("posf", [128, NT])
        c.op("dve", lambda e: e.tensor_copy(posf[:], posi[:]), rd=[bposi], wr=[bposf])
        yy, byy = T("yy", [128, NT, 8]); yi, byi = T("yi", [128, NT, 8], I32); yf, byf = T("yf", [128, NT, 8])
        sinT, bsinT = T("sinT", [128, NT, 8]); cosT, bcosT = T("cosT", [128, NT, 8])
        c.op("dve", lambda e: e.tensor_copy(yy[:], bcast(invf[:], [128, NT, 8], 1)), rd=[binvf], wr=[byy])
        c.op("dve", lambda e: e.tensor_tensor(yy[:], yy[:], bcast(posf[:], [128, NT, 8], 2), ALU.mult), rd=[byy, bposf], wr=[byy])
        for (dst, bdst, off) in ((sinT, bsinT, 0.0), (cosT, bcosT, 0.25)):
            if off != 0.0:
                c.op("dve", lambda e: e.tensor_scalar(yy[:], yy[:], off, None, ALU.add), rd=[byy], wr=[byy])
            c.op("dve", lambda e: e.tensor_copy(yi[:], yy[:]), rd=[byy], wr=[byi])
            c.op("dve", lambda e: e.tensor_copy(yf[:], yi[:]), rd=[byi], wr=[byf])
            c.op("dve", lambda e: e.tensor_tensor(yf[:], yy[:], yf[:], ALU.subtract), rd=[byy, byf], wr=[byf])
            c.op("act", lambda e, dst=dst: e.activation(dst[:], yf[:], AF.Sin, scale=2.0 * math.pi), rd=[byf], wr=[bdst])
        KT = es.enter_context(nc.sbuf_tensor("s_KT", [80, 8, S], BF16)); bKT = [Buf("KT%d" % g) for g in range(8)]
        VA = es.enter_context(nc.sbuf_tensor("s_VA", [128, NT, 8, 65], BF16)); bVA = [Buf("VA%d" % g) for g in range(8)]
        c.op("pool", lambda e: e.memset(VA[:], 1.0), wr=bVA)
        kmT, bkmT = T("kmT", [64, 8, 16], BF16); c.op("dve", lambda e: e.memset(kmT[:], 0.0), wr=[bkmT])
        km32, bkm32 = T("km32", [64, 8])
        qkvs = [T("qkv%d" % i, [128, 1536]) for i in range(2)]
        sq3, bsq3 = T("sq3", [128, 1024]); ss3, bss3 = T("ss3", [128, 16]); qn, bqn = T("qn", [128, 16, 64])
        rt, brt = T("rt", [128, 4, 16, 8])
        QA, bQA = T("QA", [128, 8, 80], BF16); KA, bKA = T("KA", [128, 8, 80], BF16)
        QT, bQT = T("QT", [64, 8, 128], BF16)
        QTAs = [T("QTA%d" % i, [80, 8, 512], BF16) for i in range(2)]
        gm, bgm = T("gm", [128, 8, 16]); mx, bmx = T("mx", [128, 8, 8]); sel, bsel = T("sel", [128, 8, 16])
        PTs = [T("PT%d" % i, [128, 512], BF16) for i in range(4)]
        OT, bOT = T("OT", [65, 512]); rr, brr = T("rr", [65, 512], BF16); r32, br32 = T("r32", [65, 512])
        c.op("dve", lambda e: e.memset(rr[:], 0.0), wr=[brr])
        yTs = [T("yT%d" % i, [64, 512], BF16) for i in range(2)]
        cnt3 = {"pt": 0, "ld": 0}

        def ld3(tt):
            c.dma("act", qkvs[tt % 2][0][:], zT[tt * 128:(tt + 1) * 128, 512:2048], wr=[qkvs[tt % 2][1]])

        def pre3(tt):
            if tt + 1 < NT:
                ld3(tt + 1)
            qkv, bqkv = qkvs[tt % 2]
            t0 = tt * 128; qb = tt // 2; g = tt // 4; ti = tt % 4
            QTA, bQTA = QTAs[g % 2]
            c.op("act", lambda e: e.activation(sq3[:], qkv[:, 0:1024], AF.Square), rd=[bqkv], wr=[bsq3])
            c.op("dve", lambda e: e.tensor_reduce(ss3[:], sq3[:].rearrange("p (h d) -> p h d", h=16), AX.X, ALU.add), rd=[bsq3], wr=[bss3])
            c.op("dve", lambda e: e.tensor_scalar(ss3[:], ss3[:], 1.0 / 64, 1e-6, ALU.mult, ALU.add), rd=[bss3], wr=[bss3])
            c.op("act", lambda e: e.activation(ss3[:], ss3[:], AF.Ln), rd=[bss3], wr=[bss3])
            c.op("act", lambda e: e.activation(ss3[:], ss3[:], AF.Exp, scale=-0.5), rd=[bss3], wr=[bss3])
            c.op("dve", lambda e: e.tensor_tensor(qn[:], qkv[:, 0:1024].rearrange("p (h d) -> p h d", h=16), bcast(ss3[:], [128, 16, 64], 2), ALU.mult), rd=[bqkv, bss3], wr=[bqn])
            c.op("pool", lambda e: e.tensor_tensor(qn[:].rearrange("p h d -> p (h d)"), qn[:].rearrange("p h d -> p (h d)"), qkg[:], ALU.mult), rd=[bqn, bqkg], wr=[bqn])
            cs = cosT[:, tt:tt + 1, :].to_broadcast([128, 16, 8]); sn = sinT[:, tt:tt + 1, :].to_broadcast([128, 16, 8])
            c.op("dve", lambda e: e.tensor_tensor(rt[:, 0, :, :], qn[:, :, 0:8], cs, ALU.mult), rd=[bqn, bcosT], wr=[brt])
            c.op("dve", lambda e: e.tensor_tensor(rt[:, 1, :, :], qn[:, :, 8:16], sn, ALU.mult), rd=[bqn, bsinT], wr=[brt])
            c.op("pool", lambda e: e.tensor_tensor(rt[:, 2, :, :], qn[:, :, 8:16], cs, ALU.mult), rd=[bqn, bcosT], wr=[brt])
            c.op("pool", lambda e: e.tensor_tensor(rt[:, 3, :, :], qn[:, :, 0:8], sn, ALU.mult), rd=[bqn, bsinT], wr=[brt])
            c.op("dve", lambda e: e.tensor_tensor(qn[:, :, 0:8], rt[:, 0, :, :], rt[:, 1, :, :], ALU.subtract), rd=[brt], wr=[bqn])
            c.op("dve", lambda e: e.tensor_tensor(qn[:, :, 8:16], rt[:, 2, :, :], rt[:, 3, :, :], ALU.add), rd=[brt], wr=[bqn])
            c.op("dve", lambda e: e.tensor_copy(QA[:, :, 0:64], qn[:, 0:8, :]), rd=[bqn], wr=[bQA])
            c.op("pool", lambda e: e.tensor_copy(KA[:, :, 0:64], qn[:, 8:16, :]), rd=[bqn], wr=[bKA])
            c.op("pool", lambda e: e.memset(KA[:, :, 64:80], 0.0), wr=[bKA])
            c.op("pool", lambda e: e.memset(KA[:, :, 64 + qb:65 + qb], 1.0), wr=[bKA])
            yield
            p, bp = ps(); pb = p[:].bitcast(BF16)
            c.group("pe", [lambda e, h=h: e.transpose(pb[0:80, h * 128:(h + 1) * 128], KA[:, h, :], idb[:]) for h in range(8)], rd=[bKA, bidb], wr=[bp])
            c.op("dve", lambda e: e.tensor_copy(KT[:, :, t0:t0 + 128], pb[0:80, :].rearrange("p (h t) -> p h t", h=8)), rd=[bp], wr=[bKT[g]])
            c.op("pool", lambda e: e.tensor_copy(VA[:, tt, :, 0:64], qkv[:, 1024:1536].rearrange("p (h d) -> p h d", h=8)), rd=[bqkv], wr=[bVA[g]])
            if qb > 0:
                p, bp = ps(); pb = p[:].bitcast(BF16)
                c.group("pe", [lambda e, h=h: e.transpose(pb[0:64, h * 128:(h + 1) * 128], QA[:, h, 0:64], idb[:]) for h in range(8)], rd=[bQA, bidb], wr=[bp])
                c.op("dve", lambda e: e.tensor_copy(QT[:], pb[0:64, :].rearrange("p (h t) -> p h t", h=8)), rd=[bp], wr=[bQT])
                yield
                p, bp = ps()
                c.group("pe", [lambda e, h=h, p=p: e.matmul(p[:, h * 16:(h + 1) * 16], QT[:, h, :], kmT[:, h, :], start=True, stop=True) for h in range(8)],
                        rd=[bQT, bkmT], wr=[bp])
                c.op("dve", lambda e, p=p: e.tensor_tensor(gm[:], p[:, 0:128].rearrange("p (h j) -> p h j", h=8), pbi[:, qb:qb + 1, :].to_broadcast([128, 8, 16]), ALU.add),
                     rd=[bp, bpbi], wr=[bgm])
                for h in range(8):
                    c.op("dve", lambda e, h=h: e.max(mx[:, h, :], gm[:, h, :]), rd=[bgm], wr=[bmx])
                c.op("dve", lambda e: e.tensor_tensor(sel[:], gm[:], mx[:, :, 2:3].to_broadcast([128, 8, 16]), ALU.is_ge), rd=[bgm, bmx], wr=[bsel])
                c.op("dve", lambda e: e.tensor_tensor(sel[:], sel[:], pok[:, qb:qb + 1, :].to_broadcast([128, 8, 16]), ALU.mult), rd=[bsel, bpok], wr=[bsel])
                c.op("dve", lambda e: e.tensor_scalar(QA[:, :, 64:80], sel[:], BIG, -BIG, ALU.mult, ALU.add), rd=[bsel], wr=[bQA])
            else:
                c.op("dve", lambda e: e.memset(QA[:, :, 64:80], -BIG), wr=[bQA])
            yield
            p, bp = ps(); pb = p[:].bitcast(BF16)
            c.group("pe", [lambda e, h=h: e.transpose(pb[0:80, h * 128:(h + 1) * 128], QA[:, h, :], idb[:]) for h in range(8)], rd=[bQA, bidb], wr=[bp])
            c.op("dve", lambda e: e.tensor_copy(QTA[:, :, ti * 128:(ti + 1) * 128], pb[0:80, :].rearrange("p (h t) -> p h t", h=8)), rd=[bp], wr=[bQTA])
            if tt % 2 == 1:
                c.op("dve", lambda e: e.tensor_reduce(km32[:], KT[0:64, :, qb * 256:(qb + 1) * 256], AX.X, ALU.add), rd=[bKT[g]], wr=[bkm32])
                c.op("dve", lambda e: e.tensor_scalar(kmT[:, :, qb], km32[:], 1.0 / 256, None, ALU.mult), rd=[bkm32], wr=[bkmT])

        s65f, bs65f = T("s65f", [65, 64]); c.dma("sp", s65f[:], sel65_d, wr=[bs65f])

        def steps3(g, h):
            QTA, bQTA = QTAs[g % 2]
            pOh, bOh = pO[h % 2]
            kbufs = bKT[0:g + 1]; vbufs = bVA[0:g + 1]
            st = []
            nk = 4 * g + 2
            for kt in range(nk):
                d = {}
                def qk(d=d, kt=kt):
                    d["p"], d["bp"] = psq()
                    c.op("pe", lambda e: e.matmul(d["p"][:], KT[:, h, kt * 128:(kt + 1) * 128], QTA[:, h, :], start=True, stop=True), rd=kbufs + [bQTA], wr=[d["bp"]])
                def ex(d=d):
                    d["PT"], d["bPT"] = PTs[cnt3["pt"] % 4]; cnt3["pt"] += 1
                    c.op("act", lambda e: e.activation(d["PT"][:], d["p"][:], AF.Exp, scale=0.125), rd=[d["bp"]], wr=[d["bPT"]])
                def pv(d=d, kt=kt):
                    c.op("pe", lambda e: e.matmul(pOh[0:65, :], VA[:, kt, h, :], d["PT"][:], start=(kt == 0), stop=False), rd=vbufs + [d["bPT"]], wr=[bOh])
                st.append((qk, ex, pv))
            for half in range(2):
                for kti in range(2):
                    kt = 4 * g + 2 * half + kti
                    qs = slice(half * 256, (half + 1) * 256)
                    last = (half == 1 and kti == 1)
                    d = {}
                    def qk(d=d, kt=kt, qs=qs, kti=kti):
                        d["p"], d["bp"] = psq()
                        c.group("pe", [lambda e: e.matmul(d["p"][:, 0:256], KT[0:64, h, kt * 128:(kt + 1) * 128], QTA[0:64, h, qs], start=True, stop=False),
                                       lambda e: e.matmul(d["p"][:, 0:256], idb[:], CB[:, kti, :], start=False, stop=True)], rd=kbufs + [bQTA, bidb, bCB], wr=[d["bp"]])
                    def ex(d=d):
                        d["PT"], d["bPT"] = PTs[cnt3["pt"] % 4]; cnt3["pt"] += 1
                        c.op("act", lambda e: e.activation(d["PT"][:, 0:256], d["p"][:, 0:256], AF.Exp, scale=0.125), rd=[d["bp"]], wr=[d["bPT"]])
                    def pv(d=d, kt=kt, qs=qs, last=last):
                        c.op("pe", lambda e: e.matmul(pOh[0:65, qs], VA[:, kt, h, :], d["PT"][:, 0:256], start=False, stop=last), rd=vbufs + [d["bPT"]], wr=[bOh])
                    st.append((qk, ex, pv))

            def fin():
                c.op("dve", lambda e: e.tensor_copy(OT[:], pOh[0:65, :]), rd=[bOh], wr=[bOT])
                p, bp = ps()
                c.op("pe", lambda e: e.matmul(p[0:64, :], s65f[:], OT[:], start=True, stop=True), rd=[bs65f, bOT], wr=[bp])
                yT, byT = yTs[h % 2]
                c.op("dve", lambda e: e.reciprocal(r32[0:64, :], p[0:64, :]), rd=[bp], wr=[br32])
                c.op("dve", lambda e: e.tensor_tensor(yT[:], OT[0:64, :], r32[0:64, :], ALU.mult), rd=[bOT, br32], wr=[byT])
                c.dma("sp", ybF[h * 64:(h + 1) * 64, g * 512:(g + 1) * 512], yT[:], rd=[byT])
            return st, fin

        ld3(0)
        for tt in range(4):
            for _ in pre3(tt):
                pass
        LOOK = int(os.environ.get('LOOK', '3'))
        for g in range(8):
            allst = []
            for h in range(8):
                st, fin = steps3(g, h)
                for i, s in enumerate(st):
                    allst.append((s, fin if i == len(st) - 1 else None, h))
            n = len(allst)
            pend = [pre3(4 * (g + 1) + i) for i in range(4)] if g + 1 < 8 else []
            every = max(1, n // 18)

            def advance():
                while pend:
                    try:
                        next(pend[0])
                        return
                    except StopIteration:
                        pend.pop(0)
            for i in range(min(LOOK, n)):
                allst[i][0][0]()
            for i in range(n):
                (qk, ex, pv), fin, h = allst[i]
                ex()
                if i + LOOK < n:
                    allst[i + LOOK][0][0]()
                pv()
                if fin is not None:
                    fin()
                if i % every == every - 1:
                    advance()
            while pend:
                advance()
        pst["n"] = 8; pst["off"] = 0
        c.barrier()

    if upto <= 3:
        print("ninst", c.ninst, "nwait", c.nwait); return nc
    wup, bwup = TW("wup", [128, 8, 2 * DFF], BF16)
    g2, bg2 = TW("g2", [128, 8]); c.dma("sp", g2[:], n2g, wr=[bg2])
    wub = [Buf("wup_%d" % k) for k in range(8)]
    for k in range(8):
        c.dma("pool", wup[:, k, :], wup_d[k * 128:(k + 1) * 128, :], wr=[wub[k]])
        c.op("dve", lambda e, k=k: e.tensor_scalar(wup[:, k, :], wup[:, k, :], g2[:, k:k + 1], None, ALU.mult), rd=[wub[k], bg2], wr=[wub[k]])
    with ExitStack() as es:
        def T(name, shape, dt=F32):
            return es.enter_context(nc.sbuf_tensor("s_" + name, list(shape), dt)), Buf(name)
        idb, bidb = T("idb4", [128, 128], BF16); c.dma("pool", idb[:], ident_d, wr=[bidb])
        wba, bwba = T("wba", [128, 4, DM], BF16); wbb, bwbb = T("wbb", [128, 4, DM], BF16); wo, bwo = T("wo", [128, 8, DM], BF16)
        for k in range(4):
            c.dma("pool", wba[:, k, :], wba_d[k * 128:(k + 1) * 128, :], wr=[bwba])
            c.dma("pool", wbb[:, k, :], wbb_d[k * 128:(k + 1) * 128, :], wr=[bwbb])
        for k in range(8):
            c.dma("pool", wo[:, k, :], wout_d[k * 128:(k + 1) * 128, :], wr=[bwo])
        xts = [T("x4%d" % i, [128, DM]) for i in range(3)]
        gps = [T("gp%d" % i, [128, 2048]) for i in range(2)]
        yas = [T("ya4%d" % i, [128, 4, 128], BF16) for i in range(2)]
        ybs = [T("yb4%d" % i, [128, 4, 128], BF16) for i in range(2)]
        x1, bx1 = T("x1", [128, DM]); junk, bjunk = T("junk4", [128, DM]); h2, bh2 = T("h2", [128, DM], BF16)
        h2T, bh2T = T("h2T", [128, 8, 128], BF16)
        ss, bss = T("ss4", [128, NT]); c.op("dve", lambda e: e.memset(ss[:], 0.0), wr=[bss])
        rs, brs = T("rs4", [128, NT])
        yaFv = yaF.rearrange("(c p) t -> p c t", p=128); ybFv = ybF.rearrange("(c p) t -> p c t", p=128)
        h2Fv = h2F.rearrange("(c p) t -> p c t", p=128)

        def loads4(tt):
            i = tt % 2; t0 = tt * 128
            c.dma("sp", xts[tt % 3][0][:], x[t0:t0 + 128, :], wr=[xts[tt % 3][1]])
            c.dma("sp", gps[i][0][:], zT[t0:t0 + 128, 2048:4096], wr=[gps[i][1]])
            c.dma("sp", yas[i][0][:], yaFv[:, :, t0:t0 + 128], wr=[yas[i][1]])
            c.dma("sp", ybs[i][0][:], ybFv[:, :, t0:t0 + 128], wr=[ybs[i][1]])
        m1s = [T("m1_%d" % i, [128, DM]) for i in range(2)]; m2s = [T("m2_%d" % i, [128, DM]) for i in range(2)]
        mbs = [T("mb_%d" % i, [128, DM], BF16) for i in range(2)]; mTs = [T("mT_%d" % i, [128, 8, 128], BF16) for i in range(2)]

        def g1(tt):
            if tt + 1 < NT:
                loads4(tt + 1)
            i = tt % 2
            gp, bgp = gps[i]; ya, bya = yas[i]; yb, byb = ybs[i]
            m1, bm1 = m1s[i]; m2, bm2 = m2s[i]; mb, bmb = mbs[i]; mT, bmT = mTs[i]
            c.op("act", lambda e: e.activation(gp[:], gp[:], AF.Sigmoid), rd=[bgp], wr=[bgp])
            for half in range(2):
                hs = slice(half * 512, (half + 1) * 512)
                pa, bpa = ps()
                c.group("pe", [lambda e, k=k: e.matmul(pa[:], ya[:, k, :], wba[:, k, hs], start=(k == 0), stop=(k == 3)) for k in range(4)], rd=[bya, bwba], wr=[bpa])
                c.op("dve", lambda e: e.tensor_tensor(m1[:, hs], pa[:], gp[:, half * 512:(half + 1) * 512], ALU.mult), rd=[bpa, bgp], wr=[bm1])
                pb_, bpb_ = ps()
                c.group("pe", [lambda e, k=k: e.matmul(pb_[:], yb[:, k, :], wbb[:, k, hs], start=(k == 0), stop=(k == 3)) for k in range(4)], rd=[byb, bwbb], wr=[bpb_])
                c.op("dve", lambda e: e.tensor_tensor(m2[:, hs], pb_[:], gp[:, 1024 + half * 512:1024 + (half + 1) * 512], ALU.mult), rd=[bpb_, bgp], wr=[bm2])
            c.op("dve", lambda e: e.tensor_tensor(mb[:], m1[:], m2[:], ALU.add), rd=[bm1, bm2], wr=[bmb])
            yield
            p, bp = ps(); pb = p[:].bitcast(BF16)
            c.group("pe", [lambda e, k=k: e.transpose(pb[:, k * 128:(k + 1) * 128], mb[:, k * 128:(k + 1) * 128], idb[:]) for k in range(8)], rd=[bmb, bidb], wr=[bp])
            c.op("act", lambda e: e.copy(mT[:].rearrange("p a b -> p (a b)"), pb), rd=[bp], wr=[bmT])

        def g2(tt):
            i = tt % 2; t0 = tt * 128
            xt, bxt = xts[tt % 3]; mT, bmT = mTs[i]
            for half in range(2):
                hs = slice(half * 512, (half + 1) * 512)
                po, bpo = ps()
                c.group("pe", [lambda e, k=k: e.matmul(po[:], mT[:, k, :], wo[:, k, hs], start=(k == 0), stop=(k == 7)) for k in range(8)], rd=[bmT, bwo], wr=[bpo])
                c.op("dve", lambda e: e.tensor_tensor(x1[:, hs], po[:], xt[:, hs], ALU.add), rd=[bpo, bxt], wr=[bx1])
            c.dma("pool", x1s[t0:t0 + 128, :], x1[:], rd=[bx1])
            c.op("act", lambda e: e.activation(junk[:], x1[:], AF.Square, accum_out=ss[:, tt:tt + 1]), rd=[bx1], wr=[bjunk, bss])
            c.op("dve", lambda e: e.tensor_scalar(rs[:, tt:tt + 1], ss[:, tt:tt + 1], 1.0 / DM, 1e-6, ALU.mult, ALU.add), rd=[bss], wr=[brs])
            c.op("act", lambda e: e.activation(rs[:, tt:tt + 1], rs[:, tt:tt + 1], AF.Ln), rd=[brs], wr=[brs])
            c.op("act", lambda e: e.activation(rs[:, tt:tt + 1], rs[:, tt:tt + 1], AF.Exp, scale=-0.5), rd=[brs], wr=[brs])
            c.op("dve", lambda e: e.tensor_scalar(h2[:], x1[:], rs[:, tt:tt + 1], None, ALU.mult), rd=[bx1, brs], wr=[bh2])
            yield
            p, bp = ps(); pb = p[:].bitcast(BF16)
            c.group("pe", [lambda e, k=k: e.transpose(pb[:, k * 128:(k + 1) * 128], h2[:, k * 128:(k + 1) * 128], idb[:]) for k in range(8)], rd=[bh2, bidb], wr=[bp])
            c.op("act", lambda e: e.copy(h2T[:].rearrange("p a b -> p (a b)"), pb), rd=[bp], wr=[bh2T])
            c.dma("pool", h2Fv[:, :, t0:t0 + 128], h2T[:], rd=[bh2T])

        def run4(gens):
            alive = [g for g in gens if g is not None]
            while alive:
                for g in list(alive):
                    try:
                        next(g)
                    except StopIteration:
                        alive.remove(g)
        loads4(0)
        run4([g1(0)])
        for tt in range(NT):
            run4([g1(tt + 1) if tt + 1 < NT else None, g2(tt)])
        c.barrier()

    if upto <= 4:
        print("ninst", c.ninst, "nwait", c.nwait); return nc
    with ExitStack() as es:
        def T(name, shape, dt=F32):
            return es.enter_context(nc.sbuf_tensor("s_" + name, list(shape), dt)), Buf(name)
        wdn, bwdn = T("wdn", [128, NFF, DM], BF16)
        for f in range(NFF):
            c.dma("pool", wdn[:, f, :], wdn_d[f * 128:(f + 1) * 128, :], wr=[bwdn])
        cw, bcw = T("cw", [128, NFF, 3]); c.dma("sp", cw[:].rearrange("p a b -> p (a b)"), cw_d, wr=[bcw])
        cbt, bcbt = T("cbt", [128, NFF]); c.dma("sp", cbt[:], cb_d, wr=[bcbt])
        cr, bcr = T("cr", [128, NFF, 2]); c.op("dve", lambda e: e.memset(cr[:], 0.0), wr=[bcr])
        h2s = [T("h2s%d" % i, [128, 8, 512], BF16) for i in range(2)]
        asb = [T("asb%d" % i, [128, 514]) for i in range(2)]
        acc = [T("acc%d" % i, [128, 512]) for i in range(2)]
        hg = es.enter_context(nc.sbuf_tensor("hg", [128, NFF, 512], BF16)); bhg = [Buf("hg%d" % f) for f in range(NFF)]
        x1t = [T("x1t%d" % i, [128, DM]) for i in range(2)]
        ost = [T("ost%d" % i, [128, DM]) for i in range(2)]
        h2Fv = h2F.rearrange("(c p) t -> p c t", p=128)
        c.dma("sp", h2s[0][0][:], h2Fv[:, :, 0:512], wr=[h2s[0][1]])
        nx = 0
        for st in range(8):
            if st + 1 < 8:
                c.dma("sp", h2s[(st + 1) % 2][0][:], h2Fv[:, :, (st + 1) * 512:(st + 2) * 512], wr=[h2s[(st + 1) % 2][1]])
            hT, bhT = h2s[st % 2]
            for f in range(NFF):
                pa, bpa = ps()
                c.group("pe", [lambda e, k=k: e.matmul(pa[:], wup[:, k, f * 128:(f + 1) * 128], hT[:, k, :], start=(k == 0), stop=(k == 7)) for k in range(8)], rd=[bhT] + wub, wr=[bpa])
                pg, bpg = ps()
                c.group("pe", [lambda e, k=k: e.matmul(pg[:], wup[:, k, DFF + f * 128:DFF + (f + 1) * 128], hT[:, k, :], start=(k == 0), stop=(k == 7)) for k in range(8)], rd=[bhT] + wub, wr=[bpg])
                a, ba = asb[f % 2]; ac, bac = acc[f % 2]
                c.op("act", lambda e: e.copy(a[:, 0:2], cr[:, f, :]), rd=[bcr], wr=[ba])
                c.op("act", lambda e: e.copy(a[:, 2:514], pa[:]), rd=[bpa], wr=[ba])
                c.op("act", lambda e: e.copy(cr[:, f, :], a[:, 512:514]), rd=[ba], wr=[bcr])
                c.op("dve", lambda e: e.tensor_scalar(ac[:], a[:, 0:512], cw[:, f, 0:1], None, ALU.mult), rd=[ba, bcw], wr=[bac])
                c.op("dve", lambda e: e.scalar_tensor_tensor(ac[:], a[:, 1:513], cw[:, f, 1:2], ac[:], ALU.mult, ALU.add), rd=[ba, bcw, bac], wr=[bac])
                c.op("dve", lambda e: e.scalar_tensor_tensor(ac[:], a[:, 2:514], cw[:, f, 2:3], ac[:], ALU.mult, ALU.add), rd=[ba, bcw, bac], wr=[bac])
                c.op("act", lambda e: e.activation(ac[:], ac[:], AF.Gelu, bias=cbt[:, f:f + 1]), rd=[bac, bcbt], wr=[bac])
                c.op("dve", lambda e: e.tensor_tensor(hg[:, f, :], ac[:], pg[:], ALU.mult), rd=[bac, bpg], wr=[bhg[f]])
            for sub in range(4):
                tt = st * 4 + sub; t0 = tt * 128
                xx, bxx = x1t[nx % 2]; oo, boo = ost[nx % 2]; nx += 1
                c.dma("sp", xx[:], x1s[t0:t0 + 128, :], wr=[bxx])
                for half in range(2):
                    hs = slice(half * 512, (half + 1) * 512)
                    po, bpo = ps()
                    c.group("pe", [lambda e, f=f: e.matmul(po[:], hg[:, f, sub * 128:(sub + 1) * 128], wdn[:, f, hs], start=(f == 0), stop=(f == NFF - 1)) for f in range(NFF)],
                            rd=bhg + [bwdn], wr=[bpo])
                    c.op("dve", lambda e: e.tensor_tensor(oo[:, hs], po[:], xx[:, hs], ALU.add), rd=[bpo, bxx], wr=[boo])
                c.dma("pool", out[t0:t0 + 128, :], oo[:], rd=[boo])
        c.barrier()
    es_w.close()
    print("ninst", c.ninst, "nwait", c.nwait, {e: c.cnt[e] for e in c.cnt})
    return nc


def _consts():
    i = np.arange(128)
    su = (i[:, None] < i[None, :]).astype(np.float32)
    ui = (i[:, None] <= i[None, :]).astype(np.float32)
    mu2 = np.concatenate([su, ui, su, ui], axis=1)
    sl = (i[None, :] < i[:, None]).astype(np.float32)
    sl2 = np.concatenate([sl, sl], axis=1)
    bones = ((i[:, None] // 64) == (i[None, :] // 64)).astype(np.float32)
    q2 = np.arange(256)
    cb0 = np.where(i[:, None] > q2[None, :], -BIG, 0.0).astype(np.float32)
    cb1 = np.where(i[:, None] + 128 > q2[None, :], -BIG, 0.0).astype(np.float32)
    cbias2 = np.concatenate([cb0, cb1], axis=1)
    sel65 = np.zeros((65, 64), np.float32); sel65[64, :] = 1.0
    j = np.arange(16)
    pok = (j[None, :] < j[:, None]).astype(np.float32)
    pbias = ((pok - 1.0) * 1e30).astype(np.float32)
    half = 8
    invf = (500000.0 ** (-np.arange(half, dtype=np.float32) / half)).astype(np.float32) / np.float32(2.0 * math.pi)
    return dict(ident=np.eye(128, dtype=np.float32), mu2=mu2, sl2=sl2, bones=bones, cbias2=cbias2, sel65=sel65,
                pok=pok.reshape(1, 256), pbias=pbias.reshape(1, 256), invf=invf.reshape(1, 8).astype(np.float32))


def _fm(v, n):
    return np.ascontiguousarray(np.asarray(v, np.float32).reshape(n, 128).T)


def _prep(inp):
    f = lambda a: np.ascontiguousarray(np.asarray(a, dtype=np.float32))
    w_in = f(inp["w_in"][0])
    fm_cols = np.r_[0:512, 512:1024, 1536:1600, 1600:1664, 1664:1824]
    tm_cols = np.r_[1024:1536, 1824:3360, 3360:5408]
    mu = f(inp["rwkv_mu"][0])
    mu_fm = np.zeros(1408, np.float32); mu_fm[:FMC] = mu[fm_cols]
    rk = f(inp["rwkv_r_k"][0])
    rkb = np.zeros((128, 8), np.float32)
    for h in range(8):
        rkb[(h % 2) * 64:(h % 2 + 1) * 64, h] = rk[h]
    cw = f(inp["ffn_conv_w"][0])
    cwl = np.ascontiguousarray(cw.reshape(3, NFF, 128).transpose(2, 1, 0)).reshape(128, NFF * 3)
    shared = dict(
        w_in=np.ascontiguousarray(w_in[:, np.r_[fm_cols, tm_cols]]),
        n1g=_fm(inp["norm1_g"][0], 8), mu_fm=_fm(mu_fm, 11), mu_v=f(mu[1024:1536]).reshape(1, 512),
        wdec=np.concatenate([f(inp["w_decay_up"][0]), f(inp["decay_bias"][0]).reshape(1, 512)], 0),
        waaa=np.concatenate([f(inp["w_aaa_up"][0]), f(inp["aaa_bias"][0]).reshape(1, 512)], 0),
        wgate=f(inp["w_gate_up"][0]), kk_fm=_fm(inp["rwkv_k_k"][0], 4), ka_fm=_fm(inp["rwkv_k_a"][0], 4), rkb=rkb,
        lng=f(inp["rwkv_ln_g"][0]).reshape(1, 512), lnb=f(inp["rwkv_ln_b"][0]).reshape(1, 512),
        qkg=np.concatenate([np.tile(f(inp["q_norm_g"][0]), 8), np.tile(f(inp["k_norm_g"][0]), 8)]).reshape(1, 1024),
        wba=f(inp["w_branch_a"][0]), wbb=f(inp["w_branch_b"][0]), wout=f(inp["w_out"][0]), n2g=_fm(inp["norm2_g"][0], 8),
        wup=f(inp["w_ffn_up"][0]), cw=cwl, cb=_fm(inp["ffn_conv_b"][0], NFF), wdn=f(inp["w_ffn_down"][0]),
    )
    shared.update(_consts())
    xs = np.asarray(inp["x"], np.float32); ps_ = np.asarray(inp["positions"], np.int32)
    maps = []
    for b in range(8):
        m = dict(shared)
        m["x"] = np.ascontiguousarray(xs[b])
        m["pos"] = np.ascontiguousarray(ps_[b].reshape(NT, 128).T)
        maps.append(m)
    return maps


def kernel(**inputs):
    maps = _prep(inputs)
    nc = build()
    res = run_bass_kernel_spmd(nc, maps, core_ids=list(range(8)))
    return np.stack([np.asarray(r["out"], np.float32) for r in res.results], axis=0)
```

```python
import numpy as np
import concourse.bass as bass
import concourse.mybir as mybir
from concourse.bass_utils import run_bass_kernel_spmd

F32 = mybir.dt.float32
BF16 = mybir.dt.bfloat16
I32 = mybir.dt.int32
ALU = mybir.AluOpType
AF = mybir.ActivationFunctionType
AX = mybir.AxisListType


class Buf:
    __slots__ = ("name", "lastw", "readers")

    def __init__(self, name):
        self.name = name
        self.lastw = None
        self.readers = {}


class Ctx:
    def __init__(self, nc, n_dma_sems=24):
        self.nc = nc
        self.eng = {"pe": nc.tensor, "act": nc.scalar, "dve": nc.vector,
                    "pool": nc.gpsimd, "sp": nc.sync}
        self.sem = {}
        self.cnt = {}
        self.waited = {e: {} for e in self.eng}
        self._stack = []
        for e in ("pe", "act", "dve", "pool"):
            cm = nc.semaphore("s_" + e)
            self.sem[e] = cm.__enter__()
            self._stack.append(cm)
            self.cnt[e] = 0
        self.dsem = []
        self.dpool = {"hw": [], "sw": []}
        for i in range(n_dma_sems):
            cm = nc.semaphore("d%d" % i)
            self.dsem.append([cm.__enter__(), 0])
            self._stack.append(cm)
            self.dpool["sw" if i < 8 else "hw"].append(i)
        self.dnext = {"hw": 0, "sw": 0}
        self.semh = {}
        for e in self.sem:
            self.semh[("e", e)] = self.sem[e]
        for i, (h, _) in enumerate(self.dsem):
            self.semh[("d", i)] = h
        self.nwait = 0
        self.ninst = 0

    def _wait(self, e, toks):
        w = self.waited[e]
        best = {}
        for t in toks:
            if t is None:
                continue
            k, v = t[0], t[1]
            if w.get(k, 0) >= v:
                continue
            if best.get(k, 0) < v:
                best[k] = v
        for k, v in best.items():
            self.eng[e].wait_ge(self.semh[k], v)
            w[k] = v
            self.nwait += 1

    def _deps(self, e, rd, wr):
        toks = []
        me = ("e", e)
        for b in rd:
            if b.lastw is not None:
                if not (e == "pe" and b.lastw[0] == me):
                    toks.append(b.lastw)
        for b in wr:
            if b.lastw is not None and not (e == "pe" and b.lastw[0] == me):
                toks.append(b.lastw)
            for k, t in b.readers.items():
                if not (e == "pe" and k == me):
                    toks.append(t)
        return toks

    def _mark(self, tok, rd, wr):
        for b in rd:
            b.readers[tok[0]] = tok
        for b in wr:
            b.lastw = tok
            b.readers = {}

    def op(self, e, fn, rd=(), wr=()):
        self._wait(e, self._deps(e, rd, wr))
        ins = fn(self.eng[e])
        self.cnt[e] += 1
        ins.then_inc(self.sem[e], 1)
        tok = (("e", e), self.cnt[e])
        self._mark(tok, rd, wr)
        self.ninst += 1
        return tok

    def group(self, e, fns, rd=(), wr=()):
        self._wait(e, self._deps(e, rd, wr))
        ins = None
        for fn in fns:
            ins = fn(self.eng[e])
            self.ninst += 1
        self.cnt[e] += 1
        ins.then_inc(self.sem[e], 1)
        tok = (("e", e), self.cnt[e])
        self._mark(tok, rd, wr)
        return tok

    def dma(self, q, out, in_, rd=(), wr=(), **kw):
        kind = "sw" if q == "pool" else "hw"
        pool = self.dpool[kind]
        i = pool[self.dnext[kind] % len(pool)]
        self.dnext[kind] += 1
        h, c = self.dsem[i]
        k = ("d", i)
        toks = self._deps(q, rd, wr)
        if c > 0:
            toks.append((k, 16 * c))
        self._wait(q, toks)
        self.eng[q].dma_start(out=out, in_=in_, **kw).then_inc(h, 16)
        self.dsem[i][1] = c + 1
        tok = (k, 16 * (c + 1))
        self._mark(tok, rd, wr)
        self.ninst += 1
        return tok

    def wait_all(self, e, bufs):
        toks = []
        for b in bufs:
            toks.append(b.lastw)
            toks.extend(b.readers.values())
        self._wait(e, toks)

    def barrier(self, bufs=()):
        toks = []
        for e in self.sem:
            if self.cnt[e] > 0:
                toks.append((("e", e), self.cnt[e]))
        for i, (h, c) in enumerate(self.dsem):
            if c > 0:
                toks.append((("d", i), 16 * c))
        for e in self.eng:
            self._wait(e, toks)

from contextlib import ExitStack
import math
import os

S = 4096
DM = 1024
NT = 32
FMC = 1312
TMC = 4096
DFF = 2816
NFF = 22
BIG = 30000.0


def build(debug=False, upto=99):
    nc = bass.Bass("TRN2", target_bir_lowering=False)
    okind = "ExternalOutput" if debug else "Internal"

    def DIN(name, shape, dt=F32):
        return nc.dram_tensor(name, list(shape), dt, kind="ExternalInput").ap()

    x = DIN("x", [S, DM]); pos = DIN("pos", [128, NT], I32)
    w_in = DIN("w_in", [DM, 5408]); n1g = DIN("n1g", [128, 8]); mu_fm = DIN("mu_fm", [128, 11]); mu_v = DIN("mu_v", [1, 512])
    wdec_d = DIN("wdec", [65, 512]); waaa_d = DIN("waaa", [65, 512]); wgate_d = DIN("wgate", [160, 512])
    kk_d = DIN("kk_fm", [128, 4]); ka_d = DIN("ka_fm", [128, 4]); rkb_d = DIN("rkb", [128, 8])
    lng_d = DIN("lng", [1, 512]); lnb_d = DIN("lnb", [1, 512]); qkg_d = DIN("qkg", [1, 1024])
    wba_d = DIN("wba", [512, DM]); wbb_d = DIN("wbb", [512, DM]); wout_d = DIN("wout", [DM, DM]); n2g = DIN("n2g", [128, 8])
    wup_d = DIN("wup", [DM, 2 * DFF]); cw_d = DIN("cw", [128, NFF * 3]); cb_d = DIN("cb", [128, NFF]); wdn_d = DIN("wdn", [DFF, DM])
    ident_d = DIN("ident", [128, 128]); mu2_d = DIN("mu2", [128, 512]); sl2_d = DIN("sl2", [128, 256]); bones_d = DIN("bones", [128, 128])
    cbias2_d = DIN("cbias2", [128, 512]); sel65_d = DIN("sel65", [65, 64]); pok_d = DIN("pok", [1, 256]); pbias_d = DIN("pbias", [1, 256]); invf_d = DIN("invf", [1, 8])
    out = nc.dram_tensor("out", [S, DM], F32, kind="ExternalOutput").ap()
    zF = nc.dram_tensor("zF", [1408, S], F32, kind=okind).ap()
    zT = nc.dram_tensor("zT", [S, TMC], F32, kind=okind).ap()
    yaF = nc.dram_tensor("yaF", [512, S], BF16, kind=okind).ap()
    ybF = nc.dram_tensor("ybF", [512, S], BF16, kind=okind).ap()
    x1s = nc.dram_tensor("x1s", [S, DM], F32, kind=okind).ap()
    h2F = nc.dram_tensor("h2F", [DM, S], BF16, kind=okind).ap()

    c = Ctx(nc)
    PSB = [(nc.alloc_psum_tensor("psb%d" % i, [128, 512], F32), Buf("psb%d" % i)) for i in range(8)]
    pst = {"i": 0, "n": 8, "off": 0}

    def ps():
        i = pst["off"] + pst["i"] % pst["n"]
        pst["i"] += 1
        return PSB[i]

    rr = {"i": 0}

    def ev():
        rr["i"] += 1
        return "act" if rr["i"] % 2 else "dve"

    def bcast(ap, shape, axis):
        return ap.unsqueeze(axis).to_broadcast(list(shape))

    with ExitStack() as es:
        def T(name, shape, dt=F32):
            return es.enter_context(nc.sbuf_tensor("s_" + name, list(shape), dt)), Buf(name)
        w, bw = T("w1", [128, 8, 5408], BF16)
        g1, bg1 = T("g1", [128, 8]); muf, bmuf = T("muf", [128, 11])
        idb, bidb = T("idb1", [128, 128], BF16)
        c.dma("sp", g1[:], n1g, wr=[bg1]); c.dma("sp", muf[:], mu_fm, wr=[bmuf])
        c.dma("pool", idb[:], ident_d, wr=[bidb])
        wb = [Buf("w1_%d" % k) for k in range(8)]
        for kc in range(8):
            c.dma("pool", w[:, kc, :], w_in[kc * 128:(kc + 1) * 128, :], wr=[wb[kc]])
            c.op("dve", lambda e, kc=kc: e.tensor_scalar(w[:, kc, :], w[:, kc, :], g1[:, kc:kc + 1], None, ALU.mult),
                 rd=[wb[kc], bg1], wr=[wb[kc]])
        ss, bss = T("ss1", [128, NT]); c.op("dve", lambda e: e.memset(ss[:], 0.0), wr=[bss])
        rs, brs = T("rs1", [128, NT])
        junk, bjunk = T("junk1", [128, DM])
        xts = [T("xt%d" % i, [128, DM]) for i in range(2)]
        hTs = [es.enter_context(nc.sbuf_tensor("hT%d" % i, [128, 8, 512], BF16)) for i in range(2)]
        hTb = [[Buf("hT%d_%d" % (i, s)) for s in range(4)] for i in range(2)]
        stgs = [T("stg%d" % i, [128, TMC]) for i in range(2)]
        zsb = [T("zsb%d" % j, [128, 513]) for j in range(11)]
        for j in range(11):
            c.op("pool", lambda e, j=j: e.memset(zsb[j][0][:, 0:1], 0.0), wr=[zsb[j][1]])
        tds = [T("td%d" % i, [128, 512]) for i in range(2)]
        ostg = [T("ostg%d" % i, [128, 512]) for i in range(3)]
        no = {"i": 0}
        NSTv = int(os.environ.get('NST', '8'))
        hbs = [T("hb1_%d" % i, [128, DM], BF16) for i in range(2)]

        def s1(tt):
            st, sub = tt // 4, tt % 4
            hT = hTs[st % 2]
            xt, bxt = xts[tt % 2]
            hb, bhb = hbs[tt % 2]
            c.dma("sp", xt[:], x[tt * 128:(tt + 1) * 128, :], wr=[bxt])
            c.op("act", lambda e: e.activation(junk[:], xt[:], AF.Square, accum_out=ss[:, tt:tt + 1]), rd=[bxt], wr=[bjunk, bss])
            c.op("dve", lambda e: e.tensor_scalar(rs[:, tt:tt + 1], ss[:, tt:tt + 1], 1.0 / DM, 1e-6, ALU.mult, ALU.add), rd=[bss], wr=[brs])
            c.op("act", lambda e: e.activation(rs[:, tt:tt + 1], rs[:, tt:tt + 1], AF.Ln), rd=[brs], wr=[brs])
            c.op("act", lambda e: e.activation(rs[:, tt:tt + 1], rs[:, tt:tt + 1], AF.Exp, scale=-0.5), rd=[brs], wr=[brs])
            c.op("dve", lambda e: e.tensor_scalar(hb[:], xt[:], rs[:, tt:tt + 1], None, ALU.mult), rd=[bxt, brs], wr=[bhb])
            p, bp = ps(); pb = p[:].bitcast(BF16)
            c.group("pe", [lambda e, k=k: e.transpose(pb[:, k * 128:(k + 1) * 128], hb[:, k * 128:(k + 1) * 128], idb[:]) for k in range(8)],
                    rd=[bhb, bidb], wr=[bp])
            c.op("act", lambda e: e.copy(hT[:, :, sub * 128:(sub + 1) * 128], pb.rearrange("p (k t) -> p k t", k=8)), rd=[bp], wr=[hTb[st % 2][sub]])

        def s2(tt):
            st, sub = tt // 4, tt % 4
            hT = hTs[st % 2]
            stg, bstg = stgs[tt % 2]
            for gi in range(8):
                p, bp = ps()
                c.group("pe", [lambda e, k=k, p=p: e.matmul(p[:], hT[:, k, sub * 128:(sub + 1) * 128], w[:, k, FMC + gi * 512:FMC + (gi + 1) * 512],
                                                           start=(k == 0), stop=(k == 7)) for k in range(8)],
                        rd=[hTb[st % 2][sub]] + wb, wr=[bp])
                en = ev()
                if en == "act":
                    c.op("act", lambda e, p=p: e.copy(stg[:, gi * 512:(gi + 1) * 512], p[:]), rd=[bp], wr=[bstg])
                else:
                    c.op("dve", lambda e, p=p: e.tensor_copy(stg[:, gi * 512:(gi + 1) * 512], p[:]), rd=[bp], wr=[bstg])
            c.dma("pool", zT[tt * 128:(tt + 1) * 128, :], stg[:], rd=[bstg])

        def fm(st):
            hT = hTs[st % 2]
            for j in range(11):
                ncol = 32 if j == 10 else 128
                z, bz = zsb[j]
                p, bp = ps()
                c.group("pe", [lambda e, k=k, p=p: e.matmul(p[0:ncol, :], w[:, k, j * 128:j * 128 + ncol], hT[:, k, :], start=(k == 0), stop=(k == 7)) for k in range(8)],
                        rd=hTb[st % 2] + wb, wr=[bp])
                c.op("act", lambda e, p=p: e.copy(z[0:ncol, 1:513], p[0:ncol, :]), rd=[bp], wr=[bz])
                td, btd = tds[j % 2]
                c.op("dve", lambda e: e.tensor_tensor(td[0:ncol, :], z[0:ncol, 0:512], z[0:ncol, 1:513], ALU.subtract), rd=[bz], wr=[btd])
                o, bo = ostg[no["i"] % 3]; no["i"] += 1
                c.op("dve", lambda e: e.scalar_tensor_tensor(o[0:ncol, :], td[0:ncol, :], muf[0:ncol, j:j + 1], z[0:ncol, 1:513], ALU.mult, ALU.add),
                     rd=[btd, bz, bmuf], wr=[bo])
                c.op("act", lambda e: e.copy(z[0:ncol, 0:1], z[0:ncol, 512:513]), rd=[bz], wr=[bz])
                c.dma("pool", zF[j * 128:j * 128 + ncol, st * 512:(st + 1) * 512], o[0:ncol, :], rd=[bo])

        ntl = NSTv * 4
        if ntl > 0:
            s1(0)
        for tt in range(ntl):
            if tt + 1 < ntl:
                s1(tt + 1)
            s2(tt)
            if tt % 4 == 3:
                fm(tt // 4)
        c.barrier()

    if upto <= 1:
        print("ninst", c.ninst, "nwait", c.nwait); return nc
    with ExitStack() as es:
        def T(name, shape, dt=F32):
            return es.enter_context(nc.sbuf_tensor("s_" + name, list(shape), dt)), Buf(name)
        idb, bidb = T("idb2", [128, 128], BF16); c.dma("pool", idb[:], ident_d, wr=[bidb])
        wdec, bwdec = T("wdec", [65, 512], BF16); c.dma("pool", wdec[:], wdec_d, wr=[bwdec])
        waaa, bwaaa = T("waaa", [65, 512], BF16); c.dma("pool", waaa[:], waaa_d, wr=[bwaaa])
        wgate, bwgate = T("wgate", [128, 2, 512], BF16)
        c.dma("pool", wgate[:, 0, :], wgate_d[0:128, :], wr=[bwgate]); c.dma("pool", wgate[0:32, 1, :], wgate_d[128:160, :], wr=[bwgate])
        kkf, bkkf = T("kkf", [128, 4]); c.dma("sp", kkf[:], kk_d, wr=[bkkf])
        kaf, bkaf = T("kaf", [128, 4]); c.dma("sp", kaf[:], ka_d, wr=[bkaf])
        c0f, bc0f = T("c0f", [128, 4])
        c.op("dve", lambda e: e.tensor_scalar(c0f[:], kaf[:], -1.0, 1.0, ALU.mult, ALU.add), rd=[bkaf], wr=[bc0f])
        rkb, brkb = T("rkb", [128, 8]); c.dma("sp", rkb[:], rkb_d, wr=[brkb])
        lng, blng = T("lng", [128, 512]); c.dma("sp", lng[:], lng_d.partition_broadcast(128), wr=[blng])
        lnb, blnb = T("lnb", [128, 512]); c.dma("sp", lnb[:], lnb_d.partition_broadcast(128), wr=[blnb])
        muv, bmuv = T("muv", [128, 512]); c.dma("sp", muv[:], mu_v.partition_broadcast(128), wr=[bmuv])
        MU2, bMU2 = T("MU2", [128, 512]); c.dma("sp", MU2[:], mu2_d, wr=[bMU2])
        SL2, bSL2 = T("SL2", [128, 256]); c.dma("sp", SL2[:], sl2_d, wr=[bSL2])
        bones, bbones = T("bones", [128, 128], BF16); c.dma("pool", bones[:], bones_d, wr=[bbones])
        S32, bS32 = T("S32", [128, 4, 64]); c.op("dve", lambda e: e.memset(S32[:], 0.0), wr=[bS32])
        Sb, bSb = T("Sb", [128, 4, 64], BF16); c.op("dve", lambda e: e.memset(Sb[:], 0.0), wr=[bSb])
        tha = [T("tha%d" % i, [65, 128], BF16) for i in range(2)]
        xaa = [T("xaa%d" % i, [65, 128], BF16) for i in range(2)]
        for i in range(2):
            c.op("dve", lambda e, i=i: e.memset(tha[i][0][:], 1.0), wr=[tha[i][1]])
            c.op("dve", lambda e, i=i: e.memset(xaa[i][0][:], 1.0), wr=[xaa[i][1]])
        rFs = [T("rF%d" % i, [128, 4, 128]) for i in range(2)]
        kFs = [T("kF%d" % i, [128, 4, 128]) for i in range(2)]
        xws = [T("xw%d" % i, [64, 128]) for i in range(2)]
        xas = [T("xa%d" % i, [64, 128]) for i in range(2)]
        xg0s = [T("xg0%d" % i, [128, 128]) for i in range(2)]
        xg1s = [T("xg1%d" % i, [32, 128]) for i in range(2)]
        vTs = [T("vT%d" % i, [128, 512]) for i in range(2)]
        vPs = [T("vP%d" % i, [128, 512]) for i in range(2)]
        for i in range(2):
            c.op("dve", lambda e, i=i: e.memset(vPs[i][0][:], 0.0), wr=[vPs[i][1]])
        v32s = [T("v32%d" % i, [128, 512]) for i in range(3)]; vbs = [T("vb%d" % i, [128, 512], BF16) for i in range(3)]; vtmp, bvtmp = T("vtmp", [128, 512])
        sg0, bsg0 = T("sg0", [128, 128], BF16); sg1, bsg1 = T("sg1", [32, 128], BF16)
        tg, btg = T("tg", [128, 512]); sgt, bsgt = T("sgt", [128, 128])
        logw, blogw = T("logw", [128, 4, 128]); lgi, blgi = T("lgi", [128, 4, 128]); lge, blge = T("lge", [128, 4, 128])
        alr, balr = T("alr", [128, 4, 128]); gTs = [T("gT%d" % i, [128, 512]) for i in range(3)]
        kkr, bkkr = T("kkr", [128, 4, 128]); sqb, bsqb = T("sqb", [128, 512], BF16); rn, brn = T("rn", [128, 512])
        kkn, bkkn = T("kkn", [128, 4, 128]); fF, bfF = T("fF", [128, 4, 128]); kM, bkM = T("kM", [128, 4, 128])
        gins = [T("gin%d" % i, [128, 4, 128]) for i in range(3)]; ginv, bginv = T("ginv", [128, 4, 128]); gex, bgex = T("gex", [128, 4, 128])
        ARs = [T("ARZ%d" % i, [128, 8, 2, 128], BF16) for i in range(3)]; bF, bbF = T("bF", [128, 4, 128])
        Bt, bBt = T("BtZ", [128, 8, 128], BF16); Kt, bKt = T("KtZ", [128, 8, 128], BF16)
        for (t_, b_) in (ARs[0], ARs[1], ARs[2], (Bt, bBt), (Kt, bKt)):
            c.op("pool", lambda e, t_=t_: e.memset(t_[:], 0.0), wr=[b_])
        Dd, bDd = T("Dd", [128, 4, 128]); Bh, bBh = T("Bh", [128, 4, 128], BF16); Kh, bKh = T("Kh", [128, 4, 128], BF16)
        BKhTs = [T("BKhT%d" % i, [128, 1024], BF16) for i in range(3)]
        rk, brk = T("rk", [128, 4, 128]); coefs = [T("coef%d" % i, [128, 8]) for i in range(3)]
        MABs = [T("MAB%d" % i, [128, 8, 2, 128], BF16) for i in range(3)]; MAKs = [T("MAK%d" % i, [128, 8, 2, 128], BF16) for i in range(3)]; MABTs = [T("MABT%d" % i, [128, 8, 128], BF16) for i in range(3)]
        Pk = [T("Pk%d" % i, [128, 8, 128], BF16) for i in range(2)]
        PTk = [T("PTk%d" % i, [128, 8, 128], BF16) for i in range(2)]
        ACk = [T("ACk%d" % i, [128, 8, 128], BF16) for i in range(2)]
        MinvS = [T("Minv%d" % i, [128, 8, 128], BF16) for i in range(2)]
        gMAB = [[Buf("gMAB%d_%d" % (i, g)) for g in range(2)] for i in range(3)]; gMAK = [[Buf("gMAK%d_%d" % (i, g)) for g in range(2)] for i in range(3)]; gMABT = [[Buf("gMABT%d_%d" % (i, g)) for g in range(2)] for i in range(3)]
        gPk = [[Buf("gPk%d_%d" % (i, g)) for g in range(2)] for i in range(2)]; gPTk = [[Buf("gPTk%d_%d" % (i, g)) for g in range(2)] for i in range(2)]
        gACk = [[Buf("gACk%d_%d" % (i, g)) for g in range(2)] for i in range(2)]; gMinv = [[Buf("gMinv%d_%d" % (i, g)) for g in range(2)] for i in range(2)]
        XT, bXT = T("XT", [128, 512], BF16); UT, bUT = T("UT", [128, 512], BF16)
        tmpS, btmpS = T("tmpS", [128, 4, 64])
        s1, bs1 = T("s1", [128, 8]); s2, bs2 = T("s2", [128, 8]); mean, bmean = T("mean", [128, 8]); var, bvar = T("var", [128, 8])
        sqt, bsqt = T("sqt", [128, 512]); yn, byn = T("yn", [128, 512]); bon, bbon = T("bon", [128, 512])
        yab, byab = T("yab", [128, 512], BF16); yaT, byaT = T("yaT", [128, 4, 128], BF16)

        zFr = zF[0:512, :].rearrange("(c p) t -> p c t", p=128)
        zFk = zF[512:1024, :].rearrange("(c p) t -> p c t", p=128)
        yaFv = yaF.rearrange("(c p) t -> p c t", p=128)

        def loads(ch):
            i = ch % 2; t0 = ch * 128
            c.dma("sp", rFs[i][0][:], zFr[:, :, t0:t0 + 128], wr=[rFs[i][1]])
            c.dma("sp", kFs[i][0][:], zFk[:, :, t0:t0 + 128], wr=[kFs[i][1]])
            c.dma("sp", xws[i][0][:], zF[1024:1088, t0:t0 + 128], wr=[xws[i][1]])
            c.dma("sp", xas[i][0][:], zF[1088:1152, t0:t0 + 128], wr=[xas[i][1]])
            c.dma("sp", xg0s[i][0][:], zF[1152:1280, t0:t0 + 128], wr=[xg0s[i][1]])
            c.dma("sp", xg1s[i][0][:], zF[1280:1312, t0:t0 + 128], wr=[xg1s[i][1]])
            c.dma("sp", vTs[i][0][:], zT[t0:t0 + 128, 0:512], wr=[vTs[i][1]])
            if ch == 0:
                c.dma("sp", vPs[i][0][1:128, :], zT[0:127, 0:512], wr=[vPs[i][1]])
            else:
                c.dma("sp", vPs[i][0][:], zT[t0 - 1:t0 + 127, 0:512], wr=[vPs[i][1]])

        loads(0)
        NCH = int(os.environ.get('NCH', str(NT)))
        STG = int(os.environ.get('STG', '99'))
        pqA = {"i": 0}; pqB = {"i": 0}
        pqC = {"i": 0}
        def psA():
            pqA["i"] += 1
            return PSB[pqA["i"] % 3]
        def psC():
            pqC["i"] += 1
            return PSB[3 + pqC["i"] % 3]
        def psB():
            pqB["i"] += 1
            return PSB[6 + pqB["i"] % 2]
        def stageA(ch):
            if ch + 1 < NCH:
                loads(ch + 1)
            i = ch % 2; t0 = ch * 128
            j3 = ch % 3
            v32, bv32 = v32s[j3]; vb, bvb = vbs[j3]; gT, bgT = gTs[j3]; gin, bgin = gins[j3]; AR, bAR = ARs[j3]
            BKhT, bBKhT = BKhTs[j3]; coef, bcoef = coefs[j3]; MAB, bMAB = MABs[j3]; MAK, bMAK = MAKs[j3]; MABT, bMABT = MABTs[j3]
            rF, brF = rFs[i]; kF, bkF = kFs[i]; xw, bxw = xws[i]; xa, bxa = xas[i]
            xg0, bxg0 = xg0s[i]; xg1, bxg1 = xg1s[i]; vT, bvT = vTs[i]; vP, bvP = vPs[i]
            th, bth = tha[i]; xab, bxab = xaa[i]
            c.op("pool", lambda e: e.tensor_tensor(vtmp[:], vP[:], vT[:], ALU.subtract), rd=[bvP, bvT], wr=[bvtmp])
            c.op("pool", lambda e: e.tensor_tensor(vtmp[:], vtmp[:], muv[:], ALU.mult), rd=[bvtmp, bmuv], wr=[bvtmp])
            c.op("pool", lambda e: e.tensor_tensor(v32[:], vtmp[:], vT[:], ALU.add), rd=[bvtmp, bvT], wr=[bv32])
            c.op("pool", lambda e: e.tensor_copy(vb[:], v32[:]), rd=[bv32], wr=[bvb])
            c.op("act", lambda e: e.activation(th[0:64, :], xw[:], AF.Tanh), rd=[bxw], wr=[bth])
            c.op("dve", lambda e: e.tensor_copy(xab[0:64, :], xa[:]), rd=[bxa], wr=[bxab])
            c.op("act", lambda e: e.activation(sgt[:], xg0[:], AF.Tanh, scale=0.5), rd=[bxg0], wr=[bsgt])
            c.op("dve", lambda e: e.tensor_scalar(sg0[:], sgt[:], 0.5, 0.5, ALU.mult, ALU.add), rd=[bsgt], wr=[bsg0])
            c.op("act", lambda e: e.activation(sgt[0:32, :], xg1[:], AF.Tanh, scale=0.5), rd=[bxg1], wr=[bsgt])
            c.op("dve", lambda e: e.tensor_scalar(sg1[:], sgt[0:32, :], 0.5, 0.5, ALU.mult, ALU.add), rd=[bsgt], wr=[bsg1])
            p, bp = psA()
            c.group("pe", [lambda e, q=q, p=p: e.matmul(p[:, q * 128:(q + 1) * 128], wdec[:, q * 128:(q + 1) * 128], th[:], start=True, stop=True) for q in range(4)],
                    rd=[bwdec, bth], wr=[bp])
            c.op("act", lambda e, p=p: e.activation(tg[:], p[:], AF.Tanh, scale=0.5), rd=[bp], wr=[btg])
            c.op("dve", lambda e: e.tensor_scalar(logw[:].rearrange("p a b -> p (a b)"), tg[:], -0.5 * math.exp(-0.5), -0.5 * math.exp(-0.5), ALU.mult, ALU.add),
                 rd=[btg], wr=[blogw])
            for q in range(4):
                c.op("dve", lambda e, q=q: e.tensor_tensor_scan(lgi[:, q, :], logw[:, q, :], logw[:, q, :], 0.0, ALU.add, ALU.bypass), rd=[blogw], wr=[blgi])
            c.op("pool", lambda e: e.tensor_tensor(lge[:], lgi[:], logw[:], ALU.subtract), rd=[blgi, blogw], wr=[blge])
            yield
            p, bp = psA()
            c.group("pe", [lambda e, q=q, p=p: e.matmul(p[:, q * 128:(q + 1) * 128], waaa[:, q * 128:(q + 1) * 128], xab[:], start=True, stop=True) for q in range(4)],
                    rd=[bwaaa, bxab], wr=[bp])
            c.op("act", lambda e, p=p: e.activation(tg[:], p[:], AF.Tanh, scale=0.5), rd=[bp], wr=[btg])
            c.op("dve", lambda e: e.tensor_scalar(alr[:].rearrange("p a b -> p (a b)"), tg[:], 0.5, 0.5, ALU.mult, ALU.add), rd=[btg], wr=[balr])
            p, bp = psA()
            c.group("pe", [lambda e, p=p: e.matmul(p[:], sg0[:], wgate[:, 0, :], start=True, stop=False),
                           lambda e, p=p: e.matmul(p[:], sg1[:], wgate[0:32, 1, :], start=False, stop=True)], rd=[bsg0, bsg1, bwgate], wr=[bp])
            c.op("dve", lambda e, p=p: e.tensor_copy(gT[:], p[:]), rd=[bp], wr=[bgT])
            for q in range(4):
                c.op("dve", lambda e, q=q: e.tensor_scalar(kkr[:, q, :], kF[:, q, :], kkf[:, q:q + 1], None, ALU.mult), rd=[bkF, bkkf], wr=[bkkr])
            c.op("pool", lambda e: e.tensor_tensor(sqb[:], kkr[:].rearrange("p a b -> p (a b)"), kkr[:].rearrange("p a b -> p (a b)"), ALU.mult), rd=[bkkr], wr=[bsqb])
            p, bp = psA()
            c.op("pe", lambda e, p=p: e.matmul(p[:], bones[:], sqb[:], start=True, stop=True), rd=[bbones, bsqb], wr=[bp])
            c.op("act", lambda e, p=p: e.activation(rn[:], p[:], AF.Ln), rd=[bp], wr=[brn])
            c.op("act", lambda e: e.activation(rn[:], rn[:], AF.Exp, scale=-0.5), rd=[brn], wr=[brn])
            c.op("dve", lambda e: e.tensor_tensor(kkn[:].rearrange("p a b -> p (a b)"), kkr[:].rearrange("p a b -> p (a b)"), rn[:], ALU.mult), rd=[bkkr, brn], wr=[bkkn])
            yield
            for q in range(4):
                c.op("dve", lambda e, q=q: e.tensor_scalar(fF[:, q, :], alr[:, q, :], kaf[:, q:q + 1], c0f[:, q:q + 1], ALU.mult, ALU.add), rd=[balr, bkaf, bc0f], wr=[bfF])
            c.op("pool", lambda e: e.tensor_tensor(kM[:], kF[:], fF[:], ALU.mult), rd=[bkF, bfF], wr=[bkM])
            c.op("act", lambda e: e.activation(gin[:], lgi[:], AF.Exp), rd=[blgi], wr=[bgin])
            c.op("act", lambda e: e.activation(ginv[:], lgi[:], AF.Exp, scale=-1.0), rd=[blgi], wr=[bginv])
            c.op("act", lambda e: e.activation(gex[:], lge[:], AF.Exp), rd=[blge], wr=[bgex])
            for q in range(4):
                c.op("act", lambda e, q=q: e.activation(Dd[:, q, :], lgi[:, q, :], AF.Exp, bias=lgi[:, q, 127:128], scale=-1.0), rd=[blgi], wr=[bDd])
            c.op("pool", lambda e: e.tensor_tensor(bF[:], kkn[:], alr[:], ALU.mult), rd=[bkkn, balr], wr=[bbF])
            for hh in range(2):
                r0, r1 = hh * 64, (hh + 1) * 64
                ARv = AR[r0:r1, :, :, :].rearrange("p (q two) a t -> p q two a t", two=2)[:, :, hh, :, :]
                Btv = Bt[r0:r1, :, :].rearrange("p (q two) t -> p q two t", two=2)[:, :, hh, :]
                Ktv = Kt[r0:r1, :, :].rearrange("p (q two) t -> p q two t", two=2)[:, :, hh, :]
                c.op("dve", lambda e: e.tensor_tensor(ARv[:, :, 1, :], rF[r0:r1, :, :], gin[r0:r1, :, :], ALU.mult), rd=[brF, bgin], wr=[bAR])
                c.op("dve", lambda e: e.scalar_tensor_tensor(ARv[:, :, 0, :], kkn[r0:r1, :, :], -1.0, gex[r0:r1, :, :], ALU.mult, ALU.mult), rd=[bkkn, bgex], wr=[bAR])
                c.op("dve", lambda e: e.tensor_tensor(Btv, bF[r0:r1, :, :], ginv[r0:r1, :, :], ALU.mult), rd=[bbF, bginv], wr=[bBt])
                c.op("pool", lambda e: e.tensor_tensor(Ktv, kM[r0:r1, :, :], ginv[r0:r1, :, :], ALU.mult), rd=[bkM, bginv], wr=[bKt])
            c.op("pool", lambda e: e.tensor_tensor(Bh[:], bF[:], Dd[:], ALU.mult), rd=[bbF, bDd], wr=[bBh])
            c.op("pool", lambda e: e.tensor_tensor(Kh[:], kM[:], Dd[:], ALU.mult), rd=[bkM, bDd], wr=[bKh])
            yield
            p, bp = psA(); pb = p[:].bitcast(BF16)
            c.group("pe", [lambda e, q=q: e.transpose(pb[:, q * 128:(q + 1) * 128], Bh[:, q, :], idb[:]) for q in range(4)] +
                          [lambda e, q=q: e.transpose(pb[:, 512 + q * 128:512 + (q + 1) * 128], Kh[:, q, :], idb[:]) for q in range(4)],
                    rd=[bBh, bKh, bidb], wr=[bp])
            c.op("dve", lambda e: e.tensor_copy(BKhT[:], pb), rd=[bp], wr=[bBKhT])
            c.op("pool", lambda e: e.tensor_tensor(rk[:], rF[:], kM[:], ALU.mult), rd=[brF, bkM], wr=[brk])
            p, bp = psA()
            c.group("pe", [lambda e, q=q, p=p: e.matmul(p[:, 2 * q:2 * q + 2], rk[:, q, :], rkb[:, 2 * q:2 * q + 2], start=True, stop=True) for q in range(4)],
                    rd=[brk, brkb], wr=[bp])
            c.op("dve", lambda e, p=p: e.tensor_copy(coef[:], p[:, 0:8]), rd=[bp], wr=[bcoef])
            for q in range(4):
                pA, bpA = psA(); pB, bpB = psA(); pC, bpC = psA()
                fa, fb, fc = [], [], []
                for hh in range(2):
                    h = 2 * q + hh
                    arr = AR[:, h, :, :].rearrange("p a b -> p (a b)")
                    fa.append(lambda e, hh=hh, h=h, arr=arr: e.matmul(pA[:, hh * 256:(hh + 1) * 256], Bt[:, h, :], arr, start=True, stop=True))
                    fb.append(lambda e, hh=hh, h=h, arr=arr: e.matmul(pB[:, hh * 256:(hh + 1) * 256], Kt[:, h, :], arr, start=True, stop=True))
                    fc.append(lambda e, hh=hh, h=h: e.matmul(pC[:, hh * 128:(hh + 1) * 128], AR[:, h, 0, :], Bt[:, h, :], start=True, stop=True))
                c.group("pe", fa, rd=[bBt, bAR], wr=[bpA])
                c.group("pe", fb, rd=[bKt, bAR], wr=[bpB])
                c.group("pe", fc, rd=[bBt, bAR], wr=[bpC])
                c.op("dve", lambda e: e.tensor_tensor(MAB[:, 2 * q:2 * q + 2, :, :].rearrange("p a b c -> p (a b c)"), pA[:], MU2[:], ALU.mult), rd=[bpA, bMU2], wr=[gMAB[j3][q // 2]])
                c.op("dve", lambda e: e.tensor_tensor(MAK[:, 2 * q:2 * q + 2, :, :].rearrange("p a b c -> p (a b c)"), pB[:], MU2[:], ALU.mult), rd=[bpB, bMU2], wr=[gMAK[j3][q // 2]])
                c.op("dve", lambda e: e.tensor_tensor(MABT[:, 2 * q:2 * q + 2, :].rearrange("p a b -> p (a b)"), pC[:, 0:256], SL2[:], ALU.mult), rd=[bpC, bSL2], wr=[gMABT[j3][q // 2]])
            yield

        def stageA2(ch):
            i = ch % 2
            j3 = ch % 3
            MAB, bMAB = MABs[j3]; MABT, bMABT = MABTs[j3]
            AC0, _ = ACk[0]
            for grp in range(2):
                c.op("pool", lambda e, grp=grp: e.tensor_tensor(AC0[:, 4 * grp:4 * grp + 4, :], MAB[:, 4 * grp:4 * grp + 4, 0, :], bcast(idb[:], [128, 4, 128], 1), ALU.add),
                     rd=[gMAB[j3][grp], bidb], wr=[gACk[0][grp]])
            Pp = lambda h: MAB[:, h, 0, :]
            PTp = lambda h: MABT[:, h, :]
            bPp = gMAB[j3]; bPTp = gMABT[j3]
            ACp = AC0; bACp = gACk[0]
            for lv in range(1, 7):
                Pn = Pk[lv % 2][0]; PTn = PTk[lv % 2][0]
                ACn = ACk[lv % 2][0] if lv < 6 else MinvS[i][0]
                bPn = gPk[lv % 2]; bPTn = gPTk[lv % 2]; bACn = gACk[lv % 2] if lv < 6 else gMinv[i]
                for grp in range(2):
                    hs = range(4 * grp, 4 * grp + 4)
                    gsl = slice(4 * grp, 4 * grp + 4)
                    if lv < 6:
                        p, bp = psC()
                        c.group("pe", [lambda e, h=h, p=p, Pp=Pp, PTp=PTp: e.matmul(p[:, (h % 4) * 128:(h % 4 + 1) * 128], PTp(h), Pp(h), start=True, stop=True) for h in hs],
                                rd=[bPp[grp], bPTp[grp]], wr=[bp])
                        c.op("act", lambda e, p=p: e.copy(Pn[:, gsl, :].rearrange("p a b -> p (a b)"), p[:]), rd=[bp], wr=[bPn[grp]])
                    p, bp = psC()
                    c.group("pe", [lambda e, h=h, p=p, Pp=Pp, PTp=PTp: e.matmul(p[:, (h % 4) * 128:(h % 4 + 1) * 128], Pp(h), PTp(h), start=True, stop=True) for h in hs],
                            rd=[bPp[grp], bPTp[grp]], wr=[bp])
                    c.op("act", lambda e, p=p: e.copy(PTn[:, gsl, :].rearrange("p a b -> p (a b)"), p[:]), rd=[bp], wr=[bPTn[grp]])
                for grp in range(2):
                    hs = range(4 * grp, 4 * grp + 4)
                    gsl = slice(4 * grp, 4 * grp + 4)
                    p, bp = psC()
                    fs = []
                    for h in hs:
                        fs.append(lambda e, h=h, p=p, ACp=ACp: e.matmul(p[:, (h % 4) * 128:(h % 4 + 1) * 128], idb[:], ACp[:, h, :], start=True, stop=False))
                        fs.append(lambda e, h=h, p=p, ACp=ACp: e.matmul(p[:, (h % 4) * 128:(h % 4 + 1) * 128], PTn[:, h, :], ACp[:, h, :], start=False, stop=True))
                    c.group("pe", fs, rd=[bidb, bACp[grp], bPTn[grp]], wr=[bp])
                    c.op("act", lambda e, p=p: e.copy(ACn[:, gsl, :].rearrange("p a b -> p (a b)"), p[:]), rd=[bp], wr=[bACn[grp]])
                yield
                Pp = (lambda Pn: (lambda h: Pn[:, h, :]))(Pn)
                PTp = (lambda PTn: (lambda h: PTn[:, h, :]))(PTn)
                bPp, bPTp = bPn, bPTn
                ACp, bACp = ACn, bACn
            yield

        def stageB(ch):
            i = ch % 2; t0 = ch * 128
            j3 = ch % 3
            v32, bv32 = v32s[j3]; vb, bvb = vbs[j3]; gT, bgT = gTs[j3]; gin, bgin = gins[j3]; AR, bAR = ARs[j3]
            Minv = MinvS[i][0]; bMinvL = gMinv[i]
            BKhT, bBKhT = BKhTs[j3]; coef, bcoef = coefs[j3]; MAB, bMAB = MABs[j3]; MAK, bMAK = MAKs[j3]; MABT, bMABT = MABTs[j3]
            yield
            pX, bpX = psB()
            fs = []
            for h in range(8):
                q, r0 = h // 2, (h % 2) * 64
                fs.append(lambda e, h=h: e.matmul(pX[:, h * 64:(h + 1) * 64], MAK[:, h, 0, :], vb[:, h * 64:(h + 1) * 64], start=True, stop=False))
                fs.append(lambda e, h=h, q=q: e.matmul(pX[:, h * 64:(h + 1) * 64], AR[:, h, 0, :], Sb[:, q, :], start=False, stop=True))
            c.group("pe", fs, rd=gMAK[j3] + [bvb, bAR, bSb], wr=[bpX])
            c.op("dve", lambda e: e.tensor_copy(XT[:], pX[:]), rd=[bpX], wr=[bXT])
            yield
            pU, bpU = psB()
            c.group("pe", [lambda e, h=h: e.matmul(pU[:, h * 64:(h + 1) * 64], Minv[:, h, :], XT[:, h * 64:(h + 1) * 64], start=True, stop=True) for h in range(8)],
                    rd=bMinvL + [bXT], wr=[bpU])
            c.op("dve", lambda e: e.tensor_copy(UT[:], pU[:]), rd=[bpU], wr=[bUT])
            yield
            pY, bpY = psB()
            fs = []
            for h in range(8):
                q, r0 = h // 2, (h % 2) * 64
                fs.append(lambda e, h=h: e.matmul(pY[:, h * 64:(h + 1) * 64], MAK[:, h, 1, :], vb[:, h * 64:(h + 1) * 64], start=True, stop=False))
                fs.append(lambda e, h=h: e.matmul(pY[:, h * 64:(h + 1) * 64], MAB[:, h, 1, :], UT[:, h * 64:(h + 1) * 64], start=False, stop=False))
                fs.append(lambda e, h=h, q=q: e.matmul(pY[:, h * 64:(h + 1) * 64], AR[:, h, 1, :], Sb[:, q, :], start=False, stop=True))
            c.group("pe", fs, rd=gMAK[j3] + gMAB[j3] + [bvb, bUT, bAR, bSb], wr=[bpY])
            yield
            pS, bpS = psB()
            fs = []
            for q in range(4):
                fs.append(lambda e, q=q: e.matmul(pS[:, q * 128:(q + 1) * 128], BKhT[:, q * 128:(q + 1) * 128], UT[:, q * 128:(q + 1) * 128], start=True, stop=False))
                fs.append(lambda e, q=q: e.matmul(pS[:, q * 128:(q + 1) * 128], BKhT[:, 512 + q * 128:512 + (q + 1) * 128], vb[:, q * 128:(q + 1) * 128], start=False, stop=True))
            c.group("pe", fs, rd=[bBKhT, bUT, bvb], wr=[bpS])
            pSv = pS[:].rearrange("p (q c) -> p q c", q=4)
            c.op("dve", lambda e: e.tensor_tensor(tmpS[:], S32[:], gin[:, :, 127:128].to_broadcast([128, 4, 64]), ALU.mult), rd=[bS32, bgin], wr=[btmpS])
            c.op("dve", lambda e: e.tensor_tensor(S32[0:64, :, :], tmpS[0:64, :, :], pSv[0:64, :, 0:64], ALU.add), rd=[btmpS, bpS], wr=[bS32])
            c.op("dve", lambda e: e.tensor_tensor(S32[64:128, :, :], tmpS[64:128, :, :], pSv[64:128, :, 64:128], ALU.add), rd=[btmpS, bpS], wr=[bS32])
            c.op("dve", lambda e: e.tensor_copy(Sb[:], S32[:]), rd=[bS32], wr=[bSb])
            yield
            pYv = pY[:].rearrange("p (h d) -> p h d", h=8)
            c.op("dve", lambda e: e.tensor_reduce(s1[:], pYv, AX.X, ALU.add), rd=[bpY], wr=[bs1])
            c.op("act", lambda e: e.activation(sqt[:], pY[:], AF.Square), rd=[bpY], wr=[bsqt])
            c.op("dve", lambda e: e.tensor_reduce(s2[:], sqt[:].rearrange("p (h d) -> p h d", h=8), AX.X, ALU.add), rd=[bsqt], wr=[bs2])
            c.op("dve", lambda e: e.tensor_scalar(mean[:], s1[:], 1.0 / 64, None, ALU.mult), rd=[bs1], wr=[bmean])
            c.op("dve", lambda e: e.tensor_tensor(var[:], mean[:], mean[:], ALU.mult), rd=[bmean], wr=[bvar])
            c.op("dve", lambda e: e.scalar_tensor_tensor(var[:], s2[:], 1.0 / 64, var[:], ALU.mult, ALU.subtract), rd=[bs2, bvar], wr=[bvar])
            c.op("dve", lambda e: e.tensor_scalar(var[:], var[:], 64e-5, None, ALU.add), rd=[bvar], wr=[bvar])
            c.op("act", lambda e: e.activation(var[:], var[:], AF.Ln), rd=[bvar], wr=[bvar])
            c.op("act", lambda e: e.activation(var[:], var[:], AF.Exp, scale=-0.5), rd=[bvar], wr=[bvar])
            ynv = yn[:].rearrange("p (h d) -> p h d", h=8)
            c.op("dve", lambda e: e.tensor_tensor(ynv, pYv, bcast(mean[:], [128, 8, 64], 2), ALU.subtract), rd=[bpY, bmean], wr=[byn])
            c.op("pool", lambda e: e.tensor_tensor(ynv, ynv, bcast(var[:], [128, 8, 64], 2), ALU.mult), rd=[byn, bvar], wr=[byn])
            c.op("pool", lambda e: e.tensor_tensor(yn[:], yn[:], lng[:], ALU.mult), rd=[byn, blng], wr=[byn])
            c.op("dve", lambda e: e.tensor_tensor(yn[:], yn[:], lnb[:], ALU.add), rd=[byn, blnb], wr=[byn])
            c.op("pool", lambda e: e.tensor_tensor(bon[:].rearrange("p (h d) -> p h d", h=8), v32[:].rearrange("p (h d) -> p h d", h=8), bcast(coef[:], [128, 8, 64], 2), ALU.mult),
                 rd=[bv32, bcoef], wr=[bbon])
            c.op("dve", lambda e: e.tensor_tensor(yn[:], yn[:], bon[:], ALU.add), rd=[byn, bbon], wr=[byn])
            c.op("dve", lambda e: e.tensor_tensor(yab[:], yn[:], gT[:], ALU.mult), rd=[byn, bgT], wr=[byab])
            p, bp = psB(); pb = p[:].bitcast(BF16)
            c.group("pe", [lambda e, q=q: e.transpose(pb[:, q * 128:(q + 1) * 128], yab[:, q * 128:(q + 1) * 128], idb[:]) for q in range(4)], rd=[byab, bidb], wr=[bp])
            c.op("act", lambda e: e.copy(yaT[:].rearrange("p a b -> p (a b)"), pb[:, 0:512]), rd=[bp], wr=[byaT])
            c.dma("act", yaFv[:, :, t0:t0 + 128], yaT[:], rd=[byaT])
        RUNW = [int(v) for v in os.environ.get('RUNW', '1,1,1').split(',')]
        def run(gens):
            alive = [(g, RUNW[k % 3]) for k, g in enumerate(gens) if g is not None]
            while alive:
                for (g, wgt) in list(alive):
                    for _ in range(wgt):
                        try:
                            next(g)
                        except StopIteration:
                            alive.remove((g, wgt))
                            break
        run([stageA(0)])
        run([stageA2(0), stageA(1) if NCH > 1 else None])
        for ch in range(NCH):
            run([stageB(ch), stageA2(ch + 1) if ch + 1 < NCH else None, stageA(ch + 2) if ch + 2 < NCH else None])
        c.barrier()

    if upto <= 2:
        print("ninst", c.ninst, "nwait", c.nwait); return nc
    es_w = ExitStack()
    def TW(name, shape, dt=F32):
        return es_w.enter_context(nc.sbuf_tensor("s_" + name, list(shape), dt)), Buf(name)
    with ExitStack() as es:
        def T(name, shape, dt=F32):
            return es.enter_context(nc.sbuf_tensor("s_" + name, list(shape), dt)), Buf(name)
        pst["n"] = 2; pst["i"] = 0; pst["off"] = 4
        pO = [PSB[6], PSB[7]]
        qst = {"i": 0}
        def psq():
            qst["i"] += 1
            return PSB[qst["i"] % 4]
        idb, bidb = T("idb3", [128, 128], BF16); c.dma("pool", idb[:], ident_d, wr=[bidb])
        CB, bCB = T("CB", [128, 2, 256], BF16); c.dma("pool", CB[:].rearrange("p a b -> p (a b)"), cbias2_d, wr=[bCB])
        s65, bs65 = T("s65", [65, 64], BF16); c.dma("pool", s65[:], sel65_d, wr=[bs65])
        pok, bpok = T("pok", [128, 16, 16]); c.dma("sp", pok[:].rearrange("p a b -> p (a b)"), pok_d.partition_broadcast(128), wr=[bpok])
        pbi, bpbi = T("pbi", [128, 16, 16]); c.dma("sp", pbi[:].rearrange("p a b -> p (a b)"), pbias_d.partition_broadcast(128), wr=[bpbi])
        invf, binvf = T("invf", [128, 8]); c.dma("sp", invf[:], invf_d.partition_broadcast(128), wr=[binvf])
        qkg, bqkg = T("qkg", [128, 1024]); c.dma("sp", qkg[:], qkg_d.partition_broadcast(128), wr=[bqkg])
        posi, bposi = T("posi", [128, NT], I32); c.dma("sp", posi[:], pos, wr=[bposi])
        posf, bposf = T("posf", [128, NT])
        c.op("dve", lambda e: e.tensor_copy(posf[:], posi[:]), rd=[bposi], wr=[bposf])
        yy, byy = T("yy", [128, NT, 8]); yi, byi = T("yi", [128, NT, 8], I32); yf, byf = T("yf", [128, NT, 8])
        sinT, bsinT = T("sinT", [128, NT, 8]); cosT, bcosT = T("cosT", [128, NT, 8])
        c.op("dve", lambda e: e.tensor_copy(yy[:], bcast(invf[:], [128, NT, 8], 1)), rd=[binvf], wr=[byy])
        c.op("dve", lambda e: e.tensor_tensor(yy[:], yy[:], bcast(posf[:], [128, NT, 8], 2), ALU.mult), rd=[byy, bposf], wr=[byy])
        for (dst, bdst, off) in ((sinT, bsinT, 0.0), (cosT, bcosT, 0.25)):
            if off != 0.0:
                c.op("dve", lambda e: e.tensor_scalar(yy[:], yy[:], off, None, ALU.add), rd=[byy], wr=[byy])
            c.op("dve", lambda e: e.tensor_copy(yi[:], yy[:]), rd=[byy], wr=[byi])
            c.op("dve", lambda e: e.tensor_copy(yf[:], yi[:]), rd=[byi], wr=[byf])
            c.op("dve", lambda e: e.tensor_tensor(yf[:], yy[:], yf[:], ALU.subtract), rd=[byy, byf], wr=[byf])
            c.op("act", lambda e, dst=dst: e.activation(dst[:], yf[:], AF.Sin, scale=2.0 * math.pi), rd=[byf], wr=[bdst])
        KT = es.enter_context(nc.sbuf_tensor("s_KT", [80, 8, S], BF16)); bKT = [Buf("KT%d" % g) for g in range(8)]
        VA = es.enter_context(nc.sbuf_tensor("s_VA", [128, NT, 8, 65], BF16)); bVA = [Buf("VA%d" % g) for g in range(8)]
        c.op("pool", lambda e: e.memset(VA[:], 1.0), wr=bVA)
        kmT, bkmT = T("kmT", [64, 8, 16], BF16); c.op("dve", lambda e: e.memset(kmT[:], 0.0), wr=[bkmT])
        km32, bkm32 = T("km32", [64, 8])
        qkvs = [T("qkv%d" % i, [128, 1536]) for i in range(2)]
        sq3, bsq3 = T("sq3", [128, 1024]); ss3, bss3 = T("ss3", [128, 16]); qn, bqn = T("qn", [128, 16, 64])
        rt, brt = T("rt", [128, 4, 16, 8])
        QA, bQA = T("QA", [128, 8, 80], BF16); KA, bKA = T("KA", [128, 8, 80], BF16)
        QT, bQT = T("QT", [64, 8, 128], BF16)
        QTAs = [T("QTA%d" % i, [80, 8, 512], BF16) for i in range(2)]
        gm, bgm = T("gm", [128, 8, 16]); mx, bmx = T("mx", [128, 8, 8]); sel, bsel = T("sel", [128, 8, 16])
        PTs = [T("PT%d" % i, [128, 512], BF16) for i in range(4)]
        OT, bOT = T("OT", [65, 512]); rr, brr = T("rr", [65, 512], BF16); r32, br32 = T("r32", [65, 512])
        c.op("dve", lambda e: e.memset(rr[:], 0.0), wr=[brr])
        yTs = [T("yT%d" % i, [64, 512], BF16) for i in range(2)]
        cnt3 = {"pt": 0, "ld": 0}

        def ld3(tt):
            c.dma("act", qkvs[tt % 2][0][:], zT[tt * 128:(tt + 1) * 128, 512:2048], wr=[qkvs[tt % 2][1]])

        def pre3(tt):
            if tt + 1 < NT:
                ld3(tt + 1)
            qkv, bqkv = qkvs[tt % 2]
            t0 = tt * 128; qb = tt // 2; g = tt // 4; ti = tt % 4
            QTA, bQTA = QTAs[g % 2]
            c.op("act", lambda e: e.activation(sq3[:], qkv[:, 0:1024], AF.Square), rd=[bqkv], wr=[bsq3])
            c.op("dve", lambda e: e.tensor_reduce(ss3[:], sq3[:].rearrange("p (h d) -> p h d", h=16), AX.X, ALU.add), rd=[bsq3], wr=[bss3])
            c.op("dve", lambda e: e.tensor_scalar(ss3[:], ss3[:], 1.0 / 64, 1e-6, ALU.mult, ALU.add), rd=[bss3], wr=[bss3])
            c.op("act", lambda e: e.activation(ss3[:], ss3[:], AF.Ln), rd=[bss3], wr=[bss3])
            c.op("act", lambda e: e.activation(ss3[:], ss3[:], AF.Exp, scale=-0.5), rd=[bss3], wr=[bss3])
            c.op("dve", lambda e: e.tensor_tensor(qn[:], qkv[:, 0:1024].rearrange("p (h d) -> p h d", h=16), bcast(ss3[:], [128, 16, 64], 2), ALU.mult), rd=[bqkv, bss3], wr=[bqn])
            c.op("pool", lambda e: e.tensor_tensor(qn[:].rearrange("p h d -> p (h d)"), qn[:].rearrange("p h d -> p (h d)"), qkg[:], ALU.mult), rd=[bqn, bqkg], wr=[bqn])
            cs = cosT[:, tt:tt + 1, :].to_broadcast([128, 16, 8]); sn = sinT[:, tt:tt + 1, :].to_broadcast([128, 16, 8])
            c.op("dve", lambda e: e.tensor_tensor(rt[:, 0, :, :], qn[:, :, 0:8], cs, ALU.mult), rd=[bqn, bcosT], wr=[brt])
            c.op("dve", lambda e: e.tensor_tensor(rt[:, 1, :, :], qn[:, :, 8:16], sn, ALU.mult), rd=[bqn, bsinT], wr=[brt])
            c.op("pool", lambda e: e.tensor_tensor(rt[:, 2, :, :], qn[:, :, 8:16], cs, ALU.mult), rd=[bqn, bcosT], wr=[brt])
            c.op("pool", lambda e: e.tensor_tensor(rt[:, 3, :, :], qn[:, :, 0:8], sn, ALU.mult), rd=[bqn, bsinT], wr=[brt])
            c.op("dve", lambda e: e.tensor_tensor(qn[:, :, 0:8], rt[:, 0, :, :], rt[:, 1, :, :], ALU.subtract), rd=[brt], wr=[bqn])
            c.op("dve", lambda e: e.tensor_tensor(qn[:, :, 8:16], rt[:, 2, :, :], rt[:, 3, :, :], ALU.add), rd=[brt], wr=[bqn])
            c.op("dve", lambda e: e.tensor_copy(QA[:, :, 0:64], qn[:, 0:8, :]), rd=[bqn], wr=[bQA])
            c.op("pool", lambda e: e.tensor_copy(KA[:, :, 0:64], qn[:, 8:16, :]), rd=[bqn], wr=[bKA])
            c.op("pool", lambda e: e.memset(KA[:, :, 64:80], 0.0), wr=[bKA])
            c.op("pool", lambda e: e.memset(KA[:, :, 64 + qb:65 + qb], 1.0), wr=[bKA])
            yield
            p, bp = ps(); pb = p[:].bitcast(BF16)
            c.group("pe", [lambda e, h=h: e.transpose(pb[0:80, h * 128:(h + 1) * 128], KA[:, h, :], idb[:]) for h in range(8)], rd=[bKA, bidb], wr=[bp])
            c.op("dve", lambda e: e.tensor_copy(KT[:, :, t0:t0 + 128], pb[0:80, :].rearrange("p (h t) -> p h t", h=8)), rd=[bp], wr=[bKT[g]])
            c.op("pool", lambda e: e.tensor_copy(VA[:, tt, :, 0:64], qkv[:, 1024:1536].rearrange("p (h d) -> p h d", h=8)), rd=[bqkv], wr=[bVA[g]])
            if qb > 0:
                p, bp = ps(); pb = p[:].bitcast(BF16)
                c.group("pe", [lambda e, h=h: e.transpose(pb[0:64, h * 128:(h + 1) * 128], QA[:, h, 0:64], idb[:]) for h in range(8)], rd=[bQA, bidb], wr=[bp])
                c.op("dve", lambda e: e.tensor_copy(QT[:], pb[0:64, :].rearrange("p (h t) -> p h t", h=8)), rd=[bp], wr=[bQT])
                yield
                p, bp = ps()
                c.group("pe", [lambda e, h=h, p=p: e.matmul(p[:, h * 16:(h + 1) * 16], QT[:, h, :], kmT[:, h, :], start=True, stop=True) for h in range(8)],
                        rd=[bQT, bkmT], wr=[bp])
                c.op("dve", lambda e, p=p: e.tensor_tensor(gm[:], p[:, 0:128].rearrange("p (h j) -> p h j", h=8), pbi[:, qb:qb + 1, :].to_broadcast([128, 8, 16]), ALU.add),
                     rd=[bp, bpbi], wr=[bgm])
                for h in range(8):
                    c.op("dve", lambda e, h=h: e.max(mx[:, h, :], gm[:, h, :]), rd=[bgm], wr=[bmx])
                c.op("dve", lambda e: e.tensor_tensor(sel[:], gm[:], mx[:, :, 2:3].to_broadcast([128, 8, 16]), ALU.is_ge), rd=[bgm, bmx], wr=[bsel])
                c.op("dve", lambda e: e.tensor_tensor(sel[:], sel[:], pok[:, qb:qb + 1, :].to_broadcast([128, 8, 16]), ALU.mult), rd=[bsel, bpok], wr=[bsel])
                c.op("dve", lambda e: e.tensor_scalar(QA[:, :, 64:80], sel[:], BIG, -BIG, ALU.mult, ALU.add), rd=[bsel], wr=[bQA])
            else:
                c.op("dve", lambda e: e.memset(QA[:, :, 64:80], -BIG), wr=[bQA])
            yield
            p, bp = ps(); pb = p[:].bitcast(BF16)
            c.group("pe", [lambda e, h=h: e.transpose(pb[0:80, h * 128:(h + 1) * 128], QA[:, h, :], idb[:]) for h in range(8)], rd=[bQA, bidb], wr=[bp])
            c.op("dve", lambda e: e.tensor_copy(QTA[:, :, ti * 128:(ti + 1) * 128], pb[0:80, :].rearrange("p (h t) -> p h t", h=8)), rd=[bp], wr=[bQTA])
            if tt % 2 == 1:
                c.op("dve", lambda e: e.tensor_reduce(km32[:], KT[0:64, :, qb * 256:(qb + 1) * 256], AX.X, ALU.add), rd=[bKT[g]], wr=[bkm32])
                c.op("dve", lambda e: e.tensor_scalar(kmT[:, :, qb], km32[:], 1.0 / 256, None, ALU.mult), rd=[bkm32], wr=[bkmT])

        s65f, bs65f = T("s65f", [65, 64]); c.dma("sp", s65f[:], sel65_d, wr=[bs65f])

        def steps3(g, h):
            QTA, bQTA = QTAs[g % 2]
            pOh, bOh = pO[h % 2]
            kbufs = bKT[0:g + 1]; vbufs = bVA[0:g + 1]
            st = []
            nk = 4 * g + 2
            for kt in range(nk):
                d = {}
                def qk(d=d, kt=kt):
                    d["p"], d["bp"] = psq()
                    c.op("pe", lambda e: e.matmul(d["p"][:], KT[:, h, kt * 128:(kt + 1) * 128], QTA[:, h, :], start=True, stop=True), rd=kbufs + [bQTA], wr=[d["bp"]])
                def ex(d=d):
                    d["PT"], d["bPT"] = PTs[cnt3["pt"] % 4]; cnt3["pt"] += 1
                    c.op("act", lambda e: e.activation(d["PT"][:], d["p"][:], AF.Exp, scale=0.125), rd=[d["bp"]], wr=[d["bPT"]])
                def pv(d=d, kt=kt):
                    c.op("pe", lambda e: e.matmul(pOh[0:65, :], VA[:, kt, h, :], d["PT"][:], start=(kt == 0), stop=False), rd=vbufs + [d["bPT"]], wr=[bOh])
                st.append((qk, ex, pv))
            for half in range(2):
                for kti in range(2):
                    kt = 4 * g + 2 * half + kti
                    qs = slice(half * 256, (half + 1) * 256)
                    last = (half == 1 and kti == 1)
                    d = {}
                    def qk(d=d, kt=kt, qs=qs, kti=kti):
                        d["p"], d["bp"] = psq()
                        c.group("pe", [lambda e: e.matmul(d["p"][:, 0:256], KT[0:64, h, kt * 128:(kt + 1) * 128], QTA[0:64, h, qs], start=True, stop=False),
                                       lambda e: e.matmul(d["p"][:, 0:256], idb[:], CB[:, kti, :], start=False, stop=True)], rd=kbufs + [bQTA, bidb, bCB], wr=[d["bp"]])
                    def ex(d=d):
                        d["PT"], d["bPT"] = PTs[cnt3["pt"] % 4]; cnt3["pt"] += 1
                        c.op("act", lambda e: e.activation(d["PT"][:, 0:256], d["p"][:, 0:256], AF.Exp, scale=0.125), rd=[d["bp"]], wr=[d["bPT"]])
                    def pv(d=d, kt=kt, qs=qs, last=last):
                        c.op("pe", lambda e: e.matmul(pOh[0:65, qs], VA[:, kt, h, :], d["PT"][:, 0:256], start=False, stop=last), rd=vbufs + [d["bPT"]], wr=[bOh])
                    st.append((qk, ex, pv))

            def fin():
                c.op("dve", lambda e: e.tensor_copy(OT[:], pOh[0:65, :]), rd=[bOh], wr=[bOT])
                p, bp = ps()
                c.op("pe", lambda e: e.matmul(p[0:64, :], s65f[:], OT[:], start=True, stop=True), rd=[bs65f, bOT], wr=[bp])
                yT, byT = yTs[h % 2]
                c.op("dve", lambda e: e.reciprocal(r32[0:64, :], p[0:64, :]), rd=[bp], wr=[br32])
                c.op("dve", lambda e: e.tensor_tensor(yT[:], OT[0:64, :], r32[0:64, :], ALU.mult), rd=[bOT, br32], wr=[byT])
                c.dma("sp", ybF[h * 64:(h + 1) * 64, g * 512:(g + 1) * 512], yT[:], rd=[byT])
            return st, fin

        ld3(0)
        for tt in range(4):
            for _ in pre3(tt):
                pass
        LOOK = int(os.environ.get('LOOK', '3'))
        for g in range(8):
            allst = []
            for h in range(8):
                st, fin = steps3(g, h)
                for i, s in enumerate(st):
                    allst.append((s, fin if i == len(st) - 1 else None, h))
            n = len(allst)
            pend = [pre3(4 * (g + 1) + i) for i in range(4)] if g + 1 < 8 else []
            every = max(1, n // 18)

            def advance():
                while pend:
                    try:
                        next(pend[0])
                        return
                    except StopIteration:
                        pend.pop(0)
            for i in range(min(LOOK, n)):
                allst[i][0][0]()
            for i in range(n):
                (qk, ex, pv), fin, h = allst[i]
                ex()
                if i + LOOK < n:
                    allst[i + LOOK][0][0]()
                pv()
                if fin is not None:
                    fin()
                if i % every == every - 1:
                    advance()
            while pend:
                advance()
        pst["n"] = 8; pst["off"] = 0
        c.barrier()

    if upto <= 3:
        print("ninst", c.ninst, "nwait", c.nwait); return nc
    wup, bwup = TW("wup", [128, 8, 2 * DFF], BF16)
    g2, bg2 = TW("g2", [128, 8]); c.dma("sp", g2[:], n2g, wr=[bg2])
    wub = [Buf("wup_%d" % k) for k in range(8)]
    for k in range(8):
        c.dma("pool", wup[:, k, :], wup_d[k * 128:(k + 1) * 128, :], wr=[wub[k]])
        c.op("dve", lambda e, k=k: e.tensor_scalar(wup[:, k, :], wup[:, k, :], g2[:, k:k + 1], None, ALU.mult), rd=[wub[k], bg2], wr=[wub[k]])
    with ExitStack() as es:
        def T(name, shape, dt=F32):
            return es.enter_context(nc.sbuf_tensor("s_" + name, list(shape), dt)), Buf(name)
        idb, bidb = T("idb4", [128, 128], BF16); c.dma("pool", idb[:], ident_d, wr=[bidb])
        wba, bwba = T("wba", [128, 4, DM], BF16); wbb, bwbb = T("wbb", [128, 4, DM], BF16); wo, bwo = T("wo", [128, 8, DM], BF16)
        for k in range(4):
            c.dma("pool", wba[:, k, :], wba_d[k * 128:(k + 1) * 128, :], wr=[bwba])
            c.dma("pool", wbb[:, k, :], wbb_d[k * 128:(k + 1) * 128, :], wr=[bwbb])
        for k in range(8):
            c.dma("pool", wo[:, k, :], wout_d[k * 128:(k + 1) * 128, :], wr=[bwo])
        xts = [T("x4%d" % i, [128, DM]) for i in range(3)]
        gps = [T("gp%d" % i, [128, 2048]) for i in range(2)]
        yas = [T("ya4%d" % i, [128, 4, 128], BF16) for i in range(2)]
        ybs = [T("yb4%d" % i, [128, 4, 128], BF16) for i in range(2)]
        x1, bx1 = T("x1", [128, DM]); junk, bjunk = T("junk4", [128, DM]); h2, bh2 = T("h2", [128, DM], BF16)
        h2T, bh2T = T("h2T", [128, 8, 128], BF16)
        ss, bss = T("ss4", [128, NT]); c.op("dve", lambda e: e.memset(ss[:], 0.0), wr=[bss])
        rs, brs = T("rs4", [128, NT])
        yaFv = yaF.rearrange("(c p) t -> p c t", p=128); ybFv = ybF.rearrange("(c p) t -> p c t", p=128)
        h2Fv = h2F.rearrange("(c p) t -> p c t", p=128)

        def loads4(tt):
            i = tt % 2; t0 = tt * 128
            c.dma("sp", xts[tt % 3][0][:], x[t0:t0 + 128, :], wr=[xts[tt % 3][1]])
            c.dma("sp", gps[i][0][:], zT[t0:t0 + 128, 2048:4096], wr=[gps[i][1]])
            c.dma("sp", yas[i][0][:], yaFv[:, :, t0:t0 + 128], wr=[yas[i][1]])
            c.dma("sp", ybs[i][0][:], ybFv[:, :, t0:t0 + 128], wr=[ybs[i][1]])
        m1s = [T("m1_%d" % i, [128, DM]) for i in range(2)]; m2s = [T("m2_%d" % i, [128, DM]) for i in range(2)]
        mbs = [T("mb_%d" % i, [128, DM], BF16) for i in range(2)]; mTs = [T("mT_%d" % i, [128, 8, 128], BF16) for i in range(2)]

        def g1(tt):
            if tt + 1 < NT:
                loads4(tt + 1)
            i = tt % 2
            gp, bgp = gps[i]; ya, bya = yas[i]; yb, byb = ybs[i]
            m1, bm1 = m1s[i]; m2, bm2 = m2s[i]; mb, bmb = mbs[i]; mT, bmT = mTs[i]
            c.op("act", lambda e: e.activation(gp[:], gp[:], AF.Sigmoid), rd=[bgp], wr=[bgp])
            for half in range(2):
                hs = slice(half * 512, (half + 1) * 512)
                pa, bpa = ps()
                c.group("pe", [lambda e, k=k: e.matmul(pa[:], ya[:, k, :], wba[:, k, hs], start=(k == 0), stop=(k == 3)) for k in range(4)], rd=[bya, bwba], wr=[bpa])
                c.op("dve", lambda e: e.tensor_tensor(m1[:, hs], pa[:], gp[:, half * 512:(half + 1) * 512], ALU.mult), rd=[bpa, bgp], wr=[bm1])
                pb_, bpb_ = ps()
                c.group("pe", [lambda e, k=k: e.matmul(pb_[:], yb[:, k, :], wbb[:, k, hs], start=(k == 0), stop=(k == 3)) for k in range(4)], rd=[byb, bwbb], wr=[bpb_])
                c.op("dve", lambda e: e.tensor_tensor(m2[:, hs], pb_[:], gp[:, 1024 + half * 512:1024 + (half + 1) * 512], ALU.mult), rd=[bpb_, bgp], wr=[bm2])
            c.op("dve", lambda e: e.tensor_tensor(mb[:], m1[:], m2[:], ALU.add), rd=[bm1, bm2], wr=[bmb])
            yield
            p, bp = ps(); pb = p[:].bitcast(BF16)
            c.group("pe", [lambda e, k=k: e.transpose(pb[:, k * 128:(k + 1) * 128], mb[:, k * 128:(k + 1) * 128], idb[:]) for k in range(8)], rd=[bmb, bidb], wr=[bp])
            c.op("act", lambda e: e.copy(mT[:].rearrange("p a b -> p (a b)"), pb), rd=[bp], wr=[bmT])

        def g2(tt):
            i = tt % 2; t0 = tt * 128
            xt, bxt = xts[tt % 3]; mT, bmT = mTs[i]
            for half in range(2):
                hs = slice(half * 512, (half + 1) * 512)
                po, bpo = ps()
                c.group("pe", [lambda e, k=k: e.matmul(po[:], mT[:, k, :], wo[:, k, hs], start=(k == 0), stop=(k == 7)) for k in range(8)], rd=[bmT, bwo], wr=[bpo])
                c.op("dve", lambda e: e.tensor_tensor(x1[:, hs], po[:], xt[:, hs], ALU.add), rd=[bpo, bxt], wr=[bx1])
            c.dma("pool", x1s[t0:t0 + 128, :], x1[:], rd=[bx1])
            c.op("act", lambda e: e.activation(junk[:], x1[:], AF.Square, accum_out=ss[:, tt:tt + 1]), rd=[bx1], wr=[bjunk, bss])
            c.op("dve", lambda e: e.tensor_scalar(rs[:, tt:tt + 1], ss[:, tt:tt + 1], 1.0 / DM, 1e-6, ALU.mult, ALU.add), rd=[bss], wr=[brs])
            c.op("act", lambda e: e.activation(rs[:, tt:tt + 1], rs[:, tt:tt + 1], AF.Ln), rd=[brs], wr=[brs])
            c.op("act", lambda e: e.activation(rs[:, tt:tt + 1], rs[:, tt:tt + 1], AF.Exp, scale=-0.5), rd=[brs], wr=[brs])
            c.op("dve", lambda e: e.tensor_scalar(h2[:], x1[:], rs[:, tt:tt + 1], None, ALU.mult), rd=[bx1, brs], wr=[bh2])
            yield
            p, bp = ps(); pb = p[:].bitcast(BF16)
            c.group("pe", [lambda e, k=k: e.transpose(pb[:, k * 128:(k + 1) * 128], h2[:, k * 128:(k + 1) * 128], idb[:]) for k in range(8)], rd=[bh2, bidb], wr=[bp])
            c.op("act", lambda e: e.copy(h2T[:].rearrange("p a b -> p (a b)"), pb), rd=[bp], wr=[bh2T])
            c.dma("pool", h2Fv[:, :, t0:t0 + 128], h2T[:], rd=[bh2T])

        def run4(gens):
            alive = [g for g in gens if g is not None]
            while alive:
                for g in list(alive):
                    try:
                        next(g)
                    except StopIteration:
                        alive.remove(g)
        loads4(0)
        run4([g1(0)])
        for tt in range(NT):
            run4([g1(tt + 1) if tt + 1 < NT else None, g2(tt)])
        c.barrier()

    if upto <= 4:
        print("ninst", c.ninst, "nwait", c.nwait); return nc
    with ExitStack() as es:
        def T(name, shape, dt=F32):
            return es.enter_context(nc.sbuf_tensor("s_" + name, list(shape), dt)), Buf(name)
        wdn, bwdn = T("wdn", [128, NFF, DM], BF16)
        for f in range(NFF):
            c.dma("pool", wdn[:, f, :], wdn_d[f * 128:(f + 1) * 128, :], wr=[bwdn])
        cw, bcw = T("cw", [128, NFF, 3]); c.dma("sp", cw[:].rearrange("p a b -> p (a b)"), cw_d, wr=[bcw])
        cbt, bcbt = T("cbt", [128, NFF]); c.dma("sp", cbt[:], cb_d, wr=[bcbt])
        cr, bcr = T("cr", [128, NFF, 2]); c.op("dve", lambda e: e.memset(cr[:], 0.0), wr=[bcr])
        h2s = [T("h2s%d" % i, [128, 8, 512], BF16) for i in range(2)]
        asb = [T("asb%d" % i, [128, 514]) for i in range(2)]
        acc = [T("acc%d" % i, [128, 512]) for i in range(2)]
        hg = es.enter_context(nc.sbuf_tensor("hg", [128, NFF, 512], BF16)); bhg = [Buf("hg%d" % f) for f in range(NFF)]
        x1t = [T("x1t%d" % i, [128, DM]) for i in range(2)]
        ost = [T("ost%d" % i, [128, DM]) for i in range(2)]
        h2Fv = h2F.rearrange("(c p) t -> p c t", p=128)
        c.dma("sp", h2s[0][0][:], h2Fv[:, :, 0:512], wr=[h2s[0][1]])
        nx = 0
        for st in range(8):
            if st + 1 < 8:
                c.dma("sp", h2s[(st + 1) % 2][0][:], h2Fv[:, :, (st + 1) * 512:(st + 2) * 512], wr=[h2s[(st + 1) % 2][1]])
            hT, bhT = h2s[st % 2]
            for f in range(NFF):
                pa, bpa = ps()
                c.group("pe", [lambda e, k=k: e.matmul(pa[:], wup[:, k, f * 128:(f + 1) * 128], hT[:, k, :], start=(k == 0), stop=(k == 7)) for k in range(8)], rd=[bhT] + wub, wr=[bpa])
                pg, bpg = ps()
                c.group("pe", [lambda e, k=k: e.matmul(pg[:], wup[:, k, DFF + f * 128:DFF + (f + 1) * 128], hT[:, k, :], start=(k == 0), stop=(k == 7)) for k in range(8)], rd=[bhT] + wub, wr=[bpg])
                a, ba = asb[f % 2]; ac, bac = acc[f % 2]
                c.op("act", lambda e: e.copy(a[:, 0:2], cr[:, f, :]), rd=[bcr], wr=[ba])
                c.op("act", lambda e: e.copy(a[:, 2:514], pa[:]), rd=[bpa], wr=[ba])
                c.op("act", lambda e: e.copy(cr[:, f, :], a[:, 512:514]), rd=[ba], wr=[bcr])
                c.op("dve", lambda e: e.tensor_scalar(ac[:], a[:, 0:512], cw[:, f, 0:1], None, ALU.mult), rd=[ba, bcw], wr=[bac])
                c.op("dve", lambda e: e.scalar_tensor_tensor(ac[:], a[:, 1:513], cw[:, f, 1:2], ac[:], ALU.mult, ALU.add), rd=[ba, bcw, bac], wr=[bac])
                c.op("dve", lambda e: e.scalar_tensor_tensor(ac[:], a[:, 2:514], cw[:, f, 2:3], ac[:], ALU.mult, ALU.add), rd=[ba, bcw, bac], wr=[bac])
                c.op("act", lambda e: e.activation(ac[:], ac[:], AF.Gelu, bias=cbt[:, f:f + 1]), rd=[bac, bcbt], wr=[bac])
                c.op("dve", lambda e: e.tensor_tensor(hg[:, f, :], ac[:], pg[:], ALU.mult), rd=[bac, bpg], wr=[bhg[f]])
            for sub in range(4):
                tt = st * 4 + sub; t0 = tt * 128
                xx, bxx = x1t[nx % 2]; oo, boo = ost[nx % 2]; nx += 1
                c.dma("sp", xx[:], x1s[t0:t0 + 128, :], wr=[bxx])
                for half in range(2):
                    hs = slice(half * 512, (half + 1) * 512)
                    po, bpo = ps()
                    c.group("pe", [lambda e, f=f: e.matmul(po[:], hg[:, f, sub * 128:(sub + 1) * 128], wdn[:, f, hs], start=(f == 0), stop=(f == NFF - 1)) for f in range(NFF)],
                            rd=bhg + [bwdn], wr=[bpo])
                    c.op("dve", lambda e: e.tensor_tensor(oo[:, hs], po[:], xx[:, hs], ALU.add), rd=[bpo, bxx], wr=[boo])
                c.dma("pool", out[t0:t0 + 128, :], oo[:], rd=[boo])
        c.barrier()
    es_w.close()
    print("ninst", c.ninst, "nwait", c.nwait, {e: c.cnt[e] for e in c.cnt})
    return nc


def _consts():
    i = np.arange(128)
    su = (i[:, None] < i[None, :]).astype(np.float32)
    ui = (i[:, None] <= i[None, :]).astype(np.float32)
    mu2 = np.concatenate([su, ui, su, ui], axis=1)
    sl = (i[None, :] < i[:, None]).astype(np.float32)
    sl2 = np.concatenate([sl, sl], axis=1)
    bones = ((i[:, None] // 64) == (i[None, :] // 64)).astype(np.float32)
    q2 = np.arange(256)
    cb0 = np.where(i[:, None] > q2[None, :], -BIG, 0.0).astype(np.float32)
    cb1 = np.where(i[:, None] + 128 > q2[None, :], -BIG, 0.0).astype(np.float32)
    cbias2 = np.concatenate([cb0, cb1], axis=1)
    sel65 = np.zeros((65, 64), np.float32); sel65[64, :] = 1.0
    j = np.arange(16)
    pok = (j[None, :] < j[:, None]).astype(np.float32)
    pbias = ((pok - 1.0) * 1e30).astype(np.float32)
    half = 8
    invf = (500000.0 ** (-np.arange(half, dtype=np.float32) / half)).astype(np.float32) / np.float32(2.0 * math.pi)
    return dict(ident=np.eye(128, dtype=np.float32), mu2=mu2, sl2=sl2, bones=bones, cbias2=cbias2, sel65=sel65,
                pok=pok.reshape(1, 256), pbias=pbias.reshape(1, 256), invf=invf.reshape(1, 8).astype(np.float32))


def _fm(v, n):
    return np.ascontiguousarray(np.asarray(v, np.float32).reshape(n, 128).T)


def _prep(inp):
    f = lambda a: np.ascontiguousarray(np.asarray(a, dtype=np.float32))
    w_in = f(inp["w_in"][0])
    fm_cols = np.r_[0:512, 512:1024, 1536:1600, 1600:1664, 1664:1824]
    tm_cols = np.r_[1024:1536, 1824:3360, 3360:5408]
    mu = f(inp["rwkv_mu"][0])
    mu_fm = np.zeros(1408, np.float32); mu_fm[:FMC] = mu[fm_cols]
    rk = f(inp["rwkv_r_k"][0])
    rkb = np.zeros((128, 8), np.float32)
    for h in range(8):
        rkb[(h % 2) * 64:(h % 2 + 1) * 64, h] = rk[h]
    cw = f(inp["ffn_conv_w"][0])
    cwl = np.ascontiguousarray(cw.reshape(3, NFF, 128).transpose(2, 1, 0)).reshape(128, NFF * 3)
    shared = dict(
        w_in=np.ascontiguousarray(w_in[:, np.r_[fm_cols, tm_cols]]),
        n1g=_fm(inp["norm1_g"][0], 8), mu_fm=_fm(mu_fm, 11), mu_v=f(mu[1024:1536]).reshape(1, 512),
        wdec=np.concatenate([f(inp["w_decay_up"][0]), f(inp["decay_bias"][0]).reshape(1, 512)], 0),
        waaa=np.concatenate([f(inp["w_aaa_up"][0]), f(inp["aaa_bias"][0]).reshape(1, 512)], 0),
        wgate=f(inp["w_gate_up"][0]), kk_fm=_fm(inp["rwkv_k_k"][0], 4), ka_fm=_fm(inp["rwkv_k_a"][0], 4), rkb=rkb,
        lng=f(inp["rwkv_ln_g"][0]).reshape(1, 512), lnb=f(inp["rwkv_ln_b"][0]).reshape(1, 512),
        qkg=np.concatenate([np.tile(f(inp["q_norm_g"][0]), 8), np.tile(f(inp["k_norm_g"][0]), 8)]).reshape(1, 1024),
        wba=f(inp["w_branch_a"][0]), wbb=f(inp["w_branch_b"][0]), wout=f(inp["w_out"][0]), n2g=_fm(inp["norm2_g"][0], 8),
        wup=f(inp["w_ffn_up"][0]), cw=cwl, cb=_fm(inp["ffn_conv_b"][0], NFF), wdn=f(inp["w_ffn_down"][0]),
    )
    shared.update(_consts())
    xs = np.asarray(inp["x"], np.float32); ps_ = np.asarray(inp["positions"], np.int32)
    maps = []
    for b in range(8):
        m = dict(shared)
        m["x"] = np.ascontiguousarray(xs[b])
        m["pos"] = np.ascontiguousarray(ps_[b].reshape(NT, 128).T)
        maps.append(m)
    return maps


def kernel(**inputs):
    maps = _prep(inputs)
    nc = build()
    res = run_bass_kernel_spmd(nc, maps, core_ids=list(range(8)))
    return np.stack([np.asarray(r["out"], np.float32) for r in res.results], axis=0)
```
